# Optimizing a Trainium2 kernel written in Bass

```python
import jax
import jax.numpy as jnp
from jax import lax
import numpy as np

D_MODEL = 2048
BATCH = 2
SEQ = 4096
DEPTH = 4
DEC_BATCH = 8
DEC_SEQ = 4
PAST_LEN = 16384
PAGE_SIZE = 128

HEAD_DIM = 128
MIX_WIDTH = D_MODEL
MEM_LEN = 256
MEM_HEADS = 4
MEM_WIDTH = MEM_HEADS * HEAD_DIM
TOK_WIDTH = MIX_WIDTH - MEM_WIDTH
NSA_HEADS = TOK_WIDTH // HEAD_DIM
NSA_KV_GROUPS = 2
NSA_HPG = NSA_HEADS // NSA_KV_GROUPS
NSA_KV_SLOTS = 6
NSA_KV_WIDTH = NSA_KV_SLOTS * NSA_KV_GROUPS * HEAD_DIM
NSA_IN_WIDTH = TOK_WIDTH + NSA_KV_WIDTH + 3 * NSA_HEADS + MEM_WIDTH
CMP_STRIDE = 16
CMP_BLOCK = 2 * CMP_STRIDE
CMP_HIDDEN = HEAD_DIM
SEL_BLOCK = 64
N_SEL = 16
WINDOW = 512
QBLOCK = 128
POOL_WINDOWS = (2, 4, 8, 16)
POOL_GROUPS = len(POOL_WINDOWS)
POOL_GROUP_WIDTH = TOK_WIDTH // POOL_GROUPS
POOL_HIST = max(POOL_WINDOWS) - 1
POOL_IN_WIDTH = TOK_WIDTH + MEM_WIDTH
D_FF = 4 * D_MODEL
N_POOL_LAYERS = (DEPTH + 1) // 2
N_NSA_LAYERS = DEPTH // 2
ATTN_SCALE = HEAD_DIM ** -0.5
EPS = 1e-6
NEG = -1e30
BIG = 1e30

kernel_name = 'nsa_pool_hybrid_step'


def rms_norm(x, g):
    xf = x.astype(jnp.float32)
    y = xf * lax.rsqrt(jnp.mean(xf * xf, axis=-1, keepdims=True) + EPS)
    return (y * g.astype(jnp.float32)).astype(x.dtype)


def masked_softmax(s, mask):
    p = jax.nn.softmax(jnp.where(mask, s.astype(jnp.float32), NEG), axis=-1)
    return jnp.where(mask, p, 0.0)


def sq_relu_mlp(h, w_up, w_down):
    a = jax.nn.relu(h @ w_up)
    return (a * a) @ w_down


def memory_kv(mem, g_mem, w_kv, g_k):
    b, m, _ = mem.shape
    kv = (rms_norm(mem, g_mem) @ w_kv).reshape(b, m, 2, MEM_HEADS, HEAD_DIM)
    return jnp.stack([rms_norm(kv[:, :, 0], g_k), kv[:, :, 1]], axis=2)


def memory_attend(qm, mkv, g_q):
    b, t = qm.shape[:2]
    q = rms_norm(qm, g_q)
    s = jnp.einsum('bthd,bmhd->bhtm', q, mkv[:, :, 0]).astype(jnp.float32) * ATTN_SCALE
    p = jax.nn.softmax(s, axis=-1).astype(qm.dtype)
    return jnp.einsum('bhtm,bmhd->bthd', p, mkv[:, :, 1]).reshape(b, t, MEM_WIDTH)


def pool_token_mixer(u, pos0, w_group, scale):
    b, l, c = u.shape
    uf = u.astype(jnp.float32)
    cs = jnp.concatenate([jnp.zeros((b, 1, c), jnp.float32), jnp.cumsum(uf, axis=1)], axis=1)
    t = jnp.arange(l)
    groups = []
    for gi, w in enumerate(POOL_WINDOWS):
        c0, c1 = gi * POOL_GROUP_WIDTH, (gi + 1) * POOL_GROUP_WIDTH
        lo = jnp.maximum(t + 1 - w, 0)
        cnt = jnp.minimum(pos0 + t + 1, w).astype(jnp.float32)
        win_sum = cs[:, 1:, c0:c1] - cs[:, lo, c0:c1]
        groups.append(win_sum / cnt[None, :, None] - uf[:, :, c0:c1])
    pooled = jnp.stack(groups, axis=2).astype(u.dtype)
    mixed = jnp.einsum('blgc,gcd->blgd', pooled, w_group).reshape(b, l, c)
    return mixed * scale


def nsa_project(p, gate_bias, qk_gain):
    b, t, _ = p.shape
    q = rms_norm(p[..., :TOK_WIDTH].reshape(b, t, NSA_KV_GROUPS, NSA_HPG, HEAD_DIM), qk_gain[0])
    kv = p[..., TOK_WIDTH:TOK_WIDTH + NSA_KV_WIDTH].reshape(b, t, NSA_KV_SLOTS, NSA_KV_GROUPS, HEAD_DIM)
    kv = jnp.stack([kv[:, :, 0], kv[:, :, 1], rms_norm(kv[:, :, 2], qk_gain[2]),
                    kv[:, :, 3], rms_norm(kv[:, :, 4], qk_gain[3]), kv[:, :, 5]], axis=2)
    g0 = TOK_WIDTH + NSA_KV_WIDTH
    gates = jax.nn.sigmoid(p[..., g0:g0 + 3 * NSA_HEADS] + gate_bias).reshape(b, t, NSA_KV_GROUPS, NSA_HPG, 3)
    qm = p[..., g0 + 3 * NSA_HEADS:].reshape(b, t, MEM_HEADS, HEAD_DIM)
    return q, kv, gates, qm


def nsa_compress(rows, pos_enc, w1, w2):
    b, l, g, dh = rows.shape
    n_cmp = l // CMP_STRIDE
    half = jnp.concatenate([rows, jnp.zeros((b, CMP_STRIDE, g, dh), rows.dtype)], axis=1)
    half = half.reshape(b, n_cmp + 1, CMP_STRIDE, g, dh)
    blocks = jnp.concatenate([half[:, :-1], half[:, 1:]], axis=2) + pos_enc[None, None, :, None, :]
    flat = blocks.transpose(0, 1, 3, 2, 4).reshape(b, n_cmp, g, CMP_BLOCK * dh)
    return jax.nn.gelu(flat @ w1) @ w2


def nsa_global_branches(q, q_pos, rows, g_kc, pos_enc, w1, w2):
    b, tq = q.shape[:2]
    n_rows = rows.shape[1]
    n_cmp = n_rows // CMP_STRIDE
    n_blk = n_rows // SEL_BLOCK
    kc = rms_norm(nsa_compress(rows[:, :, 0], pos_enc[0], w1[0], w2[0]), g_kc)
    vc = nsa_compress(rows[:, :, 1], pos_enc[1], w1[1], w2[1])
    c_last = jnp.arange(n_cmp) * CMP_STRIDE + (CMP_BLOCK - 1)
    mask_c = c_last[None, :] <= q_pos[:, None]
    s_c = jnp.einsum('btgid,bcgd->bgitc', q, kc).astype(jnp.float32) * ATTN_SCALE
    p_c = masked_softmax(s_c, mask_c)
    o_cmp = jnp.einsum('bgitc,bcgd->btgid', p_c.astype(q.dtype), vc)
    imp = jnp.sum(p_c, axis=2)
    imp = imp + jnp.pad(imp[..., :-1], ((0, 0), (0, 0), (0, 0), (1, 0)))
    p_s = imp.reshape(b, NSA_KV_GROUPS, tq, n_blk, SEL_BLOCK // CMP_STRIDE).sum(-1)
    j = jnp.arange(n_blk)[None, :]
    cur = (q_pos // SEL_BLOCK)[:, None]
    forced = (j == 0) | (j == cur) | (j == cur - 1)
    score = jnp.where(forced, BIG, jnp.where(j <= cur, p_s, NEG))
    n_sel = min(N_SEL, n_blk)
    top_val, top_idx = lax.top_k(score, n_sel)
    top_ok = top_val > 0.5 * NEG
    ks = rows[:, :, 2].reshape(b, n_blk, SEL_BLOCK, NSA_KV_GROUPS, HEAD_DIM).transpose(0, 3, 1, 2, 4)
    vs = rows[:, :, 3].reshape(b, n_blk, SEL_BLOCK, NSA_KV_GROUPS, HEAD_DIM).transpose(0, 3, 1, 2, 4)
    qb = min(QBLOCK, tq)
    nq = tq // qb
    bi = jnp.arange(b)[:, None, None, None]
    gi = jnp.arange(NSA_KV_GROUPS)[None, :, None, None]
    offs = jnp.arange(SEL_BLOCK)

    def one_block(args):
        q_b, pos_b, idx_b, ok_b = args
        kg = ks[bi, gi, idx_b]
        vg = vs[bi, gi, idx_b]
        kpos = idx_b[..., None] * SEL_BLOCK + offs
        mask = ok_b[..., None] & (kpos <= pos_b[None, None, :, None, None])
        s = jnp.einsum('bqgid,bgqnkd->bgiqnk', q_b, kg).astype(jnp.float32) * ATTN_SCALE
        s = s.reshape(b, NSA_KV_GROUPS, NSA_HPG, qb, n_sel * SEL_BLOCK)
        p = masked_softmax(s, mask.reshape(b, NSA_KV_GROUPS, 1, qb, n_sel * SEL_BLOCK))
        p = p.reshape(b, NSA_KV_GROUPS, NSA_HPG, qb, n_sel, SEL_BLOCK).astype(q_b.dtype)
        return jnp.einsum('bgiqnk,bgqnkd->bqgid', p, vg)

    xs = (q.reshape(b, nq, qb, NSA_KV_GROUPS, NSA_HPG, HEAD_DIM).swapaxes(0, 1),
          q_pos.reshape(nq, qb),
          top_idx.reshape(b, NSA_KV_GROUPS, nq, qb, n_sel).transpose(2, 0, 1, 3, 4),
          top_ok.reshape(b, NSA_KV_GROUPS, nq, qb, n_sel).transpose(2, 0, 1, 3, 4))
    o_slc = lax.map(one_block, xs).swapaxes(0, 1).reshape(q.shape)
    return o_cmp, o_slc


def nsa_window(q, q_pos, k_ext, v_ext, k_pos0):
    b, tq = q.shape[:2]
    n_hist = k_ext.shape[1] - tq
    qb = min(QBLOCK, tq)
    nq = tq // qb
    span = n_hist + qb
    offs = jnp.arange(span)

    def one_block(args):
        blk, q_b, pos_b = args
        start = blk * qb
        kb = lax.dynamic_slice_in_dim(k_ext, start, span, axis=1)
        vb = lax.dynamic_slice_in_dim(v_ext, start, span, axis=1)
        kpos = k_pos0 + start + offs
        d = pos_b[:, None] - kpos[None, :]
        mask = (kpos[None, :] >= 0) & (d >= 0) & (d < WINDOW)
        s = jnp.einsum('bqgid,bkgd->bgiqk', q_b, kb).astype(jnp.float32) * ATTN_SCALE
        p = masked_softmax(s, mask).astype(q_b.dtype)
        return jnp.einsum('bgiqk,bkgd->bqgid', p, vb)

    xs = (jnp.arange(nq),
          q.reshape(b, nq, qb, NSA_KV_GROUPS, NSA_HPG, HEAD_DIM).swapaxes(0, 1),
          q_pos.reshape(nq, qb))
    return lax.map(one_block, xs).swapaxes(0, 1).reshape(q.shape)


def nsa_combine(gates, o_cmp, o_slc, o_win):
    b, t = o_cmp.shape[:2]
    o = gates[..., 0:1] * o_cmp + gates[..., 1:2] * o_slc + gates[..., 2:3] * o_win
    return o.reshape(b, t, TOK_WIDTH)


def setup_inputs(seed: int = 0) -> dict:
    key = jax.random.key(seed)
    ks = jax.random.split(key, 32)
    f32 = jnp.float32
    n_pages = PAST_LEN // PAGE_SIZE
    n_phys = (DEC_BATCH * n_pages * 5) // 4
    win_buf = min(WINDOW, PAST_LEN)

    def nrm(k, shape, scale=1.0):
        return jax.random.normal(k, shape, f32) * scale

    def gain(k, shape):
        return 1.0 + 0.05 * nrm(k, shape)

    page_table = jax.random.permutation(ks[7], n_phys)[:DEC_BATCH * n_pages]
    page_table = page_table.reshape(DEC_BATCH, n_pages).astype(jnp.int32)
    return {
        'x_prompt': nrm(ks[0], (BATCH, SEQ, D_MODEL)),
        'x_sample': nrm(ks[1], (DEC_BATCH, DEC_SEQ, D_MODEL)),
        'mem_prompt': nrm(ks[2], (BATCH, MEM_LEN, D_MODEL)),
        'cache_mem_kv': nrm(ks[3], (DEPTH, DEC_BATCH, MEM_LEN, 2, MEM_HEADS, HEAD_DIM)),
        'cache_nsa_kv': nrm(ks[4], (N_NSA_LAYERS, n_phys, PAGE_SIZE, 4, NSA_KV_GROUPS, HEAD_DIM)),
        'cache_nsa_win': nrm(ks[5], (N_NSA_LAYERS, DEC_BATCH, win_buf, 2, NSA_KV_GROUPS, HEAD_DIM)),
        'state_pool': nrm(ks[6], (N_POOL_LAYERS, DEC_BATCH, POOL_HIST, TOK_WIDTH)),
        'page_table': page_table,
        'g_norm_mix': gain(ks[8], (DEPTH, D_MODEL)),
        'g_norm_mlp': gain(ks[9], (DEPTH, D_MODEL)),
        'g_norm_mem': gain(ks[10], (DEPTH, D_MODEL)),
        'w_mem_kv': nrm(ks[11], (DEPTH, D_MODEL, 2 * MEM_WIDTH), D_MODEL ** -0.5),
        'mem_qk_gain': gain(ks[12], (DEPTH, 2, HEAD_DIM)),
        'w_out': nrm(ks[13], (DEPTH, MIX_WIDTH, D_MODEL), MIX_WIDTH ** -0.5),
        'w_mlp_up': nrm(ks[14], (DEPTH, D_MODEL, D_FF), D_MODEL ** -0.5),
        'w_mlp_down': nrm(ks[15], (DEPTH, D_FF, D_MODEL), D_FF ** -0.5),
        'w_in_pool': nrm(ks[16], (N_POOL_LAYERS, D_MODEL, POOL_IN_WIDTH), D_MODEL ** -0.5),
        'w_pool_group': nrm(ks[17], (N_POOL_LAYERS, POOL_GROUPS, POOL_GROUP_WIDTH, POOL_GROUP_WIDTH),
                            POOL_GROUP_WIDTH ** -0.5),
        'pool_scale': gain(ks[18], (N_POOL_LAYERS, TOK_WIDTH)),
        'w_in_nsa': nrm(ks[19], (N_NSA_LAYERS, D_MODEL, NSA_IN_WIDTH), D_MODEL ** -0.5),
        'nsa_gate_bias': nrm(ks[20], (N_NSA_LAYERS, 3 * NSA_HEADS), 0.1),
        'nsa_qk_gain': gain(ks[21], (N_NSA_LAYERS, 4, HEAD_DIM)),
        'cmp_pos': nrm(ks[22], (N_NSA_LAYERS, 2, CMP_BLOCK, HEAD_DIM), 0.1),
        'cmp_w1': nrm(ks[23], (N_NSA_LAYERS, 2, CMP_BLOCK * HEAD_DIM, CMP_HIDDEN), (CMP_BLOCK * HEAD_DIM) ** -0.5),
        'cmp_w2': nrm(ks[24], (N_NSA_LAYERS, 2, CMP_HIDDEN, HEAD_DIM), CMP_HIDDEN ** -0.5),
    }


def reference(x_prompt, x_sample, mem_prompt, cache_mem_kv, cache_nsa_kv, cache_nsa_win, state_pool,
              page_table, g_norm_mix, g_norm_mlp, g_norm_mem, w_mem_kv, mem_qk_gain, w_out, w_mlp_up,
              w_mlp_down, w_in_pool, w_pool_group, pool_scale, w_in_nsa, nsa_gate_bias, nsa_qk_gain,
              cmp_pos, cmp_w1, cmp_w2):
    past_len = page_table.shape[1] * cache_nsa_kv.shape[2]
    win_buf = cache_nsa_win.shape[2]
    bp, tp = x_prompt.shape[:2]
    db, ts = x_sample.shape[:2]
    pos_p = jnp.arange(tp)
    pos_s = past_len + jnp.arange(ts)
    xp, xs = x_prompt, x_sample
    mem_kv_p, nsa_kv_p, nsa_kv_s, win_p, win_s, pool_p, pool_s = [], [], [], [], [], [], []
    for i in range(DEPTH):
        li = i // 2
        mkv_p = memory_kv(mem_prompt, g_norm_mem[i], w_mem_kv[i], mem_qk_gain[i, 1])
        mem_kv_p.append(mkv_p)
        hp = rms_norm(xp, g_norm_mix[i])
        hs = rms_norm(xs, g_norm_mix[i])
        if i % 2 == 0:
            pp = hp @ w_in_pool[li]
            ps = hs @ w_in_pool[li]
            up, qmp = pp[..., :TOK_WIDTH], pp[..., TOK_WIDTH:].reshape(bp, tp, MEM_HEADS, HEAD_DIM)
            us, qms = ps[..., :TOK_WIDTH], ps[..., TOK_WIDTH:].reshape(db, ts, MEM_HEADS, HEAD_DIM)
            tok_p = pool_token_mixer(up, 0, w_pool_group[li], pool_scale[li])
            ext = jnp.concatenate([state_pool[li], us], axis=1)
            tok_s = pool_token_mixer(ext, past_len - POOL_HIST, w_pool_group[li], pool_scale[li])[:, POOL_HIST:]
            pool_p.append(up[:, -POOL_HIST:])
            pool_s.append(ext[:, -POOL_HIST:])
        else:
            q_p, kv_p, gt_p, qmp = nsa_project(hp @ w_in_nsa[li], nsa_gate_bias[li], nsa_qk_gain[li])
            q_s, kv_s, gt_s, qms = nsa_project(hs @ w_in_nsa[li], nsa_gate_bias[li], nsa_qk_gain[li])
            oc_p, os_p = nsa_global_branches(q_p, pos_p, kv_p[:, :, :4], nsa_qk_gain[li, 1],
                                             cmp_pos[li], cmp_w1[li], cmp_w2[li])
            wext_p = jnp.concatenate([jnp.zeros((bp, WINDOW, 2, NSA_KV_GROUPS, HEAD_DIM), kv_p.dtype),
                                      kv_p[:, :, 4:]], axis=1)
            ow_p = nsa_window(q_p, pos_p, wext_p[:, :, 0], wext_p[:, :, 1], -WINDOW)
            tok_p = nsa_combine(gt_p, oc_p, os_p, ow_p)
            past = cache_nsa_kv[li, page_table].reshape(db, past_len, 4, NSA_KV_GROUPS, HEAD_DIM)
            n_pad = (-(past_len + ts)) % SEL_BLOCK
            rows = jnp.concatenate([past, kv_s[:, :, :4],
                                    jnp.zeros((db, n_pad, 4, NSA_KV_GROUPS, HEAD_DIM), kv_s.dtype)], axis=1)
            oc_s, os_s = nsa_global_branches(q_s, pos_s, rows, nsa_qk_gain[li, 1],
                                             cmp_pos[li], cmp_w1[li], cmp_w2[li])
            wext_s = jnp.concatenate([cache_nsa_win[li], kv_s[:, :, 4:]], axis=1)
            ow_s = nsa_window(q_s, pos_s, wext_s[:, :, 0], wext_s[:, :, 1], past_len - win_buf)
            tok_s = nsa_combine(gt_s, oc_s, os_s, ow_s)
            nsa_kv_p.append(kv_p[:, :, :4])
            nsa_kv_s.append(kv_s[:, :, :4])
            win_p.append(kv_p[:, -min(WINDOW, tp):, 4:])
            win_s.append(wext_s[:, -win_buf:])
        mem_p = memory_attend(qmp, mkv_p, mem_qk_gain[i, 0])
        mem_s = memory_attend(qms, cache_mem_kv[i], mem_qk_gain[i, 0])
        xp = xp + jnp.concatenate([tok_p, mem_p], axis=-1) @ w_out[i]
        xs = xs + jnp.concatenate([tok_s, mem_s], axis=-1) @ w_out[i]
        xp = xp + sq_relu_mlp(rms_norm(xp, g_norm_mlp[i]), w_mlp_up[i], w_mlp_down[i])
        xs = xs + sq_relu_mlp(rms_norm(xs, g_norm_mlp[i]), w_mlp_up[i], w_mlp_down[i])
    return (xp, xs, jnp.stack(mem_kv_p), jnp.stack(nsa_kv_p), jnp.stack(nsa_kv_s),
            jnp.stack(win_p), jnp.stack(win_s), jnp.stack(pool_p), jnp.stack(pool_s))
```

```python
import numpy as np
from contextlib import ExitStack
import concourse.bass as bass
import concourse.mybir as mybir
from concourse.bass_utils import run_bass_kernel_spmd

F32 = mybir.dt.float32
BF16 = mybir.dt.bfloat16
I32 = mybir.dt.int32
AF = mybir.ActivationFunctionType
ALU = mybir.AluOpType
AX = mybir.AxisListType

D = 2048
KC = 16
SEQ = 4096
TP = 512
NSB = 4
NS1 = 4
NS = NSB * NS1
NCORE = 2
NCH = SEQ // TP
DEPTH = 4
HD = 128
TOK = 1536
DFF = 8192
EPS = 1e-6
SCALE = HD ** -0.5
NSA_W = 3620
PAST = 16384
NPAGE = 128
NEGV = -1e30
BIGV = 1e30

CFG = dict(n_layers=4, n_chunks=8)


class TT:
    __slots__ = ("name", "wr", "rd")

    def __init__(self, name):
        self.name = name
        self.wr = None
        self.rd = {}


class Sched:
    ENGS = ("pe", "act", "dve", "pool", "sp")

    def __init__(self):
        self.ops = {e: [] for e in self.ENGS}
        self.cnt = {e: 0 for e in self.ENGS}
        self.known = {e: {} for e in self.ENGS}
        self.dcnt = {}

    def _waits(self, eng, reads, writes, pe_chain=False):
        deps = {}

        def add(s, v):
            if pe_chain and s == "pe" and eng == "pe":
                return
            if deps.get(s, 0) < v:
                deps[s] = v

        for r in reads:
            if r.wr is not None:
                add(*r.wr)
        for w in writes:
            if w.wr is not None:
                add(*w.wr)
            for s, v in w.rd.items():
                add(s, v)
        out = []
        kn = self.known[eng]
        for s, v in deps.items():
            if kn.get(s, 0) < v:
                kn[s] = v
                out.append((s, v))
        return out

    def op(self, eng, fn, reads=(), writes=(), pe_chain=False):
        waits = self._waits(eng, reads, writes, pe_chain)
        self.cnt[eng] += 1
        tok = (eng, self.cnt[eng])
        self.ops[eng].append((waits, fn, tok, 1))
        for r in reads:
            if r.rd.get(eng, 0) < tok[1]:
                r.rd[eng] = tok[1]
        for w in writes:
            w.wr = tok
            w.rd = {}

    def dma(self, queue, fn, sem, reads=(), writes=()):
        waits = self._waits(queue, reads, writes)
        self.dcnt[sem] = self.dcnt.get(sem, 0) + 16
        tok = (sem, self.dcnt[sem])
        self.ops[queue].append((waits, fn, tok, 16))
        for r in reads:
            if r.rd.get(sem, 0) < tok[1]:
                r.rd[sem] = tok[1]
        for w in writes:
            w.wr = tok
            w.rd = {}

    def final_waits(self):
        allv = dict(self.cnt)
        allv.update(self.dcnt)
        waits = [(s, v) for s, v in allv.items() if v > 0 and s != "sp"]
        self.ops["sp"].append((waits, None, None, 0))


class Reg:
    def __init__(self, name, handle, nbytes, gran):
        self.name = name
        self.h = handle
        self.nbytes = nbytes
        self.gran = gran
        self.tts = [TT(f"{name}{i}") for i in range((nbytes + gran - 1) // gran)]

    def t(self, b0, b1):
        return self.tts[b0 // self.gran:(b1 - 1) // self.gran + 1]

    def view(self, dtype, b0, nelem):
        esz = 4 if dtype in (F32, I32) else 2
        assert b0 % 4 == 0 and (nelem * esz) % 4 == 0
        ap = self.h[:, b0 // 4:(b0 + nelem * esz) // 4]
        if esz == 2:
            ap = ap.bitcast(dtype)
        elif dtype == I32:
            ap = ap.bitcast(I32)
        return ap


class Prog:
    def __init__(self, cfg):
        self.cfg = cfg
        self.nc = bass.Bass("TRN2", target_bir_lowering=False)
        self.s = Sched()
        self.es = ExitStack()
        self.psi = 0
        self.evi = 0
        self.sems_needed = set()

    def dram_in(self, name, shape, dt=F32):
        return self.nc.dram_tensor(name, list(shape), dt, kind="ExternalInput").ap()

    def dram_out(self, name, shape, dt=F32):
        return self.nc.dram_tensor(name, list(shape), dt, kind="ExternalOutput").ap()

    def dram_tmp(self, name, shape, dt=F32):
        return self.nc.dram_tensor(name, list(shape), dt, kind="Internal").ap()

    def sb(self, name, shape, dt=F32):
        return self.es.enter_context(self.nc.sbuf_tensor(name, list(shape), dt))

    def reg(self, name, nbytes, gran):
        h = self.sb(name, [128, nbytes // 4], F32)
        return Reg(name, h, nbytes, gran)

    def op(self, eng, fn, reads=(), writes=(), pe_chain=False):
        self.s.op(eng, fn, reads, writes, pe_chain)

    def dma(self, queue, fn, sem, reads=(), writes=()):
        self.sems_needed.add(sem)
        self.s.dma(queue, fn, sem, reads, writes)

    def I(self, eng, method, reads, writes, *args, **kw):
        self.s.op(eng, lambda e: getattr(e, method)(*args, **kw), reads, writes)

    def Dm(self, queue, sem, reads, writes, **kw):
        self.sems_needed.add(sem)
        self.s.dma(queue, lambda e: e.dma_start(**kw), sem, reads, writes)

    def idma(self, sem, reads, writes, out, in_, idx_ap):
        self.sems_needed.add(sem)
        self.s.dma("pool", lambda e: e.indirect_dma_start(out=out, out_offset=None, in_=in_,
                                                          in_offset=bass.IndirectOffsetOnAxis(ap=idx_ap, axis=0)), sem, reads, writes)

    def mm(self, ps_tt, out, lhsT, rhs, start, stop, reads):
        self.op("pe", lambda e: e.matmul(out, lhsT=lhsT, rhs=rhs, start=start, stop=stop),
                reads=reads, writes=[ps_tt], pe_chain=not start)

    def tr(self, ps_tt, out, in_, ident, reads):
        self.op("pe", lambda e: e.transpose(out=out, in_=in_, identity=ident), reads=reads, writes=[ps_tt])

    def next_ps(self, lo=0, hi=4):
        i = lo + self.psi % (hi - lo)
        self.psi += 1
        return i

    def ev_eng(self):
        self.evi += 1
        return "act" if self.evi % 2 else "dve"

    def copy(self, eng, out, in_, reads, writes):
        if eng == "act":
            self.op("act", lambda e: e.copy(out=out, in_=in_), reads, writes)
        elif eng == "dve":
            self.op("dve", lambda e: e.tensor_copy(out=out, in_=in_), reads, writes)
        else:
            self.op("pool", lambda e: e.tensor_copy(out=out, in_=in_), reads, writes)

    def build(self):
        nc = self.nc
        cfg = self.cfg
        P = self
        d = {}
        d["xp"] = P.dram_in("xp", [SEQ, D])
        d["xs"] = P.dram_in("xs", [NS, D])
        d["memp"] = P.dram_in("memp", [256, D])
        NL = cfg["n_layers"]; NPL = (NL + 1) // 2; NNL = max(NL // 2, 1)
        d["cmkv"] = P.dram_in("cmkv", [NL, NSB, 256, 1024])
        d["nsakv"] = [P.dram_in(f"nsakv{l_}", [2 * cfg.get("nsakv_rows", 1280 * 128), 512]) for l_ in range(NNL)]
        d["nsawin"] = P.dram_in("nsawin", [2, NSB, 512, 512])
        d["spool"] = P.dram_in("spool", [2, NSB, 15, TOK])
        d["ptab"] = P.dram_in("ptab", [NSB, NPAGE], I32)
        d["g_mix"] = P.dram_in("g_mix", [DEPTH, D])
        d["g_mlp"] = P.dram_in("g_mlp", [DEPTH, D])
        d["g_mem"] = P.dram_in("g_mem", [DEPTH, D])
        d["w_mem_kv"] = P.dram_in("w_mem_kv", [NL, D, 1024])
        d["mem_qk"] = P.dram_in("mem_qk", [DEPTH, 2, HD])
        d["w_out"] = P.dram_in("w_out", [NL, D, D])
        d["w_up"] = P.dram_in("w_up", [NL, D, DFF])
        d["w_down"] = P.dram_in("w_down", [NL, DFF, D])
        d["w_in_pool"] = P.dram_in("w_in_pool", [NPL, D, D])
        d["w_pg"] = P.dram_in("w_pg", [2, 4, 384, 384])
        d["pool_scale"] = P.dram_in("pool_scale", [2, TOK])
        d["w_in_nsa"] = P.dram_in("w_in_nsa", [NNL, D, NSA_W])
        d["gate_bias"] = P.dram_in("gate_bias", [2, 36])
        d["nsa_qk"] = P.dram_in("nsa_qk", [2, 4, HD])
        d["cmp_pos"] = P.dram_in("cmp_pos", [2, 2, 32, HD])
        d["cmp_w1"] = P.dram_in("cmp_w1", [2, 2, 4096, HD])
        d["cmp_w2"] = P.dram_in("cmp_w2", [2, 2, HD, HD])
        d["cst"] = P.dram_in("cst", [128, 64])
        o = {}
        o["y_p"] = P.dram_out("y_p", [SEQ, D])
        o["y_s"] = P.dram_out("y_s", [NS, D])
        o["mkv"] = P.dram_out("mkv", [DEPTH, 256, 1024])
        o["nkv_p"] = P.dram_out("nkv_p", [2, SEQ, 1024])
        o["nkv_s"] = P.dram_out("nkv_s", [2, NS, 1024])
        o["win_p"] = P.dram_out("win_p", [2, 512, 512])
        o["win_s"] = P.dram_out("win_s", [2, NSB, 512, 512])
        o["pool_p"] = P.dram_out("pool_p", [2, 15, TOK])
        o["pool_s"] = P.dram_out("pool_s", [2, NSB, 15, TOK])
        self.d, self.o = d, o
        memhat_d = P.dram_tmp("memhat_d", [128, KC, 256])
        memhat_tt = TT("memhat_d")

        TW = TP + NS
        self.TW = TW
        xT = P.sb("xT", [128, KC, TW], F32)
        x_tt = [TT(f"x{k}") for k in range(KC)]
        hreg = P.reg("hreg", KC * TW * 2, TW * 2)
        r1 = P.reg("r1", 12 * TW * 4, TW * 2)
        creg = P.reg("creg", KC * TW * 2, TW * 2)
        wb = [P.sb(f"wb{i}", [128, 4096], BF16) for i in range(2)]
        wb_tt = [TT(f"wb{i}") for i in range(2)]
        self.wbi = 0
        psb = [self.es.enter_context(nc.psum_tensor(f"ps{i}", [128, 512], F32)) for i in range(8)]
        ps_tt = [TT(f"ps{i}") for i in range(8)]
        ident = P.sb("ident", [128, 128], F32)
        identb = P.sb("identb", [128, 128], BF16)
        ones = P.sb("ones", [128, 128], F32)
        onesb = P.sb("onesb", [128, 128], BF16)
        c_tt = TT("consts")
        vcol = P.sb("vcol", [128, 256], F32)
        vcol_tt = TT("vcol")
        gkb = P.sb("gkb", [128, DEPTH, HD], F32)
        invc = P.sb("invc", [128, 4, 16], F32)
        rstd = P.sb("rstd", [128, TW], F32)
        rstd_tt = TT("rstd")
        sq = [P.sb(f"sq{i}", [128, 512], F32) for i in range(2)]
        sq_tt = [TT(f"sq{i}") for i in range(2)]
        stg = [P.sb(f"stg{i}", [128, 1024], F32) for i in range(2)]
        stg_tt = [TT(f"stg{i}") for i in range(2)]
        self.stgi = 0
        small = P.sb("small", [128, 64], F32)
        small_tt = TT("small")
        carry = P.sb("carry", [128, 2, 12, 16], F32)
        carry_tt = [TT("carry0"), TT("carry1")]
        kmT = P.sb("kmT", [128, 4, 256], BF16)
        vm = P.sb("vm", [128, 2, 4, HD], BF16)
        qns = P.sb("qns", [128, 4, NS], BF16)
        qns_tt = TT("qns")
        sthist = P.sb("sthist", [128, NSB, 12, 16], F32)
        sthist_tt = TT("sthist")
        km_tt, vm_tt = TT("kmT"), TT("vm")
        pt = [P.sb(f"pt{i}", [128, 512], BF16) for i in range(2)] + [P.sb(f"pt{i}", [128, 384], BF16) for i in range(2, 4)]
        pt_tt = [TT(f"pt{i}") for i in range(4)]
        qn = P.sb("qn", [128, TW], BF16)
        qn_tt = TT("qn")
        qraw = P.sb("qraw", [128, TW], F32)
        qraw_tt = TT("qraw")

        P.I("pool", "memset", [], [c_tt], ident[:], 0.0)
        P.I("pool", "affine_select", [c_tt], [c_tt], out=ident[:], in_=ident[:], pattern=[[-1, 128]],
            compare_op=ALU.not_equal, fill=1.0, base=0, channel_multiplier=1)
        P.I("pool", "memset", [c_tt], [c_tt], ones[:], 1.0)
        P.I("dve", "tensor_copy", [c_tt], [c_tt], out=identb[:], in_=ident[:])
        P.I("dve", "tensor_copy", [c_tt], [c_tt], out=onesb[:], in_=ones[:])
        P.Dm("sp", "s_cst", [], [c_tt], out=invc[:].rearrange("p a b -> p (a b)"), in_=d["cst"])
        for i4 in range(DEPTH):
            P.Dm("sp", "s_cst", [], [c_tt], out=gkb[:, i4, :], in_=d["mem_qk"][i4, 1, :].partition_broadcast(128))
        vst = stg[0]
        P.Dm("sp", "s_stg0", [], [stg_tt[0]], out=vst[0:64, 0:128], in_=d["g_mix"].rearrange("i (k p) -> (i k) p", p=128))
        P.Dm("sp", "s_stg0", [], [stg_tt[0]], out=vst[64:128, 0:128], in_=d["g_mlp"].rearrange("i (k p) -> (i k) p", p=128))
        P.Dm("sp", "s_stg0", [], [stg_tt[0]], out=vst[0:64, 128:256], in_=d["g_mem"].rearrange("i (k p) -> (i k) p", p=128))
        P.Dm("sp", "s_stg0", [], [stg_tt[0]], out=vst[64:88, 128:256], in_=d["pool_scale"].rearrange("i (k p) -> (i k) p", p=128))
        P.Dm("sp", "s_stg0", [], [stg_tt[0]], out=vst[88:96, 128:256], in_=d["mem_qk"].rearrange("i a p -> (i a) p"))
        P.Dm("sp", "s_stg0", [], [stg_tt[0]], out=vst[96:104, 128:256], in_=d["nsa_qk"].rearrange("i a p -> (i a) p"))
        P.tr(ps_tt[0], psb[0][:, 0:128], vst[:, 0:128], ident[:], reads=[stg_tt[0], c_tt])
        P.tr(ps_tt[1], psb[1][:, 0:104], vst[0:104, 128:256], ident[0:104, 0:104], reads=[stg_tt[0], c_tt])
        P.I("dve", "tensor_copy", [ps_tt[0]], [vcol_tt], out=vcol[:, 0:128], in_=psb[0][:, 0:128])
        P.I("dve", "tensor_copy", [ps_tt[1]], [vcol_tt], out=vcol[:, 128:232], in_=psb[1][:, 0:104])

        def gcol(kind, i, k=0):
            base = {"mix": 0, "mlp": 64, "mem": 128}.get(kind)
            if base is not None:
                c = base + i * 16 + k
            elif kind == "pscale":
                c = 128 + 64 + i * 12 + k
            elif kind == "memqk":
                c = 128 + 88 + i * 2 + k
            elif kind == "nsaqk":
                c = 128 + 96 + i * 4 + k
            return vcol[:, c:c + 1]
        self.gcol = gcol

        def rsqrt_inplace(ap, tts):
            P.I("dve", "reciprocal", tts, tts, out=ap, in_=ap)
            P.I("act", "activation", tts, tts, out=ap, in_=ap, func=AF.Sqrt)
        self.rsqrt_inplace = rsqrt_inplace

        r1_mem = r1.t(0, 2 * D * 4)
        memtok = r1.view(F32, 0, 2 * D).rearrange("p (a b) -> p a b", a=2)
        P.Dm("sp", "s_r1", [], r1_mem, out=memtok, in_=d["memp"].rearrange("(a p) f -> p a f", p=128))
        for mc in range(2):
            for hf in range(2):
                P.I("act", "activation", r1_mem, [stg_tt[1], small_tt], out=stg[1][:, 0:1024],
                    in_=memtok[:, mc, hf * 1024:(hf + 1) * 1024], func=AF.Square, accum_out=small[:, 2 * mc + hf:2 * mc + hf + 1])
        for mc in range(2):
            P.I("dve", "tensor_tensor", [small_tt], [small_tt], out=small[:, 4 + mc:5 + mc], in0=small[:, 2 * mc:2 * mc + 1],
                in1=small[:, 2 * mc + 1:2 * mc + 2], op=ALU.add)
            P.I("dve", "tensor_scalar", [small_tt], [small_tt], out=small[:, 4 + mc:5 + mc], in0=small[:, 4 + mc:5 + mc],
                scalar1=1.0 / D, scalar2=EPS, op0=ALU.mult, op1=ALU.add)
            rsqrt_inplace(small[:, 4 + mc:5 + mc], [small_tt])
            P.I("dve", "tensor_scalar", [small_tt] + r1_mem, r1_mem, out=memtok[:, mc, :], in0=memtok[:, mc, :],
                scalar1=small[:, 4 + mc:5 + mc], scalar2=None, op0=ALU.mult)
        mh_sb = hreg.view(F32, 0, 8 * 256).rearrange("p (k m) -> p k m", k=8)
        for half in range(2):
            for kk in range(8):
                k = half * 8 + kk
                for mc in range(2):
                    b = P.next_ps()
                    P.tr(ps_tt[b], psb[b][:, 0:128], memtok[:, mc, k * 128:(k + 1) * 128], ident[:], reads=r1_mem + [c_tt])
                    P.copy(P.ev_eng(), mh_sb[:, kk, mc * 128:(mc + 1) * 128], psb[b][:, 0:128], reads=[ps_tt[b]], writes=hreg.t(0, 8192))
            P.Dm("sp", "s_hreg", hreg.t(0, 8192), [memhat_tt], out=memhat_d[:, half * 8:(half + 1) * 8, :], in_=mh_sb)
        P.I("pool", "memset", [], carry_tt, carry[:].rearrange("p a b c -> p (a b c)"), 0.0)

        def ntiles(T):
            t = [(n0, 512) for n0 in range(0, TP, 512)]
            if T > TP:
                t.append((TP, T - TP))
            return t
        self.ntiles = ntiles

        NWSCR = 420
        WSL = 105
        wscr_l = [P.dram_tmp(f"wscr{i_}", [WSL, 128, 4096], BF16) for i_ in range(NWSCR // WSL)]

        class _W:
            def __getitem__(self, idx):
                slot = idx[0]
                return wscr_l[slot // WSL][(slot % WSL,) + tuple(idx[1:])]
        wscr = _W()
        self.wcache = {}
        self.wready = {}

        def load_w(src_ap, kc, ncols, key=None):
            i = self.wbi % 2
            self.wbi += 1
            n = kc * ncols
            view = wb[i][:, 0:n].rearrange("p (k m) -> p k m", k=kc)
            if key is not None and key in self.wready:
                slot, tt = self.wready[key]
                P.Dm("sp", f"s_wb{i}", [tt], [wb_tt[i]], out=wb[i][:, 0:n], in_=wscr[slot, :, 0:n])
                return view, wb_tt[i]
            P.Dm("pool", f"s_wb{i}", [], [wb_tt[i]], out=view, in_=src_ap.rearrange("(k p) m -> p k m", p=128))
            if key is not None and key not in self.wcache:
                slot = len(self.wcache)
                assert slot < NWSCR
                tt = TT(f"wscr{slot}")
                P.Dm("sp", "s_wcv", [wb_tt[i]], [tt], out=wscr[slot, :, 0:n], in_=wb[i][:, 0:n])
                self.wcache[key] = (slot, tt)
            return view, wb_tt[i]

        def w_fence():
            tot = self.s.dcnt.get("s_wcv", 0)
            for key, (slot, tt) in self.wcache.items():
                if key not in self.wready:
                    tt.wr = ("s_wcv", tot)
                    self.wready[key] = (slot, tt)
        self.w_fence = w_fence
        self.load_w = load_w

        def hT_t(k):
            return hreg.t(k * TW * 2, (k + 1) * TW * 2)

        def rmsnorm(T, gkind, li_):
            hT = hreg.view(BF16, 0, KC * TW).rearrange("p (k t) -> p k t", k=KC)
            for (n0, nn) in ntiles(T):
                b = P.next_ps()
                for k in range(KC):
                    s_ = k % 2
                    P.I("act", "activation", [x_tt[k]], [sq_tt[s_]], out=sq[s_][:, 0:nn], in_=xT[:, k, n0:n0 + nn], func=AF.Square)
                    P.mm(ps_tt[b], psb[b][:, 0:nn], ones[:], sq[s_][:, 0:nn], k == 0, k == KC - 1, reads=[sq_tt[s_], c_tt])
                P.I("dve", "tensor_scalar", [ps_tt[b]], [rstd_tt], out=rstd[:, n0:n0 + nn], in0=psb[b][:, 0:nn], scalar1=1.0 / D,
                    scalar2=EPS, op0=ALU.mult, op1=ALU.add)
            rsqrt_inplace(rstd[:, 0:T], [rstd_tt])
            for k in range(KC):
                P.I("dve", "scalar_tensor_tensor", [x_tt[k], rstd_tt, vcol_tt], hT_t(k), out=hT[:, k, 0:T], in0=xT[:, k, 0:T],
                    scalar=gcol(gkind, li_, k), in1=rstd[:, 0:T], op0=ALU.mult, op1=ALU.mult)
            return hT
        self.rmsnorm = rmsnorm
        self.hT_t = hT_t

        def dense(T, w2d, kc, ncols_total, rhs_fn, rhs_tts_fn, evac, key=None):
            ncb = min(4096 // kc, ncols_total)
            for c0 in range(0, ncols_total, ncb):
                ncols = min(ncb, ncols_total - c0)
                wv, wtt = load_w(w2d[:, c0:c0 + ncols], kc, ncols, None if key is None else (key, c0))
                for m0 in range(0, ncols, 128):
                    mw = min(128, ncols - m0)
                    for (n0, nn) in ntiles(T):
                        b = P.next_ps()
                        for k in range(kc):
                            P.mm(ps_tt[b], psb[b][0:mw, 0:nn], wv[:, k, m0:m0 + mw], rhs_fn(k, n0, nn), k == 0, k == kc - 1,
                                 reads=[wtt] + rhs_tts_fn(k))
                        evac((c0 + m0) // 128, n0, nn, psb[b], ps_tt[b], mw)
        self.dense = dense

        def norm_rows(rows_ap, tt, nh, sbase, gain_b):
            for h in range(nh):
                P.I("act", "activation", [tt], [sq_tt[0], small_tt], out=sq[0][:, 0:128], in_=rows_ap[:, h * 128:(h + 1) * 128],
                    func=AF.Square, accum_out=small[:, sbase + h:sbase + h + 1])
            P.I("dve", "tensor_scalar", [small_tt], [small_tt], out=small[:, sbase + 8:sbase + 8 + nh], in0=small[:, sbase:sbase + nh],
                scalar1=1.0 / HD, scalar2=EPS, op0=ALU.mult, op1=ALU.add)
            rsqrt_inplace(small[:, sbase + 8:sbase + 8 + nh], [small_tt])
            for h in range(nh):
                kh = rows_ap[:, h * 128:(h + 1) * 128]
                P.I("dve", "scalar_tensor_tensor", [tt, small_tt, c_tt], [tt], out=kh, in0=kh,
                    scalar=small[:, sbase + 8 + h:sbase + 9 + h], in1=gain_b, op0=ALU.mult, op1=ALU.mult)
        self.norm_rows = norm_rows

        kmT_d = P.dram_tmp("kmT_d", [DEPTH, 128, 4 * 256], BF16)
        vm_d = P.dram_tmp("vm_d", [DEPTH, 128, 2 * 4 * HD], BF16)
        kmd_tt = [TT(f"kmT_d{i_}") for i_ in range(DEPTH)]
        vmd_tt = [TT(f"vm_d{i_}") for i_ in range(DEPTH)]

        def mem_kv(li_, T, first_chunk):
            if not first_chunk:
                P.Dm("sp", "s_kmT", [kmd_tt[li_]], [km_tt], out=kmT[:].rearrange("p h m -> p (h m)"), in_=kmT_d[li_])
                P.Dm("sp", "s_vm", [vmd_tt[li_]], [vm_tt], out=vm[:].rearrange("p c h d -> p (c h d)"), in_=vm_d[li_])
                return
            memT = hreg.view(BF16, 0, KC * 256).rearrange("p (k m) -> p k m", k=KC)
            memT_t = hreg.t(0, KC * 256 * 2)
            mh = r1.view(F32, 0, KC * 256).rearrange("p (k m) -> p k m", k=KC)
            mh_t = r1.t(0, KC * 256 * 4)
            P.Dm("sp", "s_r1", [memhat_tt], mh_t, out=mh, in_=memhat_d)
            for k in range(KC):
                P.I("dve", "tensor_scalar", mh_t + [vcol_tt], memT_t, out=memT[:, k, :], in0=mh[:, k, :], scalar1=gcol("mem", li_, k),
                    scalar2=None, op0=ALU.mult)
            kvrow = [stg[0][:, 0:1024], stg[1][:, 0:1024]]
            for cb in range(4):
                wv, wtt = load_w(d["w_mem_kv"][li_, :, cb * 256:(cb + 1) * 256], KC, 256, ("memkv", li_, cb))
                for mc in range(2):
                    b = P.next_ps()
                    for k in range(KC):
                        P.mm(ps_tt[b], psb[b][:, 0:256], memT[:, k, mc * 128:(mc + 1) * 128], wv[:, k, :], k == 0, k == KC - 1,
                             reads=[wtt] + memT_t)
                    P.copy(P.ev_eng(), kvrow[mc][:, cb * 256:(cb + 1) * 256], psb[b][:, 0:256], reads=[ps_tt[b]], writes=[stg_tt[mc]])
            for mc in range(2):
                norm_rows(kvrow[mc], stg_tt[mc], 4, 16, gkb[:, li_, :])
                for h in range(4):
                    b = P.next_ps()
                    P.tr(ps_tt[b], psb[b][:, 0:128], kvrow[mc][:, h * 128:(h + 1) * 128], ident[:], reads=[stg_tt[mc], c_tt])
                    P.copy(P.ev_eng(), kmT[:, h, mc * 128:(mc + 1) * 128], psb[b][:, 0:128], reads=[ps_tt[b]], writes=[km_tt])
                P.I("pool", "tensor_copy", [stg_tt[mc]], [vm_tt], out=vm[:, mc, :, :].rearrange("p h d -> p (h d)"), in_=kvrow[mc][:, 512:1024])
                if first_chunk:
                    P.Dm("sp", f"s_stg{mc}", [stg_tt[mc]], [], out=o["mkv"][li_, mc * 128:(mc + 1) * 128, :], in_=kvrow[mc])
            P.Dm("sp", "s_kmT", [km_tt], [kmd_tt[li_]], out=kmT_d[li_], in_=kmT[:].rearrange("p h m -> p (h m)"))
            P.Dm("sp", "s_vm", [vm_tt], [vmd_tt[li_]], out=vm_d[li_], in_=vm[:].rearrange("p c h d -> p (c h d)"))

        kmTs = hreg.view(BF16, 0, NSB * 4 * 256).rearrange("p (s h m) -> p s h m", s=NSB, h=4)
        vms = hreg.view(BF16, NSB * 4 * 256 * 2, NSB * 2 * 4 * HD).rearrange("p (s c h d) -> p s c h d", s=NSB, c=2, h=4)

        def mem_attend_sample(li_):
            htt_all = hreg.tts
            for sb in range(NSB):
                for mc in range(2):
                    P.Dm("sp", f"s_stg{mc}", [], [stg_tt[mc]], out=stg[mc][:, 0:1024], in_=d["cmkv"][li_, sb, mc * 128:(mc + 1) * 128, :])
                    for h in range(4):
                        b = P.next_ps()
                        P.tr(ps_tt[b], psb[b][:, 0:128], stg[mc][:, h * 128:(h + 1) * 128], ident[:], reads=[stg_tt[mc], c_tt])
                        P.copy(P.ev_eng(), kmTs[:, sb, h, mc * 128:(mc + 1) * 128], psb[b][:, 0:128], reads=[ps_tt[b]], writes=htt_all)
                    P.I("pool", "tensor_copy", [stg_tt[mc]], htt_all, out=vms[:, sb, mc, :, :].rearrange("p h d -> p (h d)"), in_=stg[mc][:, 512:1024])
            for h in range(4):
                for sb in range(NSB):
                    n0, nn = NS1 * sb, NS1
                    bacc, bden = 4, 5
                    for mc in range(2):
                        b = P.next_ps()
                        P.mm(ps_tt[b], psb[b][:, 0:nn], kmTs[:, sb, h, mc * 128:(mc + 1) * 128], qns[:, h, n0:n0 + nn], True, True, reads=htt_all + [qns_tt])
                        P.I("act", "activation", [ps_tt[b]], [pt_tt[mc]], out=pt[mc][:, 0:nn], in_=psb[b][:, 0:nn], func=AF.Exp, scale=SCALE)
                        P.mm(ps_tt[bacc], psb[bacc][:, 0:nn], vms[:, sb, mc, h, :], pt[mc][:, 0:nn], mc == 0, mc == 1, reads=htt_all + [pt_tt[mc]])
                        P.mm(ps_tt[bden], psb[bden][:, 0:nn], onesb[:], pt[mc][:, 0:nn], mc == 0, mc == 1, reads=[c_tt, pt_tt[mc]])
                    P.I("dve", "reciprocal", [ps_tt[bden]], [sq_tt[1]], out=sq[1][:, 0:nn], in_=psb[bden][:, 0:nn])
                    P.I("dve", "tensor_tensor", [ps_tt[bacc], sq_tt[1]], cat_t(12 + h), out=catT[:, 12 + h, TP + n0:TP + n0 + nn], in0=psb[bacc][:, 0:nn],
                        in1=sq[1][:, 0:nn], op=ALU.mult)

        catT = creg.view(BF16, 0, KC * TW).rearrange("p (k t) -> p k t", k=KC)
        self.catT = catT

        def cat_t(k):
            return creg.t(k * TW * 2, (k + 1) * TW * 2)
        self.cat_t = cat_t

        def qnorm_feat(src, src_tt, T, gain_col, dst, dst_tts):
            for (n0, nn) in ntiles(T):
                b = P.next_ps()
                P.I("act", "activation", [src_tt], [sq_tt[0]], out=sq[0][:, 0:nn], in_=src[:, n0:n0 + nn], func=AF.Square)
                P.mm(ps_tt[b], psb[b][:, 0:nn], ones[:], sq[0][:, 0:nn], True, True, reads=[sq_tt[0], c_tt])
                P.I("dve", "tensor_scalar", [ps_tt[b]], [rstd_tt], out=rstd[:, n0:n0 + nn], in0=psb[b][:, 0:nn], scalar1=1.0 / HD,
                    scalar2=EPS, op0=ALU.mult, op1=ALU.add)
            rsqrt_inplace(rstd[:, 0:T], [rstd_tt])
            P.I("dve", "scalar_tensor_tensor", [src_tt, rstd_tt, vcol_tt], dst_tts, out=dst[:, 0:T], in0=src[:, 0:T], scalar=gain_col,
                in1=rstd[:, 0:T], op0=ALU.mult, op1=ALU.mult)
        self.qnorm_feat = qnorm_feat

        def mem_attend_head(li_, h, T):
            qnorm_feat(qraw, qraw_tt, T, gcol("memqk", li_, 0), qn, [qn_tt])
            if T > TP:
                P.I("pool", "tensor_copy", [qn_tt], [qns_tt], out=qns[:, h, :], in_=qn[:, TP:TP + NS])
            for (n0, nn) in [(n0, 512) for n0 in range(0, TP, 512)]:
                K_, V_, ktt, vtt = (kmT, vm, km_tt, vm_tt)
                bacc, bden = 4, 5
                for mc in range(2):
                    b = P.next_ps()
                    P.mm(ps_tt[b], psb[b][:, 0:nn], K_[:, h, mc * 128:(mc + 1) * 128], qn[:, n0:n0 + nn], True, True, reads=[ktt, qn_tt])
                    P.I("act", "activation", [ps_tt[b]], [pt_tt[mc]], out=pt[mc][:, 0:nn], in_=psb[b][:, 0:nn], func=AF.Exp, scale=SCALE)
                    P.mm(ps_tt[bacc], psb[bacc][:, 0:nn], V_[:, mc, h, :], pt[mc][:, 0:nn], mc == 0, mc == 1, reads=[vtt, pt_tt[mc]])
                    P.mm(ps_tt[bden], psb[bden][:, 0:nn], onesb[:], pt[mc][:, 0:nn], mc == 0, mc == 1, reads=[c_tt, pt_tt[mc]])
                P.I("dve", "reciprocal", [ps_tt[bden]], [sq_tt[1]], out=sq[1][:, 0:nn], in_=psb[bden][:, 0:nn])
                P.I("dve", "tensor_tensor", [ps_tt[bacc], sq_tt[1]], cat_t(12 + h), out=catT[:, 12 + h, n0:n0 + nn], in0=psb[bacc][:, 0:nn],
                    in1=sq[1][:, 0:nn], op=ALU.mult)
        self.mem_attend_head = mem_attend_head
        self.mem_kv = mem_kv

        def out_and_mlp(i, T, j):
            def ev_res(m, n0, nn, ps, ptt, mw):
                P.I("dve", "tensor_tensor", [x_tt[m], ptt], [x_tt[m]], out=xT[:, m, n0:n0 + nn], in0=xT[:, m, n0:n0 + nn], in1=ps[:, 0:nn], op=ALU.add)
            dense(T, d["w_out"][i], KC, D, lambda k, n0, nn: catT[:, k, n0:n0 + nn], lambda k: cat_t(k), ev_res, key=("out", i))
            hT = rmsnorm(T, "mlp", i)
            a_fg = r1.view(BF16, 0, 16 * TW).rearrange("p (k t) -> p k t", k=16)

            def a_t(m):
                return r1.t(m * TW * 2, (m + 1) * TW * 2)
            for fg in range(4):
                def ev_up(m, n0, nn, ps, ptt, mw):
                    P.I("act", "activation", [ptt], [sq_tt[0]], out=sq[0][:, 0:nn], in_=ps[:, 0:nn], func=AF.Relu)
                    P.I("dve", "tensor_tensor", [sq_tt[0]], a_t(m), out=a_fg[:, m, n0:n0 + nn], in0=sq[0][:, 0:nn], in1=sq[0][:, 0:nn], op=ALU.mult)
                dense(T, d["w_up"][i][:, fg * 2048:(fg + 1) * 2048], KC, 2048, lambda k, n0, nn: hT[:, k, n0:n0 + nn], hT_t, ev_up, key=("up", i, fg))
                dense(T, d["w_down"][i][fg * 2048:(fg + 1) * 2048, :], KC, D, lambda k, n0, nn: a_fg[:, k, n0:n0 + nn], a_t, ev_res, key=("down", i, fg))
        self.out_and_mlp = out_and_mlp

        def out_rows(src_fn, src_tts, ncols_feat, n_tok, dst_fn):
            nchunk = ncols_feat // 128
            s_ = None
            for c in range(nchunk):
                if c % 8 == 0:
                    s_ = self.stgi % 2
                    self.stgi += 1
                b = P.next_ps()
                P.tr(ps_tt[b], psb[b][0:n_tok, 0:128], src_fn(c), ident[:], reads=src_tts(c) + [c_tt])
                P.copy(P.ev_eng(), stg[s_][0:n_tok, (c % 8) * 128:(c % 8 + 1) * 128], psb[b][0:n_tok, 0:128], reads=[ps_tt[b]], writes=[stg_tt[s_]])
                if c % 8 == 7 or c == nchunk - 1:
                    c0 = (c // 8) * 8
                    w_ = (c - c0 + 1) * 128
                    P.Dm("sp", f"s_stg{s_}", [stg_tt[s_]], [], out=dst_fn(c0 * 128, w_), in_=stg[s_][0:n_tok, 0:w_])
        self.out_rows = out_rows

        def pool_layer(i, j, T):
            li = i // 2
            u = r1.view(F32, 0, 12 * TW).rearrange("p (k t) -> p k t", k=12)

            def u_t(c):
                return r1.t(c * TW * 4, (c + 1) * TW * 4)
            mem_kv(i, T, j == 0)
            hT = rmsnorm(T, "mix", i)

            def ev_in(m, n0, nn, ps, ptt, mw):
                if m < 12:
                    P.copy(P.ev_eng(), u[:, m, n0:n0 + nn], ps[:, 0:nn], reads=[ptt], writes=u_t(m))
                else:
                    P.copy(P.ev_eng(), qraw[:, n0:n0 + nn], ps[:, 0:nn], reads=[ptt], writes=[qraw_tt])
                    if n0 + nn >= T:
                        mem_attend_head(i, m - 12, T)
            dense(T, d["w_in_pool"][li], KC, D, lambda k, n0, nn: hT[:, k, n0:n0 + nn], hT_t, ev_in, key=("inpool", li))
            if T > TP:
                mem_attend_sample(i)
            L = 16 + TP
            E = hreg.view(F32, 0, L)
            A = hreg.view(F32, L * 4, L)
            B = hreg.view(F32, 2 * L * 4, L)
            pooled = hreg.view(BF16, 3 * L * 4, 3 * TW).rearrange("p (k t) -> p k t", k=3)
            htt = hreg.tts
            wg_v = [None, None]
            for half in range(2):
                wi = self.wbi % 2
                self.wbi += 1
                v = wb[wi][:, 0:2 * 3 * 384].rearrange("p (g k m) -> p g k m", g=2, k=3)
                for gg in range(2):
                    P.Dm("pool", f"s_wb{wi}", [], [wb_tt[wi]], out=v[:, gg, :, :],
                         in_=d["w_pg"][li, 2 * half + gg].rearrange("(k p) m -> p k m", p=128))
                wg_v[half] = (v, wb_tt[wi])
            if T > TP:
                for sb in range(NSB):
                    P.Dm("sp", "s_stg0", [], [stg_tt[0]], out=stg[0][0:15, 0:1024], in_=d["spool"][li, sb, :, 0:1024])
                    P.Dm("sp", "s_stg1", [], [stg_tt[1]], out=stg[1][0:15, 0:512], in_=d["spool"][li, sb, :, 1024:1536])
                    P.Dm("sp", "s_misc", [], [], out=o["pool_s"][li, sb, 0:11, :], in_=d["spool"][li, sb, 4:15, :])
                    for c in range(12):
                        src = stg[0][0:15, c * 128:(c + 1) * 128] if c < 8 else stg[1][0:15, (c - 8) * 128:(c - 7) * 128]
                        b = P.next_ps()
                        P.tr(ps_tt[b], psb[b][:, 0:15], src, ident[0:15, 0:15], reads=[stg_tt[0 if c < 8 else 1], c_tt])
                        P.copy(P.ev_eng(), sthist[:, sb, c, 1:16], psb[b][:, 0:15], reads=[ps_tt[b]], writes=[sthist_tt])
            for g in range(4):
                w = 2 << g
                steps = g + 1
                for cc in range(3):
                    c = 3 * g + cc
                    utt = u_t(c)
                    for part in range(1 + NSB if T > TP else 1):
                        if part == 0:
                            n = TP
                            P.I("pool", "tensor_copy", [carry_tt[li]], htt, out=E[:, 0:16], in_=carry[:, li, c, :])
                            P.I("pool", "tensor_copy", utt, htt, out=E[:, 16:16 + TP], in_=u[:, c, 0:TP])
                            ucol = u[:, c, 0:TP]
                            pout = pooled[:, cc, 0:TP]
                        else:
                            n = NS1
                            sb = part - 1
                            s0 = TP + NS1 * sb
                            P.I("pool", "tensor_copy", [sthist_tt], htt, out=E[:, 0:16], in_=sthist[:, sb, c, :])
                            P.I("pool", "tensor_copy", utt, htt, out=E[:, 16:16 + NS1], in_=u[:, c, s0:s0 + NS1])
                            ucol = u[:, c, s0:s0 + NS1]
                            pout = pooled[:, cc, s0:s0 + NS1]
                        Ln = 16 + n
                        src, dst = E, A
                        sh = 1
                        lo = 0
                        for st in range(steps):
                            lo += sh
                            P.I("dve", "tensor_tensor", htt, htt, out=dst[:, lo:Ln], in0=src[:, lo:Ln], in1=src[:, lo - sh:Ln - sh], op=ALU.add)
                            src = dst
                            dst = B if dst is A else A
                            sh *= 2
                        P.I("dve", "scalar_tensor_tensor", htt + utt, htt, out=pout, in0=src[:, 16:16 + n], scalar=1.0 / w, in1=ucol,
                            op0=ALU.mult, op1=ALU.subtract)
                        if part == 0 and j == 0:
                            P.I("dve", "tensor_tensor", htt + [c_tt], [sq_tt[1]], out=sq[1][:, 0:16], in0=src[:, 16:32], in1=invc[:, g, :], op=ALU.mult)
                            P.I("dve", "tensor_tensor", [sq_tt[1]] + utt, htt, out=pout[:, 0:16], in0=sq[1][:, 0:16], in1=ucol[:, 0:16], op=ALU.subtract)
                        if part == 0:
                            P.I("pool", "tensor_copy", utt + htt, [carry_tt[li]], out=carry[:, li, c, 1:16], in_=u[:, c, TP - 15:TP])
                wv, wtt = wg_v[g // 2]
                for mo in range(3):
                    for (n0, nn) in ntiles(T):
                        b = P.next_ps()
                        for ki in range(3):
                            P.mm(ps_tt[b], psb[b][:, 0:nn], wv[:, g % 2, ki, mo * 128:(mo + 1) * 128], pooled[:, ki, n0:n0 + nn],
                                 ki == 0, ki == 2, reads=[wtt] + htt)
                        P.I("act", "activation", [ps_tt[b], vcol_tt], cat_t(3 * g + mo), out=catT[:, 3 * g + mo, n0:n0 + nn], in_=psb[b][:, 0:nn],
                            func=AF.Identity, scale=gcol("pscale", li, 3 * g + mo))
            if j == NCH - 1:
                out_rows(lambda c: u[:, c, TP - 15:TP], u_t, TOK, 15, lambda f0, w_: o["pool_p"][li, :, f0:f0 + w_])
            if T > TP:
                for sb in range(NSB):
                    out_rows(lambda c, sb=sb: u[:, c, TP + NS1 * sb:TP + NS1 * (sb + 1)], u_t, TOK, NS1,
                             lambda f0, w_, sb=sb: o["pool_s"][li, sb, 11:15, f0:f0 + w_])
            out_and_mlp(i, T, j)

        kslcT = P.sb("kslcT", [128, SEQ], BF16)
        vslc = P.sb("vslc", [128, SEQ // 128, HD], BF16)
        kwinT = P.sb("kwinT", [128, 1024], BF16)
        vwin = P.sb("vwin", [128, 8, HD], BF16)
        msk = [P.sb(f"msk{i}", [128, 128], BF16) for i in range(2)]
        msk_tt = [TT(f"msk{i}") for i in range(2)]
        sel2s = [P.sb(f"sel2s{i}", [128, 2, 64], BF16) for i in range(2)]
        sel2s_tt = [TT(f"sel2s{i}") for i in range(2)]
        kslc_tt, vslc_tt, kwin_tt, vwin_tt = (TT(n) for n in ("kslcT", "vslc", "kwinT", "vwin"))
        cmpx = P.sb("cmpx", [128, 4, 528], BF16)
        cmpx_tt = [TT(f"cmpx{i}") for i in range(4)]
        ccarry = P.sb("ccarry", [128, 2, 4, 16], BF16)
        ccarry_tt = TT("ccarry")
        kcT = P.sb("kcT", [128, 2, 2, 256], BF16)
        vc = P.sb("vc", [128, 2, 2, 2, HD], BF16)
        kc_tt, vc_tt = TT("kcT"), TT("vc")
        w2b = P.sb("w2b", [128, 2, 2, HD], BF16)
        b1col = P.sb("b1col", [128, 4], F32)
        posT = P.sb("posT", [128, 4, 32], BF16)
        n_tt = TT("nsa_consts")
        gates = P.sb("gates", [128, TW], F32)
        gates_tt = TT("gates")
        gbias = P.sb("gbias", [128, 2], F32)
        Amat = P.sb("Amat", [128, 2, 64], F32)
        tri = P.sb("tri", [128, 128], BF16)
        tris = P.sb("tris", [128, 128], BF16)
        cmask = P.sb("cmask", [128, 2, 128], BF16)
        cmask_tt = TT("cmask")
        P32v = [stg[0][:, 0:768], stg[1][:, 0:768]]
        impT = P.sb("impT", [128, 2, 128], F32)
        impT_tt = TT("impT")
        selw = P.sb("selw", [128, 6, 64], F32)
        selw_tt = TT("selw")
        seltab = P.sb("seltab_sb", [128, 128], F32)
        seltab_tt = TT("seltab")
        m8 = P.sb("m8", [128, 16], F32)
        kvf = [P.sb(f"kvf{i}", [128, TW], F32) for i in range(2)]
        kvf_tt = [TT(f"kvf{i}") for i in range(2)]
        rowsb = [stg[i][:, 0:512].rearrange("p (a b) -> p a b", a=4) for i in range(2)]
        rowsb_tt = stg_tt
        vrow = [P.sb(f"vrow{i}", [128, 4, HD], BF16) for i in range(2)]
        vrow_tt = [TT(f"vrow{i}") for i in range(2)]
        kbf = [P.sb(f"kbf{i}", [128, TP], BF16) for i in range(2)]
        kbf_tt = [TT(f"kbf{i}") for i in range(2)]
        tokacc = P.sb("tokacc", [128, 768], F32)
        tokacc_tt = TT("tokacc")
        obr = P.sb("obr", [128, 384], F32)
        obr_tt = TT("obr")
        rden = P.sb("rden", [128, 384], F32)
        rden_tt = TT("rden")
        gm = P.sb("gm", [128, 384], F32)
        gm_tt = TT("gm")
        h1 = P.sb("h1", [128, 32], BF16)
        h1_tt = TT("h1")
        vtmp = P.sb("vtmp", [128, HD], BF16)
        vtmp_tt = TT("vtmp")
        skv = P.sb("skv", [128, 12, NS], F32)
        skv_tt = TT("skv")
        kslc_d = P.dram_tmp("kslc_d", [2, 2, 128, SEQ], BF16)
        kwin_d = P.dram_tmp("kwin_d", [2, 2, 128, SEQ], BF16)
        vslc_d = P.dram_tmp("vslc_d", [2, 2, SEQ, HD], BF16)
        vwin_d = P.dram_tmp("vwin_d", [2, 2, SEQ, HD], BF16)
        kslcd_tt, kwind_tt, vslcd_tt, vwind_tt = TT("kslc_d"), TT("kwin_d"), TT("vslc_d"), TT("vwin_d")
        d["seltab"] = P.dram_in("seltab", [SEQ // 128, 128, 128])
        self.kvfi = 0

        P.I("pool", "memset", [], [n_tt], tri[:], 1.0)
        P.I("pool", "affine_select", [n_tt], [n_tt], out=tri[:], in_=tri[:], pattern=[[1, 128]], compare_op=ALU.is_ge, fill=0.0,
            base=0, channel_multiplier=-1)
        P.I("pool", "memset", [n_tt], [n_tt], tris[:], 1.0)
        P.I("pool", "affine_select", [n_tt], [n_tt], out=tris[:], in_=tris[:], pattern=[[-1, 128]], compare_op=ALU.is_ge, fill=0.0,
            base=-1, channel_multiplier=1)
        for ck in range(2):
            for term, (lo_b, hi_b) in enumerate(((0, 3), (-1, 2))):
                dst = selw[:, term, :]
                P.I("pool", "memset", [selw_tt], [selw_tt], dst, 1.0)
                P.I("pool", "affine_select", [selw_tt], [selw_tt], out=dst, in_=dst, pattern=[[-4, 64]], compare_op=ALU.is_ge, fill=0.0,
                    base=128 * ck - lo_b, channel_multiplier=1)
                P.I("pool", "affine_select", [selw_tt], [selw_tt], out=dst, in_=dst, pattern=[[4, 64]], compare_op=ALU.is_ge, fill=0.0,
                    base=hi_b - 128 * ck, channel_multiplier=-1)
            P.I("pool", "tensor_tensor", [selw_tt], [n_tt], out=Amat[:, ck, :], in0=selw[:, 0, :], in1=selw[:, 1, :], op=ALU.add)
        P.I("pool", "memset", [n_tt], [kc_tt], kcT[:].rearrange("p a b c -> p (a b c)"), 0.0)
        P.I("pool", "memset", [n_tt], [vc_tt], vc[:].rearrange("p a b c d -> p (a b c d)"), 0.0)
        P.I("pool", "memset", [n_tt], [ccarry_tt], ccarry[:].rearrange("p a b c -> p (a b c)"), 0.0)
        P.I("pool", "memset", [n_tt], [gates_tt], gates[:], 0.0)
        P.Dm("sp", "s_cst", [], [n_tt], out=gbias[0:36, :], in_=d["gate_bias"].rearrange("a b -> b a"), allow_slow_non_contiguous=True)
        for li_ in range(NL // 2):
            for kv in range(2):
                P.Dm("pool", "s_cst", [], [n_tt], out=w2b[:, li_, kv, :], in_=d["cmp_w2"][li_, kv])
                P.Dm("sp", "s_stg0", [], [stg_tt[0]], out=stg[0][0:32, 0:128], in_=d["cmp_pos"][li_, kv])
                b = P.next_ps()
                P.tr(ps_tt[b], psb[b][:, 0:32], stg[0][0:32, 0:128], ident[0:32, 0:32], reads=[stg_tt[0], c_tt])
                P.I("dve", "tensor_copy", [ps_tt[b]], [n_tt], out=posT[:, li_ * 2 + kv, :], in_=psb[b][:, 0:32])
                wv, wtt = load_w(d["cmp_w1"][li_, kv], 32, 128, ("w1", li_, kv))
                b = P.next_ps()
                for jj in range(32):
                    P.mm(ps_tt[b], psb[b][:, 0:1], wv[:, jj, :], posT[:, li_ * 2 + kv, jj:jj + 1], jj == 0, jj == 31, reads=[wtt, n_tt])
                P.I("dve", "tensor_copy", [ps_tt[b]], [n_tt], out=b1col[:, li_ * 2 + kv:li_ * 2 + kv + 1], in_=psb[b][:, 0:1])

        def norm_from_psum(src, ptt, nn, gain_col, dst, dst_tts):
            P.I("act", "activation", [ptt], [sq_tt[0]], out=sq[0][:, 0:nn], in_=src, func=AF.Square)
            b = P.next_ps()
            P.mm(ps_tt[b], psb[b][:, 0:nn], ones[:], sq[0][:, 0:nn], True, True, reads=[sq_tt[0], c_tt])
            P.I("dve", "tensor_scalar", [ps_tt[b]], [sq_tt[1]], out=sq[1][:, 0:nn], in0=psb[b][:, 0:nn], scalar1=1.0 / HD, scalar2=EPS,
                op0=ALU.mult, op1=ALU.add)
            rsqrt_inplace(sq[1][:, 0:nn], [sq_tt[1]])
            P.I("dve", "scalar_tensor_tensor", [ptt, sq_tt[1], vcol_tt], dst_tts, out=dst, in0=src, scalar=gain_col, in1=sq[1][:, 0:nn],
                op0=ALU.mult, op1=ALU.mult)

        def bc3(ap2d):
            return ap2d.unsqueeze(1).to_broadcast([128, 3, 128])

        def nsa_layer(i, j, T):
            li = i // 2
            q = r1.view(BF16, 0, 12 * TW).rearrange("p (k t) -> p k t", k=12)

            def q_t(h):
                return r1.t(h * TW * 2, (h + 1) * TW * 2)
            mem_kv(i, T, j == 0)
            hT = rmsnorm(T, "mix", i)
            W = d["w_in_nsa"][li]
            rhs = lambda k, n0, nn: hT[:, k, n0:n0 + nn]

            def ev_q(m, n0, nn, ps, ptt, mw):
                norm_from_psum(ps[:, 0:nn], ptt, nn, gcol("nsaqk", li, 0), q[:, m, n0:n0 + nn], q_t(m))
            dense(T, W[:, 0:TOK], KC, TOK, rhs, hT_t, ev_q, key=("nsa_q", li))

            def ev_kv(m, n0, nn, ps, ptt, mw):
                slot, g = m // 2, m % 2
                samp = n0 >= TP
                if not samp:
                    fi = self.kvfi % 2
                    self.kvfi += 1
                    f, ftt = kvf[fi], kvf_tt[fi]
                    fdst, fdst_tt = f[:, 0:nn], [ftt]
                else:
                    fdst, fdst_tt = skv[:, m, :], [skv_tt]
                if slot in (2, 4):
                    norm_from_psum(ps[:, 0:nn], ptt, nn, gcol("nsaqk", li, 2 if slot == 2 else 3), fdst, fdst_tt)
                else:
                    P.copy(P.ev_eng(), fdst, ps[:, 0:nn], reads=[ptt], writes=fdst_tt)
                if samp:
                    return
                if slot in (0, 1):
                    ci = slot * 2 + g
                    P.I("pool", "tensor_copy", [ftt], [cmpx_tt[ci]], out=cmpx[:, ci, 16:16 + TP], in_=f[:, 0:TP])
                if slot in (2, 4):
                    P.I("pool", "tensor_copy", [ftt], [kbf_tt[fi]], out=kbf[fi][:, :], in_=f[:, 0:TP])
                    dst_d, dtt = (kslc_d, kslcd_tt) if slot == 2 else (kwin_d, kwind_tt)
                    P.Dm("sp", f"s_kbf{fi}", [kbf_tt[fi]], [dtt], out=dst_d[li, g, :, j * TP:(j + 1) * TP], in_=kbf[fi][:, :])
                for tb in range(TP // 128):
                    b = P.next_ps()
                    P.tr(ps_tt[b], psb[b][:, 0:128], f[:, tb * 128:(tb + 1) * 128], ident[:], reads=[ftt, c_tt])
                    P.copy(P.ev_eng(), rowsb[fi][:, tb, :], psb[b][:, 0:128], reads=[ps_tt[b]], writes=[rowsb_tt[fi]])
                    if slot in (3, 5):
                        P.I("pool", "tensor_copy", [rowsb_tt[fi]], [vrow_tt[fi]], out=vrow[fi][:, tb, :], in_=rowsb[fi][:, tb, :])
                if slot < 4:
                    c0 = slot * 256 + g * 128
                    P.Dm("sp", f"s_stg{fi}", [rowsb_tt[fi]], [],
                         out=o["nkv_p"][li, j * TP:(j + 1) * TP, c0:c0 + 128].rearrange("(tb p) d -> p tb d", p=128), in_=rowsb[fi])
                elif j == NCH - 1:
                    c0 = (slot - 4) * 256 + g * 128
                    P.Dm("sp", f"s_stg{fi}", [rowsb_tt[fi]], [],
                         out=o["win_p"][li, :, c0:c0 + 128].rearrange("(tb p) d -> p tb d", p=128), in_=rowsb[fi])
                if slot in (3, 5):
                    dst_d, dtt = (vslc_d, vslcd_tt) if slot == 3 else (vwin_d, vwind_tt)
                    P.Dm("sp", f"s_vrow{fi}", [vrow_tt[fi]], [dtt],
                         out=dst_d[li, g, j * TP:(j + 1) * TP, :].rearrange("(tb p) d -> p tb d", p=128), in_=vrow[fi][:, :, :])
            dense(T, W[:, TOK:2 * TOK], KC, TOK, rhs, hT_t, ev_kv, key=("nsa_kv", li))

            def ev_g(m, n0, nn, ps, ptt, mw):
                P.I("act", "activation", [ptt, n_tt], [gates_tt], out=gates[0:36, n0:n0 + nn], in_=ps[0:36, 0:nn], func=AF.Sigmoid,
                    bias=gbias[0:36, li:li + 1], scale=1.0)
            dense(T, W[:, 2 * TOK:2 * TOK + 36], KC, 36, rhs, hT_t, ev_g, key=("nsa_g", li))

            def ev_qm(m, n0, nn, ps, ptt, mw):
                P.copy(P.ev_eng(), qraw[:, n0:n0 + nn], ps[:, 0:nn], reads=[ptt], writes=[qraw_tt])
                if n0 + nn >= T:
                    mem_attend_head(i, m, T)
            dense(T, W[:, 2 * TOK + 36:NSA_W], KC, 512, rhs, hT_t, ev_qm, key=("nsa_qm", li))
            if T > TP:
                mem_attend_sample(i)

            compress(li, j, ccarry[:, li], ccarry_tt, kcT[:, li], [kc_tt], vc[:, li], [vc_tt])

            nkb_tot = 4 * (j + 1) if not cfg.get("no_attn") else 0
            kb0w = max(0, 4 * j - 4)
            for g in range(2 if not cfg.get("no_attn") else 0):
                P.Dm("sp", "s_kslc", [kslcd_tt], [kslc_tt], out=kslcT[:, 0:nkb_tot * 128], in_=kslc_d[li, g, :, 0:nkb_tot * 128])
                P.Dm("sp", "s_vslc", [vslcd_tt], [vslc_tt], out=vslc[:, 0:nkb_tot, :],
                     in_=vslc_d[li, g, 0:nkb_tot * 128, :].rearrange("(kb p) d -> p kb d", p=128))
                P.Dm("sp", "s_kwin", [kwind_tt], [kwin_tt], out=kwinT[:, 0:(nkb_tot - kb0w) * 128], in_=kwin_d[li, g, :, kb0w * 128:nkb_tot * 128])
                P.Dm("sp", "s_vwin", [vwind_tt], [vwin_tt], out=vwin[:, 0:nkb_tot - kb0w, :],
                     in_=vwin_d[li, g, kb0w * 128:nkb_tot * 128, :].rearrange("(kb p) d -> p kb d", p=128))
                for qt in range(TP // 128):
                    Q = 4 * j + qt
                    qc0 = qt * 128
                    ncv = 8 * Q + 7
                    nck = 1 if ncv <= 128 else 2
                    qtts = [t_ for h in range(6 * g, 6 * g + 6) for t_ in q_t(h)]

                    def qh(hh, g=g, qc0=qc0):
                        return q[:, 6 * g + 3 * hh:6 * g + 3 * hh + 3, qc0:qc0 + 128]

                    for ck in range(nck):
                        P.I("pool", "memset", [cmask_tt], [cmask_tt], cmask[:, ck, :], 1.0)
                        P.I("pool", "affine_select", [cmask_tt], [cmask_tt], out=cmask[:, ck, :], in_=cmask[:, ck, :], pattern=[[1, 128]],
                            compare_op=ALU.is_ge, fill=0.0, base=128 * Q - 31 - 2048 * ck, channel_multiplier=-16)
                    for hh in range(2):
                        for ck in range(nck):
                            bS = P.next_ps()
                            P.mm(ps_tt[bS], psb[bS][:, 0:384], kcT[:, li, g, ck * 128:(ck + 1) * 128], qh(hh), True, True, reads=[kc_tt] + qtts)
                            pv = P32v[ck][:, hh * 384:(hh + 1) * 384]
                            P.I("act", "activation", [ps_tt[bS]], [stg_tt[ck]], out=pv, in_=psb[bS][:, 0:384], func=AF.Exp, scale=SCALE)
                            P.I("dve", "tensor_tensor", [stg_tt[ck], cmask_tt], [stg_tt[ck]], out=pv.rearrange("p (h q) -> p h q", h=3),
                                in0=pv.rearrange("p (h q) -> p h q", h=3), in1=bc3(cmask[:, ck, :]), op=ALU.mult)
                            P.I("act", "copy", [stg_tt[ck]], [pt_tt[ck]], out=pt[ck][:, 0:384], in_=pv)
                            P.mm(ps_tt[4], psb[4][:, 0:384], vc[:, li, g, ck, :], pt[ck][:, 0:384], ck == 0, ck == nck - 1, reads=[vc_tt, pt_tt[ck]])
                            P.mm(ps_tt[5], psb[5][:, 0:384], ones[:], pv, ck == 0, ck == nck - 1, reads=[c_tt, stg_tt[ck]])
                        finish_branch(0, g, hh, qc0, 128, True, True, 4, 5)
                        for ck in range(nck):
                            pv = P32v[ck][:, hh * 384:(hh + 1) * 384]
                            P.I("dve", "tensor_tensor", [stg_tt[ck], rden_tt], [stg_tt[ck]], out=pv, in0=pv, in1=rden[:, :], op=ALU.mult)
                    for ck in range(nck):
                        P.I("dve", "tensor_reduce", [stg_tt[ck]], [impT_tt], out=impT[:, ck, :], in_=P32v[ck].rearrange("p (h q) -> p q h", h=6),
                            axis=AX.X, op=ALU.add)
                    for ck in range(nck):
                        P.mm(ps_tt[6], psb[6][:, 0:64], impT[:, ck, :], Amat[:, ck, :], ck == 0, ck == nck - 1, reads=[impT_tt, n_tt])
                    if g == 0:
                        P.Dm("sp", "s_seltab", [], [seltab_tt], out=seltab[:, :], in_=d["seltab"][Q])
                    sc, sc2, s_a, s_b = selw[:, 2, :], selw[:, 3, :], selw[:, 4, :], selw[:, 5, :]
                    P.I("dve", "tensor_tensor", [ps_tt[6], seltab_tt], [selw_tt], out=sc, in0=psb[6][:, 0:64], in1=seltab[:, 0:64], op=ALU.mult)
                    P.I("dve", "tensor_tensor", [selw_tt, seltab_tt], [selw_tt], out=sc, in0=sc, in1=seltab[:, 64:128], op=ALU.add)
                    topk_mask(sc, sc2, s_a, s_b, 128)

                    def smask(kb, idx, Q=Q, s_a=s_a):
                        mi = idx % 2
                        P.I("dve", "tensor_copy", [selw_tt], [sel2s_tt[mi]], out=sel2s[mi][:, :, :],
                            in_=s_a[:, 2 * kb:2 * kb + 2].unsqueeze(2).to_broadcast([128, 2, 64]))
                        b = P.next_ps()
                        P.mm(ps_tt[b], psb[b][:, 0:128], sel2s[mi][:, :, :].rearrange("p a b -> p (a b)"), identb[:], True, True,
                             reads=[sel2s_tt[mi], c_tt])
                        if kb == Q:
                            P.I("dve", "tensor_tensor", [ps_tt[b], n_tt], [msk_tt[mi]], out=msk[mi][:, :], in0=psb[b][:, 0:128], in1=tri[:], op=ALU.mult)
                        else:
                            P.I("act", "copy", [ps_tt[b]], [msk_tt[mi]], out=msk[mi][:, :], in_=psb[b][:, 0:128])
                        return (msk[mi][:, :], [msk_tt[mi]])

                    def wmask(kb, idx, Q=Q):
                        if kb == Q:
                            return (tri[:], [n_tt])
                        if kb == Q - 4:
                            return (tris[:], [n_tt])
                        return None

                    for (br, kbs, Kt, ktt, Vt, vtt, kofs, mfn) in (
                            (1, list(range(Q + 1)), kslcT, kslc_tt, vslc, vslc_tt, 0, smask),
                            (2, list(range(max(0, Q - 4), Q + 1)), kwinT, kwin_tt, vwin, vwin_tt, kb0w, wmask)):
                        n = len(kbs)

                        def pv_den(idx_, kl_, Vt=Vt, vtt=vtt, n=n):
                            for hh in range(2):
                                pi = hh + 2 * (idx_ % 2)
                                ba, bd = (4, 5) if hh == 0 else (6, 7)
                                P.mm(ps_tt[ba], psb[ba][:, 0:384], Vt[:, kl_, :], pt[pi][:, 0:384], idx_ == 0, idx_ == n - 1, reads=[vtt, pt_tt[pi]])
                                P.mm(ps_tt[bd], psb[bd][:, 0:384], onesb[:], pt[pi][:, 0:384], idx_ == 0, idx_ == n - 1, reads=[c_tt, pt_tt[pi]])
                        pend = None
                        for idx, kb in enumerate(kbs):
                            mk = mfn(kb, idx)
                            kl = kb - kofs
                            for hh in range(2):
                                bS = P.next_ps()
                                pi = hh + 2 * (idx % 2)
                                P.mm(ps_tt[bS], psb[bS][:, 0:384], Kt[:, kl * 128:(kl + 1) * 128], qh(hh), True, True, reads=[ktt] + qtts)
                                P.I("act", "activation", [ps_tt[bS]], [pt_tt[pi]], out=pt[pi][:, 0:384], in_=psb[bS][:, 0:384], func=AF.Exp, scale=SCALE)
                                if mk is not None:
                                    P.I("dve", "tensor_tensor", [pt_tt[pi]] + mk[1], [pt_tt[pi]], out=pt[pi][:, 0:384].rearrange("p (h q) -> p h q", h=3),
                                        in0=pt[pi][:, 0:384].rearrange("p (h q) -> p h q", h=3), in1=bc3(mk[0]), op=ALU.mult)
                            if pend is not None:
                                pv_den(*pend)
                            pend = (idx, kl)
                        pv_den(*pend)
                        for hh in range(2):
                            ba, bd = (4, 5) if hh == 0 else (6, 7)
                            finish_branch(br, g, hh, qc0, 128, False, False, ba, bd)
                    for hh in range(2):
                        P.I("act", "copy", [tokacc_tt], [t_ for h in range(6 * g + 3 * hh, 6 * g + 3 * hh + 3) for t_ in cat_t(h)],
                            out=catT[:, 6 * g + 3 * hh:6 * g + 3 * hh + 3, qc0:qc0 + 128],
                            in_=tokacc[:, hh * 384:(hh + 1) * 384].rearrange("p (h q) -> p h q", h=3))
            if T > TP and not cfg.get("no_sample"):
                nsa_sample(i, li, q, q_t)
            out_and_mlp(i, T, j)

        def topk_mask(sc, sc2, s_a, s_b, npart):
            P.I("dve", "max", [selw_tt], [selw_tt], out=m8[0:npart, 0:8], in_=sc)
            P.I("dve", "match_replace", [selw_tt], [selw_tt], out=sc2, in_to_replace=m8[0:npart, 0:8], in_values=sc, imm_value=-3.0e38)
            P.I("dve", "max", [selw_tt], [selw_tt], out=m8[0:npart, 8:16], in_=sc2)
            P.I("dve", "tensor_scalar", [selw_tt], [selw_tt], out=s_a, in0=sc, scalar1=m8[0:npart, 15:16], scalar2=None, op0=ALU.is_ge)
            P.I("dve", "tensor_scalar", [selw_tt], [selw_tt], out=s_b, in0=sc, scalar1=-5.0e29, scalar2=None, op0=ALU.is_gt)
            P.I("dve", "tensor_tensor", [selw_tt], [selw_tt], out=s_a, in0=s_a, in1=s_b, op=ALU.mult)

        def gate_apply(br, g, hh, gc0, nq, src, src_tts, first):
            W3 = 3 * nq
            for h3 in range(3):
                r = 3 * (6 * g + 3 * hh + h3) + br
                P.I("dve", "tensor_scalar", [gates_tt, c_tt], [gm_tt], out=gm[0:36, h3 * nq:(h3 + 1) * nq],
                    in0=gates[0:36, gc0:gc0 + nq], scalar1=ident[0:36, r:r + 1], scalar2=None, op0=ALU.mult)
            bG = P.next_ps()
            P.mm(ps_tt[bG], psb[bG][:, 0:W3], ones[0:36, :], gm[0:36, 0:W3], True, True, reads=[gm_tt, c_tt])
            ta = tokacc[:, hh * 384:hh * 384 + W3]
            if first:
                P.I("dve", "tensor_tensor", src_tts + [ps_tt[bG]], [tokacc_tt], out=ta, in0=src, in1=psb[bG][:, 0:W3], op=ALU.mult)
            else:
                P.I("dve", "tensor_tensor", src_tts + [ps_tt[bG]], [gm_tt], out=gm[:, 0:W3], in0=src, in1=psb[bG][:, 0:W3], op=ALU.mult)
                P.I("dve", "tensor_tensor", [gm_tt, tokacc_tt], [tokacc_tt], out=ta, in0=ta, in1=gm[:, 0:W3], op=ALU.add)

        def finish_branch(br, g, hh, gc0, nq, first, guard, ba, bd):
            W3 = 3 * nq
            if guard:
                P.I("dve", "tensor_scalar", [ps_tt[bd]], [rden_tt], out=rden[:, 0:W3], in0=psb[bd][:, 0:W3], scalar1=1e-30, scalar2=None, op0=ALU.max)
                P.I("dve", "reciprocal", [rden_tt], [rden_tt], out=rden[:, 0:W3], in_=rden[:, 0:W3])
            else:
                P.I("dve", "reciprocal", [ps_tt[bd]], [rden_tt], out=rden[:, 0:W3], in_=psb[bd][:, 0:W3])
            P.I("dve", "tensor_tensor", [ps_tt[ba], rden_tt], [obr_tt], out=obr[:, 0:W3], in0=psb[ba][:, 0:W3], in1=rden[:, 0:W3], op=ALU.mult)
            gate_apply(br, g, hh, gc0, nq, obr[:, 0:W3], [obr_tt], first)

        def compress(li, s_idx, carry_ap, carry_t, kc_dst, kc_dst_t, vc_dst, vc_dst_t):
            lo = 1 if s_idx == 0 else 0
            c_lo = 32 * s_idx - 1
            for kv in range(2):
                wv, wtt = load_w(d["cmp_w1"][li, kv], 32, 128, ("w1", li, kv))
                for g in range(2):
                    ci = kv * 2 + g
                    cx = cmpx[:, ci, :]
                    P.I("pool", "tensor_copy", [carry_t], [cmpx_tt[ci]], out=cx[:, 0:16], in_=carry_ap[:, ci, :])
                    P.I("pool", "tensor_copy", [cmpx_tt[ci]], [carry_t], out=carry_ap[:, ci, :], in_=cx[:, TP:TP + 16])
                    cb3 = cx.rearrange("p (c s) -> p c s", s=16)
                    bh = P.next_ps()
                    for jj in range(32):
                        P.mm(ps_tt[bh], psb[bh][:, 0:32], wv[:, jj, :], cb3[:, jj // 16:jj // 16 + 32, jj % 16], jj == 0, jj == 31,
                             reads=[wtt, cmpx_tt[ci]])
                    P.I("act", "activation", [ps_tt[bh], n_tt], [h1_tt], out=h1[:, 0:32], in_=psb[bh][:, 0:32], func=AF.Gelu_apprx_tanh,
                        bias=b1col[:, li * 2 + kv:li * 2 + kv + 1], scale=1.0)
                    b2 = P.next_ps()
                    if kv == 0:
                        P.mm(ps_tt[b2], psb[b2][:, 0:32], w2b[:, li, 0, :], h1[:, 0:32], True, True, reads=[h1_tt, n_tt])
                        norm_from_psum(psb[b2][:, lo:32], ps_tt[b2], 32 - lo, gcol("nsaqk", li, 1), kc_dst[:, g, c_lo + lo:c_lo + 32], kc_dst_t)
                    else:
                        P.mm(ps_tt[b2], psb[b2][0:32, 0:128], h1[:, 0:32], w2b[:, li, 1, :], True, True, reads=[h1_tt, n_tt])
                        P.I("dve", "tensor_copy", [ps_tt[b2]], [vtmp_tt], out=vtmp[0:32, :], in_=psb[b2][0:32, 0:128])
                        r = c_lo + lo
                        t0 = lo
                        while t0 < 32:
                            ckk, p0 = r // 128, r % 128
                            n = min(32 - t0, 128 - p0)
                            P.Dm("sp", "s_vtmp", [vtmp_tt], vc_dst_t, out=vc_dst[p0:p0 + n, g, ckk, :], in_=vtmp[t0:t0 + n, :])
                            t0 += n
                            r += n

        R1H = 12 * TW * 2
        r1s_tt = r1.t(R1H, 12 * TW * 4)
        kcTs = r1.view(BF16, R1H, 2048).rearrange("p (g c) -> p g c", g=2)
        vcs = P.sb("vcs", [128, 2, 8, HD], BF16)
        kcs_tt, vcs_tt = TT("kcTs"), TT("vcs")
        scarry = P.sb("scarry", [128, 4, 16], BF16)
        scarry_tt = TT("scarry")
        pti = P.sb("pti", [128, NPAGE], I32)
        pidx = P.sb("pidx", [128, NPAGE], I32)
        iop = P.sb("iop", [128, NPAGE], I32)
        pidxB = P.sb("pidxB", [128, NPAGE], I32)
        ptf = P.sb("ptf", [128, NPAGE], F32)
        iopf = P.sb("iopf", [128, NPAGE], F32)
        pidx_tt = TT("pidx")
        Aloc = P.sb("Aloc", [128, 34], F32)
        P32s = P.sb("P32s", [128, 8, 24], F32)
        P32s_tt = TT("P32s")
        impTs = P.sb("impTs", [128, 8, NS1], F32)
        impTs_tt = TT("impTs")
        ssel = r1.view(F32, R1H + 4096, 4 * 260).rearrange("p (a b) -> p a b", a=4)
        ssel_tt = TT("ssel")
        kTs = r1.view(BF16, R1H + 4096 + 4160, 2 * TP).rearrange("p (a b) -> p a b", a=2)
        vTs = r1.view(BF16, R1H + 4096 + 4160 + 2048, 2 * 4 * HD).rearrange("p (a b c) -> p a b c", a=2, b=4)
        kTs_tt, vTs_tt = TT("kTs"), TT("vTs")
        pts = [P.sb(f"pts{i}", [128, 24], BF16) for i in range(2)]
        pts_tt = [TT(f"pts{i}") for i in range(2)]
        newk = P.sb("newk", [128, 12, NS], BF16)
        newv = P.sb("newv", [128, 4, HD], BF16)
        newk_tt, newv_tt = TT("newk"), TT("newv")
        P.I("pool", "iota", [], [n_tt], iop[:], pattern=[[0, NPAGE]], base=0, channel_multiplier=2)
        P.I("dve", "tensor_copy", [n_tt], [n_tt], out=iopf[:, :], in_=iop[:, :])
        for term, (lo_b, hi_b) in enumerate(((0, 3), (-1, 2))):
            dst = selw[:, term, 0:34]
            P.I("pool", "memset", [selw_tt], [selw_tt], dst, 1.0)
            P.I("pool", "affine_select", [selw_tt], [selw_tt], out=dst, in_=dst, pattern=[[-4, 34]], compare_op=ALU.is_ge, fill=0.0,
                base=4 - lo_b, channel_multiplier=1)
            P.I("pool", "affine_select", [selw_tt], [selw_tt], out=dst, in_=dst, pattern=[[4, 34]], compare_op=ALU.is_ge, fill=0.0,
                base=hi_b - 4, channel_multiplier=-1)
        P.I("pool", "tensor_tensor", [selw_tt], [n_tt], out=Aloc[:, :], in0=selw[:, 0, 0:34], in1=selw[:, 1, 0:34], op=ALU.add)

        pgbuf = [(stg[0][:, 0:512], stg_tt[0], "s_stg0"), (stg[1][:, 0:512], stg_tt[1], "s_stg1"),
                 (kvf[0][:, 0:512], kvf_tt[0], "s_kvf0"), (kvf[1][:, 0:512], kvf_tt[1], "s_kvf1")]

        def nsa_sample(i, li, q, q_t):
            P.I("dve", "tensor_copy", [skv_tt], [newk_tt], out=newk[:].rearrange("p a b -> p (a b)"), in_=skv[:].rearrange("p a b -> p (a b)"))
            for m in range(12):
                slot, g = m // 2, m % 2
                b = P.next_ps()
                P.tr(ps_tt[b], psb[b][0:NS, 0:128], skv[:, m, :], ident[:], reads=[skv_tt, c_tt])
                s_ = 0 if m < 8 else 1
                cofs = (m % 8) * 128 if m < 8 else (m - 8) * 128
                P.copy(P.ev_eng(), stg[s_][0:NS, cofs:cofs + 128], psb[b][0:NS, 0:128], reads=[ps_tt[b]], writes=[stg_tt[s_]])
            P.Dm("sp", "s_stg0", [stg_tt[0]], [], out=o["nkv_s"][li, :, :], in_=stg[0][0:NS, 0:1024])
            for sb in range(NSB):
                P.Dm("sp", "s_stg1", [stg_tt[1]], [], out=o["win_s"][li, sb, 508:512, :], in_=stg[1][NS1 * sb:NS1 * (sb + 1), 0:512])
                P.Dm("sp", "s_misc", [], [], out=o["win_s"][li, sb, 0:508, :], in_=d["nsawin"][li, sb, 4:512, :])

            SSTOP = cfg.get("sample_stop", 99)
            if SSTOP <= 0:
                return
            for sb in range(NSB):
                sc0 = TP + NS1 * sb
                for vi, m in enumerate((6, 7, 10, 11)):
                    b = P.next_ps()
                    P.tr(ps_tt[b], psb[b][0:NS1, 0:128], skv[:, m, NS1 * sb:NS1 * (sb + 1)], ident[:], reads=[skv_tt, c_tt])
                    P.I("dve", "tensor_copy", [ps_tt[b]], [newv_tt], out=newv[0:NS1, vi, :], in_=psb[b][0:NS1, 0:128])
                P.Dm("sp", "s_pti", [], [pidx_tt], out=pti[:, :], in_=d["ptab"][sb, :].partition_broadcast(128))
                P.I("dve", "tensor_copy", [pidx_tt], [pidx_tt], out=ptf[:, :], in_=pti[:, :])
                P.I("dve", "scalar_tensor_tensor", [pidx_tt, n_tt], [pidx_tt], out=ptf[:, :], in0=ptf[:, :], scalar=256.0, in1=iopf[:, :],
                    op0=ALU.mult, op1=ALU.add)
                P.I("dve", "tensor_copy", [pidx_tt], [pidx_tt], out=pidx[:, :], in_=ptf[:, :])
                P.I("dve", "tensor_scalar", [pidx_tt], [pidx_tt], out=ptf[:, :], in0=ptf[:, :], scalar1=1.0, scalar2=None, op0=ALU.add)
                P.I("dve", "tensor_copy", [pidx_tt], [pidx_tt], out=pidxB[:, :], in_=ptf[:, :])
                if SSTOP <= 1:
                    continue
                P.I("pool", "memset", [], [scarry_tt], scarry[:].rearrange("p a b -> p (a b)"), 0.0)
                P.I("pool", "memset", [], r1s_tt, kcTs[:].rearrange("p a b -> p (a b)"), 0.0)
                P.I("pool", "memset", [], [vcs_tt], vcs[:].rearrange("p a b c -> p (a b c)"), 0.0)
                for s_i in range(32):
                    for pg4 in range(4):
                        pg = 4 * s_i + pg4
                        pgb, pgt, pgs = pgbuf[pg4]
                        self.idma(pgs, [pidx_tt], [pgt], pgb, d["nsakv"][li], pidx[:, pg:pg + 1])
                        for ci in range(4):
                            b = P.next_ps()
                            P.tr(ps_tt[b], psb[b][:, 0:128], pgb[:, ci * 128:(ci + 1) * 128], ident[:], reads=[pgt, c_tt])
                            P.copy(P.ev_eng(), cmpx[:, ci, 16 + pg4 * 128:16 + (pg4 + 1) * 128], psb[b][:, 0:128], reads=[ps_tt[b]], writes=[cmpx_tt[ci]])
                    compress(li, s_i, scarry, scarry_tt, kcTs, r1s_tt, vcs, [vcs_tt])
                for g in range(2):
                    if SSTOP <= 2:
                        continue
                    qtts = [t_ for h in range(6 * g, 6 * g + 6) for t_ in q_t(h)]
                    qs = q[:, 6 * g:6 * g + 6, sc0:sc0 + NS1]
                    for ck in range(8):
                        bS = P.next_ps()
                        P.mm(ps_tt[bS], psb[bS][:, 0:24], kcTs[:, g, ck * 128:(ck + 1) * 128], qs, True, True, reads=r1s_tt + qtts)
                        P.I("act", "activation", [ps_tt[bS]], [P32s_tt], out=P32s[:, ck, :], in_=psb[bS][:, 0:24], func=AF.Exp, scale=SCALE)
                        if ck == 7:
                            P.I("dve", "tensor_scalar", [P32s_tt, n_tt], [P32s_tt], out=P32s[:, 7, :], in0=P32s[:, 7, :], scalar1=lastmask[:, 0:1],
                                scalar2=None, op0=ALU.mult)
                        pi = ck % 2
                        P.I("act", "copy", [P32s_tt], [pts_tt[pi]], out=pts[pi][:, :], in_=P32s[:, ck, :])
                        P.mm(ps_tt[4], psb[4][:, 0:24], vcs[:, g, ck, :], pts[pi][:, :], ck == 0, ck == 7, reads=[vcs_tt, pts_tt[pi]])
                        P.mm(ps_tt[5], psb[5][:, 0:24], ones[:], P32s[:, ck, :], ck == 0, ck == 7, reads=[c_tt, P32s_tt])
                    for hh in range(2):
                        finish_branch_s(0, g, hh, sc0, True, True, 4, 5)
                    P.I("dve", "tensor_tensor", [P32s_tt, rden_tt], [P32s_tt], out=P32s[:, :, :], in0=P32s[:, :, :],
                        in1=rden[:, 0:24].unsqueeze(1).to_broadcast([128, 8, 24]), op=ALU.mult)
                    P.I("dve", "tensor_reduce", [P32s_tt], [impTs_tt], out=impTs[:, :, :], in_=P32s[:, :, :].rearrange("p c (h q) -> p c q h", h=6),
                        axis=AX.X, op=ALU.add)
                    sc, sc2, s_a, s_b = ssel[0:NS1, 0, 0:257], ssel[0:NS1, 1, 0:257], ssel[0:NS1, 2, 0:257], ssel[0:NS1, 3, 0:257]
                    P.I("dve", "memset", r1s_tt, r1s_tt, ssel[0:NS1, 0, :], 0.0)
                    for ck in range(8):
                        bq = P.next_ps()
                        P.mm(ps_tt[bq], psb[bq][0:NS1, 0:34], impTs[:, ck, :], Aloc[:, :], True, True, reads=[impTs_tt, n_tt])
                        b_lo = 32 * ck - 1
                        sk = 1 if ck == 0 else 0
                        P.I("dve", "tensor_tensor", [ps_tt[bq]] + r1s_tt, r1s_tt, out=ssel[0:NS1, 0, b_lo + sk:b_lo + 34],
                            in0=ssel[0:NS1, 0, b_lo + sk:b_lo + 34], in1=psb[bq][0:NS1, sk:34], op=ALU.add)
                    P.I("dve", "memset", r1s_tt, r1s_tt, ssel[0:NS1, 0, 0:1], BIGV)
                    P.I("dve", "memset", r1s_tt, r1s_tt, ssel[0:NS1, 0, 255:257], BIGV)
                    topk_mask_s(sc, sc2, s_a, s_b)
                    if SSTOP <= 3:
                        continue
                    pend2 = None

                    def pv_den2(kc2_, pg4_, mi_):
                        P.mm(ps_tt[4], psb[4][:, 0:24], vTs[:, 0, pg4_, :], pts[mi_][:, :], kc2_ == 0, False, reads=r1s_tt + [pts_tt[mi_]])
                        P.mm(ps_tt[5], psb[5][:, 0:24], onesb[:], pts[mi_][:, :], kc2_ == 0, False, reads=[c_tt, pts_tt[mi_]])
                    for s_i in range(32):
                        if pend2 is not None:
                            pv_den2(*pend2)
                            pend2 = None
                        for pg4 in range(4):
                            pg = 4 * s_i + pg4
                            pgb, pgt, pgs = pgbuf[pg4]
                            self.idma(pgs, [pidx_tt], [pgt], pgb, d["nsakv"][li], pidxB[:, pg:pg + 1])
                            b = P.next_ps()
                            P.tr(ps_tt[b], psb[b][:, 0:128], pgb[:, g * 128:(g + 1) * 128], ident[:], reads=[pgt, c_tt])
                            P.copy(P.ev_eng(), kTs[:, 0, pg4 * 128:(pg4 + 1) * 128], psb[b][:, 0:128], reads=[ps_tt[b]], writes=r1s_tt)
                            P.I("pool", "tensor_copy", [pgt], r1s_tt, out=vTs[:, 0, pg4, :], in_=pgb[:, 256 + g * 128:256 + (g + 1) * 128])
                        for pg4 in range(4):
                            kc2 = 4 * s_i + pg4
                            mi = kc2 % 2
                            P.I("dve", "tensor_copy", r1s_tt, [sel2s_tt[mi]], out=sel2s[mi][0:NS1, :, :],
                                in_=s_a[:, 2 * kc2:2 * kc2 + 2].unsqueeze(2).to_broadcast([NS1, 2, 64]))
                            bm = P.next_ps()
                            P.mm(ps_tt[bm], psb[bm][:, 0:NS1], sel2s[mi][0:NS1, :, :].rearrange("p a b -> p (a b)"), identb[0:NS1, 0:NS1], True, True,
                                 reads=[sel2s_tt[mi], c_tt])
                            P.I("act", "copy", [ps_tt[bm]], [msk_tt[mi]], out=msk[mi][:, 0:NS1], in_=psb[bm][:, 0:NS1])
                            bS = P.next_ps()
                            P.mm(ps_tt[bS], psb[bS][:, 0:24], kTs[:, 0, pg4 * 128:(pg4 + 1) * 128], qs, True, True, reads=r1s_tt + qtts)
                            P.I("act", "activation", [ps_tt[bS]], [pts_tt[mi]], out=pts[mi][:, :], in_=psb[bS][:, 0:24], func=AF.Exp, scale=SCALE)
                            P.I("dve", "tensor_tensor", [pts_tt[mi], msk_tt[mi]], [pts_tt[mi]], out=pts[mi][:, :].rearrange("p (h q) -> p h q", h=6),
                                in0=pts[mi][:, :].rearrange("p (h q) -> p h q", h=6), in1=msk[mi][:, 0:NS1].unsqueeze(1).to_broadcast([128, 6, NS1]), op=ALU.mult)
                            if pend2 is not None:
                                pv_den2(*pend2)
                            pend2 = (kc2, pg4, mi)
                    pv_den2(*pend2)
                    new_rows_attend(g, sb, qs, qtts, 4 + g, 0 + g, 4, 5, tri_small=True)
                    for hh in range(2):
                        finish_branch_s(1, g, hh, sc0, False, False, 4, 5)
                    if SSTOP <= 4:
                        continue
                    for kc in range(4):
                        s_ = kc % 2
                        P.Dm("sp", f"s_stg{s_}", [], [stg_tt[s_]], out=stg[s_][:, 0:512], in_=d["nsawin"][li, sb, kc * 128:(kc + 1) * 128, :])
                        b = P.next_ps()
                        P.tr(ps_tt[b], psb[b][:, 0:128], stg[s_][:, g * 128:(g + 1) * 128], ident[:], reads=[stg_tt[s_], c_tt])
                        P.copy(P.ev_eng(), kTs[:, 1, kc * 128:(kc + 1) * 128], psb[b][:, 0:128], reads=[ps_tt[b]], writes=r1s_tt)
                        P.I("pool", "tensor_copy", [stg_tt[s_]], r1s_tt, out=vTs[:, 1, kc, :], in_=stg[s_][:, 256 + g * 128:256 + (g + 1) * 128])
                        mi = kc % 2
                        bS = P.next_ps()
                        P.mm(ps_tt[bS], psb[bS][:, 0:24], kTs[:, 1, kc * 128:(kc + 1) * 128], qs, True, True, reads=r1s_tt + qtts)
                        P.I("act", "activation", [ps_tt[bS]], [pts_tt[mi]], out=pts[mi][:, :], in_=psb[bS][:, 0:24], func=AF.Exp, scale=SCALE)
                        if kc == 0:
                            P.I("dve", "tensor_tensor", [pts_tt[mi], n_tt], [pts_tt[mi]], out=pts[mi][:, :].rearrange("p (h q) -> p h q", h=6),
                                in0=pts[mi][:, :].rearrange("p (h q) -> p h q", h=6), in1=tris[:, 0:NS1].unsqueeze(1).to_broadcast([128, 6, NS1]), op=ALU.mult)
                        P.mm(ps_tt[4], psb[4][:, 0:24], vTs[:, 1, kc, :], pts[mi][:, :], kc == 0, False, reads=r1s_tt + [pts_tt[mi]])
                        P.mm(ps_tt[5], psb[5][:, 0:24], onesb[:], pts[mi][:, :], kc == 0, False, reads=[c_tt, pts_tt[mi]])
                    new_rows_attend(g, sb, qs, qtts, 8 + g, 2 + g, 4, 5, tri_small=True)
                    for hh in range(2):
                        finish_branch_s(2, g, hh, sc0, False, False, 4, 5)
                    for hh in range(2):
                        P.I("act", "copy", [tokacc_tt], [t_ for h in range(6 * g + 3 * hh, 6 * g + 3 * hh + 3) for t_ in cat_t(h)],
                            out=catT[:, 6 * g + 3 * hh:6 * g + 3 * hh + 3, sc0:sc0 + NS1],
                            in_=tokacc[:, hh * 384:hh * 384 + 3 * NS1].rearrange("p (h q) -> p h q", h=3))

        lastmask = P.sb("lastmask", [128, 1], F32)
        P.I("pool", "memset", [], [n_tt], lastmask[:], 1.0)
        P.I("pool", "affine_select", [n_tt], [n_tt], out=lastmask[:], in_=lastmask[:], pattern=[[0, 1]], compare_op=ALU.is_ge, fill=0.0,
            base=126, channel_multiplier=-1)

        def new_rows_attend(g, sb, qs, qtts, km, vi, ba, bd, tri_small):
            bS = P.next_ps()
            P.mm(ps_tt[bS], psb[bS][0:NS1, 0:24], newk[:, km, NS1 * sb:NS1 * (sb + 1)], qs, True, True, reads=[newk_tt] + qtts)
            P.I("act", "activation", [ps_tt[bS]], [pts_tt[0]], out=pts[0][0:NS1, :], in_=psb[bS][0:NS1, 0:24], func=AF.Exp, scale=SCALE)
            P.I("dve", "tensor_tensor", [pts_tt[0], n_tt], [pts_tt[0]], out=pts[0][0:NS1, :].rearrange("p (h q) -> p h q", h=6),
                in0=pts[0][0:NS1, :].rearrange("p (h q) -> p h q", h=6), in1=tri[0:NS1, 0:NS1].unsqueeze(1).to_broadcast([NS1, 6, NS1]), op=ALU.mult)
            P.mm(ps_tt[ba], psb[ba][:, 0:24], newv[0:NS1, vi, :], pts[0][0:NS1, :], False, True, reads=[newv_tt, pts_tt[0]])
            P.mm(ps_tt[bd], psb[bd][:, 0:24], onesb[0:NS1, :], pts[0][0:NS1, :], False, True, reads=[c_tt, pts_tt[0]])

        def finish_branch_s(br, g, hh, gc0, first, guard, ba, bd):
            W3 = 3 * NS1
            if hh == 0:
                if guard:
                    P.I("dve", "tensor_scalar", [ps_tt[bd]], [rden_tt], out=rden[:, 0:24], in0=psb[bd][:, 0:24], scalar1=1e-30, scalar2=None, op0=ALU.max)
                    P.I("dve", "reciprocal", [rden_tt], [rden_tt], out=rden[:, 0:24], in_=rden[:, 0:24])
                else:
                    P.I("dve", "reciprocal", [ps_tt[bd]], [rden_tt], out=rden[:, 0:24], in_=psb[bd][:, 0:24])
                P.I("dve", "tensor_tensor", [ps_tt[ba], rden_tt], [obr_tt], out=obr[:, 0:24], in0=psb[ba][:, 0:24], in1=rden[:, 0:24], op=ALU.mult)
            gate_apply(br, g, hh, gc0, NS1, obr[:, hh * W3:(hh + 1) * W3], [obr_tt], first)

        def topk_mask_s(sc, sc2, s_a, s_b):
            P.I("dve", "max", r1s_tt, r1s_tt, out=m8[0:NS1, 0:8], in_=sc)
            P.I("dve", "match_replace", r1s_tt, r1s_tt, out=sc2, in_to_replace=m8[0:NS1, 0:8], in_values=sc, imm_value=-3.0e38)
            P.I("dve", "max", r1s_tt, r1s_tt, out=m8[0:NS1, 8:16], in_=sc2)
            P.I("dve", "tensor_scalar", r1s_tt, r1s_tt, out=s_a, in0=sc, scalar1=m8[0:NS1, 15:16], scalar2=None, op0=ALU.is_ge)

        self.nsa_layer = nsa_layer

        w_fence()
        for j in range(cfg["n_chunks"]):
            T = TW if j == 0 else TP
            for tb in range(TP // 128):
                for half in range(2):
                    s_ = self.stgi % 2
                    self.stgi += 1
                    P.Dm("sp", f"s_stg{s_}", [], [stg_tt[s_]], out=stg[s_][:, 0:1024],
                         in_=d["xp"][j * TP + tb * 128:j * TP + (tb + 1) * 128, half * 1024:(half + 1) * 1024])
                    for kk in range(8):
                        k = half * 8 + kk
                        b = P.next_ps()
                        P.tr(ps_tt[b], psb[b][:, 0:128], stg[s_][:, kk * 128:(kk + 1) * 128], ident[:], reads=[stg_tt[s_], c_tt])
                        P.copy(P.ev_eng(), xT[:, k, tb * 128:(tb + 1) * 128], psb[b][:, 0:128], reads=[ps_tt[b]], writes=[x_tt[k]])
            if j == 0:
                for half in range(2):
                    s_ = self.stgi % 2
                    self.stgi += 1
                    P.Dm("sp", f"s_stg{s_}", [], [stg_tt[s_]], out=stg[s_][0:NS, 0:1024], in_=d["xs"][:, half * 1024:(half + 1) * 1024])
                    for kk in range(8):
                        k = half * 8 + kk
                        b = P.next_ps()
                        P.tr(ps_tt[b], psb[b][:, 0:NS], stg[s_][0:NS, kk * 128:(kk + 1) * 128], ident[0:NS, 0:NS], reads=[stg_tt[s_], c_tt])
                        P.copy(P.ev_eng(), xT[:, k, TP:TP + NS], psb[b][:, 0:NS], reads=[ps_tt[b]], writes=[x_tt[k]])
            for i in range(cfg["n_layers"]):
                if i % 2 == 0:
                    pool_layer(i, j, T)
                else:
                    nsa_layer(i, j, T)
            for tb in range(TP // 128):
                out_rows(lambda c, tb=tb: xT[:, c, tb * 128:(tb + 1) * 128], lambda c: [x_tt[c]], D, 128,
                         lambda f0, w_, tb=tb: o["y_p"][j * TP + tb * 128:j * TP + (tb + 1) * 128, f0:f0 + w_])
            if j == 0:
                out_rows(lambda c: xT[:, c, TP:TP + NS], lambda c: [x_tt[c]], D, NS, lambda f0, w_: o["y_s"][:, f0:f0 + w_])
            w_fence()

        self.s.final_waits()
        self.emit()
        return nc

    def emit(self):
        nc = self.nc
        s = self.s
        sems = {}
        for name in list(Sched.ENGS) + sorted(self.sems_needed):
            sems[name] = self.es.enter_context(nc.semaphore(name))

        def replay(eng, e):
            for (waits, fn, tok, inc) in s.ops[eng]:
                for (sn, v) in waits:
                    e.wait_ge(sems[sn], v)
                if fn is not None:
                    ins = fn(e)
                    ins.then_inc(sems[tok[0]], inc)

        with nc.Block() as block:
            @block.tensor
            def _(e):
                replay("pe", e)

            @block.scalar
            def _(e):
                replay("act", e)

            @block.vector
            def _(e):
                replay("dve", e)

            @block.gpsimd
            def _(e):
                replay("pool", e)

            @block.sync
            def _(e):
                replay("sp", e)
        self.es.close()


def make_consts():
    inv = np.zeros((128, 4, 16), np.float32)
    for g, w in enumerate((2, 4, 8, 16)):
        for t in range(16):
            inv[:, g, t] = 1.0 / min(t + 1, w)
    return inv.reshape(128, 64)


def make_seltab():
    tab = np.zeros((SEQ // 128, 128, 128), np.float32)
    b = np.arange(64)[None, :]
    for Q in range(SEQ // 128):
        pos = Q * 128 + np.arange(128)[:, None]
        cur = pos // 64
        forced = (b == 0) | (b == cur) | (b == cur - 1)
        allowed = b <= cur
        tab[Q, :, :64] = (allowed & ~forced).astype(np.float32)
        tab[Q, :, 64:] = np.where(forced, BIGV, np.where(allowed, 0.0, NEGV)).astype(np.float32)
    return tab


def kernel(x_prompt, x_sample, mem_prompt, cache_mem_kv, cache_nsa_kv, cache_nsa_win, state_pool, page_table,
           g_norm_mix, g_norm_mlp, g_norm_mem, w_mem_kv, mem_qk_gain, w_out, w_mlp_up, w_mlp_down, w_in_pool,
           w_pool_group, pool_scale, w_in_nsa, nsa_gate_bias, nsa_qk_gain, cmp_pos, cmp_w1, cmp_w2, _cfg=None):
    import time as _t
    cfg = dict(CFG)
    if _cfg:
        cfg.update(_cfg)
    f = lambda a: np.ascontiguousarray(np.asarray(a))
    _t0 = _t.time()
    prog = Prog(cfg)
    nc = prog.build()
    print('build_s', _t.time() - _t0, {e: len(v) for e, v in prog.s.ops.items()}, flush=True)
    NL = cfg["n_layers"]; NPL = (NL + 1) // 2; NNL = max(NL // 2, 1)
    nsakv = f(np.asarray(cache_nsa_kv).reshape(2, 1280 * 128, 1024)[:NNL, :cfg.get("nsakv_rows", 1280 * 128)]).reshape(NNL, -1, 512)
    shared = {
        "g_mix": f(g_norm_mix), "g_mlp": f(g_norm_mlp), "g_mem": f(g_norm_mem),
        "w_mem_kv": f(w_mem_kv[:NL]), "mem_qk": f(mem_qk_gain), "w_out": f(w_out[:NL]), "w_up": f(w_mlp_up[:NL]),
        "w_down": f(w_mlp_down[:NL]), "w_in_pool": f(w_in_pool[:NPL]), "w_pg": f(w_pool_group), "pool_scale": f(pool_scale),
        "w_in_nsa": f(w_in_nsa[:NNL]), "gate_bias": f(nsa_gate_bias), "nsa_qk": f(nsa_qk_gain), "cmp_pos": f(cmp_pos),
        "cmp_w1": f(cmp_w1), "cmp_w2": f(cmp_w2), "cst": make_consts(), "seltab": make_seltab(),
    }
    for l_ in range(NNL):
        shared[f"nsakv{l_}"] = nsakv[l_]
    in_maps = []
    for c in range(NCORE):
        sl = slice(NSB * c, NSB * (c + 1))
        m = dict(shared)
        m["xp"] = f(x_prompt[c])
        m["xs"] = f(np.asarray(x_sample)[sl]).reshape(NS, D)
        m["memp"] = f(mem_prompt[c])
        m["cmkv"] = f(np.asarray(cache_mem_kv)[:NL, sl]).reshape(NL, NSB, 256, 1024)
        m["nsawin"] = f(np.asarray(cache_nsa_win)[:, sl]).reshape(2, NSB, 512, 512)
        m["spool"] = f(np.asarray(state_pool)[:, sl])
        m["ptab"] = f(np.asarray(page_table)[sl]).astype(np.int32)
        in_maps.append(m)
    _t0 = _t.time()
    res = run_bass_kernel_spmd(nc, in_maps, core_ids=list(range(NCORE)), **({'trace': True} if cfg.get('trace') else {}))
    if cfg.get('trace'):
        print('EXEC_NS', res.exec_time_ns, flush=True)
    print('run_s', _t.time() - _t0, flush=True)
    R = res.results
    y_p = np.stack([R[b]["y_p"] for b in range(2)])
    y_s = np.concatenate([R[c]["y_s"].reshape(NSB, NS1, D) for c in range(NCORE)], axis=0)
    mkv = np.stack([R[b]["mkv"] for b in range(2)], axis=1).reshape(DEPTH, 2, 256, 2, 4, HD)
    nkv_p = np.stack([R[b]["nkv_p"] for b in range(2)], axis=1).reshape(2, 2, SEQ, 4, 2, HD)
    nkv_s = np.concatenate([R[c]["nkv_s"].reshape(2, NSB, NS1, 1024) for c in range(NCORE)], axis=1).reshape(2, 8, NS1, 4, 2, HD)
    win_p = np.stack([R[b]["win_p"] for b in range(2)], axis=1).reshape(2, 2, 512, 2, 2, HD)
    win_s = np.concatenate([R[c]["win_s"] for c in range(NCORE)], axis=1).reshape(2, 8, 512, 2, 2, HD)
    pool_p = np.stack([R[b]["pool_p"] for b in range(2)], axis=1)
    pool_s = np.concatenate([R[c]["pool_s"] for c in range(NCORE)], axis=1)
    return (y_p, y_s, mkv, nkv_p, nkv_s, win_p, win_s, pool_p, pool_s)
```

```python
import numpy as np
from contextlib import ExitStack
import concourse.bass as bass
import concourse.mybir as mybir
from concourse.bass_utils import run_bass_kernel_spmd

F32 = mybir.dt.float32
BF16 = mybir.dt.bfloat16
I32 = mybir.dt.int32
AF = mybir.ActivationFunctionType
ALU = mybir.AluOpType
AX = mybir.AxisListType

D = 2048
KC = 16
SEQ = 4096
TP = 512
NSB = 4
NS1 = 4
NS = NSB * NS1
NCORE = 2
NCH = SEQ // TP
DEPTH = 4
HD = 128
TOK = 1536
DFF = 8192
EPS = 1e-6
SCALE = HD ** -0.5
NSA_W = 3620
PAST = 16384
NPAGE = 128
NEGV = -1e30
BIGV = 1e30

CFG = dict(n_layers=4, n_chunks=8)


class TT:
    __slots__ = ("name", "wr", "rd")

    def __init__(self, name):
        self.name = name
        self.wr = None
        self.rd = {}


class Sched:
    ENGS = ("pe", "act", "dve", "pool", "sp")

    def __init__(self):
        self.ops = {e: [] for e in self.ENGS}
        self.cnt = {e: 0 for e in self.ENGS}
        self.known = {e: {} for e in self.ENGS}
        self.dcnt = {}

    def _waits(self, eng, reads, writes, pe_chain=False):
        deps = {}

        def add(s, v):
            if pe_chain and s == "pe" and eng == "pe":
                return
            if deps.get(s, 0) < v:
                deps[s] = v

        for r in reads:
            if r.wr is not None:
                add(*r.wr)
        for w in writes:
            if w.wr is not None:
                add(*w.wr)
            for s, v in w.rd.items():
                add(s, v)
        out = []
        kn = self.known[eng]
        for s, v in deps.items():
            if kn.get(s, 0) < v:
                kn[s] = v
                out.append((s, v))
        return out

    def op(self, eng, fn, reads=(), writes=(), pe_chain=False):
        waits = self._waits(eng, reads, writes, pe_chain)
        self.cnt[eng] += 1
        tok = (eng, self.cnt[eng])
        self.ops[eng].append((waits, fn, tok, 1))
        for r in reads:
            if r.rd.get(eng, 0) < tok[1]:
                r.rd[eng] = tok[1]
        for w in writes:
            w.wr = tok
            w.rd = {}

    def dma(self, queue, fn, sem, reads=(), writes=()):
        waits = self._waits(queue, reads, writes)
        self.dcnt[sem] = self.dcnt.get(sem, 0) + 16
        tok = (sem, self.dcnt[sem])
        self.ops[queue].append((waits, fn, tok, 16))
        for r in reads:
            if r.rd.get(sem, 0) < tok[1]:
                r.rd[sem] = tok[1]
        for w in writes:
            w.wr = tok
            w.rd = {}

    def final_waits(self):
        allv = dict(self.cnt)
        allv.update(self.dcnt)
        waits = [(s, v) for s, v in allv.items() if v > 0 and s != "sp"]
        self.ops["sp"].append((waits, None, None, 0))


class Reg:
    def __init__(self, name, handle, nbytes, gran):
        self.name = name
        self.h = handle
        self.nbytes = nbytes
        self.gran = gran
        self.tts = [TT(f"{name}{i}") for i in range((nbytes + gran - 1) // gran)]

    def t(self, b0, b1):
        return self.tts[b0 // self.gran:(b1 - 1) // self.gran + 1]

    def view(self, dtype, b0, nelem):
        esz = 4 if dtype in (F32, I32) else 2
        assert b0 % 4 == 0 and (nelem * esz) % 4 == 0
        ap = self.h[:, b0 // 4:(b0 + nelem * esz) // 4]
        if esz == 2:
            ap = ap.bitcast(dtype)
        elif dtype == I32:
            ap = ap.bitcast(I32)
        return ap


class Prog:
    def __init__(self, cfg):
        self.cfg = cfg
        self.nc = bass.Bass("TRN2", target_bir_lowering=False)
        self.s = Sched()
        self.es = ExitStack()
        self.psi = 0
        self.evi = 0
        self.sems_needed = set()

    def dram_in(self, name, shape, dt=F32):
        return self.nc.dram_tensor(name, list(shape), dt, kind="ExternalInput").ap()

    def dram_out(self, name, shape, dt=F32):
        return self.nc.dram_tensor(name, list(shape), dt, kind="ExternalOutput").ap()

    def dram_tmp(self, name, shape, dt=F32):
        return self.nc.dram_tensor(name, list(shape), dt, kind="Internal").ap()

    def sb(self, name, shape, dt=F32):
        return self.es.enter_context(self.nc.sbuf_tensor(name, list(shape), dt))

    def reg(self, name, nbytes, gran):
        h = self.sb(name, [128, nbytes // 4], F32)
        return Reg(name, h, nbytes, gran)

    def op(self, eng, fn, reads=(), writes=(), pe_chain=False):
        self.s.op(eng, fn, reads, writes, pe_chain)

    def dma(self, queue, fn, sem, reads=(), writes=()):
        self.sems_needed.add(sem)
        self.s.dma(queue, fn, sem, reads, writes)

    def I(self, eng, method, reads, writes, *args, **kw):
        self.s.op(eng, lambda e: getattr(e, method)(*args, **kw), reads, writes)

    def Dm(self, queue, sem, reads, writes, **kw):
        self.sems_needed.add(sem)
        self.s.dma(queue, lambda e: e.dma_start(**kw), sem, reads, writes)

    def idma(self, sem, reads, writes, out, in_, idx_ap):
        self.sems_needed.add(sem)
        self.s.dma("pool", lambda e: e.indirect_dma_start(out=out, out_offset=None, in_=in_,
                                                          in_offset=bass.IndirectOffsetOnAxis(ap=idx_ap, axis=0)), sem, reads, writes)

    def mm(self, ps_tt, out, lhsT, rhs, start, stop, reads):
        self.op("pe", lambda e: e.matmul(out, lhsT=lhsT, rhs=rhs, start=start, stop=stop),
                reads=reads, writes=[ps_tt], pe_chain=not start)

    def tr(self, ps_tt, out, in_, ident, reads):
        self.op("pe", lambda e: e.transpose(out=out, in_=in_, identity=ident), reads=reads, writes=[ps_tt])

    def next_ps(self, lo=0, hi=4):
        i = lo + self.psi % (hi - lo)
        self.psi += 1
        return i

    def ev_eng(self):
        self.evi += 1
        return "act" if self.evi % 2 else "dve"

    def copy(self, eng, out, in_, reads, writes):
        if eng == "act":
            self.op("act", lambda e: e.copy(out=out, in_=in_), reads, writes)
        elif eng == "dve":
            self.op("dve", lambda e: e.tensor_copy(out=out, in_=in_), reads, writes)
        else:
            self.op("pool", lambda e: e.tensor_copy(out=out, in_=in_), reads, writes)

    def build(self):
        nc = self.nc
        cfg = self.cfg
        P = self
        d = {}
        d["xp"] = P.dram_in("xp", [SEQ, D])
        d["xs"] = P.dram_in("xs", [NS, D])
        d["memp"] = P.dram_in("memp", [256, D])
        NL = cfg["n_layers"]; NPL = (NL + 1) // 2; NNL = max(NL // 2, 1)
        d["cmkv"] = P.dram_in("cmkv", [NL, NSB, 256, 1024])
        d["nsakv"] = [P.dram_in(f"nsakv{l_}", [2 * cfg.get("nsakv_rows", 1280 * 128), 512]) for l_ in range(NNL)]
        d["nsawin"] = P.dram_in("nsawin", [2, NSB, 512, 512])
        d["spool"] = P.dram_in("spool", [2, NSB, 15, TOK])
        d["ptab"] = P.dram_in("ptab", [NSB, NPAGE], I32)
        d["g_mix"] = P.dram_in("g_mix", [DEPTH, D])
        d["g_mlp"] = P.dram_in("g_mlp", [DEPTH, D])
        d["g_mem"] = P.dram_in("g_mem", [DEPTH, D])
        d["w_mem_kv"] = P.dram_in("w_mem_kv", [NL, D, 1024])
        d["mem_qk"] = P.dram_in("mem_qk", [DEPTH, 2, HD])
        d["w_out"] = P.dram_in("w_out", [NL, D, D])
        d["w_up"] = P.dram_in("w_up", [NL, D, DFF])
        d["w_down"] = P.dram_in("w_down", [NL, DFF, D])
        d["w_in_pool"] = P.dram_in("w_in_pool", [NPL, D, D])
        d["w_pg"] = P.dram_in("w_pg", [2, 4, 384, 384])
        d["pool_scale"] = P.dram_in("pool_scale", [2, TOK])
        d["w_in_nsa"] = P.dram_in("w_in_nsa", [NNL, D, NSA_W])
        d["gate_bias"] = P.dram_in("gate_bias", [2, 36])
        d["nsa_qk"] = P.dram_in("nsa_qk", [2, 4, HD])
        d["cmp_pos"] = P.dram_in("cmp_pos", [2, 2, 32, HD])
        d["cmp_w1"] = P.dram_in("cmp_w1", [2, 2, 4096, HD])
        d["cmp_w2"] = P.dram_in("cmp_w2", [2, 2, HD, HD])
        d["cst"] = P.dram_in("cst", [128, 64])
        o = {}
        o["y_p"] = P.dram_out("y_p", [SEQ, D])
        o["y_s"] = P.dram_out("y_s", [NS, D])
        o["mkv"] = P.dram_out("mkv", [DEPTH, 256, 1024])
        o["nkv_p"] = P.dram_out("nkv_p", [2, SEQ, 1024])
        o["nkv_s"] = P.dram_out("nkv_s", [2, NS, 1024])
        o["win_p"] = P.dram_out("win_p", [2, 512, 512])
        o["win_s"] = P.dram_out("win_s", [2, NSB, 512, 512])
        o["pool_p"] = P.dram_out("pool_p", [2, 15, TOK])
        o["pool_s"] = P.dram_out("pool_s", [2, NSB, 15, TOK])
        self.d, self.o = d, o
        memhat_d = P.dram_tmp("memhat_d", [128, KC, 256])
        memhat_tt = TT("memhat_d")

        TW = TP + NS
        self.TW = TW
        xT = P.sb("xT", [128, KC, TW], F32)
        x_tt = [TT(f"x{k}") for k in range(KC)]
        hreg = P.reg("hreg", KC * TW * 2, TW * 2)
        r1 = P.reg("r1", 12 * TW * 4, TW * 2)
        creg = P.reg("creg", KC * TW * 2, TW * 2)
        wb = [P.sb(f"wb{i}", [128, 4096], BF16) for i in range(2)]
        wb_tt = [TT(f"wb{i}") for i in range(2)]
        self.wbi = 0
        psb = [self.es.enter_context(nc.psum_tensor(f"ps{i}", [128, 512], F32)) for i in range(8)]
        ps_tt = [TT(f"ps{i}") for i in range(8)]
        ident = P.sb("ident", [128, 128], F32)
        identb = P.sb("identb", [128, 128], BF16)
        ones = P.sb("ones", [128, 128], F32)
        onesb = P.sb("onesb", [128, 128], BF16)
        c_tt = TT("consts")
        vcol = P.sb("vcol", [128, 256], F32)
        vcol_tt = TT("vcol")
        gkb = P.sb("gkb", [128, DEPTH, HD], F32)
        invc = P.sb("invc", [128, 4, 16], F32)
        rstd = P.sb("rstd", [128, TW], F32)
        rstd_tt = TT("rstd")
        sq = [P.sb(f"sq{i}", [128, 512], F32) for i in range(2)]
        sq_tt = [TT(f"sq{i}") for i in range(2)]
        stg = [P.sb(f"stg{i}", [128, 1024], F32) for i in range(2)]
        stg_tt = [TT(f"stg{i}") for i in range(2)]
        self.stgi = 0
        small = P.sb("small", [128, 64], F32)
        small_tt = TT("small")
        carry = P.sb("carry", [128, 2, 12, 16], F32)
        carry_tt = [TT("carry0"), TT("carry1")]
        kmT = P.sb("kmT", [128, 4, 256], BF16)
        vm = P.sb("vm", [128, 2, 4, HD], BF16)
        qns = P.sb("qns", [128, 4, NS], BF16)
        qns_tt = TT("qns")
        sthist = P.sb("sthist", [128, NSB, 12, 16], F32)
        sthist_tt = TT("sthist")
        km_tt, vm_tt = TT("kmT"), TT("vm")
        pt = [P.sb(f"pt{i}", [128, 512], BF16) for i in range(2)] + [P.sb(f"pt{i}", [128, 384], BF16) for i in range(2, 4)]
        pt_tt = [TT(f"pt{i}") for i in range(4)]
        qn = P.sb("qn", [128, TW], BF16)
        qn_tt = TT("qn")
        qraw = P.sb("qraw", [128, TW], F32)
        qraw_tt = TT("qraw")

        P.I("pool", "memset", [], [c_tt], ident[:], 0.0)
        P.I("pool", "affine_select", [c_tt], [c_tt], out=ident[:], in_=ident[:], pattern=[[-1, 128]],
            compare_op=ALU.not_equal, fill=1.0, base=0, channel_multiplier=1)
        P.I("pool", "memset", [c_tt], [c_tt], ones[:], 1.0)
        P.I("dve", "tensor_copy", [c_tt], [c_tt], out=identb[:], in_=ident[:])
        P.I("dve", "tensor_copy", [c_tt], [c_tt], out=onesb[:], in_=ones[:])
        P.Dm("sp", "s_cst", [], [c_tt], out=invc[:].rearrange("p a b -> p (a b)"), in_=d["cst"])
        for i4 in range(DEPTH):
            P.Dm("sp", "s_cst", [], [c_tt], out=gkb[:, i4, :], in_=d["mem_qk"][i4, 1, :].partition_broadcast(128))
        vst = stg[0]
        P.Dm("sp", "s_stg0", [], [stg_tt[0]], out=vst[0:64, 0:128], in_=d["g_mix"].rearrange("i (k p) -> (i k) p", p=128))
        P.Dm("sp", "s_stg0", [], [stg_tt[0]], out=vst[64:128, 0:128], in_=d["g_mlp"].rearrange("i (k p) -> (i k) p", p=128))
        P.Dm("sp", "s_stg0", [], [stg_tt[0]], out=vst[0:64, 128:256], in_=d["g_mem"].rearrange("i (k p) -> (i k) p", p=128))
        P.Dm("sp", "s_stg0", [], [stg_tt[0]], out=vst[64:88, 128:256], in_=d["pool_scale"].rearrange("i (k p) -> (i k) p", p=128))
        P.Dm("sp", "s_stg0", [], [stg_tt[0]], out=vst[88:96, 128:256], in_=d["mem_qk"].rearrange("i a p -> (i a) p"))
        P.Dm("sp", "s_stg0", [], [stg_tt[0]], out=vst[96:104, 128:256], in_=d["nsa_qk"].rearrange("i a p -> (i a) p"))
        P.tr(ps_tt[0], psb[0][:, 0:128], vst[:, 0:128], ident[:], reads=[stg_tt[0], c_tt])
        P.tr(ps_tt[1], psb[1][:, 0:104], vst[0:104, 128:256], ident[0:104, 0:104], reads=[stg_tt[0], c_tt])
        P.I("dve", "tensor_copy", [ps_tt[0]], [vcol_tt], out=vcol[:, 0:128], in_=psb[0][:, 0:128])
        P.I("dve", "tensor_copy", [ps_tt[1]], [vcol_tt], out=vcol[:, 128:232], in_=psb[1][:, 0:104])

        def gcol(kind, i, k=0):
            base = {"mix": 0, "mlp": 64, "mem": 128}.get(kind)
            if base is not None:
                c = base + i * 16 + k
            elif kind == "pscale":
                c = 128 + 64 + i * 12 + k
            elif kind == "memqk":
                c = 128 + 88 + i * 2 + k
            elif kind == "nsaqk":
                c = 128 + 96 + i * 4 + k
            return vcol[:, c:c + 1]
        self.gcol = gcol

        def rsqrt_inplace(ap, tts):
            P.I("dve", "reciprocal", tts, tts, out=ap, in_=ap)
            P.I("act", "activation", tts, tts, out=ap, in_=ap, func=AF.Sqrt)
        self.rsqrt_inplace = rsqrt_inplace

        r1_mem = r1.t(0, 2 * D * 4)
        memtok = r1.view(F32, 0, 2 * D).rearrange("p (a b) -> p a b", a=2)
        P.Dm("sp", "s_r1", [], r1_mem, out=memtok, in_=d["memp"].rearrange("(a p) f -> p a f", p=128))
        for mc in range(2):
            for hf in range(2):
                P.I("act", "activation", r1_mem, [stg_tt[1], small_tt], out=stg[1][:, 0:1024],
                    in_=memtok[:, mc, hf * 1024:(hf + 1) * 1024], func=AF.Square, accum_out=small[:, 2 * mc + hf:2 * mc + hf + 1])
        for mc in range(2):
            P.I("dve", "tensor_tensor", [small_tt], [small_tt], out=small[:, 4 + mc:5 + mc], in0=small[:, 2 * mc:2 * mc + 1],
                in1=small[:, 2 * mc + 1:2 * mc + 2], op=ALU.add)
            P.I("dve", "tensor_scalar", [small_tt], [small_tt], out=small[:, 4 + mc:5 + mc], in0=small[:, 4 + mc:5 + mc],
                scalar1=1.0 / D, scalar2=EPS, op0=ALU.mult, op1=ALU.add)
            rsqrt_inplace(small[:, 4 + mc:5 + mc], [small_tt])
            P.I("dve", "tensor_scalar", [small_tt] + r1_mem, r1_mem, out=memtok[:, mc, :], in0=memtok[:, mc, :],
                scalar1=small[:, 4 + mc:5 + mc], scalar2=None, op0=ALU.mult)
        mh_sb = hreg.view(F32, 0, 8 * 256).rearrange("p (k m) -> p k m", k=8)
        for half in range(2):
            for kk in range(8):
                k = half * 8 + kk
                for mc in range(2):
                    b = P.next_ps()
                    P.tr(ps_tt[b], psb[b][:, 0:128], memtok[:, mc, k * 128:(k + 1) * 128], ident[:], reads=r1_mem + [c_tt])
                    P.copy(P.ev_eng(), mh_sb[:, kk, mc * 128:(mc + 1) * 128], psb[b][:, 0:128], reads=[ps_tt[b]], writes=hreg.t(0, 8192))
            P.Dm("sp", "s_hreg", hreg.t(0, 8192), [memhat_tt], out=memhat_d[:, half * 8:(half + 1) * 8, :], in_=mh_sb)
        P.I("pool", "memset", [], carry_tt, carry[:].rearrange("p a b c -> p (a b c)"), 0.0)

        def ntiles(T):
            t = [(n0, 512) for n0 in range(0, TP, 512)]
            if T > TP:
                t.append((TP, T - TP))
            return t
        self.ntiles = ntiles

        NWSCR = 420
        WSL = 105
        wscr_l = [P.dram_tmp(f"wscr{i_}", [WSL, 128, 4096], BF16) for i_ in range(NWSCR // WSL)]

        class _W:
            def __getitem__(self, idx):
                slot = idx[0]
                return wscr_l[slot // WSL][(slot % WSL,) + tuple(idx[1:])]
        wscr = _W()
        self.wcache = {}
        self.wready = {}

        def load_w(src_ap, kc, ncols, key=None):
            i = self.wbi % 2
            self.wbi += 1
            n = kc * ncols
            view = wb[i][:, 0:n].rearrange("p (k m) -> p k m", k=kc)
            if key is not None and key in self.wready:
                slot, tt = self.wready[key]
                P.Dm("sp", f"s_wb{i}", [tt], [wb_tt[i]], out=wb[i][:, 0:n], in_=wscr[slot, :, 0:n])
                return view, wb_tt[i]
            P.Dm("pool", f"s_wb{i}", [], [wb_tt[i]], out=view, in_=src_ap.rearrange("(k p) m -> p k m", p=128))
            if key is not None and key not in self.wcache:
                slot = len(self.wcache)
                assert slot < NWSCR
                tt = TT(f"wscr{slot}")
                P.Dm("sp", "s_wcv", [wb_tt[i]], [tt], out=wscr[slot, :, 0:n], in_=wb[i][:, 0:n])
                self.wcache[key] = (slot, tt)
            return view, wb_tt[i]

        def w_fence():
            tot = self.s.dcnt.get("s_wcv", 0)
            for key, (slot, tt) in self.wcache.items():
                if key not in self.wready:
                    tt.wr = ("s_wcv", tot)
                    self.wready[key] = (slot, tt)
        self.w_fence = w_fence
        self.load_w = load_w

        def hT_t(k):
            return hreg.t(k * TW * 2, (k + 1) * TW * 2)

        def rmsnorm(T, gkind, li_):
            hT = hreg.view(BF16, 0, KC * TW).rearrange("p (k t) -> p k t", k=KC)
            for (n0, nn) in ntiles(T):
                b = P.next_ps()
                for k in range(KC):
                    s_ = k % 2
                    P.I("act", "activation", [x_tt[k]], [sq_tt[s_]], out=sq[s_][:, 0:nn], in_=xT[:, k, n0:n0 + nn], func=AF.Square)
                    P.mm(ps_tt[b], psb[b][:, 0:nn], ones[:], sq[s_][:, 0:nn], k == 0, k == KC - 1, reads=[sq_tt[s_], c_tt])
                P.I("dve", "tensor_scalar", [ps_tt[b]], [rstd_tt], out=rstd[:, n0:n0 + nn], in0=psb[b][:, 0:nn], scalar1=1.0 / D,
                    scalar2=EPS, op0=ALU.mult, op1=ALU.add)
            rsqrt_inplace(rstd[:, 0:T], [rstd_tt])
            for k in range(KC):
                P.I("dve", "scalar_tensor_tensor", [x_tt[k], rstd_tt, vcol_tt], hT_t(k), out=hT[:, k, 0:T], in0=xT[:, k, 0:T],
                    scalar=gcol(gkind, li_, k), in1=rstd[:, 0:T], op0=ALU.mult, op1=ALU.mult)
            return hT
        self.rmsnorm = rmsnorm
        self.hT_t = hT_t

        def dense(T, w2d, kc, ncols_total, rhs_fn, rhs_tts_fn, evac, key=None):
            ncb = min(4096 // kc, ncols_total)
            for c0 in range(0, ncols_total, ncb):
                ncols = min(ncb, ncols_total - c0)
                wv, wtt = load_w(w2d[:, c0:c0 + ncols], kc, ncols, None if key is None else (key, c0))
                for m0 in range(0, ncols, 128):
                    mw = min(128, ncols - m0)
                    for (n0, nn) in ntiles(T):
                        b = P.next_ps()
                        for k in range(kc):
                            P.mm(ps_tt[b], psb[b][0:mw, 0:nn], wv[:, k, m0:m0 + mw], rhs_fn(k, n0, nn), k == 0, k == kc - 1,
                                 reads=[wtt] + rhs_tts_fn(k))
                        evac((c0 + m0) // 128, n0, nn, psb[b], ps_tt[b], mw)
        self.dense = dense

        def norm_rows(rows_ap, tt, nh, sbase, gain_b):
            for h in range(nh):
                P.I("act", "activation", [tt], [sq_tt[0], small_tt], out=sq[0][:, 0:128], in_=rows_ap[:, h * 128:(h + 1) * 128],
                    func=AF.Square, accum_out=small[:, sbase + h:sbase + h + 1])
            P.I("dve", "tensor_scalar", [small_tt], [small_tt], out=small[:, sbase + 8:sbase + 8 + nh], in0=small[:, sbase:sbase + nh],
                scalar1=1.0 / HD, scalar2=EPS, op0=ALU.mult, op1=ALU.add)
            rsqrt_inplace(small[:, sbase + 8:sbase + 8 + nh], [small_tt])
            for h in range(nh):
                kh = rows_ap[:, h * 128:(h + 1) * 128]
                P.I("dve", "scalar_tensor_tensor", [tt, small_tt, c_tt], [tt], out=kh, in0=kh,
                    scalar=small[:, sbase + 8 + h:sbase + 9 + h], in1=gain_b, op0=ALU.mult, op1=ALU.mult)
        self.norm_rows = norm_rows

        kmT_d = P.dram_tmp("kmT_d", [DEPTH, 128, 4 * 256], BF16)
        vm_d = P.dram_tmp("vm_d", [DEPTH, 128, 2 * 4 * HD], BF16)
        kmd_tt = [TT(f"kmT_d{i_}") for i_ in range(DEPTH)]
        vmd_tt = [TT(f"vm_d{i_}") for i_ in range(DEPTH)]

        def mem_kv(li_, T, first_chunk):
            if not first_chunk:
                P.Dm("sp", "s_kmT", [kmd_tt[li_]], [km_tt], out=kmT[:].rearrange("p h m -> p (h m)"), in_=kmT_d[li_])
                P.Dm("sp", "s_vm", [vmd_tt[li_]], [vm_tt], out=vm[:].rearrange("p c h d -> p (c h d)"), in_=vm_d[li_])
                return
            memT = hreg.view(BF16, 0, KC * 256).rearrange("p (k m) -> p k m", k=KC)
            memT_t = hreg.t(0, KC * 256 * 2)
            mh = r1.view(F32, 0, KC * 256).rearrange("p (k m) -> p k m", k=KC)
            mh_t = r1.t(0, KC * 256 * 4)
            P.Dm("sp", "s_r1", [memhat_tt], mh_t, out=mh, in_=memhat_d)
            for k in range(KC):
                P.I("dve", "tensor_scalar", mh_t + [vcol_tt], memT_t, out=memT[:, k, :], in0=mh[:, k, :], scalar1=gcol("mem", li_, k),
                    scalar2=None, op0=ALU.mult)
            kvrow = [stg[0][:, 0:1024], stg[1][:, 0:1024]]
            for cb in range(4):
                wv, wtt = load_w(d["w_mem_kv"][li_, :, cb * 256:(cb + 1) * 256], KC, 256, ("memkv", li_, cb))
                for mc in range(2):
                    b = P.next_ps()
                    for k in range(KC):
                        P.mm(ps_tt[b], psb[b][:, 0:256], memT[:, k, mc * 128:(mc + 1) * 128], wv[:, k, :], k == 0, k == KC - 1,
                             reads=[wtt] + memT_t)
                    P.copy(P.ev_eng(), kvrow[mc][:, cb * 256:(cb + 1) * 256], psb[b][:, 0:256], reads=[ps_tt[b]], writes=[stg_tt[mc]])
            for mc in range(2):
                norm_rows(kvrow[mc], stg_tt[mc], 4, 16, gkb[:, li_, :])
                for h in range(4):
                    b = P.next_ps()
                    P.tr(ps_tt[b], psb[b][:, 0:128], kvrow[mc][:, h * 128:(h + 1) * 128], ident[:], reads=[stg_tt[mc], c_tt])
                    P.copy(P.ev_eng(), kmT[:, h, mc * 128:(mc + 1) * 128], psb[b][:, 0:128], reads=[ps_tt[b]], writes=[km_tt])
                P.I("pool", "tensor_copy", [stg_tt[mc]], [vm_tt], out=vm[:, mc, :, :].rearrange("p h d -> p (h d)"), in_=kvrow[mc][:, 512:1024])
                if first_chunk:
                    P.Dm("sp", f"s_stg{mc}", [stg_tt[mc]], [], out=o["mkv"][li_, mc * 128:(mc + 1) * 128, :], in_=kvrow[mc])
            P.Dm("sp", "s_kmT", [km_tt], [kmd_tt[li_]], out=kmT_d[li_], in_=kmT[:].rearrange("p h m -> p (h m)"))
            P.Dm("sp", "s_vm", [vm_tt], [vmd_tt[li_]], out=vm_d[li_], in_=vm[:].rearrange("p c h d -> p (c h d)"))

        kmTs = hreg.view(BF16, 0, NSB * 4 * 256).rearrange("p (s h m) -> p s h m", s=NSB, h=4)
        vms = hreg.view(BF16, NSB * 4 * 256 * 2, NSB * 2 * 4 * HD).rearrange("p (s c h d) -> p s c h d", s=NSB, c=2, h=4)

        def mem_attend_sample(li_):
            htt_all = hreg.tts
            for sb in range(NSB):
                for mc in range(2):
                    P.Dm("sp", f"s_stg{mc}", [], [stg_tt[mc]], out=stg[mc][:, 0:1024], in_=d["cmkv"][li_, sb, mc * 128:(mc + 1) * 128, :])
                    for h in range(4):
                        b = P.next_ps()
                        P.tr(ps_tt[b], psb[b][:, 0:128], stg[mc][:, h * 128:(h + 1) * 128], ident[:], reads=[stg_tt[mc], c_tt])
                        P.copy(P.ev_eng(), kmTs[:, sb, h, mc * 128:(mc + 1) * 128], psb[b][:, 0:128], reads=[ps_tt[b]], writes=htt_all)
                    P.I("pool", "tensor_copy", [stg_tt[mc]], htt_all, out=vms[:, sb, mc, :, :].rearrange("p h d -> p (h d)"), in_=stg[mc][:, 512:1024])
            for h in range(4):
                for sb in range(NSB):
                    n0, nn = NS1 * sb, NS1
                    bacc, bden = 4, 5
                    for mc in range(2):
                        b = P.next_ps()
                        P.mm(ps_tt[b], psb[b][:, 0:nn], kmTs[:, sb, h, mc * 128:(mc + 1) * 128], qns[:, h, n0:n0 + nn], True, True, reads=htt_all + [qns_tt])
                        P.I("act", "activation", [ps_tt[b]], [pt_tt[mc]], out=pt[mc][:, 0:nn], in_=psb[b][:, 0:nn], func=AF.Exp, scale=SCALE)
                        P.mm(ps_tt[bacc], psb[bacc][:, 0:nn], vms[:, sb, mc, h, :], pt[mc][:, 0:nn], mc == 0, mc == 1, reads=htt_all + [pt_tt[mc]])
                        P.mm(ps_tt[bden], psb[bden][:, 0:nn], onesb[:], pt[mc][:, 0:nn], mc == 0, mc == 1, reads=[c_tt, pt_tt[mc]])
                    P.I("dve", "reciprocal", [ps_tt[bden]], [sq_tt[1]], out=sq[1][:, 0:nn], in_=psb[bden][:, 0:nn])
                    P.I("dve", "tensor_tensor", [ps_tt[bacc], sq_tt[1]], cat_t(12 + h), out=catT[:, 12 + h, TP + n0:TP + n0 + nn], in0=psb[bacc][:, 0:nn],
                        in1=sq[1][:, 0:nn], op=ALU.mult)

        catT = creg.view(BF16, 0, KC * TW).rearrange("p (k t) -> p k t", k=KC)
        self.catT = catT

        def cat_t(k):
            return creg.t(k * TW * 2, (k + 1) * TW * 2)
        self.cat_t = cat_t

        def qnorm_feat(src, src_tt, T, gain_col, dst, dst_tts):
            for (n0, nn) in ntiles(T):
                b = P.next_ps()
                P.I("act", "activation", [src_tt], [sq_tt[0]], out=sq[0][:, 0:nn], in_=src[:, n0:n0 + nn], func=AF.Square)
                P.mm(ps_tt[b], psb[b][:, 0:nn], ones[:], sq[0][:, 0:nn], True, True, reads=[sq_tt[0], c_tt])
                P.I("dve", "tensor_scalar", [ps_tt[b]], [rstd_tt], out=rstd[:, n0:n0 + nn], in0=psb[b][:, 0:nn], scalar1=1.0 / HD,
                    scalar2=EPS, op0=ALU.mult, op1=ALU.add)
            rsqrt_inplace(rstd[:, 0:T], [rstd_tt])
            P.I("dve", "scalar_tensor_tensor", [src_tt, rstd_tt, vcol_tt], dst_tts, out=dst[:, 0:T], in0=src[:, 0:T], scalar=gain_col,
                in1=rstd[:, 0:T], op0=ALU.mult, op1=ALU.mult)
        self.qnorm_feat = qnorm_feat

        def mem_attend_head(li_, h, T):
            qnorm_feat(qraw, qraw_tt, T, gcol("memqk", li_, 0), qn, [qn_tt])
            if T > TP:
                P.I("pool", "tensor_copy", [qn_tt], [qns_tt], out=qns[:, h, :], in_=qn[:, TP:TP + NS])
            for (n0, nn) in [(n0, 512) for n0 in range(0, TP, 512)]:
                K_, V_, ktt, vtt = (kmT, vm, km_tt, vm_tt)
                bacc, bden = 4, 5
                for mc in range(2):
                    b = P.next_ps()
                    P.mm(ps_tt[b], psb[b][:, 0:nn], K_[:, h, mc * 128:(mc + 1) * 128], qn[:, n0:n0 + nn], True, True, reads=[ktt, qn_tt])
                    P.I("act", "activation", [ps_tt[b]], [pt_tt[mc]], out=pt[mc][:, 0:nn], in_=psb[b][:, 0:nn], func=AF.Exp, scale=SCALE)
                    P.mm(ps_tt[bacc], psb[bacc][:, 0:nn], V_[:, mc, h, :], pt[mc][:, 0:nn], mc == 0, mc == 1, reads=[vtt, pt_tt[mc]])
                    P.mm(ps_tt[bden], psb[bden][:, 0:nn], onesb[:], pt[mc][:, 0:nn], mc == 0, mc == 1, reads=[c_tt, pt_tt[mc]])
                P.I("dve", "reciprocal", [ps_tt[bden]], [sq_tt[1]], out=sq[1][:, 0:nn], in_=psb[bden][:, 0:nn])
                P.I("dve", "tensor_tensor", [ps_tt[bacc], sq_tt[1]], cat_t(12 + h), out=catT[:, 12 + h, n0:n0 + nn], in0=psb[bacc][:, 0:nn],
                    in1=sq[1][:, 0:nn], op=ALU.mult)
        self.mem_attend_head = mem_attend_head
        self.mem_kv = mem_kv

        def out_and_mlp(i, T, j):
            def ev_res(m, n0, nn, ps, ptt, mw):
                P.I("dve", "tensor_tensor", [x_tt[m], ptt], [x_tt[m]], out=xT[:, m, n0:n0 + nn], in0=xT[:, m, n0:n0 + nn], in1=ps[:, 0:nn], op=ALU.add)
            dense(T, d["w_out"][i], KC, D, lambda k, n0, nn: catT[:, k, n0:n0 + nn], lambda k: cat_t(k), ev_res, key=("out", i))
            hT = rmsnorm(T, "mlp", i)
            a_fg = r1.view(BF16, 0, 16 * TW).rearrange("p (k t) -> p k t", k=16)

            def a_t(m):
                return r1.t(m * TW * 2, (m + 1) * TW * 2)
            for fg in range(4):
                def ev_up(m, n0, nn, ps, ptt, mw):
                    P.I("act", "activation", [ptt], [sq_tt[0]], out=sq[0][:, 0:nn], in_=ps[:, 0:nn], func=AF.Relu)
                    P.I("dve", "tensor_tensor", [sq_tt[0]], a_t(m), out=a_fg[:, m, n0:n0 + nn], in0=sq[0][:, 0:nn], in1=sq[0][:, 0:nn], op=ALU.mult)
                dense(T, d["w_up"][i][:, fg * 2048:(fg + 1) * 2048], KC, 2048, lambda k, n0, nn: hT[:, k, n0:n0 + nn], hT_t, ev_up, key=("up", i, fg))
                dense(T, d["w_down"][i][fg * 2048:(fg + 1) * 2048, :], KC, D, lambda k, n0, nn: a_fg[:, k, n0:n0 + nn], a_t, ev_res, key=("down", i, fg))
        self.out_and_mlp = out_and_mlp

        def out_rows(src_fn, src_tts, ncols_feat, n_tok, dst_fn):
            nchunk = ncols_feat // 128
            s_ = None
            for c in range(nchunk):
                if c % 8 == 0:
                    s_ = self.stgi % 2
                    self.stgi += 1
                b = P.next_ps()
                P.tr(ps_tt[b], psb[b][0:n_tok, 0:128], src_fn(c), ident[:], reads=src_tts(c) + [c_tt])
                P.copy(P.ev_eng(), stg[s_][0:n_tok, (c % 8) * 128:(c % 8 + 1) * 128], psb[b][0:n_tok, 0:128], reads=[ps_tt[b]], writes=[stg_tt[s_]])
                if c % 8 == 7 or c == nchunk - 1:
                    c0 = (c // 8) * 8
                    w_ = (c - c0 + 1) * 128
                    P.Dm("sp", f"s_stg{s_}", [stg_tt[s_]], [], out=dst_fn(c0 * 128, w_), in_=stg[s_][0:n_tok, 0:w_])
        self.out_rows = out_rows

        def pool_layer(i, j, T):
            li = i // 2
            u = r1.view(F32, 0, 12 * TW).rearrange("p (k t) -> p k t", k=12)

            def u_t(c):
                return r1.t(c * TW * 4, (c + 1) * TW * 4)
            mem_kv(i, T, j == 0)
            hT = rmsnorm(T, "mix", i)

            def ev_in(m, n0, nn, ps, ptt, mw):
                if m < 12:
                    P.copy(P.ev_eng(), u[:, m, n0:n0 + nn], ps[:, 0:nn], reads=[ptt], writes=u_t(m))
                else:
                    P.copy(P.ev_eng(), qraw[:, n0:n0 + nn], ps[:, 0:nn], reads=[ptt], writes=[qraw_tt])
                    if n0 + nn >= T:
                        mem_attend_head(i, m - 12, T)
            dense(T, d["w_in_pool"][li], KC, D, lambda k, n0, nn: hT[:, k, n0:n0 + nn], hT_t, ev_in, key=("inpool", li))
            if T > TP:
                mem_attend_sample(i)
            L = 16 + TP
            E = hreg.view(F32, 0, L)
            A = hreg.view(F32, L * 4, L)
            B = hreg.view(F32, 2 * L * 4, L)
            pooled = hreg.view(BF16, 3 * L * 4, 3 * TW).rearrange("p (k t) -> p k t", k=3)
            htt = hreg.tts
            wg_v = [None, None]
            for half in range(2):
                wi = self.wbi % 2
                self.wbi += 1
                v = wb[wi][:, 0:2 * 3 * 384].rearrange("p (g k m) -> p g k m", g=2, k=3)
                for gg in range(2):
                    P.Dm("pool", f"s_wb{wi}", [], [wb_tt[wi]], out=v[:, gg, :, :],
                         in_=d["w_pg"][li, 2 * half + gg].rearrange("(k p) m -> p k m", p=128))
                wg_v[half] = (v, wb_tt[wi])
            if T > TP:
                for sb in range(NSB):
                    P.Dm("sp", "s_stg0", [], [stg_tt[0]], out=stg[0][0:15, 0:1024], in_=d["spool"][li, sb, :, 0:1024])
                    P.Dm("sp", "s_stg1", [], [stg_tt[1]], out=stg[1][0:15, 0:512], in_=d["spool"][li, sb, :, 1024:1536])
                    P.Dm("sp", "s_misc", [], [], out=o["pool_s"][li, sb, 0:11, :], in_=d["spool"][li, sb, 4:15, :])
                    for c in range(12):
                        src = stg[0][0:15, c * 128:(c + 1) * 128] if c < 8 else stg[1][0:15, (c - 8) * 128:(c - 7) * 128]
                        b = P.next_ps()
                        P.tr(ps_tt[b], psb[b][:, 0:15], src, ident[0:15, 0:15], reads=[stg_tt[0 if c < 8 else 1], c_tt])
                        P.copy(P.ev_eng(), sthist[:, sb, c, 1:16], psb[b][:, 0:15], reads=[ps_tt[b]], writes=[sthist_tt])
            for g in range(4):
                w = 2 << g
                steps = g + 1
                for cc in range(3):
                    c = 3 * g + cc
                    utt = u_t(c)
                    for part in range(1 + NSB if T > TP else 1):
                        if part == 0:
                            n = TP
                            P.I("pool", "tensor_copy", [carry_tt[li]], htt, out=E[:, 0:16], in_=carry[:, li, c, :])
                            P.I("pool", "tensor_copy", utt, htt, out=E[:, 16:16 + TP], in_=u[:, c, 0:TP])
                            ucol = u[:, c, 0:TP]
                            pout = pooled[:, cc, 0:TP]
                        else:
                            n = NS1
                            sb = part - 1
                            s0 = TP + NS1 * sb
                            P.I("pool", "tensor_copy", [sthist_tt], htt, out=E[:, 0:16], in_=sthist[:, sb, c, :])
                            P.I("pool", "tensor_copy", utt, htt, out=E[:, 16:16 + NS1], in_=u[:, c, s0:s0 + NS1])
                            ucol = u[:, c, s0:s0 + NS1]
                            pout = pooled[:, cc, s0:s0 + NS1]
                        Ln = 16 + n
                        src, dst = E, A
                        sh = 1
                        lo = 0
                        for st in range(steps):
                            lo += sh
                            P.I("dve", "tensor_tensor", htt, htt, out=dst[:, lo:Ln], in0=src[:, lo:Ln], in1=src[:, lo - sh:Ln - sh], op=ALU.add)
                            src = dst
                            dst = B if dst is A else A
                            sh *= 2
                        P.I("dve", "scalar_tensor_tensor", htt + utt, htt, out=pout, in0=src[:, 16:16 + n], scalar=1.0 / w, in1=ucol,
                            op0=ALU.mult, op1=ALU.subtract)
                        if part == 0 and j == 0:
                            P.I("dve", "tensor_tensor", htt + [c_tt], [sq_tt[1]], out=sq[1][:, 0:16], in0=src[:, 16:32], in1=invc[:, g, :], op=ALU.mult)
                            P.I("dve", "tensor_tensor", [sq_tt[1]] + utt, htt, out=pout[:, 0:16], in0=sq[1][:, 0:16], in1=ucol[:, 0:16], op=ALU.subtract)
                        if part == 0:
                            P.I("pool", "tensor_copy", utt + htt, [carry_tt[li]], out=carry[:, li, c, 1:16], in_=u[:, c, TP - 15:TP])
                wv, wtt = wg_v[g // 2]
                for mo in range(3):
                    for (n0, nn) in ntiles(T):
                        b = P.next_ps()
                        for ki in range(3):
                            P.mm(ps_tt[b], psb[b][:, 0:nn], wv[:, g % 2, ki, mo * 128:(mo + 1) * 128], pooled[:, ki, n0:n0 + nn],
                                 ki == 0, ki == 2, reads=[wtt] + htt)
                        P.I("act", "activation", [ps_tt[b], vcol_tt], cat_t(3 * g + mo), out=catT[:, 3 * g + mo, n0:n0 + nn], in_=psb[b][:, 0:nn],
                            func=AF.Identity, scale=gcol("pscale", li, 3 * g + mo))
            if j == NCH - 1:
                out_rows(lambda c: u[:, c, TP - 15:TP], u_t, TOK, 15, lambda f0, w_: o["pool_p"][li, :, f0:f0 + w_])
            if T > TP:
                for sb in range(NSB):
                    out_rows(lambda c, sb=sb: u[:, c, TP + NS1 * sb:TP + NS1 * (sb + 1)], u_t, TOK, NS1,
                             lambda f0, w_, sb=sb: o["pool_s"][li, sb, 11:15, f0:f0 + w_])
            out_and_mlp(i, T, j)

        kslcT = P.sb("kslcT", [128, SEQ], BF16)
        vslc = P.sb("vslc", [128, SEQ // 128, HD], BF16)
        kwinT = P.sb("kwinT", [128, 1024], BF16)
        vwin = P.sb("vwin", [128, 8, HD], BF16)
        msk = [P.sb(f"msk{i}", [128, 128], BF16) for i in range(2)]
        msk_tt = [TT(f"msk{i}") for i in range(2)]
        sel2s = [P.sb(f"sel2s{i}", [128, 2, 64], BF16) for i in range(2)]
        sel2s_tt = [TT(f"sel2s{i}") for i in range(2)]
        kslc_tt, vslc_tt, kwin_tt, vwin_tt = (TT(n) for n in ("kslcT", "vslc", "kwinT", "vwin"))
        cmpx = P.sb("cmpx", [128, 4, 528], BF16)
        cmpx_tt = [TT(f"cmpx{i}") for i in range(4)]
        ccarry = P.sb("ccarry", [128, 2, 4, 16], BF16)
        ccarry_tt = TT("ccarry")
        kcT = P.sb("kcT", [128, 2, 2, 256], BF16)
        vc = P.sb("vc", [128, 2, 2, 2, HD], BF16)
        kc_tt, vc_tt = TT("kcT"), TT("vc")
        w2b = P.sb("w2b", [128, 2, 2, HD], BF16)
        b1col = P.sb("b1col", [128, 4], F32)
        posT = P.sb("posT", [128, 4, 32], BF16)
        n_tt = TT("nsa_consts")
        gates = P.sb("gates", [128, TW], F32)
        gates_tt = TT("gates")
        gbias = P.sb("gbias", [128, 2], F32)
        Amat = P.sb("Amat", [128, 2, 64], F32)
        tri = P.sb("tri", [128, 128], BF16)
        tris = P.sb("tris", [128, 128], BF16)
        cmask = P.sb("cmask", [128, 2, 128], BF16)
        cmask_tt = TT("cmask")
        P32v = [stg[0][:, 0:768], stg[1][:, 0:768]]
        impT = P.sb("impT", [128, 2, 128], F32)
        impT_tt = TT("impT")
        selw = P.sb("selw", [128, 6, 64], F32)
        selw_tt = TT("selw")
        seltab = P.sb("seltab_sb", [128, 128], F32)
        seltab_tt = TT("seltab")
        m8 = P.sb("m8", [128, 16], F32)
        kvf = [P.sb(f"kvf{i}", [128, TW], F32) for i in range(2)]
        kvf_tt = [TT(f"kvf{i}") for i in range(2)]
        rowsb = [stg[i][:, 0:512].rearrange("p (a b) -> p a b", a=4) for i in range(2)]
        rowsb_tt = stg_tt
        vrow = [P.sb(f"vrow{i}", [128, 4, HD], BF16) for i in range(2)]
        vrow_tt = [TT(f"vrow{i}") for i in range(2)]
        kbf = [P.sb(f"kbf{i}", [128, TP], BF16) for i in range(2)]
        kbf_tt = [TT(f"kbf{i}") for i in range(2)]
        tokacc = P.sb("tokacc", [128, 768], F32)
        tokacc_tt = TT("tokacc")
        obr = P.sb("obr", [128, 384], F32)
        obr_tt = TT("obr")
        rden = P.sb("rden", [128, 384], F32)
        rden_tt = TT("rden")
        gm = P.sb("gm", [128, 384], F32)
        gm_tt = TT("gm")
        h1 = P.sb("h1", [128, 32], BF16)
        h1_tt = TT("h1")
        vtmp = P.sb("vtmp", [128, HD], BF16)
        vtmp_tt = TT("vtmp")
        skv = P.sb("skv", [128, 12, NS], F32)
        skv_tt = TT("skv")
        kslc_d = P.dram_tmp("kslc_d", [2, 2, 128, SEQ], BF16)
        kwin_d = P.dram_tmp("kwin_d", [2, 2, 128, SEQ], BF16)
        vslc_d = P.dram_tmp("vslc_d", [2, 2, SEQ, HD], BF16)
        vwin_d = P.dram_tmp("vwin_d", [2, 2, SEQ, HD], BF16)
        kslcd_tt, kwind_tt, vslcd_tt, vwind_tt = TT("kslc_d"), TT("kwin_d"), TT("vslc_d"), TT("vwin_d")
        d["seltab"] = P.dram_in("seltab", [SEQ // 128, 128, 128])
        self.kvfi = 0

        P.I("pool", "memset", [], [n_tt], tri[:], 1.0)
        P.I("pool", "affine_select", [n_tt], [n_tt], out=tri[:], in_=tri[:], pattern=[[1, 128]], compare_op=ALU.is_ge, fill=0.0,
            base=0, channel_multiplier=-1)
        P.I("pool", "memset", [n_tt], [n_tt], tris[:], 1.0)
        P.I("pool", "affine_select", [n_tt], [n_tt], out=tris[:], in_=tris[:], pattern=[[-1, 128]], compare_op=ALU.is_ge, fill=0.0,
            base=-1, channel_multiplier=1)
        for ck in range(2):
            for term, (lo_b, hi_b) in enumerate(((0, 3), (-1, 2))):
                dst = selw[:, term, :]
                P.I("pool", "memset", [selw_tt], [selw_tt], dst, 1.0)
                P.I("pool", "affine_select", [selw_tt], [selw_tt], out=dst, in_=dst, pattern=[[-4, 64]], compare_op=ALU.is_ge, fill=0.0,
                    base=128 * ck - lo_b, channel_multiplier=1)
                P.I("pool", "affine_select", [selw_tt], [selw_tt], out=dst, in_=dst, pattern=[[4, 64]], compare_op=ALU.is_ge, fill=0.0,
                    base=hi_b - 128 * ck, channel_multiplier=-1)
            P.I("pool", "tensor_tensor", [selw_tt], [n_tt], out=Amat[:, ck, :], in0=selw[:, 0, :], in1=selw[:, 1, :], op=ALU.add)
        P.I("pool", "memset", [n_tt], [kc_tt], kcT[:].rearrange("p a b c -> p (a b c)"), 0.0)
        P.I("pool", "memset", [n_tt], [vc_tt], vc[:].rearrange("p a b c d -> p (a b c d)"), 0.0)
        P.I("pool", "memset", [n_tt], [ccarry_tt], ccarry[:].rearrange("p a b c -> p (a b c)"), 0.0)
        P.I("pool", "memset", [n_tt], [gates_tt], gates[:], 0.0)
        P.Dm("sp", "s_cst", [], [n_tt], out=gbias[0:36, :], in_=d["gate_bias"].rearrange("a b -> b a"), allow_slow_non_contiguous=True)
        for li_ in range(NL // 2):
            for kv in range(2):
                P.Dm("pool", "s_cst", [], [n_tt], out=w2b[:, li_, kv, :], in_=d["cmp_w2"][li_, kv])
                P.Dm("sp", "s_stg0", [], [stg_tt[0]], out=stg[0][0:32, 0:128], in_=d["cmp_pos"][li_, kv])
                b = P.next_ps()
                P.tr(ps_tt[b], psb[b][:, 0:32], stg[0][0:32, 0:128], ident[0:32, 0:32], reads=[stg_tt[0], c_tt])
                P.I("dve", "tensor_copy", [ps_tt[b]], [n_tt], out=posT[:, li_ * 2 + kv, :], in_=psb[b][:, 0:32])
                wv, wtt = load_w(d["cmp_w1"][li_, kv], 32, 128, ("w1", li_, kv))
                b = P.next_ps()
                for jj in range(32):
                    P.mm(ps_tt[b], psb[b][:, 0:1], wv[:, jj, :], posT[:, li_ * 2 + kv, jj:jj + 1], jj == 0, jj == 31, reads=[wtt, n_tt])
                P.I("dve", "tensor_copy", [ps_tt[b]], [n_tt], out=b1col[:, li_ * 2 + kv:li_ * 2 + kv + 1], in_=psb[b][:, 0:1])

        def norm_from_psum(src, ptt, nn, gain_col, dst, dst_tts, sa=None, sb_=None):
            (a_ap, a_tt) = sa if sa is not None else (sq[0], sq_tt[0])
            (r_ap, r_tt) = sb_ if sb_ is not None else (sq[1], sq_tt[1])
            P.I("act", "activation", [ptt], [a_tt], out=a_ap[:, 0:nn], in_=src, func=AF.Square)
            b = P.next_ps()
            P.mm(ps_tt[b], psb[b][:, 0:nn], ones[:], a_ap[:, 0:nn], True, True, reads=[a_tt, c_tt])
            P.I("dve", "tensor_scalar", [ps_tt[b]], [r_tt], out=r_ap[:, 0:nn], in0=psb[b][:, 0:nn], scalar1=1.0 / HD, scalar2=EPS,
                op0=ALU.mult, op1=ALU.add)
            rsqrt_inplace(r_ap[:, 0:nn], [r_tt])
            P.I("dve", "scalar_tensor_tensor", [ptt, r_tt, vcol_tt], dst_tts, out=dst, in0=src, scalar=gain_col, in1=r_ap[:, 0:nn],
                op0=ALU.mult, op1=ALU.mult)

        def bc3(ap2d):
            return ap2d.unsqueeze(1).to_broadcast([128, 3, 128])

        def nsa_layer(i, j, T):
            li = i // 2
            q = r1.view(BF16, 0, 12 * TW).rearrange("p (k t) -> p k t", k=12)

            def q_t(h):
                return r1.t(h * TW * 2, (h + 1) * TW * 2)
            mem_kv(i, T, j == 0)
            hT = rmsnorm(T, "mix", i)
            W = d["w_in_nsa"][li]
            rhs = lambda k, n0, nn: hT[:, k, n0:n0 + nn]

            def ev_q(m, n0, nn, ps, ptt, mw):
                norm_from_psum(ps[:, 0:nn], ptt, nn, gcol("nsaqk", li, 0), q[:, m, n0:n0 + nn], q_t(m))
            dense(T, W[:, 0:TOK], KC, TOK, rhs, hT_t, ev_q, key=("nsa_q", li))

            def ev_kv(m, n0, nn, ps, ptt, mw):
                slot, g = m // 2, m % 2
                samp = n0 >= TP
                if not samp:
                    fi = self.kvfi % 2
                    self.kvfi += 1
                    f, ftt = kvf[fi], kvf_tt[fi]
                    fdst, fdst_tt = f[:, 0:nn], [ftt]
                else:
                    fdst, fdst_tt = skv[:, m, :], [skv_tt]
                if slot in (2, 4):
                    norm_from_psum(ps[:, 0:nn], ptt, nn, gcol("nsaqk", li, 2 if slot == 2 else 3), fdst, fdst_tt)
                else:
                    P.copy(P.ev_eng(), fdst, ps[:, 0:nn], reads=[ptt], writes=fdst_tt)
                if samp:
                    return
                if slot in (0, 1):
                    ci = slot * 2 + g
                    P.I("pool", "tensor_copy", [ftt], [cmpx_tt[ci]], out=cmpx[:, ci, 16:16 + TP], in_=f[:, 0:TP])
                if slot in (2, 4):
                    P.I("pool", "tensor_copy", [ftt], [kbf_tt[fi]], out=kbf[fi][:, :], in_=f[:, 0:TP])
                    dst_d, dtt = (kslc_d, kslcd_tt) if slot == 2 else (kwin_d, kwind_tt)
                    P.Dm("sp", f"s_kbf{fi}", [kbf_tt[fi]], [dtt], out=dst_d[li, g, :, j * TP:(j + 1) * TP], in_=kbf[fi][:, :])
                for tb in range(TP // 128):
                    b = P.next_ps()
                    P.tr(ps_tt[b], psb[b][:, 0:128], f[:, tb * 128:(tb + 1) * 128], ident[:], reads=[ftt, c_tt])
                    P.copy(P.ev_eng(), rowsb[fi][:, tb, :], psb[b][:, 0:128], reads=[ps_tt[b]], writes=[rowsb_tt[fi]])
                    if slot in (3, 5):
                        P.I("pool", "tensor_copy", [rowsb_tt[fi]], [vrow_tt[fi]], out=vrow[fi][:, tb, :], in_=rowsb[fi][:, tb, :])
                if slot < 4:
                    c0 = slot * 256 + g * 128
                    P.Dm("sp", f"s_stg{fi}", [rowsb_tt[fi]], [],
                         out=o["nkv_p"][li, j * TP:(j + 1) * TP, c0:c0 + 128].rearrange("(tb p) d -> p tb d", p=128), in_=rowsb[fi])
                elif j == NCH - 1:
                    c0 = (slot - 4) * 256 + g * 128
                    P.Dm("sp", f"s_stg{fi}", [rowsb_tt[fi]], [],
                         out=o["win_p"][li, :, c0:c0 + 128].rearrange("(tb p) d -> p tb d", p=128), in_=rowsb[fi])
                if slot in (3, 5):
                    dst_d, dtt = (vslc_d, vslcd_tt) if slot == 3 else (vwin_d, vwind_tt)
                    P.Dm("sp", f"s_vrow{fi}", [vrow_tt[fi]], [dtt],
                         out=dst_d[li, g, j * TP:(j + 1) * TP, :].rearrange("(tb p) d -> p tb d", p=128), in_=vrow[fi][:, :, :])
            dense(T, W[:, TOK:2 * TOK], KC, TOK, rhs, hT_t, ev_kv, key=("nsa_kv", li))

            def ev_g(m, n0, nn, ps, ptt, mw):
                P.I("act", "activation", [ptt, n_tt], [gates_tt], out=gates[0:36, n0:n0 + nn], in_=ps[0:36, 0:nn], func=AF.Sigmoid,
                    bias=gbias[0:36, li:li + 1], scale=1.0)
            dense(T, W[:, 2 * TOK:2 * TOK + 36], KC, 36, rhs, hT_t, ev_g, key=("nsa_g", li))

            def ev_qm(m, n0, nn, ps, ptt, mw):
                P.copy(P.ev_eng(), qraw[:, n0:n0 + nn], ps[:, 0:nn], reads=[ptt], writes=[qraw_tt])
                if n0 + nn >= T:
                    mem_attend_head(i, m, T)
            dense(T, W[:, 2 * TOK + 36:NSA_W], KC, 512, rhs, hT_t, ev_qm, key=("nsa_qm", li))
            if T > TP:
                mem_attend_sample(i)

            compress(li, j, ccarry[:, li], ccarry_tt, kcT[:, li], [kc_tt], vc[:, li], [vc_tt])

            nkb_tot = 4 * (j + 1) if not cfg.get("no_attn") else 0
            kb0w = max(0, 4 * j - 4)
            for g in range(2 if not cfg.get("no_attn") else 0):
                P.Dm("sp", "s_kslc", [kslcd_tt], [kslc_tt], out=kslcT[:, 0:nkb_tot * 128], in_=kslc_d[li, g, :, 0:nkb_tot * 128])
                P.Dm("sp", "s_vslc", [vslcd_tt], [vslc_tt], out=vslc[:, 0:nkb_tot, :],
                     in_=vslc_d[li, g, 0:nkb_tot * 128, :].rearrange("(kb p) d -> p kb d", p=128))
                P.Dm("sp", "s_kwin", [kwind_tt], [kwin_tt], out=kwinT[:, 0:(nkb_tot - kb0w) * 128], in_=kwin_d[li, g, :, kb0w * 128:nkb_tot * 128])
                P.Dm("sp", "s_vwin", [vwind_tt], [vwin_tt], out=vwin[:, 0:nkb_tot - kb0w, :],
                     in_=vwin_d[li, g, kb0w * 128:nkb_tot * 128, :].rearrange("(kb p) d -> p kb d", p=128))
                for qt in range(TP // 128):
                    Q = 4 * j + qt
                    qc0 = qt * 128
                    ncv = 8 * Q + 7
                    nck = 1 if ncv <= 128 else 2
                    qtts = [t_ for h in range(6 * g, 6 * g + 6) for t_ in q_t(h)]

                    def qh(hh, g=g, qc0=qc0):
                        return q[:, 6 * g + 3 * hh:6 * g + 3 * hh + 3, qc0:qc0 + 128]

                    for ck in range(nck):
                        P.I("pool", "memset", [cmask_tt], [cmask_tt], cmask[:, ck, :], 1.0)
                        P.I("pool", "affine_select", [cmask_tt], [cmask_tt], out=cmask[:, ck, :], in_=cmask[:, ck, :], pattern=[[1, 128]],
                            compare_op=ALU.is_ge, fill=0.0, base=128 * Q - 31 - 2048 * ck, channel_multiplier=-16)
                    for hh in range(2):
                        for ck in range(nck):
                            bS = P.next_ps()
                            P.mm(ps_tt[bS], psb[bS][:, 0:384], kcT[:, li, g, ck * 128:(ck + 1) * 128], qh(hh), True, True, reads=[kc_tt] + qtts)
                            pv = P32v[ck][:, hh * 384:(hh + 1) * 384]
                            P.I("act", "activation", [ps_tt[bS]], [stg_tt[ck]], out=pv, in_=psb[bS][:, 0:384], func=AF.Exp, scale=SCALE)
                            P.I("dve", "tensor_tensor", [stg_tt[ck], cmask_tt], [stg_tt[ck]], out=pv.rearrange("p (h q) -> p h q", h=3),
                                in0=pv.rearrange("p (h q) -> p h q", h=3), in1=bc3(cmask[:, ck, :]), op=ALU.mult)
                            P.I("act", "copy", [stg_tt[ck]], [pt_tt[ck]], out=pt[ck][:, 0:384], in_=pv)
                            P.mm(ps_tt[4], psb[4][:, 0:384], vc[:, li, g, ck, :], pt[ck][:, 0:384], ck == 0, ck == nck - 1, reads=[vc_tt, pt_tt[ck]])
                            P.mm(ps_tt[5], psb[5][:, 0:384], ones[:], pv, ck == 0, ck == nck - 1, reads=[c_tt, stg_tt[ck]])
                        finish_branch(0, g, hh, qc0, 128, True, True, 4, 5)
                        for ck in range(nck):
                            pv = P32v[ck][:, hh * 384:(hh + 1) * 384]
                            P.I("dve", "tensor_tensor", [stg_tt[ck], rden_tt], [stg_tt[ck]], out=pv, in0=pv, in1=rden[:, :], op=ALU.mult)
                    for ck in range(nck):
                        P.I("dve", "tensor_reduce", [stg_tt[ck]], [impT_tt], out=impT[:, ck, :], in_=P32v[ck].rearrange("p (h q) -> p q h", h=6),
                            axis=AX.X, op=ALU.add)
                    for ck in range(nck):
                        P.mm(ps_tt[6], psb[6][:, 0:64], impT[:, ck, :], Amat[:, ck, :], ck == 0, ck == nck - 1, reads=[impT_tt, n_tt])
                    if g == 0:
                        P.Dm("sp", "s_seltab", [], [seltab_tt], out=seltab[:, :], in_=d["seltab"][Q])
                    sc, sc2, s_a, s_b = selw[:, 2, :], selw[:, 3, :], selw[:, 4, :], selw[:, 5, :]
                    P.I("dve", "tensor_tensor", [ps_tt[6], seltab_tt], [selw_tt], out=sc, in0=psb[6][:, 0:64], in1=seltab[:, 0:64], op=ALU.mult)
                    P.I("dve", "tensor_tensor", [selw_tt, seltab_tt], [selw_tt], out=sc, in0=sc, in1=seltab[:, 64:128], op=ALU.add)
                    topk_mask(sc, sc2, s_a, s_b, 128)

                    def smask(kb, idx, Q=Q, s_a=s_a):
                        mi = idx % 2
                        P.I("dve", "tensor_copy", [selw_tt], [sel2s_tt[mi]], out=sel2s[mi][:, :, :],
                            in_=s_a[:, 2 * kb:2 * kb + 2].unsqueeze(2).to_broadcast([128, 2, 64]))
                        b = P.next_ps()
                        P.mm(ps_tt[b], psb[b][:, 0:128], sel2s[mi][:, :, :].rearrange("p a b -> p (a b)"), identb[:], True, True,
                             reads=[sel2s_tt[mi], c_tt])
                        if kb == Q:
                            P.I("dve", "tensor_tensor", [ps_tt[b], n_tt], [msk_tt[mi]], out=msk[mi][:, :], in0=psb[b][:, 0:128], in1=tri[:], op=ALU.mult)
                        else:
                            P.I("act", "copy", [ps_tt[b]], [msk_tt[mi]], out=msk[mi][:, :], in_=psb[b][:, 0:128])
                        return (msk[mi][:, :], [msk_tt[mi]])

                    def wmask(kb, idx, Q=Q):
                        if kb == Q:
                            return (tri[:], [n_tt])
                        if kb == Q - 4:
                            return (tris[:], [n_tt])
                        return None

                    for (br, kbs, Kt, ktt, Vt, vtt, kofs, mfn) in (
                            (1, list(range(Q + 1)), kslcT, kslc_tt, vslc, vslc_tt, 0, smask),
                            (2, list(range(max(0, Q - 4), Q + 1)), kwinT, kwin_tt, vwin, vwin_tt, kb0w, wmask)):
                        n = len(kbs)

                        def pv_den(idx_, kl_, Vt=Vt, vtt=vtt, n=n):
                            for hh in range(2):
                                pi = hh + 2 * (idx_ % 2)
                                ba, bd = (4, 5) if hh == 0 else (6, 7)
                                P.mm(ps_tt[ba], psb[ba][:, 0:384], Vt[:, kl_, :], pt[pi][:, 0:384], idx_ == 0, idx_ == n - 1, reads=[vtt, pt_tt[pi]])
                                P.mm(ps_tt[bd], psb[bd][:, 0:384], onesb[:], pt[pi][:, 0:384], idx_ == 0, idx_ == n - 1, reads=[c_tt, pt_tt[pi]])
                        pend = None
                        for idx, kb in enumerate(kbs):
                            mk = mfn(kb, idx)
                            kl = kb - kofs
                            for hh in range(2):
                                bS = P.next_ps()
                                pi = hh + 2 * (idx % 2)
                                P.mm(ps_tt[bS], psb[bS][:, 0:384], Kt[:, kl * 128:(kl + 1) * 128], qh(hh), True, True, reads=[ktt] + qtts)
                                P.I("act", "activation", [ps_tt[bS]], [pt_tt[pi]], out=pt[pi][:, 0:384], in_=psb[bS][:, 0:384], func=AF.Exp, scale=SCALE)
                                if mk is not None:
                                    P.I("dve", "tensor_tensor", [pt_tt[pi]] + mk[1], [pt_tt[pi]], out=pt[pi][:, 0:384].rearrange("p (h q) -> p h q", h=3),
                                        in0=pt[pi][:, 0:384].rearrange("p (h q) -> p h q", h=3), in1=bc3(mk[0]), op=ALU.mult)
                            if pend is not None:
                                pv_den(*pend)
                            pend = (idx, kl)
                        pv_den(*pend)
                        for hh in range(2):
                            ba, bd = (4, 5) if hh == 0 else (6, 7)
                            finish_branch(br, g, hh, qc0, 128, False, False, ba, bd)
                    for hh in range(2):
                        P.I("act", "copy", [tokacc_tt], [t_ for h in range(6 * g + 3 * hh, 6 * g + 3 * hh + 3) for t_ in cat_t(h)],
                            out=catT[:, 6 * g + 3 * hh:6 * g + 3 * hh + 3, qc0:qc0 + 128],
                            in_=tokacc[:, hh * 384:(hh + 1) * 384].rearrange("p (h q) -> p h q", h=3))
            if T > TP and not cfg.get("no_sample"):
                nsa_sample(i, li, q, q_t)
            out_and_mlp(i, T, j)

        def topk_mask(sc, sc2, s_a, s_b, npart):
            P.I("dve", "max", [selw_tt], [selw_tt], out=m8[0:npart, 0:8], in_=sc)
            P.I("dve", "match_replace", [selw_tt], [selw_tt], out=sc2, in_to_replace=m8[0:npart, 0:8], in_values=sc, imm_value=-3.0e38)
            P.I("dve", "max", [selw_tt], [selw_tt], out=m8[0:npart, 8:16], in_=sc2)
            P.I("dve", "tensor_scalar", [selw_tt], [selw_tt], out=s_a, in0=sc, scalar1=m8[0:npart, 15:16], scalar2=None, op0=ALU.is_ge)
            P.I("dve", "tensor_scalar", [selw_tt], [selw_tt], out=s_b, in0=sc, scalar1=-5.0e29, scalar2=None, op0=ALU.is_gt)
            P.I("dve", "tensor_tensor", [selw_tt], [selw_tt], out=s_a, in0=s_a, in1=s_b, op=ALU.mult)

        def gate_apply(br, g, hh, gc0, nq, src, src_tts, first):
            W3 = 3 * nq
            for h3 in range(3):
                r = 3 * (6 * g + 3 * hh + h3) + br
                P.I("dve", "tensor_scalar", [gates_tt, c_tt], [gm_tt], out=gm[0:36, h3 * nq:(h3 + 1) * nq],
                    in0=gates[0:36, gc0:gc0 + nq], scalar1=ident[0:36, r:r + 1], scalar2=None, op0=ALU.mult)
            bG = P.next_ps()
            P.mm(ps_tt[bG], psb[bG][:, 0:W3], ones[0:36, :], gm[0:36, 0:W3], True, True, reads=[gm_tt, c_tt])
            ta = tokacc[:, hh * 384:hh * 384 + W3]
            if first:
                P.I("dve", "tensor_tensor", src_tts + [ps_tt[bG]], [tokacc_tt], out=ta, in0=src, in1=psb[bG][:, 0:W3], op=ALU.mult)
            else:
                P.I("dve", "tensor_tensor", src_tts + [ps_tt[bG]], [gm_tt], out=gm[:, 0:W3], in0=src, in1=psb[bG][:, 0:W3], op=ALU.mult)
                P.I("dve", "tensor_tensor", [gm_tt, tokacc_tt], [tokacc_tt], out=ta, in0=ta, in1=gm[:, 0:W3], op=ALU.add)

        def finish_branch(br, g, hh, gc0, nq, first, guard, ba, bd):
            W3 = 3 * nq
            if guard:
                P.I("dve", "tensor_scalar", [ps_tt[bd]], [rden_tt], out=rden[:, 0:W3], in0=psb[bd][:, 0:W3], scalar1=1e-30, scalar2=None, op0=ALU.max)
                P.I("dve", "reciprocal", [rden_tt], [rden_tt], out=rden[:, 0:W3], in_=rden[:, 0:W3])
            else:
                P.I("dve", "reciprocal", [ps_tt[bd]], [rden_tt], out=rden[:, 0:W3], in_=psb[bd][:, 0:W3])
            P.I("dve", "tensor_tensor", [ps_tt[ba], rden_tt], [obr_tt], out=obr[:, 0:W3], in0=psb[ba][:, 0:W3], in1=rden[:, 0:W3], op=ALU.mult)
            gate_apply(br, g, hh, gc0, nq, obr[:, 0:W3], [obr_tt], first)

        h1s = [(P.sb(f"h1s{i_}", [128, 32], BF16), TT(f"h1s{i_}")) for i_ in range(4)]
        nsq = [(P.sb(f"nsq{i_}", [128, 32], F32), TT(f"nsq{i_}")) for i_ in range(2)]
        nrs = [(P.sb(f"nrs{i_}", [128, 32], F32), TT(f"nrs{i_}")) for i_ in range(2)]
        vtmps = [(P.sb(f"vtmps{i_}", [32, HD], BF16), TT(f"vtmps{i_}")) for i_ in range(2)]

        def compress(li, s_idx, carry_ap, carry_t, kc_dst, kc_dst_t, vc_dst, vc_dst_t):
            lo = 1 if s_idx == 0 else 0
            c_lo = 32 * s_idx - 1
            for kv in range(2):
                wv, wtt = load_w(d["cmp_w1"][li, kv], 32, 128, ("w1", li, kv))
                for g in range(2):
                    ci = kv * 2 + g
                    cx = cmpx[:, ci, :]
                    P.I("pool", "tensor_copy", [carry_t], [cmpx_tt[ci]], out=cx[:, 0:16], in_=carry_ap[:, ci, :])
                    P.I("pool", "tensor_copy", [cmpx_tt[ci]], [carry_t], out=carry_ap[:, ci, :], in_=cx[:, TP:TP + 16])
                    cb3 = cx.rearrange("p (c s) -> p c s", s=16)
                    bh = P.next_ps()
                    for jj in range(32):
                        P.mm(ps_tt[bh], psb[bh][:, 0:32], wv[:, jj, :], cb3[:, jj // 16:jj // 16 + 32, jj % 16], jj == 0, jj == 31,
                             reads=[wtt, cmpx_tt[ci]])
                    h1, h1_tt = h1s[ci]
                    P.I("act", "activation", [ps_tt[bh], n_tt], [h1_tt], out=h1[:, 0:32], in_=psb[bh][:, 0:32], func=AF.Gelu_apprx_tanh,
                        bias=b1col[:, li * 2 + kv:li * 2 + kv + 1], scale=1.0)
                    b2 = P.next_ps()
                    if kv == 0:
                        P.mm(ps_tt[b2], psb[b2][:, 0:32], w2b[:, li, 0, :], h1[:, 0:32], True, True, reads=[h1_tt, n_tt])
                        norm_from_psum(psb[b2][:, lo:32], ps_tt[b2], 32 - lo, gcol("nsaqk", li, 1), kc_dst[:, g, c_lo + lo:c_lo + 32], kc_dst_t,
                                       sa=nsq[g], sb_=nrs[g])
                    else:
                        vtmp, vtmp_tt = vtmps[g]
                        P.mm(ps_tt[b2], psb[b2][0:32, 0:128], h1[:, 0:32], w2b[:, li, 1, :], True, True, reads=[h1_tt, n_tt])
                        P.I("dve", "tensor_copy", [ps_tt[b2]], [vtmp_tt], out=vtmp[0:32, :], in_=psb[b2][0:32, 0:128])
                        r = c_lo + lo
                        t0 = lo
                        while t0 < 32:
                            ckk, p0 = r // 128, r % 128
                            n = min(32 - t0, 128 - p0)
                            P.Dm("sp", f"s_vtmps{g}", [vtmp_tt], vc_dst_t, out=vc_dst[p0:p0 + n, g, ckk, :], in_=vtmp[t0:t0 + n, :])
                            t0 += n
                            r += n

        R1H = 12 * TW * 2
        r1s_tt = r1.t(R1H, 12 * TW * 4)
        kcTs = r1.view(BF16, R1H, 2048).rearrange("p (g c) -> p g c", g=2)
        vcs = P.sb("vcs", [128, 2, 8, HD], BF16)
        kcs_tt, vcs_tt = TT("kcTs"), TT("vcs")
        scarry = P.sb("scarry", [128, 4, 16], BF16)
        scarry_tt = TT("scarry")
        pti = P.sb("pti", [128, NPAGE], I32)
        pidx = P.sb("pidx", [128, NPAGE], I32)
        iop = P.sb("iop", [128, NPAGE], I32)
        pidxB = P.sb("pidxB", [128, NPAGE], I32)
        ptf = P.sb("ptf", [128, NPAGE], F32)
        iopf = P.sb("iopf", [128, NPAGE], F32)
        pidx_tt = TT("pidx")
        Aloc = P.sb("Aloc", [128, 34], F32)
        P32s = P.sb("P32s", [128, 8, 24], F32)
        P32s_tt = TT("P32s")
        impTs = P.sb("impTs", [128, 8, NS1], F32)
        impTs_tt = TT("impTs")
        ssel = r1.view(F32, R1H + 4096, 4 * 260).rearrange("p (a b) -> p a b", a=4)
        ssel_tt = TT("ssel")
        kTs = r1.view(BF16, R1H + 4096 + 4160, 2 * TP).rearrange("p (a b) -> p a b", a=2)
        vTs = r1.view(BF16, R1H + 4096 + 4160 + 2048, 2 * 4 * HD).rearrange("p (a b c) -> p a b c", a=2, b=4)
        kTs_tt, vTs_tt = TT("kTs"), TT("vTs")
        pts = [P.sb(f"pts{i}", [128, 24], BF16) for i in range(2)]
        pts_tt = [TT(f"pts{i}") for i in range(2)]
        newk = P.sb("newk", [128, 12, NS], BF16)
        newv = P.sb("newv", [128, 4, HD], BF16)
        newk_tt, newv_tt = TT("newk"), TT("newv")
        P.I("pool", "iota", [], [n_tt], iop[:], pattern=[[0, NPAGE]], base=0, channel_multiplier=2)
        P.I("dve", "tensor_copy", [n_tt], [n_tt], out=iopf[:, :], in_=iop[:, :])
        for term, (lo_b, hi_b) in enumerate(((0, 3), (-1, 2))):
            dst = selw[:, term, 0:34]
            P.I("pool", "memset", [selw_tt], [selw_tt], dst, 1.0)
            P.I("pool", "affine_select", [selw_tt], [selw_tt], out=dst, in_=dst, pattern=[[-4, 34]], compare_op=ALU.is_ge, fill=0.0,
                base=4 - lo_b, channel_multiplier=1)
            P.I("pool", "affine_select", [selw_tt], [selw_tt], out=dst, in_=dst, pattern=[[4, 34]], compare_op=ALU.is_ge, fill=0.0,
                base=hi_b - 4, channel_multiplier=-1)
        P.I("pool", "tensor_tensor", [selw_tt], [n_tt], out=Aloc[:, :], in0=selw[:, 0, 0:34], in1=selw[:, 1, 0:34], op=ALU.add)

        pgbuf = [(stg[0][:, 0:512], stg_tt[0], "s_stg0"), (stg[1][:, 0:512], stg_tt[1], "s_stg1"),
                 (kvf[0][:, 0:512], kvf_tt[0], "s_kvf0"), (kvf[1][:, 0:512], kvf_tt[1], "s_kvf1")]

        def nsa_sample(i, li, q, q_t):
            P.I("dve", "tensor_copy", [skv_tt], [newk_tt], out=newk[:].rearrange("p a b -> p (a b)"), in_=skv[:].rearrange("p a b -> p (a b)"))
            for m in range(12):
                slot, g = m // 2, m % 2
                b = P.next_ps()
                P.tr(ps_tt[b], psb[b][0:NS, 0:128], skv[:, m, :], ident[:], reads=[skv_tt, c_tt])
                s_ = 0 if m < 8 else 1
                cofs = (m % 8) * 128 if m < 8 else (m - 8) * 128
                P.copy(P.ev_eng(), stg[s_][0:NS, cofs:cofs + 128], psb[b][0:NS, 0:128], reads=[ps_tt[b]], writes=[stg_tt[s_]])
            P.Dm("sp", "s_stg0", [stg_tt[0]], [], out=o["nkv_s"][li, :, :], in_=stg[0][0:NS, 0:1024])
            for sb in range(NSB):
                P.Dm("sp", "s_stg1", [stg_tt[1]], [], out=o["win_s"][li, sb, 508:512, :], in_=stg[1][NS1 * sb:NS1 * (sb + 1), 0:512])
                P.Dm("sp", "s_misc", [], [], out=o["win_s"][li, sb, 0:508, :], in_=d["nsawin"][li, sb, 4:512, :])

            SSTOP = cfg.get("sample_stop", 99)
            if SSTOP <= 0:
                return
            for sb in range(NSB):
                sc0 = TP + NS1 * sb
                for vi, m in enumerate((6, 7, 10, 11)):
                    b = P.next_ps()
                    P.tr(ps_tt[b], psb[b][0:NS1, 0:128], skv[:, m, NS1 * sb:NS1 * (sb + 1)], ident[:], reads=[skv_tt, c_tt])
                    P.I("dve", "tensor_copy", [ps_tt[b]], [newv_tt], out=newv[0:NS1, vi, :], in_=psb[b][0:NS1, 0:128])
                P.Dm("sp", "s_pti", [], [pidx_tt], out=pti[:, :], in_=d["ptab"][sb, :].partition_broadcast(128))
                P.I("dve", "tensor_copy", [pidx_tt], [pidx_tt], out=ptf[:, :], in_=pti[:, :])
                P.I("dve", "scalar_tensor_tensor", [pidx_tt, n_tt], [pidx_tt], out=ptf[:, :], in0=ptf[:, :], scalar=256.0, in1=iopf[:, :],
                    op0=ALU.mult, op1=ALU.add)
                P.I("dve", "tensor_copy", [pidx_tt], [pidx_tt], out=pidx[:, :], in_=ptf[:, :])
                P.I("dve", "tensor_scalar", [pidx_tt], [pidx_tt], out=ptf[:, :], in0=ptf[:, :], scalar1=1.0, scalar2=None, op0=ALU.add)
                P.I("dve", "tensor_copy", [pidx_tt], [pidx_tt], out=pidxB[:, :], in_=ptf[:, :])
                if SSTOP <= 1:
                    continue
                P.I("pool", "memset", [], [scarry_tt], scarry[:].rearrange("p a b -> p (a b)"), 0.0)
                P.I("pool", "memset", [], r1s_tt, kcTs[:].rearrange("p a b -> p (a b)"), 0.0)
                P.I("pool", "memset", [], [vcs_tt], vcs[:].rearrange("p a b c -> p (a b c)"), 0.0)
                for s_i in range(32):
                    for pg4 in range(4):
                        pg = 4 * s_i + pg4
                        pgb, pgt, pgs = pgbuf[pg4]
                        self.idma(pgs, [pidx_tt], [pgt], pgb, d["nsakv"][li], pidx[:, pg:pg + 1])
                        for ci in range(4):
                            b = P.next_ps()
                            P.tr(ps_tt[b], psb[b][:, 0:128], pgb[:, ci * 128:(ci + 1) * 128], ident[:], reads=[pgt, c_tt])
                            P.copy(P.ev_eng(), cmpx[:, ci, 16 + pg4 * 128:16 + (pg4 + 1) * 128], psb[b][:, 0:128], reads=[ps_tt[b]], writes=[cmpx_tt[ci]])
                    compress(li, s_i, scarry, scarry_tt, kcTs, r1s_tt, vcs, [vcs_tt])
                for g in range(2):
                    if SSTOP <= 2:
                        continue
                    qtts = [t_ for h in range(6 * g, 6 * g + 6) for t_ in q_t(h)]
                    qs = q[:, 6 * g:6 * g + 6, sc0:sc0 + NS1]
                    for ck in range(8):
                        bS = P.next_ps()
                        P.mm(ps_tt[bS], psb[bS][:, 0:24], kcTs[:, g, ck * 128:(ck + 1) * 128], qs, True, True, reads=r1s_tt + qtts)
                        P.I("act", "activation", [ps_tt[bS]], [P32s_tt], out=P32s[:, ck, :], in_=psb[bS][:, 0:24], func=AF.Exp, scale=SCALE)
                        if ck == 7:
                            P.I("dve", "tensor_scalar", [P32s_tt, n_tt], [P32s_tt], out=P32s[:, 7, :], in0=P32s[:, 7, :], scalar1=lastmask[:, 0:1],
                                scalar2=None, op0=ALU.mult)
                        pi = ck % 2
                        P.I("act", "copy", [P32s_tt], [pts_tt[pi]], out=pts[pi][:, :], in_=P32s[:, ck, :])
                        P.mm(ps_tt[4], psb[4][:, 0:24], vcs[:, g, ck, :], pts[pi][:, :], ck == 0, ck == 7, reads=[vcs_tt, pts_tt[pi]])
                        P.mm(ps_tt[5], psb[5][:, 0:24], ones[:], P32s[:, ck, :], ck == 0, ck == 7, reads=[c_tt, P32s_tt])
                    for hh in range(2):
                        finish_branch_s(0, g, hh, sc0, True, True, 4, 5)
                    P.I("dve", "tensor_tensor", [P32s_tt, rden_tt], [P32s_tt], out=P32s[:, :, :], in0=P32s[:, :, :],
                        in1=rden[:, 0:24].unsqueeze(1).to_broadcast([128, 8, 24]), op=ALU.mult)
                    P.I("dve", "tensor_reduce", [P32s_tt], [impTs_tt], out=impTs[:, :, :], in_=P32s[:, :, :].rearrange("p c (h q) -> p c q h", h=6),
                        axis=AX.X, op=ALU.add)
                    sc, sc2, s_a, s_b = ssel[0:NS1, 0, 0:257], ssel[0:NS1, 1, 0:257], ssel[0:NS1, 2, 0:257], ssel[0:NS1, 3, 0:257]
                    P.I("dve", "memset", r1s_tt, r1s_tt, ssel[0:NS1, 0, :], 0.0)
                    for ck in range(8):
                        bq = P.next_ps()
                        P.mm(ps_tt[bq], psb[bq][0:NS1, 0:34], impTs[:, ck, :], Aloc[:, :], True, True, reads=[impTs_tt, n_tt])
                        b_lo = 32 * ck - 1
                        sk = 1 if ck == 0 else 0
                        P.I("dve", "tensor_tensor", [ps_tt[bq]] + r1s_tt, r1s_tt, out=ssel[0:NS1, 0, b_lo + sk:b_lo + 34],
                            in0=ssel[0:NS1, 0, b_lo + sk:b_lo + 34], in1=psb[bq][0:NS1, sk:34], op=ALU.add)
                    P.I("dve", "memset", r1s_tt, r1s_tt, ssel[0:NS1, 0, 0:1], BIGV)
                    P.I("dve", "memset", r1s_tt, r1s_tt, ssel[0:NS1, 0, 255:257], BIGV)
                    topk_mask_s(sc, sc2, s_a, s_b)
                    if SSTOP <= 3:
                        continue
                    pend2 = None

                    def pv_den2(kc2_, pg4_, mi_):
                        P.mm(ps_tt[4], psb[4][:, 0:24], vTs[:, 0, pg4_, :], pts[mi_][:, :], kc2_ == 0, False, reads=r1s_tt + [pts_tt[mi_]])
                        P.mm(ps_tt[5], psb[5][:, 0:24], onesb[:], pts[mi_][:, :], kc2_ == 0, False, reads=[c_tt, pts_tt[mi_]])
                    for s_i in range(32):
                        if pend2 is not None:
                            pv_den2(*pend2)
                            pend2 = None
                        for pg4 in range(4):
                            pg = 4 * s_i + pg4
                            pgb, pgt, pgs = pgbuf[pg4]
                            self.idma(pgs, [pidx_tt], [pgt], pgb, d["nsakv"][li], pidxB[:, pg:pg + 1])
                            b = P.next_ps()
                            P.tr(ps_tt[b], psb[b][:, 0:128], pgb[:, g * 128:(g + 1) * 128], ident[:], reads=[pgt, c_tt])
                            P.copy(P.ev_eng(), kTs[:, 0, pg4 * 128:(pg4 + 1) * 128], psb[b][:, 0:128], reads=[ps_tt[b]], writes=r1s_tt)
                            P.I("pool", "tensor_copy", [pgt], r1s_tt, out=vTs[:, 0, pg4, :], in_=pgb[:, 256 + g * 128:256 + (g + 1) * 128])
                        for pg4 in range(4):
                            kc2 = 4 * s_i + pg4
                            mi = kc2 % 2
                            P.I("dve", "tensor_copy", r1s_tt, [sel2s_tt[mi]], out=sel2s[mi][0:NS1, :, :],
                                in_=s_a[:, 2 * kc2:2 * kc2 + 2].unsqueeze(2).to_broadcast([NS1, 2, 64]))
                            bm = P.next_ps()
                            P.mm(ps_tt[bm], psb[bm][:, 0:NS1], sel2s[mi][0:NS1, :, :].rearrange("p a b -> p (a b)"), identb[0:NS1, 0:NS1], True, True,
                                 reads=[sel2s_tt[mi], c_tt])
                            P.I("act", "copy", [ps_tt[bm]], [msk_tt[mi]], out=msk[mi][:, 0:NS1], in_=psb[bm][:, 0:NS1])
                            bS = P.next_ps()
                            P.mm(ps_tt[bS], psb[bS][:, 0:24], kTs[:, 0, pg4 * 128:(pg4 + 1) * 128], qs, True, True, reads=r1s_tt + qtts)
                            P.I("act", "activation", [ps_tt[bS]], [pts_tt[mi]], out=pts[mi][:, :], in_=psb[bS][:, 0:24], func=AF.Exp, scale=SCALE)
                            P.I("dve", "tensor_tensor", [pts_tt[mi], msk_tt[mi]], [pts_tt[mi]], out=pts[mi][:, :].rearrange("p (h q) -> p h q", h=6),
                                in0=pts[mi][:, :].rearrange("p (h q) -> p h q", h=6), in1=msk[mi][:, 0:NS1].unsqueeze(1).to_broadcast([128, 6, NS1]), op=ALU.mult)
                            if pend2 is not None:
                                pv_den2(*pend2)
                            pend2 = (kc2, pg4, mi)
                    pv_den2(*pend2)
                    new_rows_attend(g, sb, qs, qtts, 4 + g, 0 + g, 4, 5, tri_small=True)
                    for hh in range(2):
                        finish_branch_s(1, g, hh, sc0, False, False, 4, 5)
                    if SSTOP <= 4:
                        continue
                    for kc in range(4):
                        s_ = kc % 2
                        P.Dm("sp", f"s_stg{s_}", [], [stg_tt[s_]], out=stg[s_][:, 0:512], in_=d["nsawin"][li, sb, kc * 128:(kc + 1) * 128, :])
                        b = P.next_ps()
                        P.tr(ps_tt[b], psb[b][:, 0:128], stg[s_][:, g * 128:(g + 1) * 128], ident[:], reads=[stg_tt[s_], c_tt])
                        P.copy(P.ev_eng(), kTs[:, 1, kc * 128:(kc + 1) * 128], psb[b][:, 0:128], reads=[ps_tt[b]], writes=r1s_tt)
                        P.I("pool", "tensor_copy", [stg_tt[s_]], r1s_tt, out=vTs[:, 1, kc, :], in_=stg[s_][:, 256 + g * 128:256 + (g + 1) * 128])
                        mi = kc % 2
                        bS = P.next_ps()
                        P.mm(ps_tt[bS], psb[bS][:, 0:24], kTs[:, 1, kc * 128:(kc + 1) * 128], qs, True, True, reads=r1s_tt + qtts)
                        P.I("act", "activation", [ps_tt[bS]], [pts_tt[mi]], out=pts[mi][:, :], in_=psb[bS][:, 0:24], func=AF.Exp, scale=SCALE)
                        if kc == 0:
                            P.I("dve", "tensor_tensor", [pts_tt[mi], n_tt], [pts_tt[mi]], out=pts[mi][:, :].rearrange("p (h q) -> p h q", h=6),
                                in0=pts[mi][:, :].rearrange("p (h q) -> p h q", h=6), in1=tris[:, 0:NS1].unsqueeze(1).to_broadcast([128, 6, NS1]), op=ALU.mult)
                        P.mm(ps_tt[4], psb[4][:, 0:24], vTs[:, 1, kc, :], pts[mi][:, :], kc == 0, False, reads=r1s_tt + [pts_tt[mi]])
                        P.mm(ps_tt[5], psb[5][:, 0:24], onesb[:], pts[mi][:, :], kc == 0, False, reads=[c_tt, pts_tt[mi]])
                    new_rows_attend(g, sb, qs, qtts, 8 + g, 2 + g, 4, 5, tri_small=True)
                    for hh in range(2):
                        finish_branch_s(2, g, hh, sc0, False, False, 4, 5)
                    for hh in range(2):
                        P.I("act", "copy", [tokacc_tt], [t_ for h in range(6 * g + 3 * hh, 6 * g + 3 * hh + 3) for t_ in cat_t(h)],
                            out=catT[:, 6 * g + 3 * hh:6 * g + 3 * hh + 3, sc0:sc0 + NS1],
                            in_=tokacc[:, hh * 384:hh * 384 + 3 * NS1].rearrange("p (h q) -> p h q", h=3))

        lastmask = P.sb("lastmask", [128, 1], F32)
        P.I("pool", "memset", [], [n_tt], lastmask[:], 1.0)
        P.I("pool", "affine_select", [n_tt], [n_tt], out=lastmask[:], in_=lastmask[:], pattern=[[0, 1]], compare_op=ALU.is_ge, fill=0.0,
            base=126, channel_multiplier=-1)

        def new_rows_attend(g, sb, qs, qtts, km, vi, ba, bd, tri_small):
            bS = P.next_ps()
            P.mm(ps_tt[bS], psb[bS][0:NS1, 0:24], newk[:, km, NS1 * sb:NS1 * (sb + 1)], qs, True, True, reads=[newk_tt] + qtts)
            P.I("act", "activation", [ps_tt[bS]], [pts_tt[0]], out=pts[0][0:NS1, :], in_=psb[bS][0:NS1, 0:24], func=AF.Exp, scale=SCALE)
            P.I("dve", "tensor_tensor", [pts_tt[0], n_tt], [pts_tt[0]], out=pts[0][0:NS1, :].rearrange("p (h q) -> p h q", h=6),
                in0=pts[0][0:NS1, :].rearrange("p (h q) -> p h q", h=6), in1=tri[0:NS1, 0:NS1].unsqueeze(1).to_broadcast([NS1, 6, NS1]), op=ALU.mult)
            P.mm(ps_tt[ba], psb[ba][:, 0:24], newv[0:NS1, vi, :], pts[0][0:NS1, :], False, True, reads=[newv_tt, pts_tt[0]])
            P.mm(ps_tt[bd], psb[bd][:, 0:24], onesb[0:NS1, :], pts[0][0:NS1, :], False, True, reads=[c_tt, pts_tt[0]])

        def finish_branch_s(br, g, hh, gc0, first, guard, ba, bd):
            W3 = 3 * NS1
            if hh == 0:
                if guard:
                    P.I("dve", "tensor_scalar", [ps_tt[bd]], [rden_tt], out=rden[:, 0:24], in0=psb[bd][:, 0:24], scalar1=1e-30, scalar2=None, op0=ALU.max)
                    P.I("dve", "reciprocal", [rden_tt], [rden_tt], out=rden[:, 0:24], in_=rden[:, 0:24])
                else:
                    P.I("dve", "reciprocal", [ps_tt[bd]], [rden_tt], out=rden[:, 0:24], in_=psb[bd][:, 0:24])
                P.I("dve", "tensor_tensor", [ps_tt[ba], rden_tt], [obr_tt], out=obr[:, 0:24], in0=psb[ba][:, 0:24], in1=rden[:, 0:24], op=ALU.mult)
            gate_apply(br, g, hh, gc0, NS1, obr[:, hh * W3:(hh + 1) * W3], [obr_tt], first)

        def topk_mask_s(sc, sc2, s_a, s_b):
            P.I("dve", "max", r1s_tt, r1s_tt, out=m8[0:NS1, 0:8], in_=sc)
            P.I("dve", "match_replace", r1s_tt, r1s_tt, out=sc2, in_to_replace=m8[0:NS1, 0:8], in_values=sc, imm_value=-3.0e38)
            P.I("dve", "max", r1s_tt, r1s_tt, out=m8[0:NS1, 8:16], in_=sc2)
            P.I("dve", "tensor_scalar", r1s_tt, r1s_tt, out=s_a, in0=sc, scalar1=m8[0:NS1, 15:16], scalar2=None, op0=ALU.is_ge)

        self.nsa_layer = nsa_layer

        w_fence()
        for j in range(cfg["n_chunks"]):
            T = TW if j == 0 else TP
            for tb in range(TP // 128):
                for half in range(2):
                    s_ = self.stgi % 2
                    self.stgi += 1
                    P.Dm("sp", f"s_stg{s_}", [], [stg_tt[s_]], out=stg[s_][:, 0:1024],
                         in_=d["xp"][j * TP + tb * 128:j * TP + (tb + 1) * 128, half * 1024:(half + 1) * 1024])
                    for kk in range(8):
                        k = half * 8 + kk
                        b = P.next_ps()
                        P.tr(ps_tt[b], psb[b][:, 0:128], stg[s_][:, kk * 128:(kk + 1) * 128], ident[:], reads=[stg_tt[s_], c_tt])
                        P.copy(P.ev_eng(), xT[:, k, tb * 128:(tb + 1) * 128], psb[b][:, 0:128], reads=[ps_tt[b]], writes=[x_tt[k]])
            if j == 0:
                for half in range(2):
                    s_ = self.stgi % 2
                    self.stgi += 1
                    P.Dm("sp", f"s_stg{s_}", [], [stg_tt[s_]], out=stg[s_][0:NS, 0:1024], in_=d["xs"][:, half * 1024:(half + 1) * 1024])
                    for kk in range(8):
                        k = half * 8 + kk
                        b = P.next_ps()
                        P.tr(ps_tt[b], psb[b][:, 0:NS], stg[s_][0:NS, kk * 128:(kk + 1) * 128], ident[0:NS, 0:NS], reads=[stg_tt[s_], c_tt])
                        P.copy(P.ev_eng(), xT[:, k, TP:TP + NS], psb[b][:, 0:NS], reads=[ps_tt[b]], writes=[x_tt[k]])
            for i in range(cfg["n_layers"]):
                if i % 2 == 0:
                    pool_layer(i, j, T)
                else:
                    nsa_layer(i, j, T)
            for tb in range(TP // 128):
                out_rows(lambda c, tb=tb: xT[:, c, tb * 128:(tb + 1) * 128], lambda c: [x_tt[c]], D, 128,
                         lambda f0, w_, tb=tb: o["y_p"][j * TP + tb * 128:j * TP + (tb + 1) * 128, f0:f0 + w_])
            if j == 0:
                out_rows(lambda c: xT[:, c, TP:TP + NS], lambda c: [x_tt[c]], D, NS, lambda f0, w_: o["y_s"][:, f0:f0 + w_])
            w_fence()

        self.sbuf_left = self.nc.sbuf_bytes_remaining
        self.s.final_waits()
        self.emit()
        return nc

    def emit(self):
        nc = self.nc
        s = self.s
        sems = {}
        for name in list(Sched.ENGS) + sorted(self.sems_needed):
            sems[name] = self.es.enter_context(nc.semaphore(name))

        def replay(eng, e):
            for (waits, fn, tok, inc) in s.ops[eng]:
                for (sn, v) in waits:
                    e.wait_ge(sems[sn], v)
                if fn is not None:
                    ins = fn(e)
                    ins.then_inc(sems[tok[0]], inc)

        with nc.Block() as block:
            @block.tensor
            def _(e):
                replay("pe", e)

            @block.scalar
            def _(e):
                replay("act", e)

            @block.vector
            def _(e):
                replay("dve", e)

            @block.gpsimd
            def _(e):
                replay("pool", e)

            @block.sync
            def _(e):
                replay("sp", e)
        self.es.close()


def make_consts():
    inv = np.zeros((128, 4, 16), np.float32)
    for g, w in enumerate((2, 4, 8, 16)):
        for t in range(16):
            inv[:, g, t] = 1.0 / min(t + 1, w)
    return inv.reshape(128, 64)


def make_seltab():
    tab = np.zeros((SEQ // 128, 128, 128), np.float32)
    b = np.arange(64)[None, :]
    for Q in range(SEQ // 128):
        pos = Q * 128 + np.arange(128)[:, None]
        cur = pos // 64
        forced = (b == 0) | (b == cur) | (b == cur - 1)
        allowed = b <= cur
        tab[Q, :, :64] = (allowed & ~forced).astype(np.float32)
        tab[Q, :, 64:] = np.where(forced, BIGV, np.where(allowed, 0.0, NEGV)).astype(np.float32)
    return tab


def kernel(x_prompt, x_sample, mem_prompt, cache_mem_kv, cache_nsa_kv, cache_nsa_win, state_pool, page_table,
           g_norm_mix, g_norm_mlp, g_norm_mem, w_mem_kv, mem_qk_gain, w_out, w_mlp_up, w_mlp_down, w_in_pool,
           w_pool_group, pool_scale, w_in_nsa, nsa_gate_bias, nsa_qk_gain, cmp_pos, cmp_w1, cmp_w2, _cfg=None):
    import time as _t
    cfg = dict(CFG)
    if _cfg:
        cfg.update(_cfg)
    f = lambda a: np.ascontiguousarray(np.asarray(a))
    _t0 = _t.time()
    prog = Prog(cfg)
    nc = prog.build()
    print('build_s', _t.time() - _t0, {e: len(v) for e, v in prog.s.ops.items()}, flush=True)
    NL = cfg["n_layers"]; NPL = (NL + 1) // 2; NNL = max(NL // 2, 1)
    nsakv = f(np.asarray(cache_nsa_kv).reshape(2, 1280 * 128, 1024)[:NNL, :cfg.get("nsakv_rows", 1280 * 128)]).reshape(NNL, -1, 512)
    shared = {
        "g_mix": f(g_norm_mix), "g_mlp": f(g_norm_mlp), "g_mem": f(g_norm_mem),
        "w_mem_kv": f(w_mem_kv[:NL]), "mem_qk": f(mem_qk_gain), "w_out": f(w_out[:NL]), "w_up": f(w_mlp_up[:NL]),
        "w_down": f(w_mlp_down[:NL]), "w_in_pool": f(w_in_pool[:NPL]), "w_pg": f(w_pool_group), "pool_scale": f(pool_scale),
        "w_in_nsa": f(w_in_nsa[:NNL]), "gate_bias": f(nsa_gate_bias), "nsa_qk": f(nsa_qk_gain), "cmp_pos": f(cmp_pos),
        "cmp_w1": f(cmp_w1), "cmp_w2": f(cmp_w2), "cst": make_consts(), "seltab": make_seltab(),
    }
    for l_ in range(NNL):
        shared[f"nsakv{l_}"] = nsakv[l_]
    in_maps = []
    for c in range(NCORE):
        sl = slice(NSB * c, NSB * (c + 1))
        m = dict(shared)
        m["xp"] = f(x_prompt[c])
        m["xs"] = f(np.asarray(x_sample)[sl]).reshape(NS, D)
        m["memp"] = f(mem_prompt[c])
        m["cmkv"] = f(np.asarray(cache_mem_kv)[:NL, sl]).reshape(NL, NSB, 256, 1024)
        m["nsawin"] = f(np.asarray(cache_nsa_win)[:, sl]).reshape(2, NSB, 512, 512)
        m["spool"] = f(np.asarray(state_pool)[:, sl])
        m["ptab"] = f(np.asarray(page_table)[sl]).astype(np.int32)
        in_maps.append(m)
    _t0 = _t.time()
    res = run_bass_kernel_spmd(nc, in_maps, core_ids=list(range(NCORE)), **({'trace': True} if cfg.get('trace') else {}))
    if cfg.get('trace'):
        print('EXEC_NS', res.exec_time_ns, flush=True)
    print('run_s', _t.time() - _t0, flush=True)
    R = res.results
    y_p = np.stack([R[b]["y_p"] for b in range(2)])
    y_s = np.concatenate([R[c]["y_s"].reshape(NSB, NS1, D) for c in range(NCORE)], axis=0)
    mkv = np.stack([R[b]["mkv"] for b in range(2)], axis=1).reshape(DEPTH, 2, 256, 2, 4, HD)
    nkv_p = np.stack([R[b]["nkv_p"] for b in range(2)], axis=1).reshape(2, 2, SEQ, 4, 2, HD)
    nkv_s = np.concatenate([R[c]["nkv_s"].reshape(2, NSB, NS1, 1024) for c in range(NCORE)], axis=1).reshape(2, 8, NS1, 4, 2, HD)
    win_p = np.stack([R[b]["win_p"] for b in range(2)], axis=1).reshape(2, 2, 512, 2, 2, HD)
    win_s = np.concatenate([R[c]["win_s"] for c in range(NCORE)], axis=1).reshape(2, 8, 512, 2, 2, HD)
    pool_p = np.stack([R[b]["pool_p"] for b in range(2)], axis=1)
    pool_s = np.concatenate([R[c]["pool_s"] for c in range(NCORE)], axis=1)
    return (y_p, y_s, mkv, nkv_p, nkv_s, win_p, win_s, pool_p, pool_s)
```

```python
import numpy as np
from contextlib import ExitStack
import concourse.bass as bass
import concourse.mybir as mybir
from concourse.bass_utils import run_bass_kernel_spmd

F32 = mybir.dt.float32
BF16 = mybir.dt.bfloat16
I32 = mybir.dt.int32
AF = mybir.ActivationFunctionType
ALU = mybir.AluOpType
AX = mybir.AxisListType

D = 2048
KC = 16
SEQ = 4096
TP = 512
NSB = 4
NS1 = 4
NS = NSB * NS1
NCORE = 2
NCH = SEQ // TP
DEPTH = 4
HD = 128
TOK = 1536
DFF = 8192
EPS = 1e-6
SCALE = HD ** -0.5
NSA_W = 3620
PAST = 16384
NPAGE = 128
NEGV = -1e30
BIGV = 1e30

CFG = dict(n_layers=4, n_chunks=8)


class TT:
    __slots__ = ("name", "wr", "rd")

    def __init__(self, name):
        self.name = name
        self.wr = None
        self.rd = {}


class Sched:
    ENGS = ("pe", "act", "dve", "pool", "sp")

    def __init__(self):
        self.ops = {e: [] for e in self.ENGS}
        self.cnt = {e: 0 for e in self.ENGS}
        self.known = {e: {} for e in self.ENGS}
        self.dcnt = {}

    def _waits(self, eng, reads, writes, pe_chain=False):
        deps = {}

        def add(s, v):
            if pe_chain and s == "pe" and eng == "pe":
                return
            if deps.get(s, 0) < v:
                deps[s] = v

        for r in reads:
            if r.wr is not None:
                add(*r.wr)
        for w in writes:
            if w.wr is not None:
                add(*w.wr)
            for s, v in w.rd.items():
                add(s, v)
        out = []
        kn = self.known[eng]
        for s, v in deps.items():
            if kn.get(s, 0) < v:
                kn[s] = v
                out.append((s, v))
        return out

    def op(self, eng, fn, reads=(), writes=(), pe_chain=False):
        waits = self._waits(eng, reads, writes, pe_chain)
        self.cnt[eng] += 1
        tok = (eng, self.cnt[eng])
        self.ops[eng].append((waits, fn, tok, 1))
        for r in reads:
            if r.rd.get(eng, 0) < tok[1]:
                r.rd[eng] = tok[1]
        for w in writes:
            w.wr = tok
            w.rd = {}

    def dma(self, queue, fn, sem, reads=(), writes=()):
        waits = self._waits(queue, reads, writes)
        self.dcnt[sem] = self.dcnt.get(sem, 0) + 16
        tok = (sem, self.dcnt[sem])
        self.ops[queue].append((waits, fn, tok, 16))
        for r in reads:
            if r.rd.get(sem, 0) < tok[1]:
                r.rd[sem] = tok[1]
        for w in writes:
            w.wr = tok
            w.rd = {}

    def final_waits(self):
        allv = dict(self.cnt)
        allv.update(self.dcnt)
        waits = [(s, v) for s, v in allv.items() if v > 0 and s != "sp"]
        self.ops["sp"].append((waits, None, None, 0))


class Reg:
    def __init__(self, name, handle, nbytes, gran):
        self.name = name
        self.h = handle
        self.nbytes = nbytes
        self.gran = gran
        self.tts = [TT(f"{name}{i}") for i in range((nbytes + gran - 1) // gran)]

    def t(self, b0, b1):
        return self.tts[b0 // self.gran:(b1 - 1) // self.gran + 1]

    def view(self, dtype, b0, nelem):
        esz = 4 if dtype in (F32, I32) else 2
        assert b0 % 4 == 0 and (nelem * esz) % 4 == 0
        ap = self.h[:, b0 // 4:(b0 + nelem * esz) // 4]
        if esz == 2:
            ap = ap.bitcast(dtype)
        elif dtype == I32:
            ap = ap.bitcast(I32)
        return ap


class Prog:
    def __init__(self, cfg):
        self.cfg = cfg
        self.nc = bass.Bass("TRN2", target_bir_lowering=False)
        self.s = Sched()
        self.es = ExitStack()
        self.psi = 0
        self.evi = 0
        self.sems_needed = set()

    def dram_in(self, name, shape, dt=F32):
        return self.nc.dram_tensor(name, list(shape), dt, kind="ExternalInput").ap()

    def dram_out(self, name, shape, dt=F32):
        return self.nc.dram_tensor(name, list(shape), dt, kind="ExternalOutput").ap()

    def dram_tmp(self, name, shape, dt=F32):
        return self.nc.dram_tensor(name, list(shape), dt, kind="Internal").ap()

    def sb(self, name, shape, dt=F32):
        return self.es.enter_context(self.nc.sbuf_tensor(name, list(shape), dt))

    def reg(self, name, nbytes, gran):
        h = self.sb(name, [128, nbytes // 4], F32)
        return Reg(name, h, nbytes, gran)

    def op(self, eng, fn, reads=(), writes=(), pe_chain=False):
        self.s.op(eng, fn, reads, writes, pe_chain)

    def dma(self, queue, fn, sem, reads=(), writes=()):
        self.sems_needed.add(sem)
        self.s.dma(queue, fn, sem, reads, writes)

    def I(self, eng, method, reads, writes, *args, **kw):
        self.s.op(eng, lambda e: getattr(e, method)(*args, **kw), reads, writes)

    def Dm(self, queue, sem, reads, writes, **kw):
        self.sems_needed.add(sem)
        self.s.dma(queue, lambda e: e.dma_start(**kw), sem, reads, writes)

    def idma(self, sem, reads, writes, out, in_, idx_ap):
        self.sems_needed.add(sem)
        self.s.dma("pool", lambda e: e.indirect_dma_start(out=out, out_offset=None, in_=in_,
                                                          in_offset=bass.IndirectOffsetOnAxis(ap=idx_ap, axis=0)), sem, reads, writes)

    def mm(self, ps_tt, out, lhsT, rhs, start, stop, reads):
        self.op("pe", lambda e: e.matmul(out, lhsT=lhsT, rhs=rhs, start=start, stop=stop),
                reads=reads, writes=[ps_tt], pe_chain=not start)

    def tr(self, ps_tt, out, in_, ident, reads):
        self.op("pe", lambda e: e.transpose(out=out, in_=in_, identity=ident), reads=reads, writes=[ps_tt])

    def next_ps(self, lo=0, hi=4):
        i = lo + self.psi % (hi - lo)
        self.psi += 1
        return i

    def ev_eng(self):
        self.evi += 1
        return "act" if self.evi % 2 else "dve"

    def copy(self, eng, out, in_, reads, writes):
        if eng == "act":
            self.op("act", lambda e: e.copy(out=out, in_=in_), reads, writes)
        elif eng == "dve":
            self.op("dve", lambda e: e.tensor_copy(out=out, in_=in_), reads, writes)
        else:
            self.op("pool", lambda e: e.tensor_copy(out=out, in_=in_), reads, writes)

    def build(self):
        nc = self.nc
        cfg = self.cfg
        P = self
        d = {}
        d["xp"] = P.dram_in("xp", [SEQ, D])
        d["xs"] = P.dram_in("xs", [NS, D])
        d["memp"] = P.dram_in("memp", [256, D])
        NL = cfg["n_layers"]; NPL = (NL + 1) // 2; NNL = max(NL // 2, 1)
        d["cmkv"] = P.dram_in("cmkv", [NL, NSB, 256, 1024])
        d["nsakv"] = [P.dram_in(f"nsakv{l_}", [2 * cfg.get("nsakv_rows", 1280 * 128), 512]) for l_ in range(NNL)]
        d["nsawin"] = P.dram_in("nsawin", [2, NSB, 512, 512])
        d["spool"] = P.dram_in("spool", [2, NSB, 15, TOK])
        d["ptab"] = P.dram_in("ptab", [NSB, NPAGE], I32)
        d["g_mix"] = P.dram_in("g_mix", [DEPTH, D])
        d["g_mlp"] = P.dram_in("g_mlp", [DEPTH, D])
        d["g_mem"] = P.dram_in("g_mem", [DEPTH, D])
        d["w_mem_kv"] = P.dram_in("w_mem_kv", [NL, D, 1024])
        d["mem_qk"] = P.dram_in("mem_qk", [DEPTH, 2, HD])
        d["w_out"] = P.dram_in("w_out", [NL, D, D])
        d["w_up"] = P.dram_in("w_up", [NL, D, DFF])
        d["w_down"] = P.dram_in("w_down", [NL, DFF, D])
        d["w_in_pool"] = P.dram_in("w_in_pool", [NPL, D, D])
        d["w_pg"] = P.dram_in("w_pg", [2, 4, 384, 384])
        d["pool_scale"] = P.dram_in("pool_scale", [2, TOK])
        d["w_in_nsa"] = P.dram_in("w_in_nsa", [NNL, D, NSA_W])
        d["gate_bias"] = P.dram_in("gate_bias", [2, 36])
        d["nsa_qk"] = P.dram_in("nsa_qk", [2, 4, HD])
        d["cmp_pos"] = P.dram_in("cmp_pos", [2, 2, 32, HD])
        d["cmp_w1"] = P.dram_in("cmp_w1", [2, 2, 4096, HD])
        d["cmp_w2"] = P.dram_in("cmp_w2", [2, 2, HD, HD])
        d["cst"] = P.dram_in("cst", [128, 64])
        o = {}
        o["y_p"] = P.dram_out("y_p", [SEQ, D])
        o["y_s"] = P.dram_out("y_s", [NS, D])
        o["mkv"] = P.dram_out("mkv", [DEPTH, 256, 1024])
        o["nkv_p"] = P.dram_out("nkv_p", [2, SEQ, 1024])
        o["nkv_s"] = P.dram_out("nkv_s", [2, NS, 1024])
        o["win_p"] = P.dram_out("win_p", [2, 512, 512])
        o["win_s"] = P.dram_out("win_s", [2, NSB, 512, 512])
        o["pool_p"] = P.dram_out("pool_p", [2, 15, TOK])
        o["pool_s"] = P.dram_out("pool_s", [2, NSB, 15, TOK])
        self.d, self.o = d, o
        memhat_d = P.dram_tmp("memhat_d", [128, KC, 256])
        memhat_tt = TT("memhat_d")

        TW = TP + NS
        self.TW = TW
        xT = P.sb("xT", [128, KC, TW], F32)
        x_tt = [TT(f"x{k}") for k in range(KC)]
        hreg = P.reg("hreg", KC * TW * 2, TW * 2)
        r1 = P.reg("r1", 12 * TW * 4, TW * 2)
        creg = P.reg("creg", KC * TW * 2, TW * 2)
        wb = [P.sb(f"wb{i}", [128, 4096], BF16) for i in range(2)]
        wb_tt = [TT(f"wb{i}") for i in range(2)]
        self.wbi = 0
        psb = [self.es.enter_context(nc.psum_tensor(f"ps{i}", [128, 512], F32)) for i in range(8)]
        ps_tt = [TT(f"ps{i}") for i in range(8)]
        ident = P.sb("ident", [128, 128], F32)
        identb = P.sb("identb", [128, 128], BF16)
        ones = P.sb("ones", [128, 128], F32)
        onesb = P.sb("onesb", [128, 128], BF16)
        c_tt = TT("consts")
        vcol = P.sb("vcol", [128, 256], F32)
        vcol_tt = TT("vcol")
        gkb = P.sb("gkb", [128, DEPTH, HD], F32)
        invc = P.sb("invc", [128, 4, 16], F32)
        rstd = P.sb("rstd", [128, TW], F32)
        rstd_tt = TT("rstd")
        sq = [P.sb(f"sq{i}", [128, 512], F32) for i in range(2)]
        sq_tt = [TT(f"sq{i}") for i in range(2)]
        stg = [P.sb(f"stg{i}", [128, 1024], F32) for i in range(2)]
        stg_tt = [TT(f"stg{i}") for i in range(2)]
        self.stgi = 0
        small = P.sb("small", [128, 64], F32)
        small_tt = TT("small")
        carry = P.sb("carry", [128, 2, 12, 16], F32)
        carry_tt = [TT("carry0"), TT("carry1")]
        kmT = P.sb("kmT", [128, 4, 256], BF16)
        vm = P.sb("vm", [128, 2, 4, HD], BF16)
        qns = P.sb("qns", [128, 4, NS], BF16)
        qns_tt = TT("qns")
        sthist = P.sb("sthist", [128, NSB, 12, 16], F32)
        sthist_tt = TT("sthist")
        km_tt, vm_tt = TT("kmT"), TT("vm")
        pt = [P.sb(f"pt{i}", [128, 512], BF16) for i in range(2)] + [P.sb(f"pt{i}", [128, 384], BF16) for i in range(2, 4)]
        pt_tt = [TT(f"pt{i}") for i in range(4)]
        qn = P.sb("qn", [128, TW], BF16)
        qn_tt = TT("qn")
        qraw = P.sb("qraw", [128, TW], F32)
        qraw_tt = TT("qraw")

        P.I("pool", "memset", [], [c_tt], ident[:], 0.0)
        P.I("pool", "affine_select", [c_tt], [c_tt], out=ident[:], in_=ident[:], pattern=[[-1, 128]],
            compare_op=ALU.not_equal, fill=1.0, base=0, channel_multiplier=1)
        P.I("pool", "memset", [c_tt], [c_tt], ones[:], 1.0)
        P.I("dve", "tensor_copy", [c_tt], [c_tt], out=identb[:], in_=ident[:])
        P.I("dve", "tensor_copy", [c_tt], [c_tt], out=onesb[:], in_=ones[:])
        P.Dm("sp", "s_cst", [], [c_tt], out=invc[:].rearrange("p a b -> p (a b)"), in_=d["cst"])
        for i4 in range(DEPTH):
            P.Dm("sp", "s_cst", [], [c_tt], out=gkb[:, i4, :], in_=d["mem_qk"][i4, 1, :].partition_broadcast(128))
        vst = stg[0]
        P.Dm("sp", "s_stg0", [], [stg_tt[0]], out=vst[0:64, 0:128], in_=d["g_mix"].rearrange("i (k p) -> (i k) p", p=128))
        P.Dm("sp", "s_stg0", [], [stg_tt[0]], out=vst[64:128, 0:128], in_=d["g_mlp"].rearrange("i (k p) -> (i k) p", p=128))
        P.Dm("sp", "s_stg0", [], [stg_tt[0]], out=vst[0:64, 128:256], in_=d["g_mem"].rearrange("i (k p) -> (i k) p", p=128))
        P.Dm("sp", "s_stg0", [], [stg_tt[0]], out=vst[64:88, 128:256], in_=d["pool_scale"].rearrange("i (k p) -> (i k) p", p=128))
        P.Dm("sp", "s_stg0", [], [stg_tt[0]], out=vst[88:96, 128:256], in_=d["mem_qk"].rearrange("i a p -> (i a) p"))
        P.Dm("sp", "s_stg0", [], [stg_tt[0]], out=vst[96:104, 128:256], in_=d["nsa_qk"].rearrange("i a p -> (i a) p"))
        P.tr(ps_tt[0], psb[0][:, 0:128], vst[:, 0:128], ident[:], reads=[stg_tt[0], c_tt])
        P.tr(ps_tt[1], psb[1][:, 0:104], vst[0:104, 128:256], ident[0:104, 0:104], reads=[stg_tt[0], c_tt])
        P.I("dve", "tensor_copy", [ps_tt[0]], [vcol_tt], out=vcol[:, 0:128], in_=psb[0][:, 0:128])
        P.I("dve", "tensor_copy", [ps_tt[1]], [vcol_tt], out=vcol[:, 128:232], in_=psb[1][:, 0:104])

        def gcol(kind, i, k=0):
            base = {"mix": 0, "mlp": 64, "mem": 128}.get(kind)
            if base is not None:
                c = base + i * 16 + k
            elif kind == "pscale":
                c = 128 + 64 + i * 12 + k
            elif kind == "memqk":
                c = 128 + 88 + i * 2 + k
            elif kind == "nsaqk":
                c = 128 + 96 + i * 4 + k
            return vcol[:, c:c + 1]
        self.gcol = gcol

        def rsqrt_inplace(ap, tts):
            P.I("dve", "reciprocal", tts, tts, out=ap, in_=ap)
            P.I("act", "activation", tts, tts, out=ap, in_=ap, func=AF.Sqrt)
        self.rsqrt_inplace = rsqrt_inplace

        r1_mem = r1.t(0, 2 * D * 4)
        memtok = r1.view(F32, 0, 2 * D).rearrange("p (a b) -> p a b", a=2)
        P.Dm("sp", "s_r1", [], r1_mem, out=memtok, in_=d["memp"].rearrange("(a p) f -> p a f", p=128))
        for mc in range(2):
            for hf in range(2):
                P.I("act", "activation", r1_mem, [stg_tt[1], small_tt], out=stg[1][:, 0:1024],
                    in_=memtok[:, mc, hf * 1024:(hf + 1) * 1024], func=AF.Square, accum_out=small[:, 2 * mc + hf:2 * mc + hf + 1])
        for mc in range(2):
            P.I("dve", "tensor_tensor", [small_tt], [small_tt], out=small[:, 4 + mc:5 + mc], in0=small[:, 2 * mc:2 * mc + 1],
                in1=small[:, 2 * mc + 1:2 * mc + 2], op=ALU.add)
            P.I("dve", "tensor_scalar", [small_tt], [small_tt], out=small[:, 4 + mc:5 + mc], in0=small[:, 4 + mc:5 + mc],
                scalar1=1.0 / D, scalar2=EPS, op0=ALU.mult, op1=ALU.add)
            rsqrt_inplace(small[:, 4 + mc:5 + mc], [small_tt])
            P.I("dve", "tensor_scalar", [small_tt] + r1_mem, r1_mem, out=memtok[:, mc, :], in0=memtok[:, mc, :],
                scalar1=small[:, 4 + mc:5 + mc], scalar2=None, op0=ALU.mult)
        mh_sb = hreg.view(F32, 0, 8 * 256).rearrange("p (k m) -> p k m", k=8)
        for half in range(2):
            for kk in range(8):
                k = half * 8 + kk
                for mc in range(2):
                    b = P.next_ps()
                    P.tr(ps_tt[b], psb[b][:, 0:128], memtok[:, mc, k * 128:(k + 1) * 128], ident[:], reads=r1_mem + [c_tt])
                    P.copy(P.ev_eng(), mh_sb[:, kk, mc * 128:(mc + 1) * 128], psb[b][:, 0:128], reads=[ps_tt[b]], writes=hreg.t(0, 8192))
            P.Dm("sp", "s_hreg", hreg.t(0, 8192), [memhat_tt], out=memhat_d[:, half * 8:(half + 1) * 8, :], in_=mh_sb)
        P.I("pool", "memset", [], carry_tt, carry[:].rearrange("p a b c -> p (a b c)"), 0.0)

        def ntiles(T):
            t = [(n0, 512) for n0 in range(0, TP, 512)]
            if T > TP:
                t.append((TP, T - TP))
            return t
        self.ntiles = ntiles

        NWSCR = 420
        WSL = 105
        wscr_l = [P.dram_tmp(f"wscr{i_}", [WSL, 128, 4096], BF16) for i_ in range(NWSCR // WSL)]

        class _W:
            def __getitem__(self, idx):
                slot = idx[0]
                return wscr_l[slot // WSL][(slot % WSL,) + tuple(idx[1:])]
        wscr = _W()
        self.wcache = {}
        self.wready = {}

        def load_w(src_ap, kc, ncols, key=None):
            i = self.wbi % 2
            self.wbi += 1
            n = kc * ncols
            view = wb[i][:, 0:n].rearrange("p (k m) -> p k m", k=kc)
            if key is not None and key in self.wready:
                slot, tt = self.wready[key]
                P.Dm("sp", f"s_wb{i}", [tt], [wb_tt[i]], out=wb[i][:, 0:n], in_=wscr[slot, :, 0:n])
                return view, wb_tt[i]
            P.Dm("pool", f"s_wb{i}", [], [wb_tt[i]], out=view, in_=src_ap.rearrange("(k p) m -> p k m", p=128))
            if key is not None and key not in self.wcache:
                slot = len(self.wcache)
                assert slot < NWSCR
                tt = TT(f"wscr{slot}")
                P.Dm("sp", "s_wcv", [wb_tt[i]], [tt], out=wscr[slot, :, 0:n], in_=wb[i][:, 0:n])
                self.wcache[key] = (slot, tt)
            return view, wb_tt[i]

        def w_fence():
            tot = self.s.dcnt.get("s_wcv", 0)
            for key, (slot, tt) in self.wcache.items():
                if key not in self.wready:
                    tt.wr = ("s_wcv", tot)
                    self.wready[key] = (slot, tt)
        self.w_fence = w_fence
        self.load_w = load_w

        def hT_t(k):
            return hreg.t(k * TW * 2, (k + 1) * TW * 2)

        def rmsnorm(T, gkind, li_):
            hT = hreg.view(BF16, 0, KC * TW).rearrange("p (k t) -> p k t", k=KC)
            for (n0, nn) in ntiles(T):
                b = P.next_ps()
                for k in range(KC):
                    s_ = k % 2
                    P.I("act", "activation", [x_tt[k]], [sq_tt[s_]], out=sq[s_][:, 0:nn], in_=xT[:, k, n0:n0 + nn], func=AF.Square)
                    P.mm(ps_tt[b], psb[b][:, 0:nn], ones[:], sq[s_][:, 0:nn], k == 0, k == KC - 1, reads=[sq_tt[s_], c_tt])
                P.I("dve", "tensor_scalar", [ps_tt[b]], [rstd_tt], out=rstd[:, n0:n0 + nn], in0=psb[b][:, 0:nn], scalar1=1.0 / D,
                    scalar2=EPS, op0=ALU.mult, op1=ALU.add)
            rsqrt_inplace(rstd[:, 0:T], [rstd_tt])
            for k in range(KC):
                P.I("dve", "scalar_tensor_tensor", [x_tt[k], rstd_tt, vcol_tt], hT_t(k), out=hT[:, k, 0:T], in0=xT[:, k, 0:T],
                    scalar=gcol(gkind, li_, k), in1=rstd[:, 0:T], op0=ALU.mult, op1=ALU.mult)
            return hT
        self.rmsnorm = rmsnorm
        self.hT_t = hT_t

        def dense(T, w2d, kc, ncols_total, rhs_fn, rhs_tts_fn, evac, key=None):
            ncb = min(4096 // kc, ncols_total)
            for c0 in range(0, ncols_total, ncb):
                ncols = min(ncb, ncols_total - c0)
                wv, wtt = load_w(w2d[:, c0:c0 + ncols], kc, ncols, None if key is None else (key, c0))
                for m0 in range(0, ncols, 128):
                    mw = min(128, ncols - m0)
                    for (n0, nn) in ntiles(T):
                        b = P.next_ps()
                        for k in range(kc):
                            P.mm(ps_tt[b], psb[b][0:mw, 0:nn], wv[:, k, m0:m0 + mw], rhs_fn(k, n0, nn), k == 0, k == kc - 1,
                                 reads=[wtt] + rhs_tts_fn(k))
                        evac((c0 + m0) // 128, n0, nn, psb[b], ps_tt[b], mw)
        self.dense = dense

        def norm_rows(rows_ap, tt, nh, sbase, gain_b):
            for h in range(nh):
                P.I("act", "activation", [tt], [sq_tt[0], small_tt], out=sq[0][:, 0:128], in_=rows_ap[:, h * 128:(h + 1) * 128],
                    func=AF.Square, accum_out=small[:, sbase + h:sbase + h + 1])
            P.I("dve", "tensor_scalar", [small_tt], [small_tt], out=small[:, sbase + 8:sbase + 8 + nh], in0=small[:, sbase:sbase + nh],
                scalar1=1.0 / HD, scalar2=EPS, op0=ALU.mult, op1=ALU.add)
            rsqrt_inplace(small[:, sbase + 8:sbase + 8 + nh], [small_tt])
            for h in range(nh):
                kh = rows_ap[:, h * 128:(h + 1) * 128]
                P.I("dve", "scalar_tensor_tensor", [tt, small_tt, c_tt], [tt], out=kh, in0=kh,
                    scalar=small[:, sbase + 8 + h:sbase + 9 + h], in1=gain_b, op0=ALU.mult, op1=ALU.mult)
        self.norm_rows = norm_rows

        kmT_d = P.dram_tmp("kmT_d", [DEPTH, 128, 4 * 256], BF16)
        vm_d = P.dram_tmp("vm_d", [DEPTH, 128, 2 * 4 * HD], BF16)
        kmd_tt = [TT(f"kmT_d{i_}") for i_ in range(DEPTH)]
        vmd_tt = [TT(f"vm_d{i_}") for i_ in range(DEPTH)]

        def mem_kv(li_, T, first_chunk):
            if not first_chunk:
                P.Dm("sp", "s_kmT", [kmd_tt[li_]], [km_tt], out=kmT[:].rearrange("p h m -> p (h m)"), in_=kmT_d[li_])
                P.Dm("sp", "s_vm", [vmd_tt[li_]], [vm_tt], out=vm[:].rearrange("p c h d -> p (c h d)"), in_=vm_d[li_])
                return
            memT = hreg.view(BF16, 0, KC * 256).rearrange("p (k m) -> p k m", k=KC)
            memT_t = hreg.t(0, KC * 256 * 2)
            mh = r1.view(F32, 0, KC * 256).rearrange("p (k m) -> p k m", k=KC)
            mh_t = r1.t(0, KC * 256 * 4)
            P.Dm("sp", "s_r1", [memhat_tt], mh_t, out=mh, in_=memhat_d)
            for k in range(KC):
                P.I("dve", "tensor_scalar", mh_t + [vcol_tt], memT_t, out=memT[:, k, :], in0=mh[:, k, :], scalar1=gcol("mem", li_, k),
                    scalar2=None, op0=ALU.mult)
            kvrow = [stg[0][:, 0:1024], stg[1][:, 0:1024]]
            for cb in range(4):
                wv, wtt = load_w(d["w_mem_kv"][li_, :, cb * 256:(cb + 1) * 256], KC, 256, ("memkv", li_, cb))
                for mc in range(2):
                    b = P.next_ps()
                    for k in range(KC):
                        P.mm(ps_tt[b], psb[b][:, 0:256], memT[:, k, mc * 128:(mc + 1) * 128], wv[:, k, :], k == 0, k == KC - 1,
                             reads=[wtt] + memT_t)
                    P.copy(P.ev_eng(), kvrow[mc][:, cb * 256:(cb + 1) * 256], psb[b][:, 0:256], reads=[ps_tt[b]], writes=[stg_tt[mc]])
            for mc in range(2):
                norm_rows(kvrow[mc], stg_tt[mc], 4, 16, gkb[:, li_, :])
                for h in range(4):
                    b = P.next_ps()
                    P.tr(ps_tt[b], psb[b][:, 0:128], kvrow[mc][:, h * 128:(h + 1) * 128], ident[:], reads=[stg_tt[mc], c_tt])
                    P.copy(P.ev_eng(), kmT[:, h, mc * 128:(mc + 1) * 128], psb[b][:, 0:128], reads=[ps_tt[b]], writes=[km_tt])
                P.I("pool", "tensor_copy", [stg_tt[mc]], [vm_tt], out=vm[:, mc, :, :].rearrange("p h d -> p (h d)"), in_=kvrow[mc][:, 512:1024])
                if first_chunk:
                    P.Dm("sp", f"s_stg{mc}", [stg_tt[mc]], [], out=o["mkv"][li_, mc * 128:(mc + 1) * 128, :], in_=kvrow[mc])
            P.Dm("sp", "s_kmT", [km_tt], [kmd_tt[li_]], out=kmT_d[li_], in_=kmT[:].rearrange("p h m -> p (h m)"))
            P.Dm("sp", "s_vm", [vm_tt], [vmd_tt[li_]], out=vm_d[li_], in_=vm[:].rearrange("p c h d -> p (c h d)"))

        kmTs = hreg.view(BF16, 0, NSB * 4 * 256).rearrange("p (s h m) -> p s h m", s=NSB, h=4)
        vms = hreg.view(BF16, NSB * 4 * 256 * 2, NSB * 2 * 4 * HD).rearrange("p (s c h d) -> p s c h d", s=NSB, c=2, h=4)

        def mem_attend_sample(li_):
            htt_all = hreg.tts
            for sb in range(NSB):
                for mc in range(2):
                    P.Dm("sp", f"s_stg{mc}", [], [stg_tt[mc]], out=stg[mc][:, 0:1024], in_=d["cmkv"][li_, sb, mc * 128:(mc + 1) * 128, :])
                    for h in range(4):
                        b = P.next_ps()
                        P.tr(ps_tt[b], psb[b][:, 0:128], stg[mc][:, h * 128:(h + 1) * 128], ident[:], reads=[stg_tt[mc], c_tt])
                        P.copy(P.ev_eng(), kmTs[:, sb, h, mc * 128:(mc + 1) * 128], psb[b][:, 0:128], reads=[ps_tt[b]], writes=htt_all)
                    P.I("pool", "tensor_copy", [stg_tt[mc]], htt_all, out=vms[:, sb, mc, :, :].rearrange("p h d -> p (h d)"), in_=stg[mc][:, 512:1024])
            for h in range(4):
                for sb in range(NSB):
                    n0, nn = NS1 * sb, NS1
                    bacc, bden = 4, 5
                    for mc in range(2):
                        b = P.next_ps()
                        P.mm(ps_tt[b], psb[b][:, 0:nn], kmTs[:, sb, h, mc * 128:(mc + 1) * 128], qns[:, h, n0:n0 + nn], True, True, reads=htt_all + [qns_tt])
                        P.I("act", "activation", [ps_tt[b]], [pt_tt[mc]], out=pt[mc][:, 0:nn], in_=psb[b][:, 0:nn], func=AF.Exp, scale=SCALE)
                        P.mm(ps_tt[bacc], psb[bacc][:, 0:nn], vms[:, sb, mc, h, :], pt[mc][:, 0:nn], mc == 0, mc == 1, reads=htt_all + [pt_tt[mc]])
                        P.mm(ps_tt[bden], psb[bden][:, 0:nn], onesb[:], pt[mc][:, 0:nn], mc == 0, mc == 1, reads=[c_tt, pt_tt[mc]])
                    P.I("dve", "reciprocal", [ps_tt[bden]], [sq_tt[1]], out=sq[1][:, 0:nn], in_=psb[bden][:, 0:nn])
                    P.I("dve", "tensor_tensor", [ps_tt[bacc], sq_tt[1]], cat_t(12 + h), out=catT[:, 12 + h, TP + n0:TP + n0 + nn], in0=psb[bacc][:, 0:nn],
                        in1=sq[1][:, 0:nn], op=ALU.mult)

        catT = creg.view(BF16, 0, KC * TW).rearrange("p (k t) -> p k t", k=KC)
        self.catT = catT

        def cat_t(k):
            return creg.t(k * TW * 2, (k + 1) * TW * 2)
        self.cat_t = cat_t

        def qnorm_feat(src, src_tt, T, gain_col, dst, dst_tts):
            for (n0, nn) in ntiles(T):
                b = P.next_ps()
                P.I("act", "activation", [src_tt], [sq_tt[0]], out=sq[0][:, 0:nn], in_=src[:, n0:n0 + nn], func=AF.Square)
                P.mm(ps_tt[b], psb[b][:, 0:nn], ones[:], sq[0][:, 0:nn], True, True, reads=[sq_tt[0], c_tt])
                P.I("dve", "tensor_scalar", [ps_tt[b]], [rstd_tt], out=rstd[:, n0:n0 + nn], in0=psb[b][:, 0:nn], scalar1=1.0 / HD,
                    scalar2=EPS, op0=ALU.mult, op1=ALU.add)
            rsqrt_inplace(rstd[:, 0:T], [rstd_tt])
            P.I("dve", "scalar_tensor_tensor", [src_tt, rstd_tt, vcol_tt], dst_tts, out=dst[:, 0:T], in0=src[:, 0:T], scalar=gain_col,
                in1=rstd[:, 0:T], op0=ALU.mult, op1=ALU.mult)
        self.qnorm_feat = qnorm_feat

        def mem_attend_head(li_, h, T):
            qnorm_feat(qraw, qraw_tt, T, gcol("memqk", li_, 0), qn, [qn_tt])
            if T > TP:
                P.I("pool", "tensor_copy", [qn_tt], [qns_tt], out=qns[:, h, :], in_=qn[:, TP:TP + NS])
            for (n0, nn) in [(n0, 512) for n0 in range(0, TP, 512)]:
                K_, V_, ktt, vtt = (kmT, vm, km_tt, vm_tt)
                bacc, bden = 4, 5
                for mc in range(2):
                    b = P.next_ps()
                    P.mm(ps_tt[b], psb[b][:, 0:nn], K_[:, h, mc * 128:(mc + 1) * 128], qn[:, n0:n0 + nn], True, True, reads=[ktt, qn_tt])
                    P.I("act", "activation", [ps_tt[b]], [pt_tt[mc]], out=pt[mc][:, 0:nn], in_=psb[b][:, 0:nn], func=AF.Exp, scale=SCALE)
                    P.mm(ps_tt[bacc], psb[bacc][:, 0:nn], V_[:, mc, h, :], pt[mc][:, 0:nn], mc == 0, mc == 1, reads=[vtt, pt_tt[mc]])
                    P.mm(ps_tt[bden], psb[bden][:, 0:nn], onesb[:], pt[mc][:, 0:nn], mc == 0, mc == 1, reads=[c_tt, pt_tt[mc]])
                P.I("dve", "reciprocal", [ps_tt[bden]], [sq_tt[1]], out=sq[1][:, 0:nn], in_=psb[bden][:, 0:nn])
                P.I("dve", "tensor_tensor", [ps_tt[bacc], sq_tt[1]], cat_t(12 + h), out=catT[:, 12 + h, n0:n0 + nn], in0=psb[bacc][:, 0:nn],
                    in1=sq[1][:, 0:nn], op=ALU.mult)
        self.mem_attend_head = mem_attend_head
        self.mem_kv = mem_kv

        def out_and_mlp(i, T, j):
            def ev_res(m, n0, nn, ps, ptt, mw):
                P.I("dve", "tensor_tensor", [x_tt[m], ptt], [x_tt[m]], out=xT[:, m, n0:n0 + nn], in0=xT[:, m, n0:n0 + nn], in1=ps[:, 0:nn], op=ALU.add)
            dense(T, d["w_out"][i], KC, D, lambda k, n0, nn: catT[:, k, n0:n0 + nn], lambda k: cat_t(k), ev_res, key=("out", i))
            hT = rmsnorm(T, "mlp", i)
            a_fg = r1.view(BF16, 0, 16 * TW).rearrange("p (k t) -> p k t", k=16)

            def a_t(m):
                return r1.t(m * TW * 2, (m + 1) * TW * 2)
            for fg in range(4):
                def ev_up(m, n0, nn, ps, ptt, mw):
                    P.I("act", "activation", [ptt], [sq_tt[0]], out=sq[0][:, 0:nn], in_=ps[:, 0:nn], func=AF.Relu)
                    P.I("dve", "tensor_tensor", [sq_tt[0]], a_t(m), out=a_fg[:, m, n0:n0 + nn], in0=sq[0][:, 0:nn], in1=sq[0][:, 0:nn], op=ALU.mult)
                dense(T, d["w_up"][i][:, fg * 2048:(fg + 1) * 2048], KC, 2048, lambda k, n0, nn: hT[:, k, n0:n0 + nn], hT_t, ev_up, key=("up", i, fg))
                dense(T, d["w_down"][i][fg * 2048:(fg + 1) * 2048, :], KC, D, lambda k, n0, nn: a_fg[:, k, n0:n0 + nn], a_t, ev_res, key=("down", i, fg))
        self.out_and_mlp = out_and_mlp

        def out_rows(src_fn, src_tts, ncols_feat, n_tok, dst_fn):
            nchunk = ncols_feat // 128
            s_ = None
            for c in range(nchunk):
                if c % 8 == 0:
                    s_ = self.stgi % 2
                    self.stgi += 1
                b = P.next_ps()
                P.tr(ps_tt[b], psb[b][0:n_tok, 0:128], src_fn(c), ident[:], reads=src_tts(c) + [c_tt])
                P.copy(P.ev_eng(), stg[s_][0:n_tok, (c % 8) * 128:(c % 8 + 1) * 128], psb[b][0:n_tok, 0:128], reads=[ps_tt[b]], writes=[stg_tt[s_]])
                if c % 8 == 7 or c == nchunk - 1:
                    c0 = (c // 8) * 8
                    w_ = (c - c0 + 1) * 128
                    P.Dm("sp", f"s_stg{s_}", [stg_tt[s_]], [], out=dst_fn(c0 * 128, w_), in_=stg[s_][0:n_tok, 0:w_])
        self.out_rows = out_rows

        def pool_layer(i, j, T):
            li = i // 2
            u = r1.view(F32, 0, 12 * TW).rearrange("p (k t) -> p k t", k=12)

            def u_t(c):
                return r1.t(c * TW * 4, (c + 1) * TW * 4)
            mem_kv(i, T, j == 0)
            hT = rmsnorm(T, "mix", i)

            def ev_in(m, n0, nn, ps, ptt, mw):
                if m < 12:
                    P.copy(P.ev_eng(), u[:, m, n0:n0 + nn], ps[:, 0:nn], reads=[ptt], writes=u_t(m))
                else:
                    P.copy(P.ev_eng(), qraw[:, n0:n0 + nn], ps[:, 0:nn], reads=[ptt], writes=[qraw_tt])
                    if n0 + nn >= T:
                        mem_attend_head(i, m - 12, T)
            dense(T, d["w_in_pool"][li], KC, D, lambda k, n0, nn: hT[:, k, n0:n0 + nn], hT_t, ev_in, key=("inpool", li))
            if T > TP:
                mem_attend_sample(i)
            L = 16 + TP
            E = hreg.view(F32, 0, L)
            A = hreg.view(F32, L * 4, L)
            B = hreg.view(F32, 2 * L * 4, L)
            pooled = hreg.view(BF16, 3 * L * 4, 3 * TW).rearrange("p (k t) -> p k t", k=3)
            htt = hreg.tts
            wg_v = [None, None]
            for half in range(2):
                wi = self.wbi % 2
                self.wbi += 1
                v = wb[wi][:, 0:2 * 3 * 384].rearrange("p (g k m) -> p g k m", g=2, k=3)
                for gg in range(2):
                    P.Dm("pool", f"s_wb{wi}", [], [wb_tt[wi]], out=v[:, gg, :, :],
                         in_=d["w_pg"][li, 2 * half + gg].rearrange("(k p) m -> p k m", p=128))
                wg_v[half] = (v, wb_tt[wi])
            if T > TP:
                for sb in range(NSB):
                    P.Dm("sp", "s_stg0", [], [stg_tt[0]], out=stg[0][0:15, 0:1024], in_=d["spool"][li, sb, :, 0:1024])
                    P.Dm("sp", "s_stg1", [], [stg_tt[1]], out=stg[1][0:15, 0:512], in_=d["spool"][li, sb, :, 1024:1536])
                    P.Dm("sp", "s_misc", [], [], out=o["pool_s"][li, sb, 0:11, :], in_=d["spool"][li, sb, 4:15, :])
                    for c in range(12):
                        src = stg[0][0:15, c * 128:(c + 1) * 128] if c < 8 else stg[1][0:15, (c - 8) * 128:(c - 7) * 128]
                        b = P.next_ps()
                        P.tr(ps_tt[b], psb[b][:, 0:15], src, ident[0:15, 0:15], reads=[stg_tt[0 if c < 8 else 1], c_tt])
                        P.copy(P.ev_eng(), sthist[:, sb, c, 1:16], psb[b][:, 0:15], reads=[ps_tt[b]], writes=[sthist_tt])
            for g in range(4):
                w = 2 << g
                steps = g + 1
                for cc in range(3):
                    c = 3 * g + cc
                    utt = u_t(c)
                    for part in range(1 + NSB if T > TP else 1):
                        if part == 0:
                            n = TP
                            P.I("pool", "tensor_copy", [carry_tt[li]], htt, out=E[:, 0:16], in_=carry[:, li, c, :])
                            P.I("pool", "tensor_copy", utt, htt, out=E[:, 16:16 + TP], in_=u[:, c, 0:TP])
                            ucol = u[:, c, 0:TP]
                            pout = pooled[:, cc, 0:TP]
                        else:
                            n = NS1
                            sb = part - 1
                            s0 = TP + NS1 * sb
                            P.I("pool", "tensor_copy", [sthist_tt], htt, out=E[:, 0:16], in_=sthist[:, sb, c, :])
                            P.I("pool", "tensor_copy", utt, htt, out=E[:, 16:16 + NS1], in_=u[:, c, s0:s0 + NS1])
                            ucol = u[:, c, s0:s0 + NS1]
                            pout = pooled[:, cc, s0:s0 + NS1]
                        Ln = 16 + n
                        src, dst = E, A
                        sh = 1
                        lo = 0
                        for st in range(steps):
                            lo += sh
                            P.I("dve", "tensor_tensor", htt, htt, out=dst[:, lo:Ln], in0=src[:, lo:Ln], in1=src[:, lo - sh:Ln - sh], op=ALU.add)
                            src = dst
                            dst = B if dst is A else A
                            sh *= 2
                        P.I("dve", "scalar_tensor_tensor", htt + utt, htt, out=pout, in0=src[:, 16:16 + n], scalar=1.0 / w, in1=ucol,
                            op0=ALU.mult, op1=ALU.subtract)
                        if part == 0 and j == 0:
                            P.I("dve", "tensor_tensor", htt + [c_tt], [sq_tt[1]], out=sq[1][:, 0:16], in0=src[:, 16:32], in1=invc[:, g, :], op=ALU.mult)
                            P.I("dve", "tensor_tensor", [sq_tt[1]] + utt, htt, out=pout[:, 0:16], in0=sq[1][:, 0:16], in1=ucol[:, 0:16], op=ALU.subtract)
                        if part == 0:
                            P.I("pool", "tensor_copy", utt + htt, [carry_tt[li]], out=carry[:, li, c, 1:16], in_=u[:, c, TP - 15:TP])
                wv, wtt = wg_v[g // 2]
                for mo in range(3):
                    for (n0, nn) in ntiles(T):
                        b = P.next_ps()
                        for ki in range(3):
                            P.mm(ps_tt[b], psb[b][:, 0:nn], wv[:, g % 2, ki, mo * 128:(mo + 1) * 128], pooled[:, ki, n0:n0 + nn],
                                 ki == 0, ki == 2, reads=[wtt] + htt)
                        P.I("act", "activation", [ps_tt[b], vcol_tt], cat_t(3 * g + mo), out=catT[:, 3 * g + mo, n0:n0 + nn], in_=psb[b][:, 0:nn],
                            func=AF.Identity, scale=gcol("pscale", li, 3 * g + mo))
            if j == NCH - 1:
                out_rows(lambda c: u[:, c, TP - 15:TP], u_t, TOK, 15, lambda f0, w_: o["pool_p"][li, :, f0:f0 + w_])
            if T > TP:
                for sb in range(NSB):
                    out_rows(lambda c, sb=sb: u[:, c, TP + NS1 * sb:TP + NS1 * (sb + 1)], u_t, TOK, NS1,
                             lambda f0, w_, sb=sb: o["pool_s"][li, sb, 11:15, f0:f0 + w_])
            out_and_mlp(i, T, j)

        kslcT = P.sb("kslcT", [128, SEQ], BF16)
        vslc = P.sb("vslc", [128, SEQ // 128, HD], BF16)
        kwinT = P.sb("kwinT", [128, 1024], BF16)
        vwin = P.sb("vwin", [128, 8, HD], BF16)
        msk = [P.sb(f"msk{i}", [128, 128], BF16) for i in range(2)]
        msk_tt = [TT(f"msk{i}") for i in range(2)]
        sel2s = [P.sb(f"sel2s{i}", [128, 2, 64], BF16) for i in range(2)]
        sel2s_tt = [TT(f"sel2s{i}") for i in range(2)]
        kslc_tt, vslc_tt, kwin_tt, vwin_tt = (TT(n) for n in ("kslcT", "vslc", "kwinT", "vwin"))
        cmpx = P.sb("cmpx", [128, 4, 528], BF16)
        cmpx_tt = [TT(f"cmpx{i}") for i in range(4)]
        ccarry = P.sb("ccarry", [128, 2, 4, 16], BF16)
        ccarry_tt = TT("ccarry")
        kcT = P.sb("kcT", [128, 2, 2, 256], BF16)
        vc = P.sb("vc", [128, 2, 2, 2, HD], BF16)
        kc_tt, vc_tt = TT("kcT"), TT("vc")
        w2b = P.sb("w2b", [128, 2, 2, HD], BF16)
        b1col = P.sb("b1col", [128, 4], F32)
        posT = P.sb("posT", [128, 4, 32], BF16)
        n_tt = TT("nsa_consts")
        gates = P.sb("gates", [128, TW], F32)
        gates_tt = TT("gates")
        gbias = P.sb("gbias", [128, 2], F32)
        Amat = P.sb("Amat", [128, 2, 64], F32)
        tri = P.sb("tri", [128, 128], BF16)
        tris = P.sb("tris", [128, 128], BF16)
        cmask = P.sb("cmask", [128, 2, 128], BF16)
        cmask_tt = TT("cmask")
        P32v = [stg[0][:, 0:768], stg[1][:, 0:768]]
        impT = P.sb("impT", [128, 2, 128], F32)
        impT_tt = TT("impT")
        selw = P.sb("selw", [128, 6, 64], F32)
        selw_tt = TT("selw")
        seltab = P.sb("seltab_sb", [128, 128], F32)
        seltab_tt = TT("seltab")
        m8 = P.sb("m8", [128, 16], F32)
        kvf = [P.sb(f"kvf{i}", [128, TW], F32) for i in range(2)]
        kvf_tt = [TT(f"kvf{i}") for i in range(2)]
        rowsb = [stg[i][:, 0:512].rearrange("p (a b) -> p a b", a=4) for i in range(2)]
        rowsb_tt = stg_tt
        vrow = [P.sb(f"vrow{i}", [128, 4, HD], BF16) for i in range(2)]
        vrow_tt = [TT(f"vrow{i}") for i in range(2)]
        kbf = [P.sb(f"kbf{i}", [128, TP], BF16) for i in range(2)]
        kbf_tt = [TT(f"kbf{i}") for i in range(2)]
        tokacc = P.sb("tokacc", [128, 768], F32)
        tokacc_tt = TT("tokacc")
        obr = P.sb("obr", [128, 384], F32)
        obr_tt = TT("obr")
        rden = P.sb("rden", [128, 384], F32)
        rden_tt = TT("rden")
        gm = P.sb("gm", [128, 384], F32)
        gm_tt = TT("gm")
        h1 = P.sb("h1", [128, 32], BF16)
        h1_tt = TT("h1")
        vtmp = P.sb("vtmp", [128, HD], BF16)
        vtmp_tt = TT("vtmp")
        skv = P.sb("skv", [128, 12, NS], F32)
        skv_tt = TT("skv")
        kslc_d = P.dram_tmp("kslc_d", [2, 2, 128, SEQ], BF16)
        kwin_d = P.dram_tmp("kwin_d", [2, 2, 128, SEQ], BF16)
        vslc_d = P.dram_tmp("vslc_d", [2, 2, SEQ, HD], BF16)
        vwin_d = P.dram_tmp("vwin_d", [2, 2, SEQ, HD], BF16)
        kslcd_tt, kwind_tt, vslcd_tt, vwind_tt = TT("kslc_d"), TT("kwin_d"), TT("vslc_d"), TT("vwin_d")
        d["seltab"] = P.dram_in("seltab", [SEQ // 128, 128, 128])
        self.kvfi = 0

        P.I("pool", "memset", [], [n_tt], tri[:], 1.0)
        P.I("pool", "affine_select", [n_tt], [n_tt], out=tri[:], in_=tri[:], pattern=[[1, 128]], compare_op=ALU.is_ge, fill=0.0,
            base=0, channel_multiplier=-1)
        P.I("pool", "memset", [n_tt], [n_tt], tris[:], 1.0)
        P.I("pool", "affine_select", [n_tt], [n_tt], out=tris[:], in_=tris[:], pattern=[[-1, 128]], compare_op=ALU.is_ge, fill=0.0,
            base=-1, channel_multiplier=1)
        for ck in range(2):
            for term, (lo_b, hi_b) in enumerate(((0, 3), (-1, 2))):
                dst = selw[:, term, :]
                P.I("pool", "memset", [selw_tt], [selw_tt], dst, 1.0)
                P.I("pool", "affine_select", [selw_tt], [selw_tt], out=dst, in_=dst, pattern=[[-4, 64]], compare_op=ALU.is_ge, fill=0.0,
                    base=128 * ck - lo_b, channel_multiplier=1)
                P.I("pool", "affine_select", [selw_tt], [selw_tt], out=dst, in_=dst, pattern=[[4, 64]], compare_op=ALU.is_ge, fill=0.0,
                    base=hi_b - 128 * ck, channel_multiplier=-1)
            P.I("pool", "tensor_tensor", [selw_tt], [n_tt], out=Amat[:, ck, :], in0=selw[:, 0, :], in1=selw[:, 1, :], op=ALU.add)
        P.I("pool", "memset", [n_tt], [kc_tt], kcT[:].rearrange("p a b c -> p (a b c)"), 0.0)
        P.I("pool", "memset", [n_tt], [vc_tt], vc[:].rearrange("p a b c d -> p (a b c d)"), 0.0)
        P.I("pool", "memset", [n_tt], [ccarry_tt], ccarry[:].rearrange("p a b c -> p (a b c)"), 0.0)
        P.I("pool", "memset", [n_tt], [gates_tt], gates[:], 0.0)
        P.Dm("sp", "s_cst", [], [n_tt], out=gbias[0:36, :], in_=d["gate_bias"].rearrange("a b -> b a"), allow_slow_non_contiguous=True)
        for li_ in range(NL // 2):
            for kv in range(2):
                P.Dm("pool", "s_cst", [], [n_tt], out=w2b[:, li_, kv, :], in_=d["cmp_w2"][li_, kv])
                P.Dm("sp", "s_stg0", [], [stg_tt[0]], out=stg[0][0:32, 0:128], in_=d["cmp_pos"][li_, kv])
                b = P.next_ps()
                P.tr(ps_tt[b], psb[b][:, 0:32], stg[0][0:32, 0:128], ident[0:32, 0:32], reads=[stg_tt[0], c_tt])
                P.I("dve", "tensor_copy", [ps_tt[b]], [n_tt], out=posT[:, li_ * 2 + kv, :], in_=psb[b][:, 0:32])
                wv, wtt = load_w(d["cmp_w1"][li_, kv], 32, 128, ("w1", li_, kv))
                b = P.next_ps()
                for jj in range(32):
                    P.mm(ps_tt[b], psb[b][:, 0:1], wv[:, jj, :], posT[:, li_ * 2 + kv, jj:jj + 1], jj == 0, jj == 31, reads=[wtt, n_tt])
                P.I("dve", "tensor_copy", [ps_tt[b]], [n_tt], out=b1col[:, li_ * 2 + kv:li_ * 2 + kv + 1], in_=psb[b][:, 0:1])

        def norm_from_psum(src, ptt, nn, gain_col, dst, dst_tts, sa=None, sb_=None):
            (a_ap, a_tt) = sa if sa is not None else (sq[0], sq_tt[0])
            (r_ap, r_tt) = sb_ if sb_ is not None else (sq[1], sq_tt[1])
            P.I("act", "activation", [ptt], [a_tt], out=a_ap[:, 0:nn], in_=src, func=AF.Square)
            b = P.next_ps()
            P.mm(ps_tt[b], psb[b][:, 0:nn], ones[:], a_ap[:, 0:nn], True, True, reads=[a_tt, c_tt])
            P.I("dve", "tensor_scalar", [ps_tt[b]], [r_tt], out=r_ap[:, 0:nn], in0=psb[b][:, 0:nn], scalar1=1.0 / HD, scalar2=EPS,
                op0=ALU.mult, op1=ALU.add)
            rsqrt_inplace(r_ap[:, 0:nn], [r_tt])
            P.I("dve", "scalar_tensor_tensor", [ptt, r_tt, vcol_tt], dst_tts, out=dst, in0=src, scalar=gain_col, in1=r_ap[:, 0:nn],
                op0=ALU.mult, op1=ALU.mult)

        def bc3(ap2d):
            return ap2d.unsqueeze(1).to_broadcast([128, 3, 128])

        def nsa_layer(i, j, T):
            li = i // 2
            q = r1.view(BF16, 0, 12 * TW).rearrange("p (k t) -> p k t", k=12)

            def q_t(h):
                return r1.t(h * TW * 2, (h + 1) * TW * 2)
            mem_kv(i, T, j == 0)
            hT = rmsnorm(T, "mix", i)
            W = d["w_in_nsa"][li]
            rhs = lambda k, n0, nn: hT[:, k, n0:n0 + nn]

            def ev_q(m, n0, nn, ps, ptt, mw):
                norm_from_psum(ps[:, 0:nn], ptt, nn, gcol("nsaqk", li, 0), q[:, m, n0:n0 + nn], q_t(m))
            dense(T, W[:, 0:TOK], KC, TOK, rhs, hT_t, ev_q, key=("nsa_q", li))

            def ev_kv(m, n0, nn, ps, ptt, mw):
                slot, g = m // 2, m % 2
                samp = n0 >= TP
                if not samp:
                    fi = self.kvfi % 2
                    self.kvfi += 1
                    f, ftt = kvf[fi], kvf_tt[fi]
                    fdst, fdst_tt = f[:, 0:nn], [ftt]
                else:
                    fdst, fdst_tt = skv[:, m, :], [skv_tt]
                if slot in (2, 4):
                    norm_from_psum(ps[:, 0:nn], ptt, nn, gcol("nsaqk", li, 2 if slot == 2 else 3), fdst, fdst_tt)
                else:
                    P.copy(P.ev_eng(), fdst, ps[:, 0:nn], reads=[ptt], writes=fdst_tt)
                if samp:
                    return
                if slot in (0, 1):
                    ci = slot * 2 + g
                    P.I("pool", "tensor_copy", [ftt], [cmpx_tt[ci]], out=cmpx[:, ci, 16:16 + TP], in_=f[:, 0:TP])
                if slot in (2, 4):
                    P.I("pool", "tensor_copy", [ftt], [kbf_tt[fi]], out=kbf[fi][:, :], in_=f[:, 0:TP])
                    dst_d, dtt = (kslc_d, kslcd_tt) if slot == 2 else (kwin_d, kwind_tt)
                    P.Dm("sp", f"s_kbf{fi}", [kbf_tt[fi]], [dtt], out=dst_d[li, g, :, j * TP:(j + 1) * TP], in_=kbf[fi][:, :])
                for tb in range(TP // 128):
                    b = P.next_ps()
                    P.tr(ps_tt[b], psb[b][:, 0:128], f[:, tb * 128:(tb + 1) * 128], ident[:], reads=[ftt, c_tt])
                    P.copy(P.ev_eng(), rowsb[fi][:, tb, :], psb[b][:, 0:128], reads=[ps_tt[b]], writes=[rowsb_tt[fi]])
                    if slot in (3, 5):
                        P.I("pool", "tensor_copy", [rowsb_tt[fi]], [vrow_tt[fi]], out=vrow[fi][:, tb, :], in_=rowsb[fi][:, tb, :])
                if slot < 4:
                    c0 = slot * 256 + g * 128
                    P.Dm("sp", f"s_stg{fi}", [rowsb_tt[fi]], [],
                         out=o["nkv_p"][li, j * TP:(j + 1) * TP, c0:c0 + 128].rearrange("(tb p) d -> p tb d", p=128), in_=rowsb[fi])
                elif j == NCH - 1:
                    c0 = (slot - 4) * 256 + g * 128
                    P.Dm("sp", f"s_stg{fi}", [rowsb_tt[fi]], [],
                         out=o["win_p"][li, :, c0:c0 + 128].rearrange("(tb p) d -> p tb d", p=128), in_=rowsb[fi])
                if slot in (3, 5):
                    dst_d, dtt = (vslc_d, vslcd_tt) if slot == 3 else (vwin_d, vwind_tt)
                    P.Dm("sp", f"s_vrow{fi}", [vrow_tt[fi]], [dtt],
                         out=dst_d[li, g, j * TP:(j + 1) * TP, :].rearrange("(tb p) d -> p tb d", p=128), in_=vrow[fi][:, :, :])
            dense(T, W[:, TOK:2 * TOK], KC, TOK, rhs, hT_t, ev_kv, key=("nsa_kv", li))

            def ev_g(m, n0, nn, ps, ptt, mw):
                P.I("act", "activation", [ptt, n_tt], [gates_tt], out=gates[0:36, n0:n0 + nn], in_=ps[0:36, 0:nn], func=AF.Sigmoid,
                    bias=gbias[0:36, li:li + 1], scale=1.0)
            dense(T, W[:, 2 * TOK:2 * TOK + 36], KC, 36, rhs, hT_t, ev_g, key=("nsa_g", li))

            def ev_qm(m, n0, nn, ps, ptt, mw):
                P.copy(P.ev_eng(), qraw[:, n0:n0 + nn], ps[:, 0:nn], reads=[ptt], writes=[qraw_tt])
                if n0 + nn >= T:
                    mem_attend_head(i, m, T)
            dense(T, W[:, 2 * TOK + 36:NSA_W], KC, 512, rhs, hT_t, ev_qm, key=("nsa_qm", li))
            if T > TP:
                mem_attend_sample(i)

            compress(li, j, ccarry[:, li], ccarry_tt, kcT[:, li], [kc_tt], vc[:, li], [vc_tt])

            nkb_tot = 4 * (j + 1) if not cfg.get("no_attn") else 0
            kb0w = max(0, 4 * j - 4)
            for g in range(2 if not cfg.get("no_attn") else 0):
                P.Dm("sp", "s_kslc", [kslcd_tt], [kslc_tt], out=kslcT[:, 0:nkb_tot * 128], in_=kslc_d[li, g, :, 0:nkb_tot * 128])
                P.Dm("sp", "s_vslc", [vslcd_tt], [vslc_tt], out=vslc[:, 0:nkb_tot, :],
                     in_=vslc_d[li, g, 0:nkb_tot * 128, :].rearrange("(kb p) d -> p kb d", p=128))
                P.Dm("sp", "s_kwin", [kwind_tt], [kwin_tt], out=kwinT[:, 0:(nkb_tot - kb0w) * 128], in_=kwin_d[li, g, :, kb0w * 128:nkb_tot * 128])
                P.Dm("sp", "s_vwin", [vwind_tt], [vwin_tt], out=vwin[:, 0:nkb_tot - kb0w, :],
                     in_=vwin_d[li, g, kb0w * 128:nkb_tot * 128, :].rearrange("(kb p) d -> p kb d", p=128))
                for qt in range(TP // 128):
                    Q = 4 * j + qt
                    qc0 = qt * 128
                    ncv = 8 * Q + 7
                    nck = 1 if ncv <= 128 else 2
                    qtts = [t_ for h in range(6 * g, 6 * g + 6) for t_ in q_t(h)]

                    def qh(hh, g=g, qc0=qc0):
                        return q[:, 6 * g + 3 * hh:6 * g + 3 * hh + 3, qc0:qc0 + 128]

                    for ck in range(nck):
                        P.I("pool", "memset", [cmask_tt], [cmask_tt], cmask[:, ck, :], 1.0)
                        P.I("pool", "affine_select", [cmask_tt], [cmask_tt], out=cmask[:, ck, :], in_=cmask[:, ck, :], pattern=[[1, 128]],
                            compare_op=ALU.is_ge, fill=0.0, base=128 * Q - 31 - 2048 * ck, channel_multiplier=-16)
                    for hh in range(2):
                        for ck in range(nck):
                            bS = P.next_ps()
                            P.mm(ps_tt[bS], psb[bS][:, 0:384], kcT[:, li, g, ck * 128:(ck + 1) * 128], qh(hh), True, True, reads=[kc_tt] + qtts)
                            pv = P32v[ck][:, hh * 384:(hh + 1) * 384]
                            P.I("act", "activation", [ps_tt[bS]], [stg_tt[ck]], out=pv, in_=psb[bS][:, 0:384], func=AF.Exp, scale=SCALE)
                            P.I("dve", "tensor_tensor", [stg_tt[ck], cmask_tt], [stg_tt[ck]], out=pv.rearrange("p (h q) -> p h q", h=3),
                                in0=pv.rearrange("p (h q) -> p h q", h=3), in1=bc3(cmask[:, ck, :]), op=ALU.mult)
                            P.I("act", "copy", [stg_tt[ck]], [pt_tt[ck]], out=pt[ck][:, 0:384], in_=pv)
                            P.mm(ps_tt[4], psb[4][:, 0:384], vc[:, li, g, ck, :], pt[ck][:, 0:384], ck == 0, ck == nck - 1, reads=[vc_tt, pt_tt[ck]])
                            P.mm(ps_tt[5], psb[5][:, 0:384], ones[:], pv, ck == 0, ck == nck - 1, reads=[c_tt, stg_tt[ck]])
                        finish_branch(0, g, hh, qc0, 128, True, True, 4, 5)
                        for ck in range(nck):
                            pv = P32v[ck][:, hh * 384:(hh + 1) * 384]
                            P.I("dve", "tensor_tensor", [stg_tt[ck], rden_tt], [stg_tt[ck]], out=pv, in0=pv, in1=rden[:, :], op=ALU.mult)
                    for ck in range(nck):
                        P.I("dve", "tensor_reduce", [stg_tt[ck]], [impT_tt], out=impT[:, ck, :], in_=P32v[ck].rearrange("p (h q) -> p q h", h=6),
                            axis=AX.X, op=ALU.add)
                    for ck in range(nck):
                        P.mm(ps_tt[6], psb[6][:, 0:64], impT[:, ck, :], Amat[:, ck, :], ck == 0, ck == nck - 1, reads=[impT_tt, n_tt])
                    if g == 0:
                        P.Dm("sp", "s_seltab", [], [seltab_tt], out=seltab[:, :], in_=d["seltab"][Q])
                    sc, sc2, s_a, s_b = selw[:, 2, :], selw[:, 3, :], selw[:, 4, :], selw[:, 5, :]
                    P.I("dve", "tensor_tensor", [ps_tt[6], seltab_tt], [selw_tt], out=sc, in0=psb[6][:, 0:64], in1=seltab[:, 0:64], op=ALU.mult)
                    P.I("dve", "tensor_tensor", [selw_tt, seltab_tt], [selw_tt], out=sc, in0=sc, in1=seltab[:, 64:128], op=ALU.add)
                    topk_mask(sc, sc2, s_a, s_b, 128)

                    def smask(kb, idx, Q=Q, s_a=s_a):
                        mi = idx % 2
                        P.I("dve", "tensor_copy", [selw_tt], [sel2s_tt[mi]], out=sel2s[mi][:, :, :],
                            in_=s_a[:, 2 * kb:2 * kb + 2].unsqueeze(2).to_broadcast([128, 2, 64]))
                        b = P.next_ps()
                        P.mm(ps_tt[b], psb[b][:, 0:128], sel2s[mi][:, :, :].rearrange("p a b -> p (a b)"), identb[:], True, True,
                             reads=[sel2s_tt[mi], c_tt])
                        if kb == Q:
                            P.I("dve", "tensor_tensor", [ps_tt[b], n_tt], [msk_tt[mi]], out=msk[mi][:, :], in0=psb[b][:, 0:128], in1=tri[:], op=ALU.mult)
                        else:
                            P.I("act", "copy", [ps_tt[b]], [msk_tt[mi]], out=msk[mi][:, :], in_=psb[b][:, 0:128])
                        return (msk[mi][:, :], [msk_tt[mi]])

                    def wmask(kb, idx, Q=Q):
                        if kb == Q:
                            return (tri[:], [n_tt])
                        if kb == Q - 4:
                            return (tris[:], [n_tt])
                        return None

                    for (br, kbs, Kt, ktt, Vt, vtt, kofs, mfn) in (
                            (1, list(range(Q + 1)), kslcT, kslc_tt, vslc, vslc_tt, 0, smask),
                            (2, list(range(max(0, Q - 4), Q + 1)), kwinT, kwin_tt, vwin, vwin_tt, kb0w, wmask)):
                        n = len(kbs)

                        def pv_den(idx_, kl_, Vt=Vt, vtt=vtt, n=n):
                            for hh in range(2):
                                pi = hh + 2 * (idx_ % 2)
                                ba, bd = (4, 5) if hh == 0 else (6, 7)
                                P.mm(ps_tt[ba], psb[ba][:, 0:384], Vt[:, kl_, :], pt[pi][:, 0:384], idx_ == 0, idx_ == n - 1, reads=[vtt, pt_tt[pi]])
                                P.mm(ps_tt[bd], psb[bd][:, 0:384], onesb[:], pt[pi][:, 0:384], idx_ == 0, idx_ == n - 1, reads=[c_tt, pt_tt[pi]])
                        pend = None
                        for idx, kb in enumerate(kbs):
                            mk = mfn(kb, idx)
                            kl = kb - kofs
                            for hh in range(2):
                                bS = P.next_ps()
                                pi = hh + 2 * (idx % 2)
                                P.mm(ps_tt[bS], psb[bS][:, 0:384], Kt[:, kl * 128:(kl + 1) * 128], qh(hh), True, True, reads=[ktt] + qtts)
                                P.I("act", "activation", [ps_tt[bS]], [pt_tt[pi]], out=pt[pi][:, 0:384], in_=psb[bS][:, 0:384], func=AF.Exp, scale=SCALE)
                                if mk is not None:
                                    P.I("dve", "tensor_tensor", [pt_tt[pi]] + mk[1], [pt_tt[pi]], out=pt[pi][:, 0:384].rearrange("p (h q) -> p h q", h=3),
                                        in0=pt[pi][:, 0:384].rearrange("p (h q) -> p h q", h=3), in1=bc3(mk[0]), op=ALU.mult)
                            if pend is not None:
                                pv_den(*pend)
                            pend = (idx, kl)
                        pv_den(*pend)
                        for hh in range(2):
                            ba, bd = (4, 5) if hh == 0 else (6, 7)
                            finish_branch(br, g, hh, qc0, 128, False, False, ba, bd)
                    for hh in range(2):
                        P.I("act", "copy", [tokacc_tt], [t_ for h in range(6 * g + 3 * hh, 6 * g + 3 * hh + 3) for t_ in cat_t(h)],
                            out=catT[:, 6 * g + 3 * hh:6 * g + 3 * hh + 3, qc0:qc0 + 128],
                            in_=tokacc[:, hh * 384:(hh + 1) * 384].rearrange("p (h q) -> p h q", h=3))
            if T > TP and not cfg.get("no_sample"):
                nsa_sample(i, li, q, q_t)
            out_and_mlp(i, T, j)

        def topk_mask(sc, sc2, s_a, s_b, npart):
            P.I("dve", "max", [selw_tt], [selw_tt], out=m8[0:npart, 0:8], in_=sc)
            P.I("dve", "match_replace", [selw_tt], [selw_tt], out=sc2, in_to_replace=m8[0:npart, 0:8], in_values=sc, imm_value=-3.0e38)
            P.I("dve", "max", [selw_tt], [selw_tt], out=m8[0:npart, 8:16], in_=sc2)
            P.I("dve", "tensor_scalar", [selw_tt], [selw_tt], out=s_a, in0=sc, scalar1=m8[0:npart, 15:16], scalar2=None, op0=ALU.is_ge)
            P.I("dve", "tensor_scalar", [selw_tt], [selw_tt], out=s_b, in0=sc, scalar1=-5.0e29, scalar2=None, op0=ALU.is_gt)
            P.I("dve", "tensor_tensor", [selw_tt], [selw_tt], out=s_a, in0=s_a, in1=s_b, op=ALU.mult)

        def gate_apply(br, g, hh, gc0, nq, src, src_tts, first):
            W3 = 3 * nq
            for h3 in range(3):
                r = 3 * (6 * g + 3 * hh + h3) + br
                P.I("dve", "tensor_scalar", [gates_tt, c_tt], [gm_tt], out=gm[0:36, h3 * nq:(h3 + 1) * nq],
                    in0=gates[0:36, gc0:gc0 + nq], scalar1=ident[0:36, r:r + 1], scalar2=None, op0=ALU.mult)
            bG = P.next_ps()
            P.mm(ps_tt[bG], psb[bG][:, 0:W3], ones[0:36, :], gm[0:36, 0:W3], True, True, reads=[gm_tt, c_tt])
            ta = tokacc[:, hh * 384:hh * 384 + W3]
            if first:
                P.I("dve", "tensor_tensor", src_tts + [ps_tt[bG]], [tokacc_tt], out=ta, in0=src, in1=psb[bG][:, 0:W3], op=ALU.mult)
            else:
                P.I("dve", "tensor_tensor", src_tts + [ps_tt[bG]], [gm_tt], out=gm[:, 0:W3], in0=src, in1=psb[bG][:, 0:W3], op=ALU.mult)
                P.I("dve", "tensor_tensor", [gm_tt, tokacc_tt], [tokacc_tt], out=ta, in0=ta, in1=gm[:, 0:W3], op=ALU.add)

        def finish_branch(br, g, hh, gc0, nq, first, guard, ba, bd):
            W3 = 3 * nq
            if guard:
                P.I("dve", "tensor_scalar", [ps_tt[bd]], [rden_tt], out=rden[:, 0:W3], in0=psb[bd][:, 0:W3], scalar1=1e-30, scalar2=None, op0=ALU.max)
                P.I("dve", "reciprocal", [rden_tt], [rden_tt], out=rden[:, 0:W3], in_=rden[:, 0:W3])
            else:
                P.I("dve", "reciprocal", [ps_tt[bd]], [rden_tt], out=rden[:, 0:W3], in_=psb[bd][:, 0:W3])
            P.I("dve", "tensor_tensor", [ps_tt[ba], rden_tt], [obr_tt], out=obr[:, 0:W3], in0=psb[ba][:, 0:W3], in1=rden[:, 0:W3], op=ALU.mult)
            gate_apply(br, g, hh, gc0, nq, obr[:, 0:W3], [obr_tt], first)

        h1s = [(P.sb(f"h1s{i_}", [128, 32], BF16), TT(f"h1s{i_}")) for i_ in range(4)]
        nsq = [(P.sb(f"nsq{i_}", [128, 32], F32), TT(f"nsq{i_}")) for i_ in range(2)]
        nrs = [(P.sb(f"nrs{i_}", [128, 32], F32), TT(f"nrs{i_}")) for i_ in range(2)]
        vtmps = [(P.sb(f"vtmps{i_}", [32, HD], BF16), TT(f"vtmps{i_}")) for i_ in range(2)]

        def compress(li, s_idx, carry_ap, carry_t, kc_dst, kc_dst_t, vc_dst, vc_dst_t):
            lo = 1 if s_idx == 0 else 0
            c_lo = 32 * s_idx - 1
            for kv in range(2):
                wv, wtt = load_w(d["cmp_w1"][li, kv], 32, 128, ("w1", li, kv))
                for g in range(2):
                    ci = kv * 2 + g
                    cx = cmpx[:, ci, :]
                    P.I("pool", "tensor_copy", [carry_t], [cmpx_tt[ci]], out=cx[:, 0:16], in_=carry_ap[:, ci, :])
                    P.I("pool", "tensor_copy", [cmpx_tt[ci]], [carry_t], out=carry_ap[:, ci, :], in_=cx[:, TP:TP + 16])
                    cb3 = cx.rearrange("p (c s) -> p c s", s=16)
                    bh = P.next_ps()
                    for jj in range(32):
                        P.mm(ps_tt[bh], psb[bh][:, 0:32], wv[:, jj, :], cb3[:, jj // 16:jj // 16 + 32, jj % 16], jj == 0, jj == 31,
                             reads=[wtt, cmpx_tt[ci]])
                    h1, h1_tt = h1s[ci]
                    P.I("act", "activation", [ps_tt[bh], n_tt], [h1_tt], out=h1[:, 0:32], in_=psb[bh][:, 0:32], func=AF.Gelu_apprx_tanh,
                        bias=b1col[:, li * 2 + kv:li * 2 + kv + 1], scale=1.0)
                    b2 = P.next_ps()
                    if kv == 0:
                        P.mm(ps_tt[b2], psb[b2][:, 0:32], w2b[:, li, 0, :], h1[:, 0:32], True, True, reads=[h1_tt, n_tt])
                        norm_from_psum(psb[b2][:, lo:32], ps_tt[b2], 32 - lo, gcol("nsaqk", li, 1), kc_dst[:, g, c_lo + lo:c_lo + 32], kc_dst_t,
                                       sa=nsq[g], sb_=nrs[g])
                    else:
                        vtmp, vtmp_tt = vtmps[g]
                        P.mm(ps_tt[b2], psb[b2][0:32, 0:128], h1[:, 0:32], w2b[:, li, 1, :], True, True, reads=[h1_tt, n_tt])
                        P.I("dve", "tensor_copy", [ps_tt[b2]], [vtmp_tt], out=vtmp[0:32, :], in_=psb[b2][0:32, 0:128])
                        r = c_lo + lo
                        t0 = lo
                        while t0 < 32:
                            ckk, p0 = r // 128, r % 128
                            n = min(32 - t0, 128 - p0)
                            P.Dm("sp", f"s_vtmps{g}", [vtmp_tt], vc_dst_t, out=vc_dst[p0:p0 + n, g, ckk, :], in_=vtmp[t0:t0 + n, :])
                            t0 += n
                            r += n

        R1H = 12 * TW * 2
        r1s_tt = r1.t(R1H, 12 * TW * 4)
        kcTs = r1.view(BF16, R1H, 2048).rearrange("p (g c) -> p g c", g=2)
        vcs = P.sb("vcs", [128, 2, 8, HD], BF16)
        kcs_tt, vcs_tt = TT("kcTs"), TT("vcs")
        scarry = P.sb("scarry", [128, 4, 16], BF16)
        scarry_tt = TT("scarry")
        pti = P.sb("pti", [128, NPAGE], I32)
        pidx = P.sb("pidx", [128, NPAGE], I32)
        iop = P.sb("iop", [128, NPAGE], I32)
        pidxB = P.sb("pidxB", [128, NPAGE], I32)
        ptf = P.sb("ptf", [128, NPAGE], F32)
        iopf = P.sb("iopf", [128, NPAGE], F32)
        pidx_tt = TT("pidx")
        Aloc = P.sb("Aloc", [128, 34], F32)
        P32s = P.sb("P32s", [128, 8, 24], F32)
        P32s_tt = TT("P32s")
        impTs = P.sb("impTs", [128, 8, NS1], F32)
        impTs_tt = TT("impTs")
        ssel = r1.view(F32, R1H + 4096, 4 * 260).rearrange("p (a b) -> p a b", a=4)
        ssel_tt = TT("ssel")
        kTs = r1.view(BF16, R1H + 4096 + 4160, 2 * TP).rearrange("p (a b) -> p a b", a=2)
        vTs = r1.view(BF16, R1H + 4096 + 4160 + 2048, 2 * 4 * HD).rearrange("p (a b c) -> p a b c", a=2, b=4)
        kTs_tt, vTs_tt = TT("kTs"), TT("vTs")
        pts = [P.sb(f"pts{i}", [128, 24], BF16) for i in range(2)]
        pts_tt = [TT(f"pts{i}") for i in range(2)]
        newk = P.sb("newk", [128, 12, NS], BF16)
        newv = P.sb("newv", [128, 4, HD], BF16)
        newk_tt, newv_tt = TT("newk"), TT("newv")
        P.I("pool", "iota", [], [n_tt], iop[:], pattern=[[0, NPAGE]], base=0, channel_multiplier=2)
        P.I("dve", "tensor_copy", [n_tt], [n_tt], out=iopf[:, :], in_=iop[:, :])
        for term, (lo_b, hi_b) in enumerate(((0, 3), (-1, 2))):
            dst = selw[:, term, 0:34]
            P.I("pool", "memset", [selw_tt], [selw_tt], dst, 1.0)
            P.I("pool", "affine_select", [selw_tt], [selw_tt], out=dst, in_=dst, pattern=[[-4, 34]], compare_op=ALU.is_ge, fill=0.0,
                base=4 - lo_b, channel_multiplier=1)
            P.I("pool", "affine_select", [selw_tt], [selw_tt], out=dst, in_=dst, pattern=[[4, 34]], compare_op=ALU.is_ge, fill=0.0,
                base=hi_b - 4, channel_multiplier=-1)
        P.I("pool", "tensor_tensor", [selw_tt], [n_tt], out=Aloc[:, :], in0=selw[:, 0, 0:34], in1=selw[:, 1, 0:34], op=ALU.add)

        pgbuf = [(stg[0][:, 0:512], stg_tt[0], "s_stg0"), (stg[1][:, 0:512], stg_tt[1], "s_stg1"),
                 (kvf[0][:, 0:512], kvf_tt[0], "s_kvf0"), (kvf[1][:, 0:512], kvf_tt[1], "s_kvf1")]

        pscr = P.dram_tmp("pscr", [NPAGE, 128, 512], F32)
        pscr_tt = [TT(f"pscr{i_}") for i_ in range(NPAGE)]

        def nsa_sample(i, li, q, q_t):
            P.I("dve", "tensor_copy", [skv_tt], [newk_tt], out=newk[:].rearrange("p a b -> p (a b)"), in_=skv[:].rearrange("p a b -> p (a b)"))
            for m in range(12):
                slot, g = m // 2, m % 2
                b = P.next_ps()
                P.tr(ps_tt[b], psb[b][0:NS, 0:128], skv[:, m, :], ident[:], reads=[skv_tt, c_tt])
                s_ = 0 if m < 8 else 1
                cofs = (m % 8) * 128 if m < 8 else (m - 8) * 128
                P.copy(P.ev_eng(), stg[s_][0:NS, cofs:cofs + 128], psb[b][0:NS, 0:128], reads=[ps_tt[b]], writes=[stg_tt[s_]])
            P.Dm("sp", "s_stg0", [stg_tt[0]], [], out=o["nkv_s"][li, :, :], in_=stg[0][0:NS, 0:1024])
            for sb in range(NSB):
                P.Dm("sp", "s_stg1", [stg_tt[1]], [], out=o["win_s"][li, sb, 508:512, :], in_=stg[1][NS1 * sb:NS1 * (sb + 1), 0:512])
                P.Dm("sp", "s_misc", [], [], out=o["win_s"][li, sb, 0:508, :], in_=d["nsawin"][li, sb, 4:512, :])

            SSTOP = cfg.get("sample_stop", 99)
            if SSTOP <= 0:
                return
            for sb in range(NSB):
                sc0 = TP + NS1 * sb
                for vi, m in enumerate((6, 7, 10, 11)):
                    b = P.next_ps()
                    P.tr(ps_tt[b], psb[b][0:NS1, 0:128], skv[:, m, NS1 * sb:NS1 * (sb + 1)], ident[:], reads=[skv_tt, c_tt])
                    P.I("dve", "tensor_copy", [ps_tt[b]], [newv_tt], out=newv[0:NS1, vi, :], in_=psb[b][0:NS1, 0:128])
                P.Dm("sp", "s_pti", [], [pidx_tt], out=pti[:, :], in_=d["ptab"][sb, :].partition_broadcast(128))
                P.I("dve", "tensor_copy", [pidx_tt], [pidx_tt], out=ptf[:, :], in_=pti[:, :])
                P.I("dve", "scalar_tensor_tensor", [pidx_tt, n_tt], [pidx_tt], out=ptf[:, :], in0=ptf[:, :], scalar=256.0, in1=iopf[:, :],
                    op0=ALU.mult, op1=ALU.add)
                P.I("dve", "tensor_copy", [pidx_tt], [pidx_tt], out=pidx[:, :], in_=ptf[:, :])
                P.I("dve", "tensor_scalar", [pidx_tt], [pidx_tt], out=ptf[:, :], in0=ptf[:, :], scalar1=1.0, scalar2=None, op0=ALU.add)
                P.I("dve", "tensor_copy", [pidx_tt], [pidx_tt], out=pidxB[:, :], in_=ptf[:, :])
                if SSTOP <= 1:
                    continue
                P.I("pool", "memset", [], [scarry_tt], scarry[:].rearrange("p a b -> p (a b)"), 0.0)
                P.I("pool", "memset", [], r1s_tt, kcTs[:].rearrange("p a b -> p (a b)"), 0.0)
                P.I("pool", "memset", [], [vcs_tt], vcs[:].rearrange("p a b c -> p (a b c)"), 0.0)
                for s_i in range(32):
                    for pg4 in range(4):
                        pg = 4 * s_i + pg4
                        pgb, pgt, pgs = pgbuf[pg4]
                        self.idma(pgs, [pidx_tt], [pgt], pgb, d["nsakv"][li], pidx[:, pg:pg + 1])
                        for ci in range(4):
                            b = P.next_ps()
                            P.tr(ps_tt[b], psb[b][:, 0:128], pgb[:, ci * 128:(ci + 1) * 128], ident[:], reads=[pgt, c_tt])
                            P.copy(P.ev_eng(), cmpx[:, ci, 16 + pg4 * 128:16 + (pg4 + 1) * 128], psb[b][:, 0:128], reads=[ps_tt[b]], writes=[cmpx_tt[ci]])
                    compress(li, s_i, scarry, scarry_tt, kcTs, r1s_tt, vcs, [vcs_tt])
                for g in range(2):
                    if SSTOP <= 2:
                        continue
                    qtts = [t_ for h in range(6 * g, 6 * g + 6) for t_ in q_t(h)]
                    qs = q[:, 6 * g:6 * g + 6, sc0:sc0 + NS1]
                    for ck in range(8):
                        bS = P.next_ps()
                        P.mm(ps_tt[bS], psb[bS][:, 0:24], kcTs[:, g, ck * 128:(ck + 1) * 128], qs, True, True, reads=r1s_tt + qtts)
                        P.I("act", "activation", [ps_tt[bS]], [P32s_tt], out=P32s[:, ck, :], in_=psb[bS][:, 0:24], func=AF.Exp, scale=SCALE)
                        if ck == 7:
                            P.I("dve", "tensor_scalar", [P32s_tt, n_tt], [P32s_tt], out=P32s[:, 7, :], in0=P32s[:, 7, :], scalar1=lastmask[:, 0:1],
                                scalar2=None, op0=ALU.mult)
                        pi = ck % 2
                        P.I("act", "copy", [P32s_tt], [pts_tt[pi]], out=pts[pi][:, :], in_=P32s[:, ck, :])
                        P.mm(ps_tt[4], psb[4][:, 0:24], vcs[:, g, ck, :], pts[pi][:, :], ck == 0, ck == 7, reads=[vcs_tt, pts_tt[pi]])
                        P.mm(ps_tt[5], psb[5][:, 0:24], ones[:], P32s[:, ck, :], ck == 0, ck == 7, reads=[c_tt, P32s_tt])
                    for hh in range(2):
                        finish_branch_s(0, g, hh, sc0, True, True, 4, 5)
                    P.I("dve", "tensor_tensor", [P32s_tt, rden_tt], [P32s_tt], out=P32s[:, :, :], in0=P32s[:, :, :],
                        in1=rden[:, 0:24].unsqueeze(1).to_broadcast([128, 8, 24]), op=ALU.mult)
                    P.I("dve", "tensor_reduce", [P32s_tt], [impTs_tt], out=impTs[:, :, :], in_=P32s[:, :, :].rearrange("p c (h q) -> p c q h", h=6),
                        axis=AX.X, op=ALU.add)
                    sc, sc2, s_a, s_b = ssel[0:NS1, 0, 0:257], ssel[0:NS1, 1, 0:257], ssel[0:NS1, 2, 0:257], ssel[0:NS1, 3, 0:257]
                    P.I("dve", "memset", r1s_tt, r1s_tt, ssel[0:NS1, 0, :], 0.0)
                    for ck in range(8):
                        bq = P.next_ps()
                        P.mm(ps_tt[bq], psb[bq][0:NS1, 0:34], impTs[:, ck, :], Aloc[:, :], True, True, reads=[impTs_tt, n_tt])
                        b_lo = 32 * ck - 1
                        sk = 1 if ck == 0 else 0
                        P.I("dve", "tensor_tensor", [ps_tt[bq]] + r1s_tt, r1s_tt, out=ssel[0:NS1, 0, b_lo + sk:b_lo + 34],
                            in0=ssel[0:NS1, 0, b_lo + sk:b_lo + 34], in1=psb[bq][0:NS1, sk:34], op=ALU.add)
                    P.I("dve", "memset", r1s_tt, r1s_tt, ssel[0:NS1, 0, 0:1], BIGV)
                    P.I("dve", "memset", r1s_tt, r1s_tt, ssel[0:NS1, 0, 255:257], BIGV)
                    topk_mask_s(sc, sc2, s_a, s_b)
                    if SSTOP <= 3:
                        continue
                    pend2 = None

                    def pv_den2(kc2_, pg4_, mi_):
                        P.mm(ps_tt[4], psb[4][:, 0:24], vTs[:, 0, pg4_, :], pts[mi_][:, :], kc2_ == 0, False, reads=r1s_tt + [pts_tt[mi_]])
                        P.mm(ps_tt[5], psb[5][:, 0:24], onesb[:], pts[mi_][:, :], kc2_ == 0, False, reads=[c_tt, pts_tt[mi_]])
                    for s_i in range(32):
                        if pend2 is not None:
                            pv_den2(*pend2)
                            pend2 = None
                        for pg4 in range(4):
                            pg = 4 * s_i + pg4
                            pgb, pgt, pgs = pgbuf[pg4]
                            if g == 0:
                                self.idma(pgs, [pidx_tt], [pgt], pgb, d["nsakv"][li], pidxB[:, pg:pg + 1])
                                P.Dm("sp", "s_pscr", [pgt], [pscr_tt[pg]], out=pscr[pg], in_=pgb)
                            else:
                                P.Dm("sp", pgs, [pscr_tt[pg]], [pgt], out=pgb, in_=pscr[pg])
                            b = P.next_ps()
                            P.tr(ps_tt[b], psb[b][:, 0:128], pgb[:, g * 128:(g + 1) * 128], ident[:], reads=[pgt, c_tt])
                            P.copy(P.ev_eng(), kTs[:, 0, pg4 * 128:(pg4 + 1) * 128], psb[b][:, 0:128], reads=[ps_tt[b]], writes=r1s_tt)
                            P.I("pool", "tensor_copy", [pgt], r1s_tt, out=vTs[:, 0, pg4, :], in_=pgb[:, 256 + g * 128:256 + (g + 1) * 128])
                        for pg4 in range(4):
                            kc2 = 4 * s_i + pg4
                            mi = kc2 % 2
                            P.I("dve", "tensor_copy", r1s_tt, [sel2s_tt[mi]], out=sel2s[mi][0:NS1, :, :],
                                in_=s_a[:, 2 * kc2:2 * kc2 + 2].unsqueeze(2).to_broadcast([NS1, 2, 64]))
                            bm = P.next_ps()
                            P.mm(ps_tt[bm], psb[bm][:, 0:NS1], sel2s[mi][0:NS1, :, :].rearrange("p a b -> p (a b)"), identb[0:NS1, 0:NS1], True, True,
                                 reads=[sel2s_tt[mi], c_tt])
                            P.I("act", "copy", [ps_tt[bm]], [msk_tt[mi]], out=msk[mi][:, 0:NS1], in_=psb[bm][:, 0:NS1])
                            bS = P.next_ps()
                            P.mm(ps_tt[bS], psb[bS][:, 0:24], kTs[:, 0, pg4 * 128:(pg4 + 1) * 128], qs, True, True, reads=r1s_tt + qtts)
                            P.I("act", "activation", [ps_tt[bS]], [pts_tt[mi]], out=pts[mi][:, :], in_=psb[bS][:, 0:24], func=AF.Exp, scale=SCALE)
                            P.I("dve", "tensor_tensor", [pts_tt[mi], msk_tt[mi]], [pts_tt[mi]], out=pts[mi][:, :].rearrange("p (h q) -> p h q", h=6),
                                in0=pts[mi][:, :].rearrange("p (h q) -> p h q", h=6), in1=msk[mi][:, 0:NS1].unsqueeze(1).to_broadcast([128, 6, NS1]), op=ALU.mult)
                            if pend2 is not None:
                                pv_den2(*pend2)
                            pend2 = (kc2, pg4, mi)
                    pv_den2(*pend2)
                    if g == 0:
                        tot_ = self.s.dcnt["s_pscr"]
                        for t_ in pscr_tt:
                            t_.wr = ("s_pscr", tot_)
                    new_rows_attend(g, sb, qs, qtts, 4 + g, 0 + g, 4, 5, tri_small=True)
                    for hh in range(2):
                        finish_branch_s(1, g, hh, sc0, False, False, 4, 5)
                    if SSTOP <= 4:
                        continue
                    for kc in range(4):
                        s_ = kc % 2
                        P.Dm("sp", f"s_stg{s_}", [], [stg_tt[s_]], out=stg[s_][:, 0:512], in_=d["nsawin"][li, sb, kc * 128:(kc + 1) * 128, :])
                        b = P.next_ps()
                        P.tr(ps_tt[b], psb[b][:, 0:128], stg[s_][:, g * 128:(g + 1) * 128], ident[:], reads=[stg_tt[s_], c_tt])
                        P.copy(P.ev_eng(), kTs[:, 1, kc * 128:(kc + 1) * 128], psb[b][:, 0:128], reads=[ps_tt[b]], writes=r1s_tt)
                        P.I("pool", "tensor_copy", [stg_tt[s_]], r1s_tt, out=vTs[:, 1, kc, :], in_=stg[s_][:, 256 + g * 128:256 + (g + 1) * 128])
                        mi = kc % 2
                        bS = P.next_ps()
                        P.mm(ps_tt[bS], psb[bS][:, 0:24], kTs[:, 1, kc * 128:(kc + 1) * 128], qs, True, True, reads=r1s_tt + qtts)
                        P.I("act", "activation", [ps_tt[bS]], [pts_tt[mi]], out=pts[mi][:, :], in_=psb[bS][:, 0:24], func=AF.Exp, scale=SCALE)
                        if kc == 0:
                            P.I("dve", "tensor_tensor", [pts_tt[mi], n_tt], [pts_tt[mi]], out=pts[mi][:, :].rearrange("p (h q) -> p h q", h=6),
                                in0=pts[mi][:, :].rearrange("p (h q) -> p h q", h=6), in1=tris[:, 0:NS1].unsqueeze(1).to_broadcast([128, 6, NS1]), op=ALU.mult)
                        P.mm(ps_tt[4], psb[4][:, 0:24], vTs[:, 1, kc, :], pts[mi][:, :], kc == 0, False, reads=r1s_tt + [pts_tt[mi]])
                        P.mm(ps_tt[5], psb[5][:, 0:24], onesb[:], pts[mi][:, :], kc == 0, False, reads=[c_tt, pts_tt[mi]])
                    new_rows_attend(g, sb, qs, qtts, 8 + g, 2 + g, 4, 5, tri_small=True)
                    for hh in range(2):
                        finish_branch_s(2, g, hh, sc0, False, False, 4, 5)
                    for hh in range(2):
                        P.I("act", "copy", [tokacc_tt], [t_ for h in range(6 * g + 3 * hh, 6 * g + 3 * hh + 3) for t_ in cat_t(h)],
                            out=catT[:, 6 * g + 3 * hh:6 * g + 3 * hh + 3, sc0:sc0 + NS1],
                            in_=tokacc[:, hh * 384:hh * 384 + 3 * NS1].rearrange("p (h q) -> p h q", h=3))

        lastmask = P.sb("lastmask", [128, 1], F32)
        P.I("pool", "memset", [], [n_tt], lastmask[:], 1.0)
        P.I("pool", "affine_select", [n_tt], [n_tt], out=lastmask[:], in_=lastmask[:], pattern=[[0, 1]], compare_op=ALU.is_ge, fill=0.0,
            base=126, channel_multiplier=-1)

        def new_rows_attend(g, sb, qs, qtts, km, vi, ba, bd, tri_small):
            bS = P.next_ps()
            P.mm(ps_tt[bS], psb[bS][0:NS1, 0:24], newk[:, km, NS1 * sb:NS1 * (sb + 1)], qs, True, True, reads=[newk_tt] + qtts)
            P.I("act", "activation", [ps_tt[bS]], [pts_tt[0]], out=pts[0][0:NS1, :], in_=psb[bS][0:NS1, 0:24], func=AF.Exp, scale=SCALE)
            P.I("dve", "tensor_tensor", [pts_tt[0], n_tt], [pts_tt[0]], out=pts[0][0:NS1, :].rearrange("p (h q) -> p h q", h=6),
                in0=pts[0][0:NS1, :].rearrange("p (h q) -> p h q", h=6), in1=tri[0:NS1, 0:NS1].unsqueeze(1).to_broadcast([NS1, 6, NS1]), op=ALU.mult)
            P.mm(ps_tt[ba], psb[ba][:, 0:24], newv[0:NS1, vi, :], pts[0][0:NS1, :], False, True, reads=[newv_tt, pts_tt[0]])
            P.mm(ps_tt[bd], psb[bd][:, 0:24], onesb[0:NS1, :], pts[0][0:NS1, :], False, True, reads=[c_tt, pts_tt[0]])

        def finish_branch_s(br, g, hh, gc0, first, guard, ba, bd):
            W3 = 3 * NS1
            if hh == 0:
                if guard:
                    P.I("dve", "tensor_scalar", [ps_tt[bd]], [rden_tt], out=rden[:, 0:24], in0=psb[bd][:, 0:24], scalar1=1e-30, scalar2=None, op0=ALU.max)
                    P.I("dve", "reciprocal", [rden_tt], [rden_tt], out=rden[:, 0:24], in_=rden[:, 0:24])
                else:
                    P.I("dve", "reciprocal", [ps_tt[bd]], [rden_tt], out=rden[:, 0:24], in_=psb[bd][:, 0:24])
                P.I("dve", "tensor_tensor", [ps_tt[ba], rden_tt], [obr_tt], out=obr[:, 0:24], in0=psb[ba][:, 0:24], in1=rden[:, 0:24], op=ALU.mult)
            gate_apply(br, g, hh, gc0, NS1, obr[:, hh * W3:(hh + 1) * W3], [obr_tt], first)

        def topk_mask_s(sc, sc2, s_a, s_b):
            P.I("dve", "max", r1s_tt, r1s_tt, out=m8[0:NS1, 0:8], in_=sc)
            P.I("dve", "match_replace", r1s_tt, r1s_tt, out=sc2, in_to_replace=m8[0:NS1, 0:8], in_values=sc, imm_value=-3.0e38)
            P.I("dve", "max", r1s_tt, r1s_tt, out=m8[0:NS1, 8:16], in_=sc2)
            P.I("dve", "tensor_scalar", r1s_tt, r1s_tt, out=s_a, in0=sc, scalar1=m8[0:NS1, 15:16], scalar2=None, op0=ALU.is_ge)

        self.nsa_layer = nsa_layer

        w_fence()
        for j in range(cfg["n_chunks"]):
            T = TW if j == 0 else TP
            for tb in range(TP // 128):
                for half in range(2):
                    s_ = self.stgi % 2
                    self.stgi += 1
                    P.Dm("sp", f"s_stg{s_}", [], [stg_tt[s_]], out=stg[s_][:, 0:1024],
                         in_=d["xp"][j * TP + tb * 128:j * TP + (tb + 1) * 128, half * 1024:(half + 1) * 1024])
                    for kk in range(8):
                        k = half * 8 + kk
                        b = P.next_ps()
                        P.tr(ps_tt[b], psb[b][:, 0:128], stg[s_][:, kk * 128:(kk + 1) * 128], ident[:], reads=[stg_tt[s_], c_tt])
                        P.copy(P.ev_eng(), xT[:, k, tb * 128:(tb + 1) * 128], psb[b][:, 0:128], reads=[ps_tt[b]], writes=[x_tt[k]])
            if j == 0:
                for half in range(2):
                    s_ = self.stgi % 2
                    self.stgi += 1
                    P.Dm("sp", f"s_stg{s_}", [], [stg_tt[s_]], out=stg[s_][0:NS, 0:1024], in_=d["xs"][:, half * 1024:(half + 1) * 1024])
                    for kk in range(8):
                        k = half * 8 + kk
                        b = P.next_ps()
                        P.tr(ps_tt[b], psb[b][:, 0:NS], stg[s_][0:NS, kk * 128:(kk + 1) * 128], ident[0:NS, 0:NS], reads=[stg_tt[s_], c_tt])
                        P.copy(P.ev_eng(), xT[:, k, TP:TP + NS], psb[b][:, 0:NS], reads=[ps_tt[b]], writes=[x_tt[k]])
            for i in range(cfg["n_layers"]):
                if i % 2 == 0:
                    pool_layer(i, j, T)
                else:
                    nsa_layer(i, j, T)
            for tb in range(TP // 128):
                out_rows(lambda c, tb=tb: xT[:, c, tb * 128:(tb + 1) * 128], lambda c: [x_tt[c]], D, 128,
                         lambda f0, w_, tb=tb: o["y_p"][j * TP + tb * 128:j * TP + (tb + 1) * 128, f0:f0 + w_])
            if j == 0:
                out_rows(lambda c: xT[:, c, TP:TP + NS], lambda c: [x_tt[c]], D, NS, lambda f0, w_: o["y_s"][:, f0:f0 + w_])
            w_fence()

        self.sbuf_left = self.nc.sbuf_bytes_remaining
        self.s.final_waits()
        self.emit()
        return nc

    def emit(self):
        nc = self.nc
        s = self.s
        sems = {}
        for name in list(Sched.ENGS) + sorted(self.sems_needed):
            sems[name] = self.es.enter_context(nc.semaphore(name))

        def replay(eng, e):
            for (waits, fn, tok, inc) in s.ops[eng]:
                for (sn, v) in waits:
                    e.wait_ge(sems[sn], v)
                if fn is not None:
                    ins = fn(e)
                    ins.then_inc(sems[tok[0]], inc)

        with nc.Block() as block:
            @block.tensor
            def _(e):
                replay("pe", e)

            @block.scalar
            def _(e):
                replay("act", e)

            @block.vector
            def _(e):
                replay("dve", e)

            @block.gpsimd
            def _(e):
                replay("pool", e)

            @block.sync
            def _(e):
                replay("sp", e)
        self.es.close()


def make_consts():
    inv = np.zeros((128, 4, 16), np.float32)
    for g, w in enumerate((2, 4, 8, 16)):
        for t in range(16):
            inv[:, g, t] = 1.0 / min(t + 1, w)
    return inv.reshape(128, 64)


def make_seltab():
    tab = np.zeros((SEQ // 128, 128, 128), np.float32)
    b = np.arange(64)[None, :]
    for Q in range(SEQ // 128):
        pos = Q * 128 + np.arange(128)[:, None]
        cur = pos // 64
        forced = (b == 0) | (b == cur) | (b == cur - 1)
        allowed = b <= cur
        tab[Q, :, :64] = (allowed & ~forced).astype(np.float32)
        tab[Q, :, 64:] = np.where(forced, BIGV, np.where(allowed, 0.0, NEGV)).astype(np.float32)
    return tab


def kernel(x_prompt, x_sample, mem_prompt, cache_mem_kv, cache_nsa_kv, cache_nsa_win, state_pool, page_table,
           g_norm_mix, g_norm_mlp, g_norm_mem, w_mem_kv, mem_qk_gain, w_out, w_mlp_up, w_mlp_down, w_in_pool,
           w_pool_group, pool_scale, w_in_nsa, nsa_gate_bias, nsa_qk_gain, cmp_pos, cmp_w1, cmp_w2, _cfg=None):
    import time as _t
    cfg = dict(CFG)
    if _cfg:
        cfg.update(_cfg)
    f = lambda a: np.ascontiguousarray(np.asarray(a))
    _t0 = _t.time()
    prog = Prog(cfg)
    nc = prog.build()
    print('build_s', _t.time() - _t0, {e: len(v) for e, v in prog.s.ops.items()}, flush=True)
    NL = cfg["n_layers"]; NPL = (NL + 1) // 2; NNL = max(NL // 2, 1)
    nsakv = f(np.asarray(cache_nsa_kv).reshape(2, 1280 * 128, 1024)[:NNL, :cfg.get("nsakv_rows", 1280 * 128)]).reshape(NNL, -1, 512)
    shared = {
        "g_mix": f(g_norm_mix), "g_mlp": f(g_norm_mlp), "g_mem": f(g_norm_mem),
        "w_mem_kv": f(w_mem_kv[:NL]), "mem_qk": f(mem_qk_gain), "w_out": f(w_out[:NL]), "w_up": f(w_mlp_up[:NL]),
        "w_down": f(w_mlp_down[:NL]), "w_in_pool": f(w_in_pool[:NPL]), "w_pg": f(w_pool_group), "pool_scale": f(pool_scale),
        "w_in_nsa": f(w_in_nsa[:NNL]), "gate_bias": f(nsa_gate_bias), "nsa_qk": f(nsa_qk_gain), "cmp_pos": f(cmp_pos),
        "cmp_w1": f(cmp_w1), "cmp_w2": f(cmp_w2), "cst": make_consts(), "seltab": make_seltab(),
    }
    for l_ in range(NNL):
        shared[f"nsakv{l_}"] = nsakv[l_]
    in_maps = []
    for c in range(NCORE):
        sl = slice(NSB * c, NSB * (c + 1))
        m = dict(shared)
        m["xp"] = f(x_prompt[c])
        m["xs"] = f(np.asarray(x_sample)[sl]).reshape(NS, D)
        m["memp"] = f(mem_prompt[c])
        m["cmkv"] = f(np.asarray(cache_mem_kv)[:NL, sl]).reshape(NL, NSB, 256, 1024)
        m["nsawin"] = f(np.asarray(cache_nsa_win)[:, sl]).reshape(2, NSB, 512, 512)
        m["spool"] = f(np.asarray(state_pool)[:, sl])
        m["ptab"] = f(np.asarray(page_table)[sl]).astype(np.int32)
        in_maps.append(m)
    _t0 = _t.time()
    res = run_bass_kernel_spmd(nc, in_maps, core_ids=list(range(NCORE)), **({'trace': True} if cfg.get('trace') else {}))
    if cfg.get('trace'):
        print('EXEC_NS', res.exec_time_ns, flush=True)
    print('run_s', _t.time() - _t0, flush=True)
    R = res.results
    y_p = np.stack([R[b]["y_p"] for b in range(2)])
    y_s = np.concatenate([R[c]["y_s"].reshape(NSB, NS1, D) for c in range(NCORE)], axis=0)
    mkv = np.stack([R[b]["mkv"] for b in range(2)], axis=1).reshape(DEPTH, 2, 256, 2, 4, HD)
    nkv_p = np.stack([R[b]["nkv_p"] for b in range(2)], axis=1).reshape(2, 2, SEQ, 4, 2, HD)
    nkv_s = np.concatenate([R[c]["nkv_s"].reshape(2, NSB, NS1, 1024) for c in range(NCORE)], axis=1).reshape(2, 8, NS1, 4, 2, HD)
    win_p = np.stack([R[b]["win_p"] for b in range(2)], axis=1).reshape(2, 2, 512, 2, 2, HD)
    win_s = np.concatenate([R[c]["win_s"] for c in range(NCORE)], axis=1).reshape(2, 8, 512, 2, 2, HD)
    pool_p = np.stack([R[b]["pool_p"] for b in range(2)], axis=1)
    pool_s = np.concatenate([R[c]["pool_s"] for c in range(NCORE)], axis=1)
    return (y_p, y_s, mkv, nkv_p, nkv_s, win_p, win_s, pool_p, pool_s)
```

```python
import numpy as np
from contextlib import ExitStack
import concourse.bass as bass
import concourse.mybir as mybir
from concourse.bass_utils import run_bass_kernel_spmd

F32 = mybir.dt.float32
BF16 = mybir.dt.bfloat16
I32 = mybir.dt.int32
AF = mybir.ActivationFunctionType
ALU = mybir.AluOpType
AX = mybir.AxisListType

D = 2048
KC = 16
SEQ = 4096
TP = 512
NSB = 4
NS1 = 4
NS = NSB * NS1
NCORE = 2
NCH = SEQ // TP
DEPTH = 4
HD = 128
TOK = 1536
DFF = 8192
EPS = 1e-6
SCALE = HD ** -0.5
NSA_W = 3620
PAST = 16384
NPAGE = 128
NEGV = -1e30
BIGV = 1e30

CFG = dict(n_layers=4, n_chunks=8)


class TT:
    __slots__ = ("name", "wr", "rd")

    def __init__(self, name):
        self.name = name
        self.wr = None
        self.rd = {}


class Sched:
    ENGS = ("pe", "act", "dve", "pool", "sp")

    def __init__(self):
        self.ops = {e: [] for e in self.ENGS}
        self.cnt = {e: 0 for e in self.ENGS}
        self.known = {e: {} for e in self.ENGS}
        self.dcnt = {}

    def _waits(self, eng, reads, writes, pe_chain=False):
        deps = {}

        def add(s, v):
            if pe_chain and s == "pe" and eng == "pe":
                return
            if deps.get(s, 0) < v:
                deps[s] = v

        for r in reads:
            if r.wr is not None:
                add(*r.wr)
        for w in writes:
            if w.wr is not None:
                add(*w.wr)
            for s, v in w.rd.items():
                add(s, v)
        out = []
        kn = self.known[eng]
        for s, v in deps.items():
            if kn.get(s, 0) < v:
                kn[s] = v
                out.append((s, v))
        return out

    def op(self, eng, fn, reads=(), writes=(), pe_chain=False):
        waits = self._waits(eng, reads, writes, pe_chain)
        self.cnt[eng] += 1
        tok = (eng, self.cnt[eng])
        self.ops[eng].append((waits, fn, tok, 1))
        for r in reads:
            if r.rd.get(eng, 0) < tok[1]:
                r.rd[eng] = tok[1]
        for w in writes:
            w.wr = tok
            w.rd = {}

    def dma(self, queue, fn, sem, reads=(), writes=()):
        waits = self._waits(queue, reads, writes)
        self.dcnt[sem] = self.dcnt.get(sem, 0) + 16
        tok = (sem, self.dcnt[sem])
        self.ops[queue].append((waits, fn, tok, 16))
        for r in reads:
            if r.rd.get(sem, 0) < tok[1]:
                r.rd[sem] = tok[1]
        for w in writes:
            w.wr = tok
            w.rd = {}

    def final_waits(self):
        allv = dict(self.cnt)
        allv.update(self.dcnt)
        waits = [(s, v) for s, v in allv.items() if v > 0 and s != "sp"]
        self.ops["sp"].append((waits, None, None, 0))


class Reg:
    def __init__(self, name, handle, nbytes, gran):
        self.name = name
        self.h = handle
        self.nbytes = nbytes
        self.gran = gran
        self.tts = [TT(f"{name}{i}") for i in range((nbytes + gran - 1) // gran)]

    def t(self, b0, b1):
        return self.tts[b0 // self.gran:(b1 - 1) // self.gran + 1]

    def view(self, dtype, b0, nelem):
        esz = 4 if dtype in (F32, I32) else 2
        assert b0 % 4 == 0 and (nelem * esz) % 4 == 0
        ap = self.h[:, b0 // 4:(b0 + nelem * esz) // 4]
        if esz == 2:
            ap = ap.bitcast(dtype)
        elif dtype == I32:
            ap = ap.bitcast(I32)
        return ap


class Prog:
    def __init__(self, cfg):
        self.cfg = cfg
        self.nc = bass.Bass("TRN2", target_bir_lowering=False)
        self.s = Sched()
        self.es = ExitStack()
        self.psi = 0
        self.evi = 0
        self.sems_needed = set()

    def dram_in(self, name, shape, dt=F32):
        return self.nc.dram_tensor(name, list(shape), dt, kind="ExternalInput").ap()

    def dram_out(self, name, shape, dt=F32):
        return self.nc.dram_tensor(name, list(shape), dt, kind="ExternalOutput").ap()

    def dram_tmp(self, name, shape, dt=F32):
        return self.nc.dram_tensor(name, list(shape), dt, kind="Internal").ap()

    def sb(self, name, shape, dt=F32):
        return self.es.enter_context(self.nc.sbuf_tensor(name, list(shape), dt))

    def reg(self, name, nbytes, gran):
        h = self.sb(name, [128, nbytes // 4], F32)
        return Reg(name, h, nbytes, gran)

    def op(self, eng, fn, reads=(), writes=(), pe_chain=False):
        self.s.op(eng, fn, reads, writes, pe_chain)

    def dma(self, queue, fn, sem, reads=(), writes=()):
        self.sems_needed.add(sem)
        self.s.dma(queue, fn, sem, reads, writes)

    def I(self, eng, method, reads, writes, *args, **kw):
        self.s.op(eng, lambda e: getattr(e, method)(*args, **kw), reads, writes)

    def Dm(self, queue, sem, reads, writes, **kw):
        self.sems_needed.add(sem)
        self.s.dma(queue, lambda e: e.dma_start(**kw), sem, reads, writes)

    def idma(self, sem, reads, writes, out, in_, idx_ap):
        self.sems_needed.add(sem)
        self.s.dma("pool", lambda e: e.indirect_dma_start(out=out, out_offset=None, in_=in_,
                                                          in_offset=bass.IndirectOffsetOnAxis(ap=idx_ap, axis=0)), sem, reads, writes)

    def mm(self, ps_tt, out, lhsT, rhs, start, stop, reads):
        self.op("pe", lambda e: e.matmul(out, lhsT=lhsT, rhs=rhs, start=start, stop=stop),
                reads=reads, writes=[ps_tt], pe_chain=not start)

    def tr(self, ps_tt, out, in_, ident, reads):
        self.op("pe", lambda e: e.transpose(out=out, in_=in_, identity=ident), reads=reads, writes=[ps_tt])

    def next_ps(self, lo=0, hi=4):
        i = lo + self.psi % (hi - lo)
        self.psi += 1
        return i

    def ev_eng(self):
        self.evi += 1
        return "act" if self.evi % 2 else "dve"

    def copy(self, eng, out, in_, reads, writes):
        if eng == "act":
            self.op("act", lambda e: e.copy(out=out, in_=in_), reads, writes)
        elif eng == "dve":
            self.op("dve", lambda e: e.tensor_copy(out=out, in_=in_), reads, writes)
        else:
            self.op("pool", lambda e: e.tensor_copy(out=out, in_=in_), reads, writes)

    def build(self):
        nc = self.nc
        cfg = self.cfg
        P = self
        d = {}
        d["xp"] = P.dram_in("xp", [SEQ, D])
        d["xs"] = P.dram_in("xs", [NS, D])
        d["memp"] = P.dram_in("memp", [256, D])
        NL = cfg["n_layers"]; NPL = (NL + 1) // 2; NNL = max(NL // 2, 1)
        d["cmkv"] = P.dram_in("cmkv", [NL, NSB, 256, 1024])
        d["nsakv"] = [P.dram_in(f"nsakv{l_}", [2 * cfg.get("nsakv_rows", 1280 * 128), 512]) for l_ in range(NNL)]
        d["nsawin"] = P.dram_in("nsawin", [2, NSB, 512, 512])
        d["spool"] = P.dram_in("spool", [2, NSB, 15, TOK])
        d["ptab"] = P.dram_in("ptab", [NSB, NPAGE], I32)
        d["g_mix"] = P.dram_in("g_mix", [DEPTH, D])
        d["g_mlp"] = P.dram_in("g_mlp", [DEPTH, D])
        d["g_mem"] = P.dram_in("g_mem", [DEPTH, D])
        d["w_mem_kv"] = P.dram_in("w_mem_kv", [NL, D, 1024])
        d["mem_qk"] = P.dram_in("mem_qk", [DEPTH, 2, HD])
        d["w_out"] = P.dram_in("w_out", [NL, D, D])
        d["w_up"] = P.dram_in("w_up", [NL, D, DFF])
        d["w_down"] = P.dram_in("w_down", [NL, DFF, D])
        d["w_in_pool"] = P.dram_in("w_in_pool", [NPL, D, D])
        d["w_pg"] = P.dram_in("w_pg", [2, 4, 384, 384])
        d["pool_scale"] = P.dram_in("pool_scale", [2, TOK])
        d["w_in_nsa"] = P.dram_in("w_in_nsa", [NNL, D, NSA_W])
        d["gate_bias"] = P.dram_in("gate_bias", [2, 36])
        d["nsa_qk"] = P.dram_in("nsa_qk", [2, 4, HD])
        d["cmp_pos"] = P.dram_in("cmp_pos", [2, 2, 32, HD])
        d["cmp_w1"] = P.dram_in("cmp_w1", [2, 2, 4096, HD])
        d["cmp_w2"] = P.dram_in("cmp_w2", [2, 2, HD, HD])
        d["cst"] = P.dram_in("cst", [128, 64])
        o = {}
        o["y_p"] = P.dram_out("y_p", [SEQ, D])
        o["y_s"] = P.dram_out("y_s", [NS, D])
        o["mkv"] = P.dram_out("mkv", [DEPTH, 256, 1024])
        o["nkv_p"] = P.dram_out("nkv_p", [2, SEQ, 1024])
        o["nkv_s"] = P.dram_out("nkv_s", [2, NS, 1024])
        o["win_p"] = P.dram_out("win_p", [2, 512, 512])
        o["win_s"] = P.dram_out("win_s", [2, NSB, 512, 512])
        o["pool_p"] = P.dram_out("pool_p", [2, 15, TOK])
        o["pool_s"] = P.dram_out("pool_s", [2, NSB, 15, TOK])
        self.d, self.o = d, o
        memhat_d = P.dram_tmp("memhat_d", [128, KC, 256])
        memhat_tt = TT("memhat_d")

        TW = TP + NS
        self.TW = TW
        xT = P.sb("xT", [128, KC, TW], F32)
        x_tt = [TT(f"x{k}") for k in range(KC)]
        hreg = P.reg("hreg", KC * TW * 2, TW * 2)
        r1 = P.reg("r1", 12 * TW * 4, TW * 2)
        creg = P.reg("creg", KC * TW * 2, TW * 2)
        wb = [P.sb(f"wb{i}", [128, 4096], BF16) for i in range(2)]
        wb_tt = [TT(f"wb{i}") for i in range(2)]
        self.wbi = 0
        psb = [self.es.enter_context(nc.psum_tensor(f"ps{i}", [128, 512], F32)) for i in range(8)]
        ps_tt = [TT(f"ps{i}") for i in range(8)]
        ident = P.sb("ident", [128, 128], F32)
        identb = P.sb("identb", [128, 128], BF16)
        ones = P.sb("ones", [128, 128], F32)
        onesb = P.sb("onesb", [128, 128], BF16)
        c_tt = TT("consts")
        vcol = P.sb("vcol", [128, 256], F32)
        vcol_tt = TT("vcol")
        gkb = P.sb("gkb", [128, DEPTH, HD], F32)
        invc = P.sb("invc", [128, 4, 16], F32)
        rstd = P.sb("rstd", [128, TW], F32)
        rstd_tt = TT("rstd")
        sq = [P.sb(f"sq{i}", [128, 512], F32) for i in range(2)]
        sq_tt = [TT(f"sq{i}") for i in range(2)]
        stg = [P.sb(f"stg{i}", [128, 1024], F32) for i in range(2)]
        stg_tt = [TT(f"stg{i}") for i in range(2)]
        self.stgi = 0
        small = P.sb("small", [128, 64], F32)
        small_tt = TT("small")
        carry = P.sb("carry", [128, 2, 12, 16], F32)
        carry_tt = [TT("carry0"), TT("carry1")]
        kmT = P.sb("kmT", [128, 4, 256], BF16)
        vm = P.sb("vm", [128, 2, 4, HD], BF16)
        qns = P.sb("qns", [128, 4, NS], BF16)
        qns_tt = TT("qns")
        sthist = P.sb("sthist", [128, NSB, 12, 16], F32)
        sthist_tt = TT("sthist")
        km_tt, vm_tt = TT("kmT"), TT("vm")
        pt = [P.sb(f"pt{i}", [128, 512], BF16) for i in range(2)] + [P.sb(f"pt{i}", [128, 384], BF16) for i in range(2, 4)]
        pt_tt = [TT(f"pt{i}") for i in range(4)]
        qn = P.sb("qn", [128, TW], BF16)
        qn_tt = TT("qn")
        qraw = P.sb("qraw", [128, TW], F32)
        qraw_tt = TT("qraw")

        P.I("pool", "memset", [], [c_tt], ident[:], 0.0)
        P.I("pool", "affine_select", [c_tt], [c_tt], out=ident[:], in_=ident[:], pattern=[[-1, 128]],
            compare_op=ALU.not_equal, fill=1.0, base=0, channel_multiplier=1)
        P.I("pool", "memset", [c_tt], [c_tt], ones[:], 1.0)
        P.I("dve", "tensor_copy", [c_tt], [c_tt], out=identb[:], in_=ident[:])
        P.I("dve", "tensor_copy", [c_tt], [c_tt], out=onesb[:], in_=ones[:])
        P.Dm("sp", "s_cst", [], [c_tt], out=invc[:].rearrange("p a b -> p (a b)"), in_=d["cst"])
        for i4 in range(DEPTH):
            P.Dm("sp", "s_cst", [], [c_tt], out=gkb[:, i4, :], in_=d["mem_qk"][i4, 1, :].partition_broadcast(128))
        vst = stg[0]
        P.Dm("sp", "s_stg0", [], [stg_tt[0]], out=vst[0:64, 0:128], in_=d["g_mix"].rearrange("i (k p) -> (i k) p", p=128))
        P.Dm("sp", "s_stg0", [], [stg_tt[0]], out=vst[64:128, 0:128], in_=d["g_mlp"].rearrange("i (k p) -> (i k) p", p=128))
        P.Dm("sp", "s_stg0", [], [stg_tt[0]], out=vst[0:64, 128:256], in_=d["g_mem"].rearrange("i (k p) -> (i k) p", p=128))
        P.Dm("sp", "s_stg0", [], [stg_tt[0]], out=vst[64:88, 128:256], in_=d["pool_scale"].rearrange("i (k p) -> (i k) p", p=128))
        P.Dm("sp", "s_stg0", [], [stg_tt[0]], out=vst[88:96, 128:256], in_=d["mem_qk"].rearrange("i a p -> (i a) p"))
        P.Dm("sp", "s_stg0", [], [stg_tt[0]], out=vst[96:104, 128:256], in_=d["nsa_qk"].rearrange("i a p -> (i a) p"))
        P.tr(ps_tt[0], psb[0][:, 0:128], vst[:, 0:128], ident[:], reads=[stg_tt[0], c_tt])
        P.tr(ps_tt[1], psb[1][:, 0:104], vst[0:104, 128:256], ident[0:104, 0:104], reads=[stg_tt[0], c_tt])
        P.I("dve", "tensor_copy", [ps_tt[0]], [vcol_tt], out=vcol[:, 0:128], in_=psb[0][:, 0:128])
        P.I("dve", "tensor_copy", [ps_tt[1]], [vcol_tt], out=vcol[:, 128:232], in_=psb[1][:, 0:104])

        def gcol(kind, i, k=0):
            base = {"mix": 0, "mlp": 64, "mem": 128}.get(kind)
            if base is not None:
                c = base + i * 16 + k
            elif kind == "pscale":
                c = 128 + 64 + i * 12 + k
            elif kind == "memqk":
                c = 128 + 88 + i * 2 + k
            elif kind == "nsaqk":
                c = 128 + 96 + i * 4 + k
            return vcol[:, c:c + 1]
        self.gcol = gcol

        def rsqrt_inplace(ap, tts):
            P.I("dve", "reciprocal", tts, tts, out=ap, in_=ap)
            P.I("act", "activation", tts, tts, out=ap, in_=ap, func=AF.Sqrt)
        self.rsqrt_inplace = rsqrt_inplace

        r1_mem = r1.t(0, 2 * D * 4)
        memtok = r1.view(F32, 0, 2 * D).rearrange("p (a b) -> p a b", a=2)
        P.Dm("sp", "s_r1", [], r1_mem, out=memtok, in_=d["memp"].rearrange("(a p) f -> p a f", p=128))
        for mc in range(2):
            for hf in range(2):
                P.I("act", "activation", r1_mem, [stg_tt[1], small_tt], out=stg[1][:, 0:1024],
                    in_=memtok[:, mc, hf * 1024:(hf + 1) * 1024], func=AF.Square, accum_out=small[:, 2 * mc + hf:2 * mc + hf + 1])
        for mc in range(2):
            P.I("dve", "tensor_tensor", [small_tt], [small_tt], out=small[:, 4 + mc:5 + mc], in0=small[:, 2 * mc:2 * mc + 1],
                in1=small[:, 2 * mc + 1:2 * mc + 2], op=ALU.add)
            P.I("dve", "tensor_scalar", [small_tt], [small_tt], out=small[:, 4 + mc:5 + mc], in0=small[:, 4 + mc:5 + mc],
                scalar1=1.0 / D, scalar2=EPS, op0=ALU.mult, op1=ALU.add)
            rsqrt_inplace(small[:, 4 + mc:5 + mc], [small_tt])
            P.I("dve", "tensor_scalar", [small_tt] + r1_mem, r1_mem, out=memtok[:, mc, :], in0=memtok[:, mc, :],
                scalar1=small[:, 4 + mc:5 + mc], scalar2=None, op0=ALU.mult)
        mh_sb = hreg.view(F32, 0, 8 * 256).rearrange("p (k m) -> p k m", k=8)
        for half in range(2):
            for kk in range(8):
                k = half * 8 + kk
                for mc in range(2):
                    b = P.next_ps()
                    P.tr(ps_tt[b], psb[b][:, 0:128], memtok[:, mc, k * 128:(k + 1) * 128], ident[:], reads=r1_mem + [c_tt])
                    P.copy(P.ev_eng(), mh_sb[:, kk, mc * 128:(mc + 1) * 128], psb[b][:, 0:128], reads=[ps_tt[b]], writes=hreg.t(0, 8192))
            P.Dm("sp", "s_hreg", hreg.t(0, 8192), [memhat_tt], out=memhat_d[:, half * 8:(half + 1) * 8, :], in_=mh_sb)
        P.I("pool", "memset", [], carry_tt, carry[:].rearrange("p a b c -> p (a b c)"), 0.0)

        def ntiles(T):
            t = [(n0, 512) for n0 in range(0, TP, 512)]
            if T > TP:
                t.append((TP, T - TP))
            return t
        self.ntiles = ntiles

        NWSCR = 420
        WSL = 105
        wscr_l = [P.dram_tmp(f"wscr{i_}", [WSL, 128, 4096], BF16) for i_ in range(NWSCR // WSL)]

        class _W:
            def __getitem__(self, idx):
                slot = idx[0]
                return wscr_l[slot // WSL][(slot % WSL,) + tuple(idx[1:])]
        wscr = _W()
        self.wcache = {}
        self.wready = {}

        def load_w(src_ap, kc, ncols, key=None):
            i = self.wbi % 2
            self.wbi += 1
            n = kc * ncols
            view = wb[i][:, 0:n].rearrange("p (k m) -> p k m", k=kc)
            if key is not None and key in self.wready:
                slot, tt = self.wready[key]
                P.Dm("sp", f"s_wb{i}", [tt], [wb_tt[i]], out=wb[i][:, 0:n], in_=wscr[slot, :, 0:n])
                return view, wb_tt[i]
            P.Dm("pool", f"s_wb{i}", [], [wb_tt[i]], out=view, in_=src_ap.rearrange("(k p) m -> p k m", p=128))
            if key is not None and key not in self.wcache:
                slot = len(self.wcache)
                assert slot < NWSCR
                tt = TT(f"wscr{slot}")
                P.Dm("sp", "s_wcv", [wb_tt[i]], [tt], out=wscr[slot, :, 0:n], in_=wb[i][:, 0:n])
                self.wcache[key] = (slot, tt)
            return view, wb_tt[i]

        def w_fence():
            tot = self.s.dcnt.get("s_wcv", 0)
            for key, (slot, tt) in self.wcache.items():
                if key not in self.wready:
                    tt.wr = ("s_wcv", tot)
                    self.wready[key] = (slot, tt)
        self.w_fence = w_fence
        self.load_w = load_w

        def hT_t(k):
            return hreg.t(k * TW * 2, (k + 1) * TW * 2)

        def rmsnorm(T, gkind, li_):
            hT = hreg.view(BF16, 0, KC * TW).rearrange("p (k t) -> p k t", k=KC)
            for (n0, nn) in ntiles(T):
                b = P.next_ps()
                for k in range(KC):
                    s_ = k % 2
                    P.I("act", "activation", [x_tt[k]], [sq_tt[s_]], out=sq[s_][:, 0:nn], in_=xT[:, k, n0:n0 + nn], func=AF.Square)
                    P.mm(ps_tt[b], psb[b][:, 0:nn], ones[:], sq[s_][:, 0:nn], k == 0, k == KC - 1, reads=[sq_tt[s_], c_tt])
                P.I("dve", "tensor_scalar", [ps_tt[b]], [rstd_tt], out=rstd[:, n0:n0 + nn], in0=psb[b][:, 0:nn], scalar1=1.0 / D,
                    scalar2=EPS, op0=ALU.mult, op1=ALU.add)
            rsqrt_inplace(rstd[:, 0:T], [rstd_tt])
            for k in range(KC):
                P.I("dve", "scalar_tensor_tensor", [x_tt[k], rstd_tt, vcol_tt], hT_t(k), out=hT[:, k, 0:T], in0=xT[:, k, 0:T],
                    scalar=gcol(gkind, li_, k), in1=rstd[:, 0:T], op0=ALU.mult, op1=ALU.mult)
            return hT
        self.rmsnorm = rmsnorm
        self.hT_t = hT_t

        def dense(T, w2d, kc, ncols_total, rhs_fn, rhs_tts_fn, evac, key=None):
            ncb = min(4096 // kc, ncols_total)
            for c0 in range(0, ncols_total, ncb):
                ncols = min(ncb, ncols_total - c0)
                wv, wtt = load_w(w2d[:, c0:c0 + ncols], kc, ncols, None if key is None else (key, c0))
                for m0 in range(0, ncols, 128):
                    mw = min(128, ncols - m0)
                    for (n0, nn) in ntiles(T):
                        b = P.next_ps()
                        for k in range(kc):
                            P.mm(ps_tt[b], psb[b][0:mw, 0:nn], wv[:, k, m0:m0 + mw], rhs_fn(k, n0, nn), k == 0, k == kc - 1,
                                 reads=[wtt] + rhs_tts_fn(k))
                        evac((c0 + m0) // 128, n0, nn, psb[b], ps_tt[b], mw)
        self.dense = dense

        def norm_rows(rows_ap, tt, nh, sbase, gain_b):
            for h in range(nh):
                P.I("act", "activation", [tt], [sq_tt[0], small_tt], out=sq[0][:, 0:128], in_=rows_ap[:, h * 128:(h + 1) * 128],
                    func=AF.Square, accum_out=small[:, sbase + h:sbase + h + 1])
            P.I("dve", "tensor_scalar", [small_tt], [small_tt], out=small[:, sbase + 8:sbase + 8 + nh], in0=small[:, sbase:sbase + nh],
                scalar1=1.0 / HD, scalar2=EPS, op0=ALU.mult, op1=ALU.add)
            rsqrt_inplace(small[:, sbase + 8:sbase + 8 + nh], [small_tt])
            for h in range(nh):
                kh = rows_ap[:, h * 128:(h + 1) * 128]
                P.I("dve", "scalar_tensor_tensor", [tt, small_tt, c_tt], [tt], out=kh, in0=kh,
                    scalar=small[:, sbase + 8 + h:sbase + 9 + h], in1=gain_b, op0=ALU.mult, op1=ALU.mult)
        self.norm_rows = norm_rows

        kmT_d = P.dram_tmp("kmT_d", [DEPTH, 128, 4 * 256], BF16)
        vm_d = P.dram_tmp("vm_d", [DEPTH, 128, 2 * 4 * HD], BF16)
        kmd_tt = [TT(f"kmT_d{i_}") for i_ in range(DEPTH)]
        vmd_tt = [TT(f"vm_d{i_}") for i_ in range(DEPTH)]

        def mem_kv(li_, T, first_chunk):
            if not first_chunk:
                P.Dm("sp", "s_kmT", [kmd_tt[li_]], [km_tt], out=kmT[:].rearrange("p h m -> p (h m)"), in_=kmT_d[li_])
                P.Dm("sp", "s_vm", [vmd_tt[li_]], [vm_tt], out=vm[:].rearrange("p c h d -> p (c h d)"), in_=vm_d[li_])
                return
            memT = hreg.view(BF16, 0, KC * 256).rearrange("p (k m) -> p k m", k=KC)
            memT_t = hreg.t(0, KC * 256 * 2)
            mh = r1.view(F32, 0, KC * 256).rearrange("p (k m) -> p k m", k=KC)
            mh_t = r1.t(0, KC * 256 * 4)
            P.Dm("sp", "s_r1", [memhat_tt], mh_t, out=mh, in_=memhat_d)
            for k in range(KC):
                P.I("dve", "tensor_scalar", mh_t + [vcol_tt], memT_t, out=memT[:, k, :], in0=mh[:, k, :], scalar1=gcol("mem", li_, k),
                    scalar2=None, op0=ALU.mult)
            kvrow = [stg[0][:, 0:1024], stg[1][:, 0:1024]]
            for cb in range(4):
                wv, wtt = load_w(d["w_mem_kv"][li_, :, cb * 256:(cb + 1) * 256], KC, 256, ("memkv", li_, cb))
                for mc in range(2):
                    b = P.next_ps()
                    for k in range(KC):
                        P.mm(ps_tt[b], psb[b][:, 0:256], memT[:, k, mc * 128:(mc + 1) * 128], wv[:, k, :], k == 0, k == KC - 1,
                             reads=[wtt] + memT_t)
                    P.copy(P.ev_eng(), kvrow[mc][:, cb * 256:(cb + 1) * 256], psb[b][:, 0:256], reads=[ps_tt[b]], writes=[stg_tt[mc]])
            for mc in range(2):
                norm_rows(kvrow[mc], stg_tt[mc], 4, 16, gkb[:, li_, :])
                for h in range(4):
                    b = P.next_ps()
                    P.tr(ps_tt[b], psb[b][:, 0:128], kvrow[mc][:, h * 128:(h + 1) * 128], ident[:], reads=[stg_tt[mc], c_tt])
                    P.copy(P.ev_eng(), kmT[:, h, mc * 128:(mc + 1) * 128], psb[b][:, 0:128], reads=[ps_tt[b]], writes=[km_tt])
                P.I("pool", "tensor_copy", [stg_tt[mc]], [vm_tt], out=vm[:, mc, :, :].rearrange("p h d -> p (h d)"), in_=kvrow[mc][:, 512:1024])
                if first_chunk:
                    P.Dm("sp", f"s_stg{mc}", [stg_tt[mc]], [], out=o["mkv"][li_, mc * 128:(mc + 1) * 128, :], in_=kvrow[mc])
            P.Dm("sp", "s_kmT", [km_tt], [kmd_tt[li_]], out=kmT_d[li_], in_=kmT[:].rearrange("p h m -> p (h m)"))
            P.Dm("sp", "s_vm", [vm_tt], [vmd_tt[li_]], out=vm_d[li_], in_=vm[:].rearrange("p c h d -> p (c h d)"))

        kmTs = hreg.view(BF16, 0, NSB * 4 * 256).rearrange("p (s h m) -> p s h m", s=NSB, h=4)
        vms = hreg.view(BF16, NSB * 4 * 256 * 2, NSB * 2 * 4 * HD).rearrange("p (s c h d) -> p s c h d", s=NSB, c=2, h=4)

        def mem_attend_sample(li_):
            htt_all = hreg.tts
            for sb in range(NSB):
                for mc in range(2):
                    P.Dm("sp", f"s_stg{mc}", [], [stg_tt[mc]], out=stg[mc][:, 0:1024], in_=d["cmkv"][li_, sb, mc * 128:(mc + 1) * 128, :])
                    for h in range(4):
                        b = P.next_ps()
                        P.tr(ps_tt[b], psb[b][:, 0:128], stg[mc][:, h * 128:(h + 1) * 128], ident[:], reads=[stg_tt[mc], c_tt])
                        P.copy(P.ev_eng(), kmTs[:, sb, h, mc * 128:(mc + 1) * 128], psb[b][:, 0:128], reads=[ps_tt[b]], writes=htt_all)
                    P.I("pool", "tensor_copy", [stg_tt[mc]], htt_all, out=vms[:, sb, mc, :, :].rearrange("p h d -> p (h d)"), in_=stg[mc][:, 512:1024])
            for h in range(4):
                for sb in range(NSB):
                    n0, nn = NS1 * sb, NS1
                    bacc, bden = 4, 5
                    for mc in range(2):
                        b = P.next_ps()
                        P.mm(ps_tt[b], psb[b][:, 0:nn], kmTs[:, sb, h, mc * 128:(mc + 1) * 128], qns[:, h, n0:n0 + nn], True, True, reads=htt_all + [qns_tt])
                        P.I("act", "activation", [ps_tt[b]], [pt_tt[mc]], out=pt[mc][:, 0:nn], in_=psb[b][:, 0:nn], func=AF.Exp, scale=SCALE)
                        P.mm(ps_tt[bacc], psb[bacc][:, 0:nn], vms[:, sb, mc, h, :], pt[mc][:, 0:nn], mc == 0, mc == 1, reads=htt_all + [pt_tt[mc]])
                        P.mm(ps_tt[bden], psb[bden][:, 0:nn], onesb[:], pt[mc][:, 0:nn], mc == 0, mc == 1, reads=[c_tt, pt_tt[mc]])
                    P.I("dve", "reciprocal", [ps_tt[bden]], [sq_tt[1]], out=sq[1][:, 0:nn], in_=psb[bden][:, 0:nn])
                    P.I("dve", "tensor_tensor", [ps_tt[bacc], sq_tt[1]], cat_t(12 + h), out=catT[:, 12 + h, TP + n0:TP + n0 + nn], in0=psb[bacc][:, 0:nn],
                        in1=sq[1][:, 0:nn], op=ALU.mult)

        catT = creg.view(BF16, 0, KC * TW).rearrange("p (k t) -> p k t", k=KC)
        self.catT = catT

        def cat_t(k):
            return creg.t(k * TW * 2, (k + 1) * TW * 2)
        self.cat_t = cat_t

        def qnorm_feat(src, src_tt, T, gain_col, dst, dst_tts):
            for (n0, nn) in ntiles(T):
                b = P.next_ps()
                P.I("act", "activation", [src_tt], [sq_tt[0]], out=sq[0][:, 0:nn], in_=src[:, n0:n0 + nn], func=AF.Square)
                P.mm(ps_tt[b], psb[b][:, 0:nn], ones[:], sq[0][:, 0:nn], True, True, reads=[sq_tt[0], c_tt])
                P.I("dve", "tensor_scalar", [ps_tt[b]], [rstd_tt], out=rstd[:, n0:n0 + nn], in0=psb[b][:, 0:nn], scalar1=1.0 / HD,
                    scalar2=EPS, op0=ALU.mult, op1=ALU.add)
            rsqrt_inplace(rstd[:, 0:T], [rstd_tt])
            P.I("dve", "scalar_tensor_tensor", [src_tt, rstd_tt, vcol_tt], dst_tts, out=dst[:, 0:T], in0=src[:, 0:T], scalar=gain_col,
                in1=rstd[:, 0:T], op0=ALU.mult, op1=ALU.mult)
        self.qnorm_feat = qnorm_feat

        def mem_attend_head(li_, h, T):
            qnorm_feat(qraw, qraw_tt, T, gcol("memqk", li_, 0), qn, [qn_tt])
            if T > TP:
                P.I("pool", "tensor_copy", [qn_tt], [qns_tt], out=qns[:, h, :], in_=qn[:, TP:TP + NS])
            for (n0, nn) in [(n0, 512) for n0 in range(0, TP, 512)]:
                K_, V_, ktt, vtt = (kmT, vm, km_tt, vm_tt)
                bacc, bden = 4, 5
                for mc in range(2):
                    b = P.next_ps()
                    P.mm(ps_tt[b], psb[b][:, 0:nn], K_[:, h, mc * 128:(mc + 1) * 128], qn[:, n0:n0 + nn], True, True, reads=[ktt, qn_tt])
                    P.I("act", "activation", [ps_tt[b]], [pt_tt[mc]], out=pt[mc][:, 0:nn], in_=psb[b][:, 0:nn], func=AF.Exp, scale=SCALE)
                    P.mm(ps_tt[bacc], psb[bacc][:, 0:nn], V_[:, mc, h, :], pt[mc][:, 0:nn], mc == 0, mc == 1, reads=[vtt, pt_tt[mc]])
                    P.mm(ps_tt[bden], psb[bden][:, 0:nn], onesb[:], pt[mc][:, 0:nn], mc == 0, mc == 1, reads=[c_tt, pt_tt[mc]])
                P.I("dve", "reciprocal", [ps_tt[bden]], [sq_tt[1]], out=sq[1][:, 0:nn], in_=psb[bden][:, 0:nn])
                P.I("dve", "tensor_tensor", [ps_tt[bacc], sq_tt[1]], cat_t(12 + h), out=catT[:, 12 + h, n0:n0 + nn], in0=psb[bacc][:, 0:nn],
                    in1=sq[1][:, 0:nn], op=ALU.mult)
        self.mem_attend_head = mem_attend_head
        self.mem_kv = mem_kv

        def out_and_mlp(i, T, j):
            def ev_res(m, n0, nn, ps, ptt, mw):
                P.I("dve", "tensor_tensor", [x_tt[m], ptt], [x_tt[m]], out=xT[:, m, n0:n0 + nn], in0=xT[:, m, n0:n0 + nn], in1=ps[:, 0:nn], op=ALU.add)
            dense(T, d["w_out"][i], KC, D, lambda k, n0, nn: catT[:, k, n0:n0 + nn], lambda k: cat_t(k), ev_res, key=("out", i))
            hT = rmsnorm(T, "mlp", i)
            a_fg = r1.view(BF16, 0, 16 * TW).rearrange("p (k t) -> p k t", k=16)

            def a_t(m):
                return r1.t(m * TW * 2, (m + 1) * TW * 2)
            for fg in range(4):
                def ev_up(m, n0, nn, ps, ptt, mw):
                    s_ = m % 2
                    P.I("act", "activation", [ptt], [sq_tt[s_]], out=sq[s_][:, 0:nn], in_=ps[:, 0:nn], func=AF.Relu)
                    P.I("dve", "tensor_tensor", [sq_tt[s_]], a_t(m), out=a_fg[:, m, n0:n0 + nn], in0=sq[s_][:, 0:nn], in1=sq[s_][:, 0:nn], op=ALU.mult)
                dense(T, d["w_up"][i][:, fg * 2048:(fg + 1) * 2048], KC, 2048, lambda k, n0, nn: hT[:, k, n0:n0 + nn], hT_t, ev_up, key=("up", i, fg))
                dense(T, d["w_down"][i][fg * 2048:(fg + 1) * 2048, :], KC, D, lambda k, n0, nn: a_fg[:, k, n0:n0 + nn], a_t, ev_res, key=("down", i, fg))
        self.out_and_mlp = out_and_mlp

        def out_rows(src_fn, src_tts, ncols_feat, n_tok, dst_fn):
            nchunk = ncols_feat // 128
            s_ = None
            for c in range(nchunk):
                if c % 8 == 0:
                    s_ = self.stgi % 2
                    self.stgi += 1
                b = P.next_ps()
                P.tr(ps_tt[b], psb[b][0:n_tok, 0:128], src_fn(c), ident[:], reads=src_tts(c) + [c_tt])
                P.copy(P.ev_eng(), stg[s_][0:n_tok, (c % 8) * 128:(c % 8 + 1) * 128], psb[b][0:n_tok, 0:128], reads=[ps_tt[b]], writes=[stg_tt[s_]])
                if c % 8 == 7 or c == nchunk - 1:
                    c0 = (c // 8) * 8
                    w_ = (c - c0 + 1) * 128
                    P.Dm("sp", f"s_stg{s_}", [stg_tt[s_]], [], out=dst_fn(c0 * 128, w_), in_=stg[s_][0:n_tok, 0:w_])
        self.out_rows = out_rows

        def pool_layer(i, j, T):
            li = i // 2
            u = r1.view(F32, 0, 12 * TW).rearrange("p (k t) -> p k t", k=12)

            def u_t(c):
                return r1.t(c * TW * 4, (c + 1) * TW * 4)
            mem_kv(i, T, j == 0)
            hT = rmsnorm(T, "mix", i)

            def ev_in(m, n0, nn, ps, ptt, mw):
                if m < 12:
                    P.copy(P.ev_eng(), u[:, m, n0:n0 + nn], ps[:, 0:nn], reads=[ptt], writes=u_t(m))
                else:
                    P.copy(P.ev_eng(), qraw[:, n0:n0 + nn], ps[:, 0:nn], reads=[ptt], writes=[qraw_tt])
                    if n0 + nn >= T:
                        mem_attend_head(i, m - 12, T)
            dense(T, d["w_in_pool"][li], KC, D, lambda k, n0, nn: hT[:, k, n0:n0 + nn], hT_t, ev_in, key=("inpool", li))
            if T > TP:
                mem_attend_sample(i)
            L = 16 + TP
            E = hreg.view(F32, 0, L)
            A = hreg.view(F32, L * 4, L)
            B = hreg.view(F32, 2 * L * 4, L)
            pooled = hreg.view(BF16, 3 * L * 4, 3 * TW).rearrange("p (k t) -> p k t", k=3)
            htt = hreg.tts
            wg_v = [None, None]
            for half in range(2):
                wi = self.wbi % 2
                self.wbi += 1
                v = wb[wi][:, 0:2 * 3 * 384].rearrange("p (g k m) -> p g k m", g=2, k=3)
                for gg in range(2):
                    P.Dm("pool", f"s_wb{wi}", [], [wb_tt[wi]], out=v[:, gg, :, :],
                         in_=d["w_pg"][li, 2 * half + gg].rearrange("(k p) m -> p k m", p=128))
                wg_v[half] = (v, wb_tt[wi])
            if T > TP:
                for sb in range(NSB):
                    P.Dm("sp", "s_stg0", [], [stg_tt[0]], out=stg[0][0:15, 0:1024], in_=d["spool"][li, sb, :, 0:1024])
                    P.Dm("sp", "s_stg1", [], [stg_tt[1]], out=stg[1][0:15, 0:512], in_=d["spool"][li, sb, :, 1024:1536])
                    P.Dm("sp", "s_misc", [], [], out=o["pool_s"][li, sb, 0:11, :], in_=d["spool"][li, sb, 4:15, :])
                    for c in range(12):
                        src = stg[0][0:15, c * 128:(c + 1) * 128] if c < 8 else stg[1][0:15, (c - 8) * 128:(c - 7) * 128]
                        b = P.next_ps()
                        P.tr(ps_tt[b], psb[b][:, 0:15], src, ident[0:15, 0:15], reads=[stg_tt[0 if c < 8 else 1], c_tt])
                        P.copy(P.ev_eng(), sthist[:, sb, c, 1:16], psb[b][:, 0:15], reads=[ps_tt[b]], writes=[sthist_tt])
            for g in range(4):
                w = 2 << g
                steps = g + 1
                for cc in range(3):
                    c = 3 * g + cc
                    utt = u_t(c)
                    for part in range(1 + NSB if T > TP else 1):
                        if part == 0:
                            n = TP
                            P.I("pool", "tensor_copy", [carry_tt[li]], htt, out=E[:, 0:16], in_=carry[:, li, c, :])
                            P.I("pool", "tensor_copy", utt, htt, out=E[:, 16:16 + TP], in_=u[:, c, 0:TP])
                            ucol = u[:, c, 0:TP]
                            pout = pooled[:, cc, 0:TP]
                        else:
                            n = NS1
                            sb = part - 1
                            s0 = TP + NS1 * sb
                            P.I("pool", "tensor_copy", [sthist_tt], htt, out=E[:, 0:16], in_=sthist[:, sb, c, :])
                            P.I("pool", "tensor_copy", utt, htt, out=E[:, 16:16 + NS1], in_=u[:, c, s0:s0 + NS1])
                            ucol = u[:, c, s0:s0 + NS1]
                            pout = pooled[:, cc, s0:s0 + NS1]
                        Ln = 16 + n
                        src, dst = E, A
                        sh = 1
                        lo = 0
                        for st in range(steps):
                            lo += sh
                            P.I("dve", "tensor_tensor", htt, htt, out=dst[:, lo:Ln], in0=src[:, lo:Ln], in1=src[:, lo - sh:Ln - sh], op=ALU.add)
                            src = dst
                            dst = B if dst is A else A
                            sh *= 2
                        P.I("dve", "scalar_tensor_tensor", htt + utt, htt, out=pout, in0=src[:, 16:16 + n], scalar=1.0 / w, in1=ucol,
                            op0=ALU.mult, op1=ALU.subtract)
                        if part == 0 and j == 0:
                            P.I("dve", "tensor_tensor", htt + [c_tt], [sq_tt[1]], out=sq[1][:, 0:16], in0=src[:, 16:32], in1=invc[:, g, :], op=ALU.mult)
                            P.I("dve", "tensor_tensor", [sq_tt[1]] + utt, htt, out=pout[:, 0:16], in0=sq[1][:, 0:16], in1=ucol[:, 0:16], op=ALU.subtract)
                        if part == 0:
                            P.I("pool", "tensor_copy", utt + htt, [carry_tt[li]], out=carry[:, li, c, 1:16], in_=u[:, c, TP - 15:TP])
                wv, wtt = wg_v[g // 2]
                for mo in range(3):
                    for (n0, nn) in ntiles(T):
                        b = P.next_ps()
                        for ki in range(3):
                            P.mm(ps_tt[b], psb[b][:, 0:nn], wv[:, g % 2, ki, mo * 128:(mo + 1) * 128], pooled[:, ki, n0:n0 + nn],
                                 ki == 0, ki == 2, reads=[wtt] + htt)
                        P.I("act", "activation", [ps_tt[b], vcol_tt], cat_t(3 * g + mo), out=catT[:, 3 * g + mo, n0:n0 + nn], in_=psb[b][:, 0:nn],
                            func=AF.Identity, scale=gcol("pscale", li, 3 * g + mo))
            if j == NCH - 1:
                out_rows(lambda c: u[:, c, TP - 15:TP], u_t, TOK, 15, lambda f0, w_: o["pool_p"][li, :, f0:f0 + w_])
            if T > TP:
                for sb in range(NSB):
                    out_rows(lambda c, sb=sb: u[:, c, TP + NS1 * sb:TP + NS1 * (sb + 1)], u_t, TOK, NS1,
                             lambda f0, w_, sb=sb: o["pool_s"][li, sb, 11:15, f0:f0 + w_])
            out_and_mlp(i, T, j)

        kslcT = P.sb("kslcT", [128, SEQ], BF16)
        vslc = P.sb("vslc", [128, SEQ // 128, HD], BF16)
        kwinT = P.sb("kwinT", [128, 1024], BF16)
        vwin = P.sb("vwin", [128, 8, HD], BF16)
        msk = [P.sb(f"msk{i}", [128, 128], BF16) for i in range(2)]
        msk_tt = [TT(f"msk{i}") for i in range(2)]
        sel2s = [P.sb(f"sel2s{i}", [128, 2, 64], BF16) for i in range(2)]
        sel2s_tt = [TT(f"sel2s{i}") for i in range(2)]
        kslc_tt, vslc_tt, kwin_tt, vwin_tt = (TT(n) for n in ("kslcT", "vslc", "kwinT", "vwin"))
        cmpx = P.sb("cmpx", [128, 4, 528], BF16)
        cmpx_tt = [TT(f"cmpx{i}") for i in range(4)]
        ccarry = P.sb("ccarry", [128, 2, 4, 16], BF16)
        ccarry_tt = TT("ccarry")
        kcT = P.sb("kcT", [128, 2, 2, 256], BF16)
        vc = P.sb("vc", [128, 2, 2, 2, HD], BF16)
        kc_tt, vc_tt = TT("kcT"), TT("vc")
        w2b = P.sb("w2b", [128, 2, 2, HD], BF16)
        b1col = P.sb("b1col", [128, 4], F32)
        posT = P.sb("posT", [128, 4, 32], BF16)
        n_tt = TT("nsa_consts")
        gates = P.sb("gates", [128, TW], F32)
        gates_tt = TT("gates")
        gbias = P.sb("gbias", [128, 2], F32)
        Amat = P.sb("Amat", [128, 2, 64], F32)
        tri = P.sb("tri", [128, 128], BF16)
        tris = P.sb("tris", [128, 128], BF16)
        cmask = P.sb("cmask", [128, 2, 128], BF16)
        cmask_tt = TT("cmask")
        P32v = [stg[0][:, 0:768], stg[1][:, 0:768]]
        impT = P.sb("impT", [128, 2, 128], F32)
        impT_tt = TT("impT")
        selw = P.sb("selw", [128, 6, 64], F32)
        selw_tt = TT("selw")
        seltab = P.sb("seltab_sb", [128, 128], F32)
        seltab_tt = TT("seltab")
        m8 = P.sb("m8", [128, 16], F32)
        kvf = [P.sb(f"kvf{i}", [128, TW], F32) for i in range(2)]
        kvf_tt = [TT(f"kvf{i}") for i in range(2)]
        rowsb = [stg[i][:, 0:512].rearrange("p (a b) -> p a b", a=4) for i in range(2)]
        rowsb_tt = stg_tt
        vrow = [P.sb(f"vrow{i}", [128, 4, HD], BF16) for i in range(2)]
        vrow_tt = [TT(f"vrow{i}") for i in range(2)]
        kbf = [P.sb(f"kbf{i}", [128, TP], BF16) for i in range(2)]
        kbf_tt = [TT(f"kbf{i}") for i in range(2)]
        tokacc = P.sb("tokacc", [128, 768], F32)
        tokacc_tt = TT("tokacc")
        obr = P.sb("obr", [128, 384], F32)
        obr_tt = TT("obr")
        rden = P.sb("rden", [128, 384], F32)
        rden_tt = TT("rden")
        gm = P.sb("gm", [128, 384], F32)
        gm_tt = TT("gm")
        h1 = P.sb("h1", [128, 32], BF16)
        h1_tt = TT("h1")
        vtmp = P.sb("vtmp", [128, HD], BF16)
        vtmp_tt = TT("vtmp")
        skv = P.sb("skv", [128, 12, NS], F32)
        skv_tt = TT("skv")
        kslc_d = P.dram_tmp("kslc_d", [2, 2, 128, SEQ], BF16)
        kwin_d = P.dram_tmp("kwin_d", [2, 2, 128, SEQ], BF16)
        vslc_d = P.dram_tmp("vslc_d", [2, 2, SEQ, HD], BF16)
        vwin_d = P.dram_tmp("vwin_d", [2, 2, SEQ, HD], BF16)
        kslcd_tt, kwind_tt, vslcd_tt, vwind_tt = TT("kslc_d"), TT("kwin_d"), TT("vslc_d"), TT("vwin_d")
        d["seltab"] = P.dram_in("seltab", [SEQ // 128, 128, 128])
        self.kvfi = 0

        P.I("pool", "memset", [], [n_tt], tri[:], 1.0)
        P.I("pool", "affine_select", [n_tt], [n_tt], out=tri[:], in_=tri[:], pattern=[[1, 128]], compare_op=ALU.is_ge, fill=0.0,
            base=0, channel_multiplier=-1)
        P.I("pool", "memset", [n_tt], [n_tt], tris[:], 1.0)
        P.I("pool", "affine_select", [n_tt], [n_tt], out=tris[:], in_=tris[:], pattern=[[-1, 128]], compare_op=ALU.is_ge, fill=0.0,
            base=-1, channel_multiplier=1)
        for ck in range(2):
            for term, (lo_b, hi_b) in enumerate(((0, 3), (-1, 2))):
                dst = selw[:, term, :]
                P.I("pool", "memset", [selw_tt], [selw_tt], dst, 1.0)
                P.I("pool", "affine_select", [selw_tt], [selw_tt], out=dst, in_=dst, pattern=[[-4, 64]], compare_op=ALU.is_ge, fill=0.0,
                    base=128 * ck - lo_b, channel_multiplier=1)
                P.I("pool", "affine_select", [selw_tt], [selw_tt], out=dst, in_=dst, pattern=[[4, 64]], compare_op=ALU.is_ge, fill=0.0,
                    base=hi_b - 128 * ck, channel_multiplier=-1)
            P.I("pool", "tensor_tensor", [selw_tt], [n_tt], out=Amat[:, ck, :], in0=selw[:, 0, :], in1=selw[:, 1, :], op=ALU.add)
        P.I("pool", "memset", [n_tt], [kc_tt], kcT[:].rearrange("p a b c -> p (a b c)"), 0.0)
        P.I("pool", "memset", [n_tt], [vc_tt], vc[:].rearrange("p a b c d -> p (a b c d)"), 0.0)
        P.I("pool", "memset", [n_tt], [ccarry_tt], ccarry[:].rearrange("p a b c -> p (a b c)"), 0.0)
        P.I("pool", "memset", [n_tt], [gates_tt], gates[:], 0.0)
        P.Dm("sp", "s_cst", [], [n_tt], out=gbias[0:36, :], in_=d["gate_bias"].rearrange("a b -> b a"), allow_slow_non_contiguous=True)
        for li_ in range(NL // 2):
            for kv in range(2):
                P.Dm("pool", "s_cst", [], [n_tt], out=w2b[:, li_, kv, :], in_=d["cmp_w2"][li_, kv])
                P.Dm("sp", "s_stg0", [], [stg_tt[0]], out=stg[0][0:32, 0:128], in_=d["cmp_pos"][li_, kv])
                b = P.next_ps()
                P.tr(ps_tt[b], psb[b][:, 0:32], stg[0][0:32, 0:128], ident[0:32, 0:32], reads=[stg_tt[0], c_tt])
                P.I("dve", "tensor_copy", [ps_tt[b]], [n_tt], out=posT[:, li_ * 2 + kv, :], in_=psb[b][:, 0:32])
                wv, wtt = load_w(d["cmp_w1"][li_, kv], 32, 128, ("w1", li_, kv))
                b = P.next_ps()
                for jj in range(32):
                    P.mm(ps_tt[b], psb[b][:, 0:1], wv[:, jj, :], posT[:, li_ * 2 + kv, jj:jj + 1], jj == 0, jj == 31, reads=[wtt, n_tt])
                P.I("dve", "tensor_copy", [ps_tt[b]], [n_tt], out=b1col[:, li_ * 2 + kv:li_ * 2 + kv + 1], in_=psb[b][:, 0:1])

        def norm_from_psum(src, ptt, nn, gain_col, dst, dst_tts, sa=None, sb_=None):
            (a_ap, a_tt) = sa if sa is not None else (sq[0], sq_tt[0])
            (r_ap, r_tt) = sb_ if sb_ is not None else (sq[1], sq_tt[1])
            P.I("act", "activation", [ptt], [a_tt], out=a_ap[:, 0:nn], in_=src, func=AF.Square)
            b = P.next_ps()
            P.mm(ps_tt[b], psb[b][:, 0:nn], ones[:], a_ap[:, 0:nn], True, True, reads=[a_tt, c_tt])
            P.I("dve", "tensor_scalar", [ps_tt[b]], [r_tt], out=r_ap[:, 0:nn], in0=psb[b][:, 0:nn], scalar1=1.0 / HD, scalar2=EPS,
                op0=ALU.mult, op1=ALU.add)
            rsqrt_inplace(r_ap[:, 0:nn], [r_tt])
            P.I("dve", "scalar_tensor_tensor", [ptt, r_tt, vcol_tt], dst_tts, out=dst, in0=src, scalar=gain_col, in1=r_ap[:, 0:nn],
                op0=ALU.mult, op1=ALU.mult)

        def bc3(ap2d):
            return ap2d.unsqueeze(1).to_broadcast([128, 3, 128])

        def nsa_layer(i, j, T):
            li = i // 2
            q = r1.view(BF16, 0, 12 * TW).rearrange("p (k t) -> p k t", k=12)

            def q_t(h):
                return r1.t(h * TW * 2, (h + 1) * TW * 2)
            mem_kv(i, T, j == 0)
            hT = rmsnorm(T, "mix", i)
            W = d["w_in_nsa"][li]
            rhs = lambda k, n0, nn: hT[:, k, n0:n0 + nn]

            def ev_q(m, n0, nn, ps, ptt, mw):
                norm_from_psum(ps[:, 0:nn], ptt, nn, gcol("nsaqk", li, 0), q[:, m, n0:n0 + nn], q_t(m))
            dense(T, W[:, 0:TOK], KC, TOK, rhs, hT_t, ev_q, key=("nsa_q", li))

            def ev_kv(m, n0, nn, ps, ptt, mw):
                slot, g = m // 2, m % 2
                samp = n0 >= TP
                if not samp:
                    fi = self.kvfi % 2
                    self.kvfi += 1
                    f, ftt = kvf[fi], kvf_tt[fi]
                    fdst, fdst_tt = f[:, 0:nn], [ftt]
                else:
                    fdst, fdst_tt = skv[:, m, :], [skv_tt]
                if slot in (2, 4):
                    norm_from_psum(ps[:, 0:nn], ptt, nn, gcol("nsaqk", li, 2 if slot == 2 else 3), fdst, fdst_tt)
                else:
                    P.copy(P.ev_eng(), fdst, ps[:, 0:nn], reads=[ptt], writes=fdst_tt)
                if samp:
                    return
                if slot in (0, 1):
                    ci = slot * 2 + g
                    P.I("pool", "tensor_copy", [ftt], [cmpx_tt[ci]], out=cmpx[:, ci, 16:16 + TP], in_=f[:, 0:TP])
                if slot in (2, 4):
                    P.I("pool", "tensor_copy", [ftt], [kbf_tt[fi]], out=kbf[fi][:, :], in_=f[:, 0:TP])
                    dst_d, dtt = (kslc_d, kslcd_tt) if slot == 2 else (kwin_d, kwind_tt)
                    P.Dm("sp", f"s_kbf{fi}", [kbf_tt[fi]], [dtt], out=dst_d[li, g, :, j * TP:(j + 1) * TP], in_=kbf[fi][:, :])
                for tb in range(TP // 128):
                    b = P.next_ps()
                    P.tr(ps_tt[b], psb[b][:, 0:128], f[:, tb * 128:(tb + 1) * 128], ident[:], reads=[ftt, c_tt])
                    P.copy(P.ev_eng(), rowsb[fi][:, tb, :], psb[b][:, 0:128], reads=[ps_tt[b]], writes=[rowsb_tt[fi]])
                    if slot in (3, 5):
                        P.I("pool", "tensor_copy", [rowsb_tt[fi]], [vrow_tt[fi]], out=vrow[fi][:, tb, :], in_=rowsb[fi][:, tb, :])
                if slot < 4:
                    c0 = slot * 256 + g * 128
                    P.Dm("sp", f"s_stg{fi}", [rowsb_tt[fi]], [],
                         out=o["nkv_p"][li, j * TP:(j + 1) * TP, c0:c0 + 128].rearrange("(tb p) d -> p tb d", p=128), in_=rowsb[fi])
                elif j == NCH - 1:
                    c0 = (slot - 4) * 256 + g * 128
                    P.Dm("sp", f"s_stg{fi}", [rowsb_tt[fi]], [],
                         out=o["win_p"][li, :, c0:c0 + 128].rearrange("(tb p) d -> p tb d", p=128), in_=rowsb[fi])
                if slot in (3, 5):
                    dst_d, dtt = (vslc_d, vslcd_tt) if slot == 3 else (vwin_d, vwind_tt)
                    P.Dm("sp", f"s_vrow{fi}", [vrow_tt[fi]], [dtt],
                         out=dst_d[li, g, j * TP:(j + 1) * TP, :].rearrange("(tb p) d -> p tb d", p=128), in_=vrow[fi][:, :, :])
            dense(T, W[:, TOK:2 * TOK], KC, TOK, rhs, hT_t, ev_kv, key=("nsa_kv", li))

            def ev_g(m, n0, nn, ps, ptt, mw):
                P.I("act", "activation", [ptt, n_tt], [gates_tt], out=gates[0:36, n0:n0 + nn], in_=ps[0:36, 0:nn], func=AF.Sigmoid,
                    bias=gbias[0:36, li:li + 1], scale=1.0)
            dense(T, W[:, 2 * TOK:2 * TOK + 36], KC, 36, rhs, hT_t, ev_g, key=("nsa_g", li))

            def ev_qm(m, n0, nn, ps, ptt, mw):
                P.copy(P.ev_eng(), qraw[:, n0:n0 + nn], ps[:, 0:nn], reads=[ptt], writes=[qraw_tt])
                if n0 + nn >= T:
                    mem_attend_head(i, m, T)
            dense(T, W[:, 2 * TOK + 36:NSA_W], KC, 512, rhs, hT_t, ev_qm, key=("nsa_qm", li))
            if T > TP:
                mem_attend_sample(i)

            compress(li, j, ccarry[:, li], ccarry_tt, kcT[:, li], [kc_tt], vc[:, li], [vc_tt])

            nkb_tot = 4 * (j + 1) if not cfg.get("no_attn") else 0
            kb0w = max(0, 4 * j - 4)
            for g in range(2 if not cfg.get("no_attn") else 0):
                P.Dm("sp", "s_kslc", [kslcd_tt], [kslc_tt], out=kslcT[:, 0:nkb_tot * 128], in_=kslc_d[li, g, :, 0:nkb_tot * 128])
                P.Dm("sp", "s_vslc", [vslcd_tt], [vslc_tt], out=vslc[:, 0:nkb_tot, :],
                     in_=vslc_d[li, g, 0:nkb_tot * 128, :].rearrange("(kb p) d -> p kb d", p=128))
                P.Dm("sp", "s_kwin", [kwind_tt], [kwin_tt], out=kwinT[:, 0:(nkb_tot - kb0w) * 128], in_=kwin_d[li, g, :, kb0w * 128:nkb_tot * 128])
                P.Dm("sp", "s_vwin", [vwind_tt], [vwin_tt], out=vwin[:, 0:nkb_tot - kb0w, :],
                     in_=vwin_d[li, g, kb0w * 128:nkb_tot * 128, :].rearrange("(kb p) d -> p kb d", p=128))
                for qt in range(TP // 128):
                    Q = 4 * j + qt
                    qc0 = qt * 128
                    ncv = 8 * Q + 7
                    nck = 1 if ncv <= 128 else 2
                    qtts = [t_ for h in range(6 * g, 6 * g + 6) for t_ in q_t(h)]

                    def qh(hh, g=g, qc0=qc0):
                        return q[:, 6 * g + 3 * hh:6 * g + 3 * hh + 3, qc0:qc0 + 128]

                    for ck in range(nck):
                        P.I("pool", "memset", [cmask_tt], [cmask_tt], cmask[:, ck, :], 1.0)
                        P.I("pool", "affine_select", [cmask_tt], [cmask_tt], out=cmask[:, ck, :], in_=cmask[:, ck, :], pattern=[[1, 128]],
                            compare_op=ALU.is_ge, fill=0.0, base=128 * Q - 31 - 2048 * ck, channel_multiplier=-16)
                    for hh in range(2):
                        for ck in range(nck):
                            bS = P.next_ps()
                            P.mm(ps_tt[bS], psb[bS][:, 0:384], kcT[:, li, g, ck * 128:(ck + 1) * 128], qh(hh), True, True, reads=[kc_tt] + qtts)
                            pv = P32v[ck][:, hh * 384:(hh + 1) * 384]
                            P.I("act", "activation", [ps_tt[bS]], [stg_tt[ck]], out=pv, in_=psb[bS][:, 0:384], func=AF.Exp, scale=SCALE)
                            P.I("dve", "tensor_tensor", [stg_tt[ck], cmask_tt], [stg_tt[ck]], out=pv.rearrange("p (h q) -> p h q", h=3),
                                in0=pv.rearrange("p (h q) -> p h q", h=3), in1=bc3(cmask[:, ck, :]), op=ALU.mult)
                            P.I("act", "copy", [stg_tt[ck]], [pt_tt[ck]], out=pt[ck][:, 0:384], in_=pv)
                            P.mm(ps_tt[4], psb[4][:, 0:384], vc[:, li, g, ck, :], pt[ck][:, 0:384], ck == 0, ck == nck - 1, reads=[vc_tt, pt_tt[ck]])
                            P.mm(ps_tt[5], psb[5][:, 0:384], ones[:], pv, ck == 0, ck == nck - 1, reads=[c_tt, stg_tt[ck]])
                        finish_branch(0, g, hh, qc0, 128, True, True, 4, 5)
                        for ck in range(nck):
                            pv = P32v[ck][:, hh * 384:(hh + 1) * 384]
                            P.I("dve", "tensor_tensor", [stg_tt[ck], rden_tt], [stg_tt[ck]], out=pv, in0=pv, in1=rden[:, :], op=ALU.mult)
                    for ck in range(nck):
                        P.I("dve", "tensor_reduce", [stg_tt[ck]], [impT_tt], out=impT[:, ck, :], in_=P32v[ck].rearrange("p (h q) -> p q h", h=6),
                            axis=AX.X, op=ALU.add)
                    for ck in range(nck):
                        P.mm(ps_tt[6], psb[6][:, 0:64], impT[:, ck, :], Amat[:, ck, :], ck == 0, ck == nck - 1, reads=[impT_tt, n_tt])
                    if g == 0:
                        P.Dm("sp", "s_seltab", [], [seltab_tt], out=seltab[:, :], in_=d["seltab"][Q])
                    sc, sc2, s_a, s_b = selw[:, 2, :], selw[:, 3, :], selw[:, 4, :], selw[:, 5, :]
                    P.I("dve", "tensor_tensor", [ps_tt[6], seltab_tt], [selw_tt], out=sc, in0=psb[6][:, 0:64], in1=seltab[:, 0:64], op=ALU.mult)
                    P.I("dve", "tensor_tensor", [selw_tt, seltab_tt], [selw_tt], out=sc, in0=sc, in1=seltab[:, 64:128], op=ALU.add)
                    topk_mask(sc, sc2, s_a, s_b, 128)

                    def smask(kb, idx, Q=Q, s_a=s_a):
                        mi = idx % 2
                        P.I("dve", "tensor_copy", [selw_tt], [sel2s_tt[mi]], out=sel2s[mi][:, :, :],
                            in_=s_a[:, 2 * kb:2 * kb + 2].unsqueeze(2).to_broadcast([128, 2, 64]))
                        b = P.next_ps()
                        P.mm(ps_tt[b], psb[b][:, 0:128], sel2s[mi][:, :, :].rearrange("p a b -> p (a b)"), identb[:], True, True,
                             reads=[sel2s_tt[mi], c_tt])
                        if kb == Q:
                            P.I("dve", "tensor_tensor", [ps_tt[b], n_tt], [msk_tt[mi]], out=msk[mi][:, :], in0=psb[b][:, 0:128], in1=tri[:], op=ALU.mult)
                        else:
                            P.I("act", "copy", [ps_tt[b]], [msk_tt[mi]], out=msk[mi][:, :], in_=psb[b][:, 0:128])
                        return (msk[mi][:, :], [msk_tt[mi]])

                    def wmask(kb, idx, Q=Q):
                        if kb == Q:
                            return (tri[:], [n_tt])
                        if kb == Q - 4:
                            return (tris[:], [n_tt])
                        return None

                    for (br, kbs, Kt, ktt, Vt, vtt, kofs, mfn) in (
                            (1, list(range(Q + 1)), kslcT, kslc_tt, vslc, vslc_tt, 0, smask),
                            (2, list(range(max(0, Q - 4), Q + 1)), kwinT, kwin_tt, vwin, vwin_tt, kb0w, wmask)):
                        n = len(kbs)

                        def pv_den(idx_, kl_, Vt=Vt, vtt=vtt, n=n):
                            for hh in range(2):
                                pi = hh + 2 * (idx_ % 2)
                                ba, bd = (4, 5) if hh == 0 else (6, 7)
                                P.mm(ps_tt[ba], psb[ba][:, 0:384], Vt[:, kl_, :], pt[pi][:, 0:384], idx_ == 0, idx_ == n - 1, reads=[vtt, pt_tt[pi]])
                                P.mm(ps_tt[bd], psb[bd][:, 0:384], onesb[:], pt[pi][:, 0:384], idx_ == 0, idx_ == n - 1, reads=[c_tt, pt_tt[pi]])
                        pend = None
                        for idx, kb in enumerate(kbs):
                            mk = mfn(kb, idx)
                            kl = kb - kofs
                            for hh in range(2):
                                bS = P.next_ps()
                                pi = hh + 2 * (idx % 2)
                                P.mm(ps_tt[bS], psb[bS][:, 0:384], Kt[:, kl * 128:(kl + 1) * 128], qh(hh), True, True, reads=[ktt] + qtts)
                                P.I("act", "activation", [ps_tt[bS]], [pt_tt[pi]], out=pt[pi][:, 0:384], in_=psb[bS][:, 0:384], func=AF.Exp, scale=SCALE)
                                if mk is not None:
                                    P.I("dve", "tensor_tensor", [pt_tt[pi]] + mk[1], [pt_tt[pi]], out=pt[pi][:, 0:384].rearrange("p (h q) -> p h q", h=3),
                                        in0=pt[pi][:, 0:384].rearrange("p (h q) -> p h q", h=3), in1=bc3(mk[0]), op=ALU.mult)
                            if pend is not None:
                                pv_den(*pend)
                            pend = (idx, kl)
                        pv_den(*pend)
                        for hh in range(2):
                            ba, bd = (4, 5) if hh == 0 else (6, 7)
                            finish_branch(br, g, hh, qc0, 128, False, False, ba, bd)
                    for hh in range(2):
                        P.I("act", "copy", [tokacc_tt], [t_ for h in range(6 * g + 3 * hh, 6 * g + 3 * hh + 3) for t_ in cat_t(h)],
                            out=catT[:, 6 * g + 3 * hh:6 * g + 3 * hh + 3, qc0:qc0 + 128],
                            in_=tokacc[:, hh * 384:(hh + 1) * 384].rearrange("p (h q) -> p h q", h=3))
            if T > TP and not cfg.get("no_sample"):
                nsa_sample(i, li, q, q_t)
            out_and_mlp(i, T, j)

        def topk_mask(sc, sc2, s_a, s_b, npart):
            P.I("dve", "max", [selw_tt], [selw_tt], out=m8[0:npart, 0:8], in_=sc)
            P.I("dve", "match_replace", [selw_tt], [selw_tt], out=sc2, in_to_replace=m8[0:npart, 0:8], in_values=sc, imm_value=-3.0e38)
            P.I("dve", "max", [selw_tt], [selw_tt], out=m8[0:npart, 8:16], in_=sc2)
            P.I("dve", "tensor_scalar", [selw_tt], [selw_tt], out=s_a, in0=sc, scalar1=m8[0:npart, 15:16], scalar2=None, op0=ALU.is_ge)
            P.I("dve", "tensor_scalar", [selw_tt], [selw_tt], out=s_b, in0=sc, scalar1=-5.0e29, scalar2=None, op0=ALU.is_gt)
            P.I("dve", "tensor_tensor", [selw_tt], [selw_tt], out=s_a, in0=s_a, in1=s_b, op=ALU.mult)

        def gate_apply(br, g, hh, gc0, nq, src, src_tts, first):
            W3 = 3 * nq
            for h3 in range(3):
                r = 3 * (6 * g + 3 * hh + h3) + br
                P.I("dve", "tensor_scalar", [gates_tt, c_tt], [gm_tt], out=gm[0:36, h3 * nq:(h3 + 1) * nq],
                    in0=gates[0:36, gc0:gc0 + nq], scalar1=ident[0:36, r:r + 1], scalar2=None, op0=ALU.mult)
            bG = P.next_ps()
            P.mm(ps_tt[bG], psb[bG][:, 0:W3], ones[0:36, :], gm[0:36, 0:W3], True, True, reads=[gm_tt, c_tt])
            ta = tokacc[:, hh * 384:hh * 384 + W3]
            if first:
                P.I("dve", "tensor_tensor", src_tts + [ps_tt[bG]], [tokacc_tt], out=ta, in0=src, in1=psb[bG][:, 0:W3], op=ALU.mult)
            else:
                P.I("dve", "tensor_tensor", src_tts + [ps_tt[bG]], [gm_tt], out=gm[:, 0:W3], in0=src, in1=psb[bG][:, 0:W3], op=ALU.mult)
                P.I("dve", "tensor_tensor", [gm_tt, tokacc_tt], [tokacc_tt], out=ta, in0=ta, in1=gm[:, 0:W3], op=ALU.add)

        def finish_branch(br, g, hh, gc0, nq, first, guard, ba, bd):
            W3 = 3 * nq
            if guard:
                P.I("dve", "tensor_scalar", [ps_tt[bd]], [rden_tt], out=rden[:, 0:W3], in0=psb[bd][:, 0:W3], scalar1=1e-30, scalar2=None, op0=ALU.max)
                P.I("dve", "reciprocal", [rden_tt], [rden_tt], out=rden[:, 0:W3], in_=rden[:, 0:W3])
            else:
                P.I("dve", "reciprocal", [ps_tt[bd]], [rden_tt], out=rden[:, 0:W3], in_=psb[bd][:, 0:W3])
            P.I("dve", "tensor_tensor", [ps_tt[ba], rden_tt], [obr_tt], out=obr[:, 0:W3], in0=psb[ba][:, 0:W3], in1=rden[:, 0:W3], op=ALU.mult)
            gate_apply(br, g, hh, gc0, nq, obr[:, 0:W3], [obr_tt], first)

        h1s = [(P.sb(f"h1s{i_}", [128, 32], BF16), TT(f"h1s{i_}")) for i_ in range(4)]
        nsq = [(P.sb(f"nsq{i_}", [128, 32], F32), TT(f"nsq{i_}")) for i_ in range(2)]
        nrs = [(P.sb(f"nrs{i_}", [128, 32], F32), TT(f"nrs{i_}")) for i_ in range(2)]
        vtmps = [(P.sb(f"vtmps{i_}", [32, HD], BF16), TT(f"vtmps{i_}")) for i_ in range(2)]

        def compress(li, s_idx, carry_ap, carry_t, kc_dst, kc_dst_t, vc_dst, vc_dst_t):
            lo = 1 if s_idx == 0 else 0
            c_lo = 32 * s_idx - 1
            for kv in range(2):
                wv, wtt = load_w(d["cmp_w1"][li, kv], 32, 128, ("w1", li, kv))
                for g in range(2):
                    ci = kv * 2 + g
                    cx = cmpx[:, ci, :]
                    P.I("pool", "tensor_copy", [carry_t], [cmpx_tt[ci]], out=cx[:, 0:16], in_=carry_ap[:, ci, :])
                    P.I("pool", "tensor_copy", [cmpx_tt[ci]], [carry_t], out=carry_ap[:, ci, :], in_=cx[:, TP:TP + 16])
                    cb3 = cx.rearrange("p (c s) -> p c s", s=16)
                    bh = P.next_ps()
                    for jj in range(32):
                        P.mm(ps_tt[bh], psb[bh][:, 0:32], wv[:, jj, :], cb3[:, jj // 16:jj // 16 + 32, jj % 16], jj == 0, jj == 31,
                             reads=[wtt, cmpx_tt[ci]])
                    h1, h1_tt = h1s[ci]
                    P.I("act", "activation", [ps_tt[bh], n_tt], [h1_tt], out=h1[:, 0:32], in_=psb[bh][:, 0:32], func=AF.Gelu_apprx_tanh,
                        bias=b1col[:, li * 2 + kv:li * 2 + kv + 1], scale=1.0)
                    b2 = P.next_ps()
                    if kv == 0:
                        P.mm(ps_tt[b2], psb[b2][:, 0:32], w2b[:, li, 0, :], h1[:, 0:32], True, True, reads=[h1_tt, n_tt])
                        norm_from_psum(psb[b2][:, lo:32], ps_tt[b2], 32 - lo, gcol("nsaqk", li, 1), kc_dst[:, g, c_lo + lo:c_lo + 32], kc_dst_t,
                                       sa=nsq[g], sb_=nrs[g])
                    else:
                        vtmp, vtmp_tt = vtmps[g]
                        P.mm(ps_tt[b2], psb[b2][0:32, 0:128], h1[:, 0:32], w2b[:, li, 1, :], True, True, reads=[h1_tt, n_tt])
                        P.I("dve", "tensor_copy", [ps_tt[b2]], [vtmp_tt], out=vtmp[0:32, :], in_=psb[b2][0:32, 0:128])
                        r = c_lo + lo
                        t0 = lo
                        while t0 < 32:
                            ckk, p0 = r // 128, r % 128
                            n = min(32 - t0, 128 - p0)
                            P.Dm("sp", f"s_vtmps{g}", [vtmp_tt], vc_dst_t, out=vc_dst[p0:p0 + n, g, ckk, :], in_=vtmp[t0:t0 + n, :])
                            t0 += n
                            r += n

        R1H = 12 * TW * 2
        r1s_tt = r1.t(R1H, 12 * TW * 4)
        kcTs = r1.view(BF16, R1H, 2048).rearrange("p (g c) -> p g c", g=2)
        vcs = P.sb("vcs", [128, 2, 8, HD], BF16)
        kcs_tt, vcs_tt = TT("kcTs"), TT("vcs")
        scarry = P.sb("scarry", [128, 4, 16], BF16)
        scarry_tt = TT("scarry")
        pti = P.sb("pti", [128, NPAGE], I32)
        pidx = P.sb("pidx", [128, NPAGE], I32)
        iop = P.sb("iop", [128, NPAGE], I32)
        pidxB = P.sb("pidxB", [128, NPAGE], I32)
        ptf = P.sb("ptf", [128, NPAGE], F32)
        iopf = P.sb("iopf", [128, NPAGE], F32)
        pidx_tt = TT("pidx")
        Aloc = P.sb("Aloc", [128, 34], F32)
        P32s = P.sb("P32s", [128, 8, 24], F32)
        P32s_tt = TT("P32s")
        impTs = P.sb("impTs", [128, 8, NS1], F32)
        impTs_tt = TT("impTs")
        ssel = r1.view(F32, R1H + 4096, 4 * 260).rearrange("p (a b) -> p a b", a=4)
        ssel_tt = TT("ssel")
        kTs = r1.view(BF16, R1H + 4096 + 4160, 2 * TP).rearrange("p (a b) -> p a b", a=2)
        vTs = r1.view(BF16, R1H + 4096 + 4160 + 2048, 2 * 4 * HD).rearrange("p (a b c) -> p a b c", a=2, b=4)
        kTs_tt, vTs_tt = TT("kTs"), TT("vTs")
        pts = [P.sb(f"pts{i}", [128, 24], BF16) for i in range(2)]
        pts_tt = [TT(f"pts{i}") for i in range(2)]
        newk = P.sb("newk", [128, 12, NS], BF16)
        newv = P.sb("newv", [128, 4, HD], BF16)
        newk_tt, newv_tt = TT("newk"), TT("newv")
        P.I("pool", "iota", [], [n_tt], iop[:], pattern=[[0, NPAGE]], base=0, channel_multiplier=2)
        P.I("dve", "tensor_copy", [n_tt], [n_tt], out=iopf[:, :], in_=iop[:, :])
        for term, (lo_b, hi_b) in enumerate(((0, 3), (-1, 2))):
            dst = selw[:, term, 0:34]
            P.I("pool", "memset", [selw_tt], [selw_tt], dst, 1.0)
            P.I("pool", "affine_select", [selw_tt], [selw_tt], out=dst, in_=dst, pattern=[[-4, 34]], compare_op=ALU.is_ge, fill=0.0,
                base=4 - lo_b, channel_multiplier=1)
            P.I("pool", "affine_select", [selw_tt], [selw_tt], out=dst, in_=dst, pattern=[[4, 34]], compare_op=ALU.is_ge, fill=0.0,
                base=hi_b - 4, channel_multiplier=-1)
        P.I("pool", "tensor_tensor", [selw_tt], [n_tt], out=Aloc[:, :], in0=selw[:, 0, 0:34], in1=selw[:, 1, 0:34], op=ALU.add)

        pgbuf = [(stg[0][:, 0:512], stg_tt[0], "s_stg0"), (stg[1][:, 0:512], stg_tt[1], "s_stg1"),
                 (kvf[0][:, 0:512], kvf_tt[0], "s_kvf0"), (kvf[1][:, 0:512], kvf_tt[1], "s_kvf1")]

        pscr = P.dram_tmp("pscr", [NPAGE, 128, 512], F32)
        pscr_tt = [TT(f"pscr{i_}") for i_ in range(NPAGE)]

        def nsa_sample(i, li, q, q_t):
            P.I("dve", "tensor_copy", [skv_tt], [newk_tt], out=newk[:].rearrange("p a b -> p (a b)"), in_=skv[:].rearrange("p a b -> p (a b)"))
            for m in range(12):
                slot, g = m // 2, m % 2
                b = P.next_ps()
                P.tr(ps_tt[b], psb[b][0:NS, 0:128], skv[:, m, :], ident[:], reads=[skv_tt, c_tt])
                s_ = 0 if m < 8 else 1
                cofs = (m % 8) * 128 if m < 8 else (m - 8) * 128
                P.copy(P.ev_eng(), stg[s_][0:NS, cofs:cofs + 128], psb[b][0:NS, 0:128], reads=[ps_tt[b]], writes=[stg_tt[s_]])
            P.Dm("sp", "s_stg0", [stg_tt[0]], [], out=o["nkv_s"][li, :, :], in_=stg[0][0:NS, 0:1024])
            for sb in range(NSB):
                P.Dm("sp", "s_stg1", [stg_tt[1]], [], out=o["win_s"][li, sb, 508:512, :], in_=stg[1][NS1 * sb:NS1 * (sb + 1), 0:512])
                P.Dm("sp", "s_misc", [], [], out=o["win_s"][li, sb, 0:508, :], in_=d["nsawin"][li, sb, 4:512, :])

            SSTOP = cfg.get("sample_stop", 99)
            if SSTOP <= 0:
                return
            for sb in range(NSB):
                sc0 = TP + NS1 * sb
                for vi, m in enumerate((6, 7, 10, 11)):
                    b = P.next_ps()
                    P.tr(ps_tt[b], psb[b][0:NS1, 0:128], skv[:, m, NS1 * sb:NS1 * (sb + 1)], ident[:], reads=[skv_tt, c_tt])
                    P.I("dve", "tensor_copy", [ps_tt[b]], [newv_tt], out=newv[0:NS1, vi, :], in_=psb[b][0:NS1, 0:128])
                P.Dm("sp", "s_pti", [], [pidx_tt], out=pti[:, :], in_=d["ptab"][sb, :].partition_broadcast(128))
                P.I("dve", "tensor_copy", [pidx_tt], [pidx_tt], out=ptf[:, :], in_=pti[:, :])
                P.I("dve", "scalar_tensor_tensor", [pidx_tt, n_tt], [pidx_tt], out=ptf[:, :], in0=ptf[:, :], scalar=256.0, in1=iopf[:, :],
                    op0=ALU.mult, op1=ALU.add)
                P.I("dve", "tensor_copy", [pidx_tt], [pidx_tt], out=pidx[:, :], in_=ptf[:, :])
                P.I("dve", "tensor_scalar", [pidx_tt], [pidx_tt], out=ptf[:, :], in0=ptf[:, :], scalar1=1.0, scalar2=None, op0=ALU.add)
                P.I("dve", "tensor_copy", [pidx_tt], [pidx_tt], out=pidxB[:, :], in_=ptf[:, :])
                if SSTOP <= 1:
                    continue
                P.I("pool", "memset", [], [scarry_tt], scarry[:].rearrange("p a b -> p (a b)"), 0.0)
                P.I("pool", "memset", [], r1s_tt, kcTs[:].rearrange("p a b -> p (a b)"), 0.0)
                P.I("pool", "memset", [], [vcs_tt], vcs[:].rearrange("p a b c -> p (a b c)"), 0.0)
                for s_i in range(32):
                    for pg4 in range(4):
                        pg = 4 * s_i + pg4
                        pgb, pgt, pgs = pgbuf[pg4]
                        self.idma(pgs, [pidx_tt], [pgt], pgb, d["nsakv"][li], pidx[:, pg:pg + 1])
                        for ci in range(4):
                            b = P.next_ps()
                            P.tr(ps_tt[b], psb[b][:, 0:128], pgb[:, ci * 128:(ci + 1) * 128], ident[:], reads=[pgt, c_tt])
                            P.copy(P.ev_eng(), cmpx[:, ci, 16 + pg4 * 128:16 + (pg4 + 1) * 128], psb[b][:, 0:128], reads=[ps_tt[b]], writes=[cmpx_tt[ci]])
                    compress(li, s_i, scarry, scarry_tt, kcTs, r1s_tt, vcs, [vcs_tt])
                for g in range(2):
                    if SSTOP <= 2:
                        continue
                    qtts = [t_ for h in range(6 * g, 6 * g + 6) for t_ in q_t(h)]
                    qs = q[:, 6 * g:6 * g + 6, sc0:sc0 + NS1]
                    for ck in range(8):
                        bS = P.next_ps()
                        P.mm(ps_tt[bS], psb[bS][:, 0:24], kcTs[:, g, ck * 128:(ck + 1) * 128], qs, True, True, reads=r1s_tt + qtts)
                        P.I("act", "activation", [ps_tt[bS]], [P32s_tt], out=P32s[:, ck, :], in_=psb[bS][:, 0:24], func=AF.Exp, scale=SCALE)
                        if ck == 7:
                            P.I("dve", "tensor_scalar", [P32s_tt, n_tt], [P32s_tt], out=P32s[:, 7, :], in0=P32s[:, 7, :], scalar1=lastmask[:, 0:1],
                                scalar2=None, op0=ALU.mult)
                        pi = ck % 2
                        P.I("act", "copy", [P32s_tt], [pts_tt[pi]], out=pts[pi][:, :], in_=P32s[:, ck, :])
                        P.mm(ps_tt[4], psb[4][:, 0:24], vcs[:, g, ck, :], pts[pi][:, :], ck == 0, ck == 7, reads=[vcs_tt, pts_tt[pi]])
                        P.mm(ps_tt[5], psb[5][:, 0:24], ones[:], P32s[:, ck, :], ck == 0, ck == 7, reads=[c_tt, P32s_tt])
                    for hh in range(2):
                        finish_branch_s(0, g, hh, sc0, True, True, 4, 5)
                    P.I("dve", "tensor_tensor", [P32s_tt, rden_tt], [P32s_tt], out=P32s[:, :, :], in0=P32s[:, :, :],
                        in1=rden[:, 0:24].unsqueeze(1).to_broadcast([128, 8, 24]), op=ALU.mult)
                    P.I("dve", "tensor_reduce", [P32s_tt], [impTs_tt], out=impTs[:, :, :], in_=P32s[:, :, :].rearrange("p c (h q) -> p c q h", h=6),
                        axis=AX.X, op=ALU.add)
                    sc, sc2, s_a, s_b = ssel[0:NS1, 0, 0:257], ssel[0:NS1, 1, 0:257], ssel[0:NS1, 2, 0:257], ssel[0:NS1, 3, 0:257]
                    P.I("dve", "memset", r1s_tt, r1s_tt, ssel[0:NS1, 0, :], 0.0)
                    for ck in range(8):
                        bq = P.next_ps()
                        P.mm(ps_tt[bq], psb[bq][0:NS1, 0:34], impTs[:, ck, :], Aloc[:, :], True, True, reads=[impTs_tt, n_tt])
                        b_lo = 32 * ck - 1
                        sk = 1 if ck == 0 else 0
                        P.I("dve", "tensor_tensor", [ps_tt[bq]] + r1s_tt, r1s_tt, out=ssel[0:NS1, 0, b_lo + sk:b_lo + 34],
                            in0=ssel[0:NS1, 0, b_lo + sk:b_lo + 34], in1=psb[bq][0:NS1, sk:34], op=ALU.add)
                    P.I("dve", "memset", r1s_tt, r1s_tt, ssel[0:NS1, 0, 0:1], BIGV)
                    P.I("dve", "memset", r1s_tt, r1s_tt, ssel[0:NS1, 0, 255:257], BIGV)
                    topk_mask_s(sc, sc2, s_a, s_b)
                    if SSTOP <= 3:
                        continue
                    pend2 = None

                    def pv_den2(kc2_, pg4_, mi_):
                        P.mm(ps_tt[4], psb[4][:, 0:24], vTs[:, 0, pg4_, :], pts[mi_][:, :], kc2_ == 0, False, reads=r1s_tt + [pts_tt[mi_]])
                        P.mm(ps_tt[5], psb[5][:, 0:24], onesb[:], pts[mi_][:, :], kc2_ == 0, False, reads=[c_tt, pts_tt[mi_]])
                    for s_i in range(32):
                        if pend2 is not None:
                            pv_den2(*pend2)
                            pend2 = None
                        for pg4 in range(4):
                            pg = 4 * s_i + pg4
                            pgb, pgt, pgs = pgbuf[pg4]
                            if g == 0:
                                self.idma(pgs, [pidx_tt], [pgt], pgb, d["nsakv"][li], pidxB[:, pg:pg + 1])
                                P.Dm("sp", "s_pscr", [pgt], [pscr_tt[pg]], out=pscr[pg], in_=pgb)
                            else:
                                P.Dm("sp", pgs, [pscr_tt[pg]], [pgt], out=pgb, in_=pscr[pg])
                            b = P.next_ps()
                            P.tr(ps_tt[b], psb[b][:, 0:128], pgb[:, g * 128:(g + 1) * 128], ident[:], reads=[pgt, c_tt])
                            P.copy(P.ev_eng(), kTs[:, 0, pg4 * 128:(pg4 + 1) * 128], psb[b][:, 0:128], reads=[ps_tt[b]], writes=r1s_tt)
                            P.I("pool", "tensor_copy", [pgt], r1s_tt, out=vTs[:, 0, pg4, :], in_=pgb[:, 256 + g * 128:256 + (g + 1) * 128])
                        for pg4 in range(4):
                            kc2 = 4 * s_i + pg4
                            mi = kc2 % 2
                            P.I("dve", "tensor_copy", r1s_tt, [sel2s_tt[mi]], out=sel2s[mi][0:NS1, :, :],
                                in_=s_a[:, 2 * kc2:2 * kc2 + 2].unsqueeze(2).to_broadcast([NS1, 2, 64]))
                            bm = P.next_ps()
                            P.mm(ps_tt[bm], psb[bm][:, 0:NS1], sel2s[mi][0:NS1, :, :].rearrange("p a b -> p (a b)"), identb[0:NS1, 0:NS1], True, True,
                                 reads=[sel2s_tt[mi], c_tt])
                            P.I("act", "copy", [ps_tt[bm]], [msk_tt[mi]], out=msk[mi][:, 0:NS1], in_=psb[bm][:, 0:NS1])
                            bS = P.next_ps()
                            P.mm(ps_tt[bS], psb[bS][:, 0:24], kTs[:, 0, pg4 * 128:(pg4 + 1) * 128], qs, True, True, reads=r1s_tt + qtts)
                            P.I("act", "activation", [ps_tt[bS]], [pts_tt[mi]], out=pts[mi][:, :], in_=psb[bS][:, 0:24], func=AF.Exp, scale=SCALE)
                            P.I("dve", "tensor_tensor", [pts_tt[mi], msk_tt[mi]], [pts_tt[mi]], out=pts[mi][:, :].rearrange("p (h q) -> p h q", h=6),
                                in0=pts[mi][:, :].rearrange("p (h q) -> p h q", h=6), in1=msk[mi][:, 0:NS1].unsqueeze(1).to_broadcast([128, 6, NS1]), op=ALU.mult)
                            if pend2 is not None:
                                pv_den2(*pend2)
                            pend2 = (kc2, pg4, mi)
                    pv_den2(*pend2)
                    if g == 0:
                        tot_ = self.s.dcnt["s_pscr"]
                        for t_ in pscr_tt:
                            t_.wr = ("s_pscr", tot_)
                    new_rows_attend(g, sb, qs, qtts, 4 + g, 0 + g, 4, 5, tri_small=True)
                    for hh in range(2):
                        finish_branch_s(1, g, hh, sc0, False, False, 4, 5)
                    if SSTOP <= 4:
                        continue
                    for kc in range(4):
                        s_ = kc % 2
                        P.Dm("sp", f"s_stg{s_}", [], [stg_tt[s_]], out=stg[s_][:, 0:512], in_=d["nsawin"][li, sb, kc * 128:(kc + 1) * 128, :])
                        b = P.next_ps()
                        P.tr(ps_tt[b], psb[b][:, 0:128], stg[s_][:, g * 128:(g + 1) * 128], ident[:], reads=[stg_tt[s_], c_tt])
                        P.copy(P.ev_eng(), kTs[:, 1, kc * 128:(kc + 1) * 128], psb[b][:, 0:128], reads=[ps_tt[b]], writes=r1s_tt)
                        P.I("pool", "tensor_copy", [stg_tt[s_]], r1s_tt, out=vTs[:, 1, kc, :], in_=stg[s_][:, 256 + g * 128:256 + (g + 1) * 128])
                        mi = kc % 2
                        bS = P.next_ps()
                        P.mm(ps_tt[bS], psb[bS][:, 0:24], kTs[:, 1, kc * 128:(kc + 1) * 128], qs, True, True, reads=r1s_tt + qtts)
                        P.I("act", "activation", [ps_tt[bS]], [pts_tt[mi]], out=pts[mi][:, :], in_=psb[bS][:, 0:24], func=AF.Exp, scale=SCALE)
                        if kc == 0:
                            P.I("dve", "tensor_tensor", [pts_tt[mi], n_tt], [pts_tt[mi]], out=pts[mi][:, :].rearrange("p (h q) -> p h q", h=6),
                                in0=pts[mi][:, :].rearrange("p (h q) -> p h q", h=6), in1=tris[:, 0:NS1].unsqueeze(1).to_broadcast([128, 6, NS1]), op=ALU.mult)
                        P.mm(ps_tt[4], psb[4][:, 0:24], vTs[:, 1, kc, :], pts[mi][:, :], kc == 0, False, reads=r1s_tt + [pts_tt[mi]])
                        P.mm(ps_tt[5], psb[5][:, 0:24], onesb[:], pts[mi][:, :], kc == 0, False, reads=[c_tt, pts_tt[mi]])
                    new_rows_attend(g, sb, qs, qtts, 8 + g, 2 + g, 4, 5, tri_small=True)
                    for hh in range(2):
                        finish_branch_s(2, g, hh, sc0, False, False, 4, 5)
                    for hh in range(2):
                        P.I("act", "copy", [tokacc_tt], [t_ for h in range(6 * g + 3 * hh, 6 * g + 3 * hh + 3) for t_ in cat_t(h)],
                            out=catT[:, 6 * g + 3 * hh:6 * g + 3 * hh + 3, sc0:sc0 + NS1],
                            in_=tokacc[:, hh * 384:hh * 384 + 3 * NS1].rearrange("p (h q) -> p h q", h=3))

        lastmask = P.sb("lastmask", [128, 1], F32)
        P.I("pool", "memset", [], [n_tt], lastmask[:], 1.0)
        P.I("pool", "affine_select", [n_tt], [n_tt], out=lastmask[:], in_=lastmask[:], pattern=[[0, 1]], compare_op=ALU.is_ge, fill=0.0,
            base=126, channel_multiplier=-1)

        def new_rows_attend(g, sb, qs, qtts, km, vi, ba, bd, tri_small):
            bS = P.next_ps()
            P.mm(ps_tt[bS], psb[bS][0:NS1, 0:24], newk[:, km, NS1 * sb:NS1 * (sb + 1)], qs, True, True, reads=[newk_tt] + qtts)
            P.I("act", "activation", [ps_tt[bS]], [pts_tt[0]], out=pts[0][0:NS1, :], in_=psb[bS][0:NS1, 0:24], func=AF.Exp, scale=SCALE)
            P.I("dve", "tensor_tensor", [pts_tt[0], n_tt], [pts_tt[0]], out=pts[0][0:NS1, :].rearrange("p (h q) -> p h q", h=6),
                in0=pts[0][0:NS1, :].rearrange("p (h q) -> p h q", h=6), in1=tri[0:NS1, 0:NS1].unsqueeze(1).to_broadcast([NS1, 6, NS1]), op=ALU.mult)
            P.mm(ps_tt[ba], psb[ba][:, 0:24], newv[0:NS1, vi, :], pts[0][0:NS1, :], False, True, reads=[newv_tt, pts_tt[0]])
            P.mm(ps_tt[bd], psb[bd][:, 0:24], onesb[0:NS1, :], pts[0][0:NS1, :], False, True, reads=[c_tt, pts_tt[0]])

        def finish_branch_s(br, g, hh, gc0, first, guard, ba, bd):
            W3 = 3 * NS1
            if hh == 0:
                if guard:
                    P.I("dve", "tensor_scalar", [ps_tt[bd]], [rden_tt], out=rden[:, 0:24], in0=psb[bd][:, 0:24], scalar1=1e-30, scalar2=None, op0=ALU.max)
                    P.I("dve", "reciprocal", [rden_tt], [rden_tt], out=rden[:, 0:24], in_=rden[:, 0:24])
                else:
                    P.I("dve", "reciprocal", [ps_tt[bd]], [rden_tt], out=rden[:, 0:24], in_=psb[bd][:, 0:24])
                P.I("dve", "tensor_tensor", [ps_tt[ba], rden_tt], [obr_tt], out=obr[:, 0:24], in0=psb[ba][:, 0:24], in1=rden[:, 0:24], op=ALU.mult)
            gate_apply(br, g, hh, gc0, NS1, obr[:, hh * W3:(hh + 1) * W3], [obr_tt], first)

        def topk_mask_s(sc, sc2, s_a, s_b):
            P.I("dve", "max", r1s_tt, r1s_tt, out=m8[0:NS1, 0:8], in_=sc)
            P.I("dve", "match_replace", r1s_tt, r1s_tt, out=sc2, in_to_replace=m8[0:NS1, 0:8], in_values=sc, imm_value=-3.0e38)
            P.I("dve", "max", r1s_tt, r1s_tt, out=m8[0:NS1, 8:16], in_=sc2)
            P.I("dve", "tensor_scalar", r1s_tt, r1s_tt, out=s_a, in0=sc, scalar1=m8[0:NS1, 15:16], scalar2=None, op0=ALU.is_ge)

        self.nsa_layer = nsa_layer

        w_fence()
        for j in range(cfg["n_chunks"]):
            T = TW if j == 0 else TP
            for tb in range(TP // 128):
                for half in range(2):
                    s_ = self.stgi % 2
                    self.stgi += 1
                    P.Dm("sp", f"s_stg{s_}", [], [stg_tt[s_]], out=stg[s_][:, 0:1024],
                         in_=d["xp"][j * TP + tb * 128:j * TP + (tb + 1) * 128, half * 1024:(half + 1) * 1024])
                    for kk in range(8):
                        k = half * 8 + kk
                        b = P.next_ps()
                        P.tr(ps_tt[b], psb[b][:, 0:128], stg[s_][:, kk * 128:(kk + 1) * 128], ident[:], reads=[stg_tt[s_], c_tt])
                        P.copy(P.ev_eng(), xT[:, k, tb * 128:(tb + 1) * 128], psb[b][:, 0:128], reads=[ps_tt[b]], writes=[x_tt[k]])
            if j == 0:
                for half in range(2):
                    s_ = self.stgi % 2
                    self.stgi += 1
                    P.Dm("sp", f"s_stg{s_}", [], [stg_tt[s_]], out=stg[s_][0:NS, 0:1024], in_=d["xs"][:, half * 1024:(half + 1) * 1024])
                    for kk in range(8):
                        k = half * 8 + kk
                        b = P.next_ps()
                        P.tr(ps_tt[b], psb[b][:, 0:NS], stg[s_][0:NS, kk * 128:(kk + 1) * 128], ident[0:NS, 0:NS], reads=[stg_tt[s_], c_tt])
                        P.copy(P.ev_eng(), xT[:, k, TP:TP + NS], psb[b][:, 0:NS], reads=[ps_tt[b]], writes=[x_tt[k]])
            for i in range(cfg["n_layers"]):
                if i % 2 == 0:
                    pool_layer(i, j, T)
                else:
                    nsa_layer(i, j, T)
            for tb in range(TP // 128):
                out_rows(lambda c, tb=tb: xT[:, c, tb * 128:(tb + 1) * 128], lambda c: [x_tt[c]], D, 128,
                         lambda f0, w_, tb=tb: o["y_p"][j * TP + tb * 128:j * TP + (tb + 1) * 128, f0:f0 + w_])
            if j == 0:
                out_rows(lambda c: xT[:, c, TP:TP + NS], lambda c: [x_tt[c]], D, NS, lambda f0, w_: o["y_s"][:, f0:f0 + w_])
            w_fence()

        self.sbuf_left = self.nc.sbuf_bytes_remaining
        self.s.final_waits()
        self.emit()
        return nc

    def emit(self):
        nc = self.nc
        s = self.s
        sems = {}
        for name in list(Sched.ENGS) + sorted(self.sems_needed):
            sems[name] = self.es.enter_context(nc.semaphore(name))

        def replay(eng, e):
            for (waits, fn, tok, inc) in s.ops[eng]:
                for (sn, v) in waits:
                    e.wait_ge(sems[sn], v)
                if fn is not None:
                    ins = fn(e)
                    ins.then_inc(sems[tok[0]], inc)

        with nc.Block() as block:
            @block.tensor
            def _(e):
                replay("pe", e)

            @block.scalar
            def _(e):
                replay("act", e)

            @block.vector
            def _(e):
                replay("dve", e)

            @block.gpsimd
            def _(e):
                replay("pool", e)

            @block.sync
            def _(e):
                replay("sp", e)
        self.es.close()


def make_consts():
    inv = np.zeros((128, 4, 16), np.float32)
    for g, w in enumerate((2, 4, 8, 16)):
        for t in range(16):
            inv[:, g, t] = 1.0 / min(t + 1, w)
    return inv.reshape(128, 64)


def make_seltab():
    tab = np.zeros((SEQ // 128, 128, 128), np.float32)
    b = np.arange(64)[None, :]
    for Q in range(SEQ // 128):
        pos = Q * 128 + np.arange(128)[:, None]
        cur = pos // 64
        forced = (b == 0) | (b == cur) | (b == cur - 1)
        allowed = b <= cur
        tab[Q, :, :64] = (allowed & ~forced).astype(np.float32)
        tab[Q, :, 64:] = np.where(forced, BIGV, np.where(allowed, 0.0, NEGV)).astype(np.float32)
    return tab


def kernel(x_prompt, x_sample, mem_prompt, cache_mem_kv, cache_nsa_kv, cache_nsa_win, state_pool, page_table,
           g_norm_mix, g_norm_mlp, g_norm_mem, w_mem_kv, mem_qk_gain, w_out, w_mlp_up, w_mlp_down, w_in_pool,
           w_pool_group, pool_scale, w_in_nsa, nsa_gate_bias, nsa_qk_gain, cmp_pos, cmp_w1, cmp_w2, _cfg=None):
    import time as _t
    cfg = dict(CFG)
    if _cfg:
        cfg.update(_cfg)
    f = lambda a: np.ascontiguousarray(np.asarray(a))
    _t0 = _t.time()
    prog = Prog(cfg)
    nc = prog.build()
    print('build_s', _t.time() - _t0, {e: len(v) for e, v in prog.s.ops.items()}, flush=True)
    NL = cfg["n_layers"]; NPL = (NL + 1) // 2; NNL = max(NL // 2, 1)
    nsakv = f(np.asarray(cache_nsa_kv).reshape(2, 1280 * 128, 1024)[:NNL, :cfg.get("nsakv_rows", 1280 * 128)]).reshape(NNL, -1, 512)
    shared = {
        "g_mix": f(g_norm_mix), "g_mlp": f(g_norm_mlp), "g_mem": f(g_norm_mem),
        "w_mem_kv": f(w_mem_kv[:NL]), "mem_qk": f(mem_qk_gain), "w_out": f(w_out[:NL]), "w_up": f(w_mlp_up[:NL]),
        "w_down": f(w_mlp_down[:NL]), "w_in_pool": f(w_in_pool[:NPL]), "w_pg": f(w_pool_group), "pool_scale": f(pool_scale),
        "w_in_nsa": f(w_in_nsa[:NNL]), "gate_bias": f(nsa_gate_bias), "nsa_qk": f(nsa_qk_gain), "cmp_pos": f(cmp_pos),
        "cmp_w1": f(cmp_w1), "cmp_w2": f(cmp_w2), "cst": make_consts(), "seltab": make_seltab(),
    }
    for l_ in range(NNL):
        shared[f"nsakv{l_}"] = nsakv[l_]
    in_maps = []
    for c in range(NCORE):
        sl = slice(NSB * c, NSB * (c + 1))
        m = dict(shared)
        m["xp"] = f(x_prompt[c])
        m["xs"] = f(np.asarray(x_sample)[sl]).reshape(NS, D)
        m["memp"] = f(mem_prompt[c])
        m["cmkv"] = f(np.asarray(cache_mem_kv)[:NL, sl]).reshape(NL, NSB, 256, 1024)
        m["nsawin"] = f(np.asarray(cache_nsa_win)[:, sl]).reshape(2, NSB, 512, 512)
        m["spool"] = f(np.asarray(state_pool)[:, sl])
        m["ptab"] = f(np.asarray(page_table)[sl]).astype(np.int32)
        in_maps.append(m)
    _t0 = _t.time()
    res = run_bass_kernel_spmd(nc, in_maps, core_ids=list(range(NCORE)), **({'trace': True} if cfg.get('trace') else {}))
    if cfg.get('trace'):
        print('EXEC_NS', res.exec_time_ns, flush=True)
    print('run_s', _t.time() - _t0, flush=True)
    R = res.results
    y_p = np.stack([R[b]["y_p"] for b in range(2)])
    y_s = np.concatenate([R[c]["y_s"].reshape(NSB, NS1, D) for c in range(NCORE)], axis=0)
    mkv = np.stack([R[b]["mkv"] for b in range(2)], axis=1).reshape(DEPTH, 2, 256, 2, 4, HD)
    nkv_p = np.stack([R[b]["nkv_p"] for b in range(2)], axis=1).reshape(2, 2, SEQ, 4, 2, HD)
    nkv_s = np.concatenate([R[c]["nkv_s"].reshape(2, NSB, NS1, 1024) for c in range(NCORE)], axis=1).reshape(2, 8, NS1, 4, 2, HD)
    win_p = np.stack([R[b]["win_p"] for b in range(2)], axis=1).reshape(2, 2, 512, 2, 2, HD)
    win_s = np.concatenate([R[c]["win_s"] for c in range(NCORE)], axis=1).reshape(2, 8, 512, 2, 2, HD)
    pool_p = np.stack([R[b]["pool_p"] for b in range(2)], axis=1)
    pool_s = np.concatenate([R[c]["pool_s"] for c in range(NCORE)], axis=1)
    return (y_p, y_s, mkv, nkv_p, nkv_s, win_p, win_s, pool_p, pool_s)
```
